# Optimizing a Trainium2 kernel written in Bass

```python
import jax, jax.numpy as jnp
from jax import lax
import numpy as np

D_MODEL = 2048
BATCH = 8
SEQ = 4096
DEPTH = 4

GRID_W = 64
CTX_LEN = 256
N_MIXERS = 3
N_LAYERS_A = (DEPTH + 2) // 3
N_LAYERS_B = (DEPTH + 1) // 3
N_LAYERS_C = DEPTH // 3
N_MOD = 6
DEEPNORM_ALPHA = (2 * DEPTH) ** 0.25
DEEPNORM_BETA = (8 * DEPTH) ** -0.25
LN_EPS = 1e-6
A_HEADS = 4
A_DV = D_MODEL // A_HEADS
A_DK = A_DV // 2
A_CHUNK = 128
GATE_CAP = 15.0
A_IN = 2 * A_HEADS * A_DK + A_HEADS * A_DV + D_MODEL + 4 * A_HEADS
B_CHUNK = 128
B_INNER = 2 * D_MODEL
B_GROUPS = 8
C_HEADS = 16
C_KV_HEADS = 4
C_HEAD_DIM = D_MODEL // C_HEADS
C_QBLOCK = 128
ROPE_THETA = 10000.0
P_HEADS = 8
P_NKEYS = 128
P_EXPERTS = P_NKEYS * P_NKEYS
P_TOPK = 16
P_DQ = 256
P_BLOCK = 128

kernel_name = 'hybrid_mlstm_gmlp_gqa_peer_dit'


def layer_norm(x, g, b):
    xf = x.astype(jnp.float32)
    mu = jnp.mean(xf, -1, keepdims=True)
    var = jnp.mean(jnp.square(xf - mu), -1, keepdims=True)
    return ((xf - mu) * lax.rsqrt(var + LN_EPS) * g.astype(jnp.float32) + b.astype(jnp.float32)).astype(x.dtype)


def rms_norm(x, g):
    xf = x.astype(jnp.float32)
    return (xf * lax.rsqrt(jnp.mean(jnp.square(xf), -1, keepdims=True) + LN_EPS) * g.astype(jnp.float32)).astype(x.dtype)


def modulate(x, shift, scale):
    return x * (1 + scale) + shift


def _mlstm_project(h, w_in, b_gate):
    bsz, n, _ = h.shape
    z = h @ w_in
    qk = A_HEADS * A_DK
    vd = A_HEADS * A_DV
    q, k, v, o, g = jnp.split(z, [qk, 2 * qk, 2 * qk + vd, 2 * qk + vd + D_MODEL], axis=-1)
    def heads(t, d):
        return t.reshape(bsz, n, A_HEADS, d).transpose(0, 2, 1, 3).astype(jnp.float32)
    q = heads(q, A_DK)
    k = heads(k, A_DK) * (A_DK ** -0.5)
    v = heads(v, A_DV)
    g = g.astype(jnp.float32) + b_gate.astype(jnp.float32)
    g = GATE_CAP * jnp.tanh(g / GATE_CAP)
    g = g.reshape(bsz, n, 4, A_HEADS).transpose(2, 0, 3, 1)
    fwd = (g[0], jax.nn.log_sigmoid(g[1]))
    bwd = (g[2], jax.nn.log_sigmoid(g[3]))
    return q, k, v, o, fwd, bwd


def mlstm_scan(q, k, v, li, lf, state):
    bsz, nh, n, _ = q.shape
    nc = n // A_CHUNK
    def chunks(t):
        return jnp.moveaxis(t.reshape(bsz, nh, nc, A_CHUNK, *t.shape[3:]), 2, 0)
    mask = jnp.tril(jnp.ones((A_CHUNK, A_CHUNK), dtype=bool))
    def body(carry, inp):
        cmat, nvec, m = carry
        qc, kc, vc, lic, lfc = inp
        b = jnp.cumsum(lfc, axis=-1)
        g = b[..., -1]
        dmat = jnp.where(mask, b[..., :, None] - b[..., None, :] + lic[..., None, :], -jnp.inf)
        inter = b + m[..., None]
        mt = jnp.maximum(inter, jnp.max(dmat, -1))
        w_intra = jnp.exp(dmat - mt[..., None])
        w_inter = jnp.exp(inter - mt)
        s = jnp.einsum('bhtd,bhsd->bhts', qc, kc) * w_intra
        num = w_inter[..., None] * jnp.einsum('bhvd,bhtd->bhtv', cmat, qc) + jnp.einsum('bhts,bhsv->bhtv', s, vc)
        den = w_inter * jnp.einsum('bhd,bhtd->bht', nvec, qc) + jnp.sum(s, -1)
        h = num / jnp.maximum(jnp.abs(den), jnp.exp(-mt))[..., None]
        wk = g[..., None] - b + lic
        m_new = jnp.maximum(g + m, jnp.max(wk, -1))
        decay = jnp.exp(g + m - m_new)
        ws = jnp.exp(wk - m_new[..., None])
        cmat = decay[..., None, None] * cmat + jnp.einsum('bhsv,bhsd->bhvd', vc * ws[..., None], kc)
        nvec = decay[..., None] * nvec + jnp.einsum('bhs,bhsd->bhd', ws, kc)
        return (cmat, nvec, m_new), h
    state, hs = lax.scan(body, state, (chunks(q), chunks(k), chunks(v), chunks(li), chunks(lf)))
    hs = jnp.moveaxis(hs, 0, 2).reshape(bsz, nh, n, -1)
    return hs, state


def _mlstm_out(hsum, o, norm_g, w_out, dtype):
    bsz, nh, n, dv = hsum.shape
    hh = jnp.transpose(hsum, (0, 2, 1, 3))
    hh = hh * lax.rsqrt(jnp.mean(jnp.square(hh), -1, keepdims=True) + LN_EPS)
    hh = hh.reshape(bsz, n, nh * dv) * norm_g.astype(jnp.float32) * jax.nn.sigmoid(o.astype(jnp.float32))
    return hh.astype(dtype) @ w_out


def mlstm_mixer(h_lat, h_ctx, w_in, b_gate, norm_g, w_out, need_ctx):
    ql, kl, vl, ol, gfl, gbl = _mlstm_project(h_lat, w_in, b_gate)
    qc, kc, vc, oc, gfc, gbc = _mlstm_project(h_ctx, w_in, b_gate)
    bsz = h_lat.shape[0]
    zero = (jnp.zeros((bsz, A_HEADS, A_DV, A_DK), jnp.float32),
            jnp.zeros((bsz, A_HEADS, A_DK), jnp.float32),
            jnp.zeros((bsz, A_HEADS), jnp.float32))
    def flip(t):
        return jnp.flip(t, axis=2)
    hcf, st_f = mlstm_scan(qc, kc, vc, gfc[0], gfc[1], zero)
    hcb, st_b = mlstm_scan(flip(qc), flip(kc), flip(vc), flip(gbc[0]), flip(gbc[1]), zero)
    hlf, _ = mlstm_scan(ql, kl, vl, gfl[0], gfl[1], st_f)
    hlb, _ = mlstm_scan(flip(ql), flip(kl), flip(vl), flip(gbl[0]), flip(gbl[1]), st_b)
    y_lat = _mlstm_out(hlf + flip(hlb), ol, norm_g, w_out, h_lat.dtype)
    y_ctx = _mlstm_out(hcf + flip(hcb), oc, norm_g, w_out, h_ctx.dtype) if need_ctx else None
    return y_lat, y_ctx


def chunk_mlp_mixer(h_lat, h_ctx, w_in, b_in, ln_g_v, ln_b_v, w_s, b_s, w_out, need_ctx):
    def apply(h):
        bsz, n, _ = h.shape
        z = jax.nn.gelu(h @ w_in + b_in)
        u, v = z[..., :B_INNER], z[..., B_INNER:]
        v = layer_norm(v, ln_g_v, ln_b_v).reshape(bsz, n // B_CHUNK, B_CHUNK, B_GROUPS, B_INNER // B_GROUPS)
        sv = jnp.einsum('gts,bcsgd->bctgd', w_s, v) + b_s.T[:, :, None]
        return (u * sv.reshape(bsz, n, B_INNER)) @ w_out
    return apply(h_lat), (apply(h_ctx) if need_ctx else None)


def rope_1d(x, pos):
    half = x.shape[-1] // 2
    freqs = ROPE_THETA ** (-jnp.arange(half, dtype=jnp.float32) / half)
    ang = pos.astype(jnp.float32)[:, None] * freqs
    cos = jnp.cos(ang)[:, None, :]
    sin = jnp.sin(ang)[:, None, :]
    x1, x2 = x[..., :half], x[..., half:]
    return jnp.concatenate([x1 * cos - x2 * sin, x1 * sin + x2 * cos], -1)


def axial_rope(x, rows, cols):
    xf = x.astype(jnp.float32)
    half = x.shape[-1] // 2
    return jnp.concatenate([rope_1d(xf[..., :half], rows), rope_1d(xf[..., half:], cols)], -1).astype(x.dtype)


def gqa_mixer(h_lat, h_ctx, w_qkv, q_g, k_g, w_out, rows, cols, need_ctx):
    n_grp = C_HEADS // C_KV_HEADS
    scale = C_HEAD_DIM ** -0.5
    def project(h):
        bsz, n, _ = h.shape
        q, k, v = jnp.split(h @ w_qkv, [C_HEADS * C_HEAD_DIM, (C_HEADS + C_KV_HEADS) * C_HEAD_DIM], axis=-1)
        q = rms_norm(q.reshape(bsz, n, C_HEADS, C_HEAD_DIM), q_g)
        k = rms_norm(k.reshape(bsz, n, C_KV_HEADS, C_HEAD_DIM), k_g)
        return q, k, v.reshape(bsz, n, C_KV_HEADS, C_HEAD_DIM)
    def attend(q, k, v):
        s = jnp.einsum('bkgqd,bskd->bkgqs', q, k).astype(jnp.float32) * scale
        p = jax.nn.softmax(s, axis=-1).astype(v.dtype)
        return jnp.einsum('bkgqs,bskd->bqkgd', p, v)
    ql, kl, vl = project(h_lat)
    qc, kc, vc = project(h_ctx)
    ql = axial_rope(ql, rows, cols)
    kl = axial_rope(kl, rows, cols)
    k_all = jnp.concatenate([kl, kc], axis=1)
    v_all = jnp.concatenate([vl, vc], axis=1)
    bsz, n = h_lat.shape[:2]
    qb = ql.reshape(bsz, n // C_QBLOCK, C_QBLOCK, C_KV_HEADS, n_grp, C_HEAD_DIM).transpose(1, 0, 3, 4, 2, 5)
    ob = lax.map(lambda qblk: attend(qblk, k_all, v_all), qb)
    y_lat = ob.transpose(1, 0, 2, 3, 4, 5).reshape(bsz, n, D_MODEL) @ w_out
    y_ctx = None
    if need_ctx:
        n_c = h_ctx.shape[1]
        qcr = qc.reshape(bsz, n_c, C_KV_HEADS, n_grp, C_HEAD_DIM).transpose(0, 2, 3, 1, 4)
        y_ctx = attend(qcr, kc, vc).reshape(bsz, n_c, D_MODEL) @ w_out
    return y_lat, y_ctx


def peer_ffn(h, w_q, k1, k2, w_u, w_v):
    shp = h.shape
    tok = h.reshape(-1, P_BLOCK, shp[-1])
    half = P_DQ // 2
    def block(hb):
        q = (hb @ w_q).reshape(P_BLOCK, P_HEADS, P_DQ)
        s1 = jnp.einsum('thd,kd->thk', q[..., :half], k1).astype(jnp.float32)
        s2 = jnp.einsum('thd,kd->thk', q[..., half:], k2).astype(jnp.float32)
        v1, i1 = lax.top_k(s1, P_TOPK)
        v2, i2 = lax.top_k(s2, P_TOPK)
        cand = (v1[..., :, None] + v2[..., None, :]).reshape(P_BLOCK, P_HEADS, P_TOPK * P_TOPK)
        sc, ci = lax.top_k(cand, P_TOPK)
        e = jnp.take_along_axis(i1, ci // P_TOPK, -1) * P_NKEYS + jnp.take_along_axis(i2, ci % P_TOPK, -1)
        gate = jax.nn.softmax(sc, axis=-1)
        u = jnp.take(w_u, e, axis=0)
        a = jnp.einsum('td,thkd->thk', hb, u)
        hid = (jax.nn.gelu(a.astype(jnp.float32)) * gate).astype(hb.dtype)
        vv = jnp.take(w_v, e, axis=0)
        return jnp.einsum('thk,thkd->td', hid, vv)
    return lax.map(block, tok).reshape(shp)


def setup_inputs(seed: int = 0) -> dict:
    key = jax.random.key(seed)
    ks = iter(jax.random.split(key, 40))
    def nrm(shape, scale):
        return jax.random.normal(next(ks), shape, jnp.float32) * scale
    D = D_MODEL
    f_bias = jnp.linspace(3.0, 6.0, A_HEADS, dtype=jnp.float32)
    zeros_h = jnp.zeros((A_HEADS,), jnp.float32)
    gate_base = jnp.stack([zeros_h, f_bias, zeros_h, f_bias])
    return {
        'x': nrm((BATCH, SEQ, D), 1.0),
        'c': nrm((BATCH, D), 1.0),
        'ctx': nrm((BATCH, CTX_LEN, D), 1.0),
        'c_ctx': nrm((D,), 1.0),
        'w_mod': nrm((DEPTH, D, N_MOD * D), 0.5 * D ** -0.5),
        'b_mod': nrm((DEPTH, N_MOD * D), 0.02),
        'ln_g': 1.0 + nrm((DEPTH, 2, D), 0.02),
        'ln_b': nrm((DEPTH, 2, D), 0.02),
        'a_w_in': nrm((N_LAYERS_A, D, A_IN), D ** -0.5),
        'a_b_gate': (gate_base[None] + nrm((N_LAYERS_A, 4, A_HEADS), 0.1)).reshape(N_LAYERS_A, 4 * A_HEADS),
        'a_norm_g': 1.0 + nrm((N_LAYERS_A, A_HEADS * A_DV), 0.02),
        'a_w_out': nrm((N_LAYERS_A, A_HEADS * A_DV, D), (A_HEADS * A_DV) ** -0.5 * DEEPNORM_BETA),
        'b_w_in': nrm((N_LAYERS_B, D, 2 * B_INNER), D ** -0.5),
        'b_b_in': nrm((N_LAYERS_B, 2 * B_INNER), 0.02),
        'b_ln_g': 1.0 + nrm((N_LAYERS_B, B_INNER), 0.02),
        'b_ln_b': nrm((N_LAYERS_B, B_INNER), 0.02),
        'b_w_s': nrm((N_LAYERS_B, B_GROUPS, B_CHUNK, B_CHUNK), B_CHUNK ** -0.5),
        'b_b_s': 1.0 + nrm((N_LAYERS_B, B_GROUPS, B_CHUNK), 0.02),
        'b_w_out': nrm((N_LAYERS_B, B_INNER, D), B_INNER ** -0.5 * DEEPNORM_BETA),
        'c_w_qkv': nrm((N_LAYERS_C, D, (C_HEADS + 2 * C_KV_HEADS) * C_HEAD_DIM), D ** -0.5),
        'c_q_g': 1.0 + nrm((N_LAYERS_C, C_HEAD_DIM), 0.02),
        'c_k_g': 1.0 + nrm((N_LAYERS_C, C_HEAD_DIM), 0.02),
        'c_w_out': nrm((N_LAYERS_C, C_HEADS * C_HEAD_DIM, D), (C_HEADS * C_HEAD_DIM) ** -0.5 * DEEPNORM_BETA),
        'p_w_q': nrm((DEPTH, D, P_HEADS * P_DQ), D ** -0.5),
        'p_k1': nrm((DEPTH, P_NKEYS, P_DQ // 2), (P_DQ // 2) ** -0.5),
        'p_k2': nrm((DEPTH, P_NKEYS, P_DQ // 2), (P_DQ // 2) ** -0.5),
        'p_u': nrm((DEPTH, P_EXPERTS, D), D ** -0.5),
        'p_v': nrm((DEPTH, P_EXPERTS, D), DEEPNORM_BETA),
    }


def reference(x, c, ctx, c_ctx, w_mod, b_mod, ln_g, ln_b, a_w_in, a_b_gate, a_norm_g, a_w_out,
              b_w_in, b_b_in, b_ln_g, b_ln_b, b_w_s, b_b_s, b_w_out, c_w_qkv, c_q_g, c_k_g, c_w_out,
              p_w_q, p_k1, p_k2, p_u, p_v):
    bsz, n, _ = x.shape
    grid_rows = n // GRID_W
    rows = jnp.repeat(jnp.arange(grid_rows, dtype=jnp.int32), GRID_W)
    cols = jnp.tile(jnp.arange(GRID_W, dtype=jnp.int32), grid_rows)
    cond_lat = jax.nn.silu(c)
    cond_ctx = jax.nn.silu(c_ctx)
    x_lat, x_ctx = x, ctx
    for i in range(DEPTH):
        need_ctx = i < DEPTH - 1
        mixer, j = i % N_MIXERS, i // N_MIXERS
        mod_l = (cond_lat @ w_mod[i] + b_mod[i]).reshape(bsz, N_MOD, 1, D_MODEL)
        mod_c = (cond_ctx @ w_mod[i] + b_mod[i]).reshape(N_MOD, D_MODEL)
        h_l = modulate(x_lat, mod_l[:, 0], mod_l[:, 1])
        h_c = modulate(x_ctx, mod_c[0], mod_c[1])
        if mixer == 0:
            y_l, y_c = mlstm_mixer(h_l, h_c, a_w_in[j], a_b_gate[j], a_norm_g[j], a_w_out[j], need_ctx)
        elif mixer == 1:
            y_l, y_c = chunk_mlp_mixer(h_l, h_c, b_w_in[j], b_b_in[j], b_ln_g[j], b_ln_b[j], b_w_s[j], b_b_s[j],
                                       b_w_out[j], need_ctx)
        else:
            y_l, y_c = gqa_mixer(h_l, h_c, c_w_qkv[j], c_q_g[j], c_k_g[j], c_w_out[j], rows, cols, need_ctx)
        x_lat = layer_norm(DEEPNORM_ALPHA * x_lat + mod_l[:, 2] * y_l, ln_g[i, 0], ln_b[i, 0])
        if need_ctx:
            x_ctx = layer_norm(DEEPNORM_ALPHA * x_ctx + mod_c[2] * y_c, ln_g[i, 0], ln_b[i, 0])
        h_l = modulate(x_lat, mod_l[:, 3], mod_l[:, 4])
        f_l = peer_ffn(h_l, p_w_q[i], p_k1[i], p_k2[i], p_u[i], p_v[i])
        x_lat = layer_norm(DEEPNORM_ALPHA * x_lat + mod_l[:, 5] * f_l, ln_g[i, 1], ln_b[i, 1])
        if need_ctx:
            h_c = modulate(x_ctx, mod_c[3], mod_c[4])
            f_c = peer_ffn(h_c, p_w_q[i], p_k1[i], p_k2[i], p_u[i], p_v[i])
            x_ctx = layer_norm(DEEPNORM_ALPHA * x_ctx + mod_c[5] * f_c, ln_g[i, 1], ln_b[i, 1])
    return x_lat
```

```python
import numpy as np
from contextlib import ExitStack
import concourse.bass as bass
import concourse.mybir as mybir
from concourse.bass_utils import run_bass_kernel_spmd

F32 = mybir.dt.float32
BF16 = mybir.dt.bfloat16
U32 = mybir.dt.uint32
I32 = mybir.dt.int32
AF = mybir.ActivationFunctionType
ALU = mybir.AluOpType
AX = mybir.AxisListType

D = 2048
KC = 16
SEQ = 4096
CTX = 256
T = SEQ + CTX
NT = T // 128
NTL = SEQ // 128
DEPTH = 4
ALPHA = (2 * DEPTH) ** 0.25
EPS = 1e-6
A_IN = 6160
NEG = -1.0e30


class Buf:
    __slots__ = ("ap", "w", "r")

    def __init__(self, ap):
        self.ap = ap
        self.w = None
        self.r = {}


class Prog:
    LIMIT = 30000

    def __init__(self, nc):
        self.nc = nc
        self.engs = {"pe": nc.tensor, "dve": nc.vector, "act": nc.scalar, "pool": nc.gpsimd, "sp": nc.sync}
        self.sems = {}
        self.cnt = {}
        self.key = {}
        self.gen = {}
        for e in ("pe", "dve", "act", "pool"):
            self.gen[e] = 0
            self._newkey(e)
        self.clock = {e: {} for e in self.engs}
        self.dq = {}
        for q, n in (("sp", 16), ("pool", 16), ("act", 6)):
            keys = []
            for i in range(n):
                k = "d_%s%d" % (q, i)
                self.sems[k] = nc.alloc_semaphore(name=k)
                self.cnt[k] = 0
                keys.append(k)
            self.dq[q] = [keys, 0]
        self.n_inst = 0

    def _newkey(self, e):
        k = "%s_%d" % (e, self.gen[e])
        self.gen[e] += 1
        self.sems[k] = self.nc.alloc_semaphore(name="s_" + k)
        self.cnt[k] = 0
        self.key[e] = k

    def _wait(self, e, tok):
        if tok is None:
            return
        k, v = tok
        if self.clock[e].get(k, 0) >= v:
            return
        if e == "pe" and k.startswith("pe_"):
            return
        self.engs[e].wait_ge(self.sems[k], v)
        self.clock[e][k] = v

    def _deps(self, e, reads, writes):
        for b in reads:
            self._wait(e, b.w)
        for b in writes:
            self._wait(e, b.w)
            for kv in list(b.r.items()):
                self._wait(e, kv)

    def _mark(self, tok, reads, writes):
        k, v = tok
        for b in reads:
            b.r[k] = v
        for b in writes:
            b.w = tok
            b.r = {}

    def op(self, e, fn, reads=(), writes=()):
        self._deps(e, reads, writes)
        inst = fn(self.engs[e])
        k = self.key[e]
        inst.then_inc(self.sems[k], 1)
        self.cnt[k] += 1
        tok = (k, self.cnt[k])
        self._mark(tok, reads, writes)
        if self.cnt[k] >= self.LIMIT:
            self._newkey(e)
        self.n_inst += 1
        return tok

    def dma(self, q, out, in_, reads=(), writes=(), indirect=None, **kw):
        keys, rr = self.dq[q]
        k = keys[rr % len(keys)]
        self.dq[q][1] += 1
        if self.cnt[k] > 0:
            self._wait(q, (k, self.cnt[k]))
        self._deps(q, reads, writes)
        eng = self.engs[q]
        if indirect is not None:
            inst = eng.indirect_dma_start(out=out, out_offset=None, in_=in_, in_offset=indirect, **kw)
        else:
            inst = eng.dma_start(out=out, in_=in_, **kw)
        inst.then_inc(self.sems[k], 16)
        self.cnt[k] += 16
        if self.cnt[k] >= self.LIMIT:
            self._wait(q, (k, self.cnt[k]))
            tok = (k, self.cnt[k])
            self._mark(tok, reads, writes)
            nk = k + "n"
            self.sems[nk] = self.nc.alloc_semaphore(name=nk)
            self.cnt[nk] = 0
            keys[(rr) % len(keys)] = nk
            self.n_inst += 1
            return tok
        tok = (k, self.cnt[k])
        self._mark(tok, reads, writes)
        self.n_inst += 1
        return tok

    def barrier(self):
        for e in self.engs:
            for k, v in self.cnt.items():
                if v > 0:
                    self._wait(e, (k, v))

    def final_wait(self, e="sp"):
        for k, v in self.cnt.items():
            if v > 0:
                self._wait(e, (k, v))


def rsqrt_eps(P, dst, src_ap, reads, scale):
    P.op("dve", lambda e: e.tensor_scalar(out=dst.ap, in0=src_ap, scalar1=float(scale), scalar2=float(EPS), op0=ALU.mult, op1=ALU.add),
         reads=list(reads), writes=[dst])
    P.op("act", lambda e: e.activation(out=dst.ap, in_=dst.ap, func=AF.Sqrt), reads=[dst], writes=[dst])
    P.op("dve", lambda e: e.reciprocal(out=dst.ap, in_=dst.ap), reads=[dst], writes=[dst])


def bcast_rows(ap_row, n):
    return ap_row.to_broadcast([n, ap_row.shape[-1]])


class K:
    def __init__(self, n_layers=DEPTH, dbg=()):
        self.n_layers = n_layers
        self.dbg = set(dbg)
        nc = bass.Bass("TRN2", target_bir_lowering=False)
        self.nc = nc
        self.P = Prog(nc)
        self.es = ExitStack()
        self.inp = {}

    def din(self, name, shape, dt=F32):
        ap = self.nc.dram_tensor(name, list(shape), dt, kind="ExternalInput").ap()
        self.inp[name] = ap
        return ap

    def dscr(self, name, shape, dt=F32, out=False):
        kind = "ExternalOutput" if (out or name in self.dbg) else "Internal"
        return self.nc.dram_tensor(name, list(shape), dt, kind=kind).ap()

    def sb(self, es, name, shape, dt=F32):
        self.uid = getattr(self, "uid", 0) + 1
        t = es.enter_context(self.nc.sbuf_tensor("%s_%d" % (name, self.uid), list(shape), dt))
        return t

    def psum(self, es, name, shape, dt=F32):
        return es.enter_context(self.nc.psum_tensor(name, list(shape), dt))


def build_program(n_layers=DEPTH, dbg=(), peer=True):
    k = K(n_layers, dbg)
    nc, P = k.nc, k.P
    xin = k.din("xin", [T, D])
    cc = k.din("cc", [2, D])
    w_mod = k.din("w_mod", [DEPTH, D, 6 * D])
    b_mod = k.din("b_mod", [DEPTH, 6 * D])
    ln_g = k.din("ln_g", [DEPTH, 2, D])
    ln_b = k.din("ln_b", [DEPTH, 2, D])
    a_w_in = k.din("a_w_in", [2, D, A_IN])
    a_b_gate = k.din("a_b_gate", [2, 16])
    a_norm_g = k.din("a_norm_g", [2, D])
    a_w_out = k.din("a_w_out", [2, D, D])
    b_w_in = k.din("b_w_in", [1, D, 8192])
    b_b_in = k.din("b_b_in", [1, 8192])
    b_ln_g = k.din("b_ln_g", [1, 4096])
    b_ln_b = k.din("b_ln_b", [1, 4096])
    b_w_s = k.din("b_w_s", [1, 8, 128, 128])
    b_b_s = k.din("b_b_s", [1, 8, 128])
    b_w_out = k.din("b_w_out", [1, 4096, D])
    c_w_qkv = k.din("c_w_qkv", [1, D, 3072])
    c_q_g = k.din("c_q_g", [1, 128])
    c_k_g = k.din("c_k_g", [1, 128])
    c_w_out = k.din("c_w_out", [1, D, D])
    p_w_q = k.din("p_w_q", [DEPTH, D, D])
    p_k1 = k.din("p_k1", [DEPTH, 128, 128])
    p_k2 = k.din("p_k2", [DEPTH, 128, 128])
    p_u = k.din("p_u", [DEPTH, 16384, D])
    p_v = k.din("p_v", [DEPTH, 16384, D])
    ident_d = k.din("ident", [128, 128])
    tri_d = k.din("tri", [2, 128, 128])
    rope_d = k.din("rope", [SEQ, 2, 64])
    iota_d = k.din("iota16", [1, 16])

    yout = k.dscr("y", [SEQ, D], out=True)
    xres = k.dscr("xres", [T, D])
    modd = k.dscr("modd", [DEPTH, 2, 6 * D])
    hT_d = k.dscr("hT", [NT, 128, KC, 128], BF16)
    h_d = k.dscr("h_tm", [T, D])
    ymix = k.dscr("ymix", [T, D])
    S = dict(k=k, nc=nc, P=P, xin=xin, xres=xres, modd=modd, hT_d=hT_d, h_d=h_d, ymix=ymix, yout=yout)

    with ExitStack() as top:
        ident = Buf(k.sb(top, "ident_sb", [128, 128])[:])
        identb = Buf(k.sb(top, "identb_sb", [128, 128], BF16)[:])
        P.dma("sp", ident.ap, ident_d[:, :], writes=[ident])
        P.op("dve", lambda e: e.tensor_copy(out=identb.ap, in_=ident.ap), reads=[ident], writes=[identb])
        ps_t = [k.psum(top, "ps%d" % i, [128, 512]) for i in range(8)]
        ps = [Buf(t[:]) for t in ps_t]
        S.update(ident=ident, identb=identb, ps=ps, ps_t=ps_t)

        phase_mod(S, cc, w_mod, b_mod, n_layers)
        P.barrier()
        for li in range(n_layers):
            need_ctx = li < DEPTH - 1
            nt_out = NT if need_ctx else NTL
            src = xin if li == 0 else xres
            mixer, j = li % 3, li // 3
            phase_prep(S, src, li, 0, NT, want_h=False)
            P.barrier()
            if mixer == 0:
                mix_mlstm(S, j, a_w_in, a_b_gate, a_norm_g, a_w_out, tri_d, need_ctx)
            elif mixer == 1:
                mix_gmlp(S, j, b_w_in, b_b_in, b_ln_g, b_ln_b, b_w_s, b_b_s, b_w_out, need_ctx)
            else:
                mix_gqa(S, j, c_w_qkv, c_q_g, c_k_g, c_w_out, rope_d, need_ctx)
            P.barrier()
            phase_ln(S, src, ymix, xres, li, 0, ln_g, ln_b, nt_out)
            P.barrier()
            if "x_mid%d" % li in k.dbg:
                dump(S, xres, "x_mid%d" % li, T)
            phase_prep(S, xres, li, 1, nt_out, want_h=True)
            P.barrier()
            phase_peer(S, li, p_w_q, p_k1, p_k2, p_u, p_v, iota_d, nt_out, enable=peer)
            P.barrier()
            last = (li == n_layers - 1)
            phase_ln(S, xres, ymix, (yout if last else xres), li, 1, ln_g, ln_b, nt_out if not last else NTL)
            P.barrier()
        P.final_wait("sp")
        P.final_wait("act")
    return nc, k


def dump(S, src, name, rows):
    k, P = S["k"], S["P"]
    dst = k.dscr(name, [rows, D], out=True)
    for t in range(rows // 128):
        P.dma("sp", dst[t * 128:(t + 1) * 128, :], src[t * 128:(t + 1) * 128, :])
    P.barrier()


def phase_mod(S, cc, w_mod, b_mod, n_layers):
    k, P, ps, modd = S["k"], S["P"], S["ps"], S["modd"]
    with ExitStack() as es:
        craw = Buf(k.sb(es, "craw", [128, 2, KC])[:])
        condT = Buf(k.sb(es, "condT", [128, KC, 2])[:])
        P.dma("sp", craw.ap, cc.rearrange("r (kc p) -> p r kc", p=128), writes=[craw], allow_slow_non_contiguous=True)
        P.op("act", lambda e: e.activation(out=condT.ap.rearrange("p kc r -> p r kc"), in_=craw.ap, func=AF.Silu),
             reads=[craw], writes=[condT])
        wbuf = [Buf(k.sb(es, "wm%d" % i, [128, KC, 512])[:]) for i in range(2)]
        bbuf = [Buf(k.sb(es, "bm%d" % i, [2, 512])[:]) for i in range(2)]
        obuf = [Buf(k.sb(es, "om%d" % i, [2, 512])[:]) for i in range(2)]
        it = 0
        for li in range(n_layers):
            for nci in range(24):
                w, bb, ob, pt = wbuf[it % 2], bbuf[it % 2], obuf[it % 2], ps[it % 2]
                n0 = nci * 512
                P.dma("sp" if it % 2 == 0 else "act", w.ap, w_mod[li, :, n0:n0 + 512].rearrange("(kc p) n -> p kc n", p=128), writes=[w])
                P.dma("sp", bb.ap, b_mod[li:li + 1, n0:n0 + 512].to_broadcast([2, 512]), writes=[bb])
                for kc in range(KC):
                    P.op("pe", lambda e, kc=kc, w=w, pt=pt: e.matmul(pt.ap[0:2, :], lhsT=condT.ap[:, kc, :], rhs=w.ap[:, kc, :],
                                                                 start=(kc == 0), stop=(kc == KC - 1)),
                         reads=[condT, w], writes=[pt])
                P.op("dve", lambda e, pt=pt, bb=bb, ob=ob: e.tensor_tensor(out=ob.ap, in0=pt.ap[0:2, :], in1=bb.ap, op=ALU.add),
                     reads=[pt, bb], writes=[ob])
                P.dma("sp", modd[li, :, n0:n0 + 512], ob.ap, reads=[ob])
                it += 1


def load_mod_bcast(S, es, li, idx, rows, name, plus_one=False):
    k, P, modd = S["k"], S["P"], S["modd"]
    b = Buf(k.sb(es, name, [128, D])[:])
    P.dma("sp", b.ap, modd[li, rows:rows + 1, idx * D:(idx + 1) * D].to_broadcast([128, D]), writes=[b])
    if plus_one:
        P.op("pool", lambda e: e.tensor_scalar_add(out=b.ap, in0=b.ap, scalar1=1.0), reads=[b], writes=[b])
    return b


def phase_prep(S, src, li, sub, nt, want_h):
    k, P, ps, hT_d, h_d, ident = S["k"], S["P"], S["ps"], S["hT_d"], S["h_d"], S["ident"]
    with ExitStack() as es:
        sh = [load_mod_bcast(S, es, li, 3 * sub + 0, r, "sh%d" % r) for r in range(2)]
        sc = [load_mod_bcast(S, es, li, 3 * sub + 1, r, "sc%d" % r, plus_one=True) for r in range(2)]
        xb = [Buf(k.sb(es, "px%d" % i, [128, D])[:]) for i in range(2)]
        hb = [Buf(k.sb(es, "ph%d" % i, [128, D])[:]) for i in range(2)]
        hTb = [Buf(k.sb(es, "phT%d" % i, [128, KC, 128], BF16)[:]) for i in range(2)]
        for t in range(nt):
            r = 0 if t < NTL else 1
            x, h, hT = xb[t % 2], hb[t % 2], hTb[t % 2]
            P.dma("sp", x.ap, src[t * 128:(t + 1) * 128, :], writes=[x])
            P.op("pool", lambda e: e.tensor_tensor(out=h.ap, in0=x.ap, in1=sc[r].ap, op=ALU.mult), reads=[x, sc[r]], writes=[h])
            P.op("dve", lambda e: e.tensor_tensor(out=h.ap, in0=h.ap, in1=sh[r].ap, op=ALU.add), reads=[h, sh[r]], writes=[h])
            if want_h:
                P.dma("act", h_d[t * 128:(t + 1) * 128, :], h.ap, reads=[h])
            for g in range(4):
                pt = ps[(t * 4 + g) % 8]
                for q in range(4):
                    kc = g * 4 + q
                    P.op("pe", lambda e, kc=kc, q=q, pt=pt: e.transpose(out=pt.ap[:, q * 128:(q + 1) * 128], in_=h.ap[:, kc * 128:(kc + 1) * 128],
                                                                   identity=ident.ap), reads=[h, ident], writes=[pt])
                eng = "act" if g % 2 == 0 else "dve"
                if eng == "act":
                    P.op("act", lambda e, g=g, pt=pt: e.activation(out=hT.ap[:, g * 4:(g + 1) * 4, :], in_=pt.ap.rearrange("p (q t) -> p q t", q=4), func=AF.Copy),
                         reads=[pt], writes=[hT])
                else:
                    P.op("dve", lambda e, g=g, pt=pt: e.tensor_copy(out=hT.ap[:, g * 4:(g + 1) * 4, :], in_=pt.ap.rearrange("p (q t) -> p q t", q=4)),
                         reads=[pt], writes=[hT])
            P.dma("sp", hT_d[t], hT.ap, reads=[hT])


def phase_ln(S, xsrc, ysrc, dst, li, sub, ln_g, ln_b, nt):
    k, P = S["k"], S["P"]
    with ExitStack() as es:
        gt = [load_mod_bcast(S, es, li, 3 * sub + 2, r, "lg%d" % r) for r in range(2)]
        lg = Buf(k.sb(es, "lng", [128, D])[:])
        lb = Buf(k.sb(es, "lnb", [128, D])[:])
        P.dma("sp", lg.ap, ln_g[li, sub:sub + 1, :].to_broadcast([128, D]), writes=[lg])
        P.dma("sp", lb.ap, ln_b[li, sub:sub + 1, :].to_broadcast([128, D]), writes=[lb])
        xb = [Buf(k.sb(es, "lx%d" % i, [128, D])[:]) for i in range(2)]
        yb = [Buf(k.sb(es, "ly%d" % i, [128, D])[:]) for i in range(2)]
        st = [Buf(k.sb(es, "lst%d" % i, [128, 4, 6])[:]) for i in range(2)]
        mv = [Buf(k.sb(es, "lmv%d" % i, [128, 2])[:]) for i in range(2)]
        rs = [Buf(k.sb(es, "lrs%d" % i, [128, 1])[:]) for i in range(2)]
        for t in range(nt):
            r = 0 if t < NTL else 1
            x, y, s_, m_, r_ = xb[t % 2], yb[t % 2], st[t % 2], mv[t % 2], rs[t % 2]
            P.dma("sp", x.ap, xsrc[t * 128:(t + 1) * 128, :], writes=[x])
            P.dma("act", y.ap, ysrc[t * 128:(t + 1) * 128, :], writes=[y])
            P.op("pool", lambda e: e.tensor_tensor(out=y.ap, in0=y.ap, in1=gt[r].ap, op=ALU.mult), reads=[y, gt[r]], writes=[y])
            P.op("dve", lambda e: e.scalar_tensor_tensor(out=x.ap, in0=x.ap, scalar=float(ALPHA), in1=y.ap, op0=ALU.mult, op1=ALU.add),
                 reads=[x, y], writes=[x])
            for q in range(4):
                P.op("dve", lambda e, q=q: e.bn_stats(out=s_.ap[:, q, :], in_=x.ap[:, q * 512:(q + 1) * 512]), reads=[x], writes=[s_])
            P.op("dve", lambda e: e.bn_aggr(out=m_.ap, in_=s_.ap.rearrange("p a b -> p (a b)")), reads=[s_], writes=[m_])
            rsqrt_eps(P, r_, m_.ap[:, 1:2], [m_], 1.0)
            P.op("dve", lambda e: e.tensor_scalar(out=x.ap, in0=x.ap, scalar1=m_.ap[:, 0:1], scalar2=r_.ap[:, 0:1], op0=ALU.subtract, op1=ALU.mult),
                 reads=[x, m_, r_], writes=[x])
            P.op("pool", lambda e: e.tensor_tensor(out=x.ap, in0=x.ap, in1=lg.ap, op=ALU.mult), reads=[x, lg], writes=[x])
            P.op("dve", lambda e: e.tensor_tensor(out=x.ap, in0=x.ap, in1=lb.ap, op=ALU.add), reads=[x, lb], writes=[x])
            P.dma("sp", dst[t * 128:(t + 1) * 128, :], x.ap, reads=[x])


def load_hT_group(S, es_bufs, t0, ntile, q="sp"):
    P, hT_d = S["P"], S["hT_d"]
    g = es_bufs
    for i in range(ntile):
        P.dma(q, g.ap[:, :, i * 128:(i + 1) * 128], hT_d[t0 + i], writes=[g])
    return g


def load_w_chunk(S, wbuf, w_ap, k0_chunks, n0, n, q="pool"):
    P = S["P"]
    P.dma(q, wbuf.ap[:, 0:k0_chunks, 0:n], w_ap[:, n0:n0 + n].rearrange("(kc p) n -> p kc n", p=128), writes=[wbuf])


NPS = 8


def next_ps(S, lo=0, hi=NPS):
    r = S.setdefault("rot", 0)
    S["rot"] = r + 1
    return S["ps"][lo + r % (hi - lo)]


def linear_tm(S, src_T, kch, w_ap, N, tiles, epi, G=8, tag="l"):
    k, P = S["k"], S["P"]
    with ExitStack() as es:
        hg = [Buf(k.sb(es, tag + "hg%d" % i, [128, kch, 128], BF16)[:]) for i in range(G)]
        wb = [Buf(k.sb(es, tag + "w%d" % i, [128, kch, 512], BF16)[:]) for i in range(2)]
        it = 0
        for g0 in range(0, len(tiles), G):
            grp = tiles[g0:g0 + G]
            for i, t in enumerate(grp):
                P.dma("sp", hg[i].ap, src_T[t], writes=[hg[i]])
            for n0 in range(0, N, 512):
                n = min(512, N - n0)
                w = wb[it % 2]
                it += 1
                P.dma("pool", w.ap[:, :, 0:n], w_ap[:, n0:n0 + n].rearrange("(kc p) n -> p kc n", p=128), writes=[w])
                for i, t in enumerate(grp):
                    pt = next_ps(S)
                    for kc in range(kch):
                        P.op("pe", lambda e: e.matmul(pt.ap[:, 0:n], lhsT=hg[i].ap[:, kc, :], rhs=w.ap[:, kc, 0:n],
                                                      start=(kc == 0), stop=(kc == kch - 1)), reads=[hg[i], w], writes=[pt])
                    epi(t, n0, n, pt)


def linear_fm(S, src_T, kch, w_ap, N, tiles, epi, G=8, tag="f"):
    k, P = S["k"], S["P"]
    with ExitStack() as es:
        hg = [Buf(k.sb(es, tag + "hg%d" % i, [128, kch, 512], BF16)[:]) for i in range(G // 4)]
        wb = [Buf(k.sb(es, tag + "w%d" % i, [128, kch, 512], BF16)[:]) for i in range(2)]
        it = 0
        for g0 in range(0, len(tiles), G):
            grp = tiles[g0:g0 + G]
            blocks = [grp[i:i + 4] for i in range(0, len(grp), 4)]
            for bi, blk in enumerate(blocks):
                for i, t in enumerate(blk):
                    P.dma("sp", hg[bi].ap[:, :, i * 128:(i + 1) * 128], src_T[t], writes=[hg[bi]])
            for n0 in range(0, N, 512):
                n = min(512, N - n0)
                w = wb[it % 2]
                it += 1
                P.dma("pool", w.ap[:, :, 0:n], w_ap[:, n0:n0 + n].rearrange("(kc p) n -> p kc n", p=128), writes=[w])
                for cc in range(n // 128):
                    for bi, blk in enumerate(blocks):
                        ntok = len(blk) * 128
                        pt = next_ps(S)
                        for kc in range(kch):
                            P.op("pe", lambda e: e.matmul(pt.ap[:, 0:ntok], lhsT=w.ap[:, kc, cc * 128:(cc + 1) * 128], rhs=hg[bi].ap[:, kc, 0:ntok],
                                                          start=(kc == 0), stop=(kc == kch - 1)), reads=[hg[bi], w], writes=[pt])
                        epi(n0 // 128 + cc, blk[0], len(blk), pt)


def make_store_epi(S, es, dst, tag="st", nbuf=4, col_off=0):
    k, P = S["k"], S["P"]
    stg = [Buf(k.sb(es, tag + "%d" % i, [128, 512])[:]) for i in range(nbuf)]
    cnt = [0]

    def epi(t, n0, n, pt):
        s = stg[cnt[0] % nbuf]
        eng = "act" if cnt[0] % 2 == 0 else "dve"
        cnt[0] += 1
        if eng == "act":
            P.op("act", lambda e: e.activation(out=s.ap[:, 0:n], in_=pt.ap[:, 0:n], func=AF.Copy), reads=[pt], writes=[s])
        else:
            P.op("dve", lambda e: e.tensor_copy(out=s.ap[:, 0:n], in_=pt.ap[:, 0:n]), reads=[pt], writes=[s])
        P.dma("sp", dst[t * 128:(t + 1) * 128, col_off + n0:col_off + n0 + n], s.ap[:, 0:n], reads=[s])
    return epi


def transpose_to_T(S, src, nch, dstT, t):
    P, ident = S["P"], S["ident"]
    for g in range(nch // 4):
        pt = next_ps(S)
        for q in range(4):
            kc = g * 4 + q
            P.op("pe", lambda e: e.transpose(out=pt.ap[:, q * 128:(q + 1) * 128], in_=src.ap[:, kc * 128:(kc + 1) * 128], identity=ident.ap),
                 reads=[src, ident], writes=[pt])
        if g % 2 == 0:
            P.op("act", lambda e: e.activation(out=dstT.ap[:, g * 4:(g + 1) * 4, :], in_=pt.ap.rearrange("p (q t) -> p q t", q=4), func=AF.Copy),
                 reads=[pt], writes=[dstT])
        else:
            P.op("dve", lambda e: e.tensor_copy(out=dstT.ap[:, g * 4:(g + 1) * 4, :], in_=pt.ap.rearrange("p (q t) -> p q t", q=4)),
                 reads=[pt], writes=[dstT])


def mix_gmlp(S, j, b_w_in, b_b_in, b_ln_g, b_ln_b, b_w_s, b_b_s, b_w_out, need_ctx):
    k, P, hT_d, ymix, ident = S["k"], S["P"], S["hT_d"], S["ymix"], S["ident"]
    tiles = list(range(NT if need_ctx else NTL))
    zd = k.dscr("zd", [T, 8192])
    uvT = k.dscr("uvT", [NT, 128, 32, 128], BF16)
    with ExitStack() as es:
        bias = Buf(k.sb(es, "gb_bias", [128, 8192])[:])
        P.dma("sp", bias.ap, b_b_in[j:j + 1, :].to_broadcast([128, 8192]), writes=[bias])
        stg = [Buf(k.sb(es, "gb_st%d" % i, [128, 512])[:]) for i in range(4)]
        cnt = [0]

        def epi(t, n0, n, pt):
            s = stg[cnt[0] % 4]
            cnt[0] += 1
            P.op("dve", lambda e: e.tensor_tensor(out=s.ap, in0=pt.ap, in1=bias.ap[:, n0:n0 + n], op=ALU.add), reads=[pt, bias], writes=[s])
            P.op("act", lambda e: e.activation(out=s.ap, in_=s.ap, func=AF.Gelu_apprx_tanh), reads=[s], writes=[s])
            P.dma("sp", zd[t * 128:(t + 1) * 128, n0:n0 + n], s.ap, reads=[s])
        linear_tm(S, hT_d, KC, b_w_in[j], 8192, tiles, epi, tag="gb1")
    P.barrier()
    with ExitStack() as es:
        lg = Buf(k.sb(es, "gb_lg", [128, 4096])[:])
        lb = Buf(k.sb(es, "gb_lb", [128, 4096])[:])
        P.dma("sp", lg.ap, b_ln_g[j:j + 1, :].to_broadcast([128, 4096]), writes=[lg])
        P.dma("sp", lb.ap, b_ln_b[j:j + 1, :].to_broadcast([128, 4096]), writes=[lb])
        wsr = Buf(k.sb(es, "gb_wsr", [128, 8, 128])[:])
        wsT = Buf(k.sb(es, "gb_wsT", [128, 8, 128], BF16)[:])
        bsT = Buf(k.sb(es, "gb_bsT", [128, 8])[:])
        P.dma("sp", wsr.ap, b_w_s[j].rearrange("g t s -> t g s"), writes=[wsr])
        P.dma("sp", bsT.ap, b_b_s[j].rearrange("g t -> t g"), writes=[bsT], allow_slow_non_contiguous=True)
        for g in range(8):
            pt = next_ps(S)
            P.op("pe", lambda e: e.transpose(out=pt.ap[:, 0:128], in_=wsr.ap[:, g, :], identity=ident.ap), reads=[wsr, ident], writes=[pt])
            P.op("dve", lambda e: e.tensor_copy(out=wsT.ap[:, g, :], in_=pt.ap[:, 0:128]), reads=[pt], writes=[wsT])
        zb = [Buf(k.sb(es, "gb_z%d" % i, [128, 8192])[:]) for i in range(2)]
        vb = [Buf(k.sb(es, "gb_v%d" % i, [128, 4096], BF16)[:]) for i in range(2)]
        uT = [Buf(k.sb(es, "gb_uT%d" % i, [128, 32, 128], BF16)[:]) for i in range(2)]
        st = [Buf(k.sb(es, "gb_bs%d" % i, [128, 8, 6])[:]) for i in range(2)]
        mv = [Buf(k.sb(es, "gb_mv%d" % i, [128, 2])[:]) for i in range(2)]
        rs = [Buf(k.sb(es, "gb_rs%d" % i, [128, 1])[:]) for i in range(2)]
        for t in tiles:
            z, v, u_T, s_, m_, r_ = zb[t % 2], vb[t % 2], uT[t % 2], st[t % 2], mv[t % 2], rs[t % 2]
            P.dma("sp", z.ap[:, 0:4096], zd[t * 128:(t + 1) * 128, 0:4096], writes=[z])
            P.dma("act", z.ap[:, 4096:8192], zd[t * 128:(t + 1) * 128, 4096:8192], writes=[z])
            for q in range(8):
                P.op("dve", lambda e: e.bn_stats(out=s_.ap[:, q, :], in_=z.ap[:, 4096 + q * 512:4096 + (q + 1) * 512]), reads=[z], writes=[s_])
            P.op("dve", lambda e: e.bn_aggr(out=m_.ap, in_=s_.ap.rearrange("p a b -> p (a b)")), reads=[s_], writes=[m_])
            rsqrt_eps(P, r_, m_.ap[:, 1:2], [m_], 1.0)
            P.op("dve", lambda e: e.tensor_scalar(out=z.ap[:, 4096:8192], in0=z.ap[:, 4096:8192], scalar1=m_.ap[:, 0:1], scalar2=r_.ap[:, 0:1],
                                                  op0=ALU.subtract, op1=ALU.mult), reads=[z, m_, r_], writes=[z])
            P.op("pool", lambda e: e.tensor_tensor(out=z.ap[:, 4096:8192], in0=z.ap[:, 4096:8192], in1=lg.ap, op=ALU.mult), reads=[z, lg], writes=[z])
            P.op("dve", lambda e: e.tensor_tensor(out=v.ap, in0=z.ap[:, 4096:8192], in1=lb.ap, op=ALU.add), reads=[z, lb], writes=[v])
            for g in range(8):
                pt = next_ps(S)
                P.op("pe", lambda e: e.matmul(pt.ap, lhsT=wsT.ap[:, g, :], rhs=v.ap[:, g * 512:(g + 1) * 512], start=True, stop=True),
                     reads=[wsT, v], writes=[pt])
                P.op("dve", lambda e: e.scalar_tensor_tensor(out=z.ap[:, g * 512:(g + 1) * 512], in0=pt.ap, scalar=bsT.ap[:, g:g + 1],
                                                             in1=z.ap[:, g * 512:(g + 1) * 512], op0=ALU.add, op1=ALU.mult),
                     reads=[pt, bsT, z], writes=[z])
            transpose_to_T(S, z, 32, u_T, t)
            P.dma("sp", uvT[t], u_T.ap, reads=[u_T])
    P.barrier()
    with ExitStack() as es:
        epi = make_store_epi(S, es, ymix, tag="gb3s")
        linear_tm(S, uvT, 32, b_w_out[j], D, tiles, epi, G=4, tag="gb3")


def mix_gqa(S, j, c_w_qkv, c_q_g, c_k_g, c_w_out, rope_d, need_ctx):
    k, P, hT_d, ymix, ident = S["k"], S["P"], S["hT_d"], S["ymix"], S["ident"]
    tiles = list(range(NT))
    qkv_d = k.dscr("qkv_d", [T, 3072])
    qkT_d = k.dscr("qkT_d", [20, 128, T], BF16)
    v_d = k.dscr("v_d", [NT, 128, 512], BF16)
    with ExitStack() as es:
        epi = make_store_epi(S, es, qkv_d, tag="gq1s")
        linear_tm(S, hT_d, KC, c_w_qkv[j], 3072, tiles, epi, tag="gq1")
    P.barrier()
    with ExitStack() as es:
        gq = Buf(k.sb(es, "gq_g", [128, 20, 128])[:])
        P.dma("sp", gq.ap[:, 0:16, :], c_q_g[j:j + 1, :].unsqueeze(1).to_broadcast([128, 16, 128]), writes=[gq])
        P.dma("sp", gq.ap[:, 16:20, :], c_k_g[j:j + 1, :].unsqueeze(1).to_broadcast([128, 4, 128]), writes=[gq])
        xb = [Buf(k.sb(es, "gq_x%d" % i, [128, 3072])[:]) for i in range(2)]
        sqb = Buf(k.sb(es, "gq_sq", [128, 2560])[:])
        ssum = Buf(k.sb(es, "gq_ss", [128, 20])[:])
        rp = [Buf(k.sb(es, "gq_rp%d" % i, [128, 2, 64])[:]) for i in range(2)]
        tA = Buf(k.sb(es, "gq_tA", [128, 20, 2, 32])[:])
        tB = Buf(k.sb(es, "gq_tB", [128, 20, 2, 32])[:])
        tC = Buf(k.sb(es, "gq_tC", [128, 20, 2, 32])[:])
        qT = [Buf(k.sb(es, "gq_qT%d" % i, [128, 20, 128], BF16)[:]) for i in range(2)]
        vbf = [Buf(k.sb(es, "gq_vb%d" % i, [128, 512], BF16)[:]) for i in range(2)]
        for t in tiles:
            x, r_, q_T, v_ = xb[t % 2], rp[t % 2], qT[t % 2], vbf[t % 2]
            P.dma("sp", x.ap, qkv_d[t * 128:(t + 1) * 128, :], writes=[x])
            P.op("act", lambda e: e.activation(out=sqb.ap, in_=x.ap[:, 0:2560], func=AF.Square), reads=[x], writes=[sqb])
            P.op("dve", lambda e: e.tensor_reduce(out=ssum.ap, in_=sqb.ap.rearrange("p (h d) -> p h d", d=128), axis=AX.X, op=ALU.add),
                 reads=[sqb], writes=[ssum])
            rsqrt_eps(P, ssum, ssum.ap, [ssum], 1.0 / 128.0)
            x3 = x.ap[:, 0:2560].rearrange("p (h d) -> p h d", d=128)
            P.op("dve", lambda e: e.tensor_tensor(out=x3, in0=x3, in1=ssum.ap.unsqueeze(2).to_broadcast([128, 20, 128]), op=ALU.mult),
                 reads=[x, ssum], writes=[x])
            P.op("pool", lambda e: e.tensor_tensor(out=x3, in0=x3, in1=gq.ap, op=ALU.mult), reads=[x, gq], writes=[x])
            if t < NTL:
                P.dma("act", r_.ap, rope_d[t * 128:(t + 1) * 128, :, :], writes=[r_])
                x5 = x.ap[:, 0:2560].rearrange("p (h a b c) -> p h a b c", a=2, b=2, c=32)
                x1 = x5[:, :, :, 0, :]
                x2 = x5[:, :, :, 1, :]
                cosb = r_.ap[:, 0, :].rearrange("p (a c) -> p a c", a=2).unsqueeze(1).to_broadcast([128, 20, 2, 32])
                sinb = r_.ap[:, 1, :].rearrange("p (a c) -> p a c", a=2).unsqueeze(1).to_broadcast([128, 20, 2, 32])
                P.op("dve", lambda e: e.tensor_tensor(out=tA.ap, in0=x1, in1=sinb, op=ALU.mult), reads=[x, r_], writes=[tA])
                P.op("pool", lambda e: e.tensor_tensor(out=tB.ap, in0=x2, in1=sinb, op=ALU.mult), reads=[x, r_], writes=[tB])
                P.op("dve", lambda e: e.tensor_tensor(out=tC.ap, in0=x2, in1=cosb, op=ALU.mult), reads=[x, r_], writes=[tC])
                P.op("dve", lambda e: e.tensor_tensor(out=x1, in0=x1, in1=cosb, op=ALU.mult), reads=[x, r_], writes=[x])
                P.op("dve", lambda e: e.tensor_tensor(out=x1, in0=x1, in1=tB.ap, op=ALU.subtract), reads=[x, tB], writes=[x])
                P.op("dve", lambda e: e.tensor_tensor(out=x2, in0=tA.ap, in1=tC.ap, op=ALU.add), reads=[tA, tC], writes=[x])
            transpose_to_T(S, x, 20, q_T, t)
            P.dma("sp", qkT_d[:, :, t * 128:(t + 1) * 128].rearrange("h d t -> d h t"), q_T.ap, reads=[q_T])
            P.op("pool", lambda e: e.tensor_copy(out=v_.ap, in_=x.ap[:, 2560:3072]), reads=[x], writes=[v_])
            P.dma("sp", v_d[t], v_.ap, reads=[v_])
    P.barrier()
    scale = 128.0 ** -0.5
    ps = S["ps"]
    with ExitStack() as es:
        kT = Buf(k.sb(es, "ga_kT", [128, T], BF16)[:])
        vv = Buf(k.sb(es, "ga_v", [128, NT, 128], BF16)[:])
        qTb = [Buf(k.sb(es, "ga_qT%d" % i, [128, T], BF16)[:]) for i in range(2)]
        ones = Buf(k.sb(es, "ga_ones", [128, 128], BF16)[:])
        P.op("pool", lambda e: e.memset(ones.ap, 1.0), writes=[ones])
        pT = [Buf(k.sb(es, "ga_pT%d" % i, [128, 512], BF16)[:]) for i in range(3)]
        rec = [Buf(k.sb(es, "ga_rec%d" % i, [128, 512])[:]) for i in range(2)]
        oT = [Buf(k.sb(es, "ga_oT%d" % i, [128, 512], BF16)[:]) for i in range(2)]
        blocks = [(q0, 512, list(range(NT))) for q0 in range(0, SEQ, 512)]
        if need_ctx:
            blocks.append((SEQ, CTX, [32, 33]))
        bc = 0
        pc = 0
        for kh in range(4):
            P.dma("sp", kT.ap, qkT_d[16 + kh], writes=[kT])
            P.dma("act", vv.ap, v_d[:, :, kh * 128:(kh + 1) * 128].rearrange("t p d -> p t d"), writes=[vv])
            for hq in range(4):
                h = kh * 4 + hq
                q_ = qTb[h % 2]
                P.dma("sp", q_.ap, qkT_d[h], writes=[q_])
                for (q0, nq, kts) in blocks:
                    acc_o, acc_s = ps[2 + 2 * (bc % 2)], ps[3 + 2 * (bc % 2)]
                    r_, o_ = rec[bc % 2], oT[bc % 2]
                    bc += 1
                    for ki, kt in enumerate(kts):
                        st = ps[pc % 2]
                        p_ = pT[pc % 3]
                        pc += 1
                        P.op("pe", lambda e: e.matmul(st.ap[:, 0:nq], lhsT=kT.ap[:, kt * 128:(kt + 1) * 128], rhs=q_.ap[:, q0:q0 + nq], start=True, stop=True),
                             reads=[kT, q_], writes=[st])
                        P.op("act", lambda e: e.activation(out=p_.ap[:, 0:nq], in_=st.ap[:, 0:nq], func=AF.Exp, scale=float(scale)), reads=[st], writes=[p_])
                        P.op("pe", lambda e: e.matmul(acc_o.ap[:, 0:nq], lhsT=vv.ap[:, kt, :], rhs=p_.ap[:, 0:nq], start=(ki == 0), stop=(ki == len(kts) - 1)),
                             reads=[vv, p_], writes=[acc_o])
                        P.op("pe", lambda e: e.matmul(acc_s.ap[:, 0:nq], lhsT=ones.ap, rhs=p_.ap[:, 0:nq], start=(ki == 0), stop=(ki == len(kts) - 1)),
                             reads=[ones, p_], writes=[acc_s])
                    P.op("dve", lambda e: e.reciprocal(out=r_.ap[:, 0:nq], in_=acc_s.ap[:, 0:nq]), reads=[acc_s], writes=[r_])
                    P.op("dve", lambda e: e.tensor_tensor(out=o_.ap[:, 0:nq], in0=acc_o.ap[:, 0:nq], in1=r_.ap[:, 0:nq], op=ALU.mult), reads=[acc_o, r_], writes=[o_])
                    t0 = q0 // 128
                    P.dma("sp", hT_d[t0:t0 + nq // 128, :, h, :].rearrange("t p q -> p t q"), o_.ap[:, 0:nq].rearrange("p (t q) -> p t q", q=128), reads=[o_])
    P.barrier()
    with ExitStack() as es:
        epi = make_store_epi(S, es, ymix, tag="gq4s")
        linear_tm(S, hT_d, KC, c_w_out[j], D, list(range(NT if need_ctx else NTL)), epi, tag="gq4")


def mix_mlstm(S, j, a_w_in, a_b_gate, a_norm_g, a_w_out, tri_d, need_ctx):
    k, P, hT_d, ymix, ident, ps = S["k"], S["P"], S["hT_d"], S["ymix"], S["ident"], S["ps"]
    tiles = list(range(NT))
    qkT_d = k.dscr("a_qkT%d" % j, [NT, 128, 16, 128], BF16)
    kt_d = k.dscr("a_kt%d" % j, [T, 1024], BF16)
    v_d = k.dscr("a_v%d" % j, [T, 2048], BF16)
    o_d = k.dscr("a_o%d" % j, [T, 2048])
    g_d = k.dscr("a_g%d" % j, [T, 16])
    hs_d = [k.dscr("a_hs%d_%d" % (j, d_), [T, 2048]) for d_ in range(2)]
    with ExitStack() as es:
        stg = [Buf(k.sb(es, "ma_fs%d" % i, [128, 512], BF16)[:]) for i in range(4)]
        cnt = [0]

        def epi_f(c, t0, ntl, pt):
            s = stg[cnt[0] % 4]
            cnt[0] += 1
            n = ntl * 128
            sc = 1.0 if c < 8 else 1.0 / 16.0
            P.op("act", lambda e: e.activation(out=s.ap[:, 0:n], in_=pt.ap[:, 0:n], func=AF.Copy, scale=float(sc)), reads=[pt], writes=[s])
            P.dma("sp", qkT_d[t0:t0 + ntl, :, c, :].rearrange("t p q -> p t q"), s.ap[:, 0:n].rearrange("p (t q) -> p t q", q=128), reads=[s])
        linear_fm(S, hT_d, KC, a_w_in[j][:, 0:2048], 2048, tiles, epi_f, tag="ma1")
    P.barrier()
    with ExitStack() as es:
        stb = [Buf(k.sb(es, "ma_sb%d" % i, [128, 512], BF16)[:]) for i in range(3)]
        stf = [Buf(k.sb(es, "ma_sf%d" % i, [128, 512])[:]) for i in range(3)]
        cnt = [0]

        def epi_t(t, n0, n, pt):
            i = cnt[0]
            cnt[0] += 1
            rows = slice(t * 128, (t + 1) * 128)
            if n0 < 1024:
                s = stb[i % 3]
                P.op("act", lambda e: e.activation(out=s.ap, in_=pt.ap, func=AF.Copy, scale=1.0 / 16.0), reads=[pt], writes=[s])
                P.dma("sp", kt_d[rows, n0:n0 + 512], s.ap, reads=[s])
            elif n0 < 3072:
                s = stb[i % 3]
                P.op("dve", lambda e: e.tensor_copy(out=s.ap, in_=pt.ap), reads=[pt], writes=[s])
                P.dma("sp", v_d[rows, n0 - 1024:n0 - 1024 + 512], s.ap, reads=[s])
            elif n0 < 5120:
                s = stf[i % 3]
                P.op("act", lambda e: e.activation(out=s.ap, in_=pt.ap, func=AF.Sigmoid), reads=[pt], writes=[s])
                P.dma("sp", o_d[rows, n0 - 3072:n0 - 3072 + 512], s.ap, reads=[s])
            else:
                s = stf[i % 3]
                P.op("dve", lambda e: e.tensor_copy(out=s.ap[:, 0:16], in_=pt.ap[:, 0:16]), reads=[pt], writes=[s])
                P.dma("sp", g_d[rows, :], s.ap[:, 0:16], reads=[s])
        linear_tm(S, hT_d, KC, a_w_in[j][:, 1024:A_IN], A_IN - 1024, tiles, epi_t, tag="ma2")
    P.barrier()
    with ExitStack() as es:
        tri = Buf(k.sb(es, "ma_tri", [128, 2, 128])[:])
        P.dma("sp", tri.ap, tri_d.rearrange("a s t -> s a t"), writes=[tri])
        onesf = Buf(k.sb(es, "ma_1f", [128, 128])[:])
        onesb = Buf(k.sb(es, "ma_1b", [128, 2], BF16)[:])
        P.op("pool", lambda e: e.memset(onesf.ap, 1.0), writes=[onesf])
        P.op("pool", lambda e: e.memset(onesb.ap, 1.0), writes=[onesb])
        G = Buf(k.sb(es, "ma_G", [128, NT, 16])[:])
        bg = Buf(k.sb(es, "ma_bg", [128, 16])[:])
        P.dma("sp", G.ap, g_d.rearrange("(c s) g -> s c g", s=128), writes=[G])
        P.dma("sp", bg.ap, a_b_gate[j:j + 1, :].to_broadcast([128, 16]), writes=[bg])
        P.op("dve", lambda e: e.tensor_tensor(out=G.ap, in0=G.ap, in1=bg.ap.unsqueeze(1).to_broadcast([128, NT, 16]), op=ALU.add), reads=[G, bg], writes=[G])
        P.op("act", lambda e: e.activation(out=G.ap, in_=G.ap, func=AF.Tanh, scale=1.0 / 15.0), reads=[G], writes=[G])
        P.op("dve", lambda e: e.tensor_scalar_mul(out=G.ap, in0=G.ap, scalar1=15.0), reads=[G], writes=[G])
        LI = Buf(k.sb(es, "ma_LI", [128, NT, 8])[:])
        LFN = Buf(k.sb(es, "ma_LFN", [128, NT, 8])[:])
        G4 = G.ap.rearrange("p c (a h) -> p c a h", a=4)
        LI4 = LI.ap.rearrange("p c (a h) -> p c a h", a=2)
        LF4 = LFN.ap.rearrange("p c (a h) -> p c a h", a=2)
        for d_ in range(2):
            P.op("dve", lambda e: e.tensor_copy(out=LI4[:, :, d_, :], in_=G4[:, :, 2 * d_, :]), reads=[G], writes=[LI])
            P.op("act", lambda e: e.activation(out=LF4[:, :, d_, :], in_=G4[:, :, 2 * d_ + 1, :], func=AF.Exp, scale=-1.0), reads=[G], writes=[LFN])
        P.op("act", lambda e: e.activation(out=LFN.ap, in_=LFN.ap, func=AF.Ln, bias=1.0), reads=[LFN], writes=[LFN])
        NB = Buf(k.sb(es, "ma_NB", [128, NT, 8])[:])
        GT = Buf(k.sb(es, "ma_GT", [128, NT, 8])[:])
        AA = Buf(k.sb(es, "ma_A", [128, NT, 8])[:])
        NB4 = NB.ap.rearrange("p c (a h) -> p c a h", a=2)
        pt = ps[0]
        for d_ in range(2):
            P.op("pe", lambda e: e.matmul(pt.ap[:, 0:NT * 4], lhsT=tri.ap[:, d_, :], rhs=LF4[:, :, d_, :], start=True, stop=True), reads=[tri, LFN], writes=[pt])
            P.op("dve", lambda e: e.tensor_copy(out=NB4[:, :, d_, :], in_=pt.ap[:, 0:NT * 4].rearrange("p (c h) -> p c h", h=4)), reads=[pt], writes=[NB])
        pt = ps[1]
        P.op("pe", lambda e: e.matmul(pt.ap[:, 0:NT * 8], lhsT=onesf.ap, rhs=LFN.ap, start=True, stop=True), reads=[onesf, LFN], writes=[pt])
        P.op("dve", lambda e: e.tensor_copy(out=GT.ap, in_=pt.ap[:, 0:NT * 8].rearrange("p (c h) -> p c h", h=8)), reads=[pt], writes=[GT])
        P.op("dve", lambda e: e.tensor_tensor(out=AA.ap, in0=LI.ap, in1=NB.ap, op=ALU.add), reads=[LI, NB], writes=[AA])
        AMX = Buf(k.sb(es, "ma_AMX", [128, NT * 8])[:])
        A2 = AA.ap.rearrange("p c h -> p (c h)")
        col = Buf(k.sb(es, "ma_col", [128, 1])[:])
        colb = Buf(k.sb(es, "ma_colb", [128, 128])[:])
        for (c0, n) in ((0, 128), (128, 128), (256, NT * 8 - 256)):
            pt = ps[2]
            P.op("pe", lambda e: e.transpose(out=pt.ap[0:n, 0:128], in_=A2[:, c0:c0 + n], identity=ident.ap), reads=[AA, ident], writes=[pt])
            P.op("dve", lambda e: e.reduce_max(out=col.ap[0:n, :], in_=pt.ap[0:n, 0:128], axis=AX.X), reads=[pt], writes=[col])
            P.op("dve", lambda e: e.tensor_copy(out=colb.ap[0:n, :], in_=col.ap[0:n, 0:1].to_broadcast([n, 128])), reads=[col], writes=[colb])
            pt2 = ps[3]
            P.op("pe", lambda e: e.matmul(pt2.ap[:, 0:n], lhsT=colb.ap[0:n, :], rhs=ident.ap[0:n, 0:n], start=True, stop=True), reads=[colb, ident], writes=[pt2])
            P.op("dve", lambda e: e.tensor_copy(out=AMX.ap[:, c0:c0 + n], in_=pt2.ap[:, 0:n]), reads=[pt2], writes=[AMX])
        AMX3 = AMX.ap.rearrange("p (c h) -> p c h", h=8)
        RR = Buf(k.sb(es, "ma_R", [128, NT, 8])[:])
        DM = Buf(k.sb(es, "ma_DM", [128, NT, 8])[:])
        mm = Buf(k.sb(es, "ma_m", [128, 8])[:])
        P.op("dve", lambda e: e.memset(mm.ap, 0.0), writes=[mm])
        order = [[32, 33] + list(range(32)), [33, 32] + list(range(31, -1, -1))]
        for i in range(NT):
            for d_ in range(2):
                c = order[d_][i]
                hs = slice(d_ * 4, d_ * 4 + 4)
                P.op("dve", lambda e: e.tensor_tensor(out=RR.ap[:, c, hs], in0=mm.ap[:, hs], in1=AMX3[:, c, hs], op=ALU.max), reads=[mm, AMX], writes=[RR])
                P.op("dve", lambda e: e.tensor_tensor(out=DM.ap[:, c, hs], in0=mm.ap[:, hs], in1=RR.ap[:, c, hs], op=ALU.subtract), reads=[mm, RR], writes=[DM])
                P.op("dve", lambda e: e.tensor_tensor(out=mm.ap[:, hs], in0=RR.ap[:, c, hs], in1=GT.ap[:, c, hs], op=ALU.subtract), reads=[RR, GT], writes=[mm])
        EE = Buf(k.sb(es, "ma_E", [128, NT, 8])[:])
        CL = Buf(k.sb(es, "ma_CL", [128, NT, 8])[:])
        P.op("dve", lambda e: e.tensor_tensor(out=EE.ap, in0=AA.ap, in1=RR.ap, op=ALU.subtract), reads=[AA, RR], writes=[EE])
        P.op("act", lambda e: e.activation(out=EE.ap, in_=EE.ap, func=AF.Exp), reads=[EE], writes=[EE])
        P.op("dve", lambda e: e.tensor_tensor(out=CL.ap, in0=NB.ap, in1=RR.ap, op=ALU.subtract), reads=[NB, RR], writes=[CL])
        P.op("act", lambda e: e.activation(out=CL.ap, in_=CL.ap, func=AF.Exp), reads=[CL], writes=[CL])
        P.op("act", lambda e: e.activation(out=DM.ap, in_=DM.ap, func=AF.Exp), reads=[DM], writes=[DM])
        C32 = [Buf(k.sb(es, "ma_C%d" % i, [128, 2, 512])[:]) for i in range(8)]
        N32 = [Buf(k.sb(es, "ma_N%d" % i, [128, 2])[:]) for i in range(8)]
        Cb = [Buf(k.sb(es, "ma_Cb%d" % i, [128, 2, 512], BF16)[:]) for i in range(8)]
        Nb = [Buf(k.sb(es, "ma_Nb%d" % i, [128, 2], BF16)[:]) for i in range(8)]
        for i in range(8):
            P.op("pool", lambda e: e.memset(C32[i].ap, 0.0), writes=[C32[i]])
            P.op("pool", lambda e: e.memset(N32[i].ap, 0.0), writes=[N32[i]])
        qk = [[Buf(k.sb(es, "ma_qk%d_%d" % (d_, i), [128, 16, 128], BF16)[:]) for i in range(2)] for d_ in range(2)]
        ktm = [[Buf(k.sb(es, "ma_kt%d_%d" % (d_, i), [128, 1024], BF16)[:]) for i in range(2)] for d_ in range(2)]
        vtm = [[Buf(k.sb(es, "ma_vt%d_%d" % (d_, i), [128, 2048], BF16)[:]) for i in range(2)] for d_ in range(2)]
        St = [Buf(k.sb(es, "ma_St%d" % i, [128, 128], BF16)[:]) for i in range(4)]
        ktl = [Buf(k.sb(es, "ma_ktl%d" % i, [128, 256], BF16)[:]) for i in range(4)]
        dn = [Buf(k.sb(es, "ma_dn%d" % i, [128, 1])[:]) for i in range(4)]
        ho = [Buf(k.sb(es, "ma_ho%d" % i, [128, 512])[:]) for i in range(4)]
        it = 0
        for i in range(NT):
            for d_ in range(2):
                c = order[d_][i]
                rows = slice(c * 128, (c + 1) * 128)
                qk_, kt_, vt_ = qk[d_][i % 2], ktm[d_][i % 2], vtm[d_][i % 2]
                P.dma("sp", qk_.ap, qkT_d[c], writes=[qk_])
                P.dma("act", kt_.ap, kt_d[rows, :], writes=[kt_])
                P.dma("sp", vt_.ap, v_d[rows, :], writes=[vt_])
                for hh in range(4):
                    ch = d_ * 4 + hh
                    e_ = EE.ap[:, c, ch:ch + 1]
                    cd = DM.ap[:, c, ch:ch + 1]
                    cl = CL.ap[:, c, ch:ch + 1]
                    s_t, k_l, d_n, h_o = St[it % 4], ktl[it % 4], dn[it % 4], ho[it % 4]
                    p_st, p_num = ps[it % 2], ps[2 + it % 2]
                    p_c = [ps[4], ps[5]]
                    p_den, p_nu = ps[6], ps[7]
                    it += 1
                    vh = vt_.ap[:, hh * 512:(hh + 1) * 512]
                    P.op("dve", lambda e: e.tensor_scalar(out=Cb[ch].ap, in0=C32[ch].ap, scalar1=cd, scalar2=None, op0=ALU.mult), reads=[C32[ch], DM], writes=[Cb[ch]])
                    P.op("dve", lambda e: e.tensor_scalar(out=Nb[ch].ap, in0=N32[ch].ap, scalar1=cd, scalar2=None, op0=ALU.mult), reads=[N32[ch], DM], writes=[Nb[ch]])
                    for dc in range(2):
                        P.op("pe", lambda e: e.matmul(p_st.ap[:, 0:128], lhsT=qk_.ap[:, 8 + hh * 2 + dc, :], rhs=qk_.ap[:, hh * 2 + dc, :], start=(dc == 0), stop=(dc == 1)),
                             reads=[qk_], writes=[p_st])
                    P.op("dve", lambda e: e.scalar_tensor_tensor(out=s_t.ap, in0=p_st.ap[:, 0:128], scalar=e_, in1=tri.ap[:, d_, :], op0=ALU.mult, op1=ALU.mult),
                         reads=[p_st, EE, tri], writes=[s_t])
                    for dc in range(2):
                        P.op("pe", lambda e: e.matmul(p_num.ap, lhsT=qk_.ap[:, hh * 2 + dc, :], rhs=Cb[ch].ap[:, dc, :], start=(dc == 0), stop=False),
                             reads=[qk_, Cb[ch]], writes=[p_num])
                    P.op("pe", lambda e: e.matmul(p_num.ap, lhsT=s_t.ap, rhs=vh, start=False, stop=True), reads=[s_t, vt_], writes=[p_num])
                    for dc in range(2):
                        P.op("pe", lambda e: e.matmul(p_den.ap[:, 0:1], lhsT=qk_.ap[:, hh * 2 + dc, :], rhs=Nb[ch].ap[:, dc:dc + 1], start=(dc == 0), stop=False),
                             reads=[qk_, Nb[ch]], writes=[p_den])
                    P.op("pe", lambda e: e.matmul(p_den.ap[:, 0:1], lhsT=s_t.ap, rhs=onesb.ap[:, 0:1], start=False, stop=True), reads=[s_t, onesb], writes=[p_den])
                    P.op("act", lambda e: e.activation(out=d_n.ap, in_=p_den.ap[:, 0:1], func=AF.Abs), reads=[p_den], writes=[d_n])
                    P.op("dve", lambda e: e.tensor_tensor(out=d_n.ap, in0=d_n.ap, in1=cl, op=ALU.max), reads=[d_n, CL], writes=[d_n])
                    P.op("dve", lambda e: e.reciprocal(out=d_n.ap, in_=d_n.ap), reads=[d_n], writes=[d_n])
                    P.op("act", lambda e: e.activation(out=h_o.ap, in_=p_num.ap, func=AF.Copy, scale=d_n.ap[:, 0:1]), reads=[p_num, d_n], writes=[h_o])
                    P.dma("sp", hs_d[d_][rows, hh * 512:(hh + 1) * 512], h_o.ap, reads=[h_o])
                    P.op("pool", lambda e: e.tensor_scalar(out=k_l.ap, in0=kt_.ap[:, hh * 256:(hh + 1) * 256], scalar1=e_, scalar2=None, op0=ALU.mult),
                         reads=[kt_, EE], writes=[k_l])
                    for dc in range(2):
                        P.op("pe", lambda e: e.matmul(p_c[dc].ap, lhsT=k_l.ap[:, dc * 128:(dc + 1) * 128], rhs=vh, start=True, stop=True), reads=[k_l, vt_], writes=[p_c[dc]])
                        P.op("pe", lambda e: e.matmul(p_nu.ap[:, dc:dc + 1], lhsT=k_l.ap[:, dc * 128:(dc + 1) * 128], rhs=onesb.ap[:, 0:1], start=True, stop=True),
                             reads=[k_l, onesb], writes=[p_nu])
                    for dc in range(2):
                        P.op("dve", lambda e: e.scalar_tensor_tensor(out=C32[ch].ap[:, dc, :], in0=C32[ch].ap[:, dc, :], scalar=cd, in1=p_c[dc].ap, op0=ALU.mult, op1=ALU.add),
                             reads=[C32[ch], DM, p_c[dc]], writes=[C32[ch]])
                    P.op("dve", lambda e: e.scalar_tensor_tensor(out=N32[ch].ap, in0=N32[ch].ap, scalar=cd, in1=p_nu.ap[:, 0:2], op0=ALU.mult, op1=ALU.add),
                         reads=[N32[ch], DM, p_nu], writes=[N32[ch]])
    P.barrier()
    out_tiles = list(range(NT if need_ctx else NTL))
    with ExitStack() as es:
        ng = Buf(k.sb(es, "mo_ng", [128, D])[:])
        P.dma("sp", ng.ap, a_norm_g[j:j + 1, :].to_broadcast([128, D]), writes=[ng])
        hb = [[Buf(k.sb(es, "mo_h%d_%d" % (d_, i), [128, D])[:]) for i in range(2)] for d_ in range(2)]
        ob = [Buf(k.sb(es, "mo_o%d" % i, [128, D])[:]) for i in range(2)]
        sq = Buf(k.sb(es, "mo_sq", [128, D])[:])
        ss = Buf(k.sb(es, "mo_ss", [128, 4])[:])
        hT = [Buf(k.sb(es, "mo_hT%d" % i, [128, KC, 128], BF16)[:]) for i in range(2)]
        for t in out_tiles:
            rows = slice(t * 128, (t + 1) * 128)
            h0, h1, o_, h_T = hb[0][t % 2], hb[1][t % 2], ob[t % 2], hT[t % 2]
            P.dma("sp", h0.ap, hs_d[0][rows, :], writes=[h0])
            P.dma("act", h1.ap, hs_d[1][rows, :], writes=[h1])
            P.dma("sp", o_.ap, o_d[rows, :], writes=[o_])
            P.op("pool", lambda e: e.tensor_tensor(out=h0.ap, in0=h0.ap, in1=h1.ap, op=ALU.add), reads=[h0, h1], writes=[h0])
            P.op("act", lambda e: e.activation(out=sq.ap, in_=h0.ap, func=AF.Square), reads=[h0], writes=[sq])
            P.op("dve", lambda e: e.tensor_reduce(out=ss.ap, in_=sq.ap.rearrange("p (h d) -> p h d", d=512), axis=AX.X, op=ALU.add), reads=[sq], writes=[ss])
            rsqrt_eps(P, ss, ss.ap, [ss], 1.0 / 512.0)
            h3 = h0.ap.rearrange("p (h d) -> p h d", d=512)
            P.op("dve", lambda e: e.tensor_tensor(out=h3, in0=h3, in1=ss.ap.unsqueeze(2).to_broadcast([128, 4, 512]), op=ALU.mult), reads=[h0, ss], writes=[h0])
            P.op("pool", lambda e: e.tensor_tensor(out=o_.ap, in0=o_.ap, in1=ng.ap, op=ALU.mult), reads=[o_, ng], writes=[o_])
            P.op("dve", lambda e: e.tensor_tensor(out=h0.ap, in0=h0.ap, in1=o_.ap, op=ALU.mult), reads=[h0, o_], writes=[h0])
            transpose_to_T(S, h0, KC, h_T, t)
            P.dma("sp", hT_d[t], h_T.ap, reads=[h_T])
    P.barrier()
    with ExitStack() as es:
        epi = make_store_epi(S, es, ymix, tag="mo5s")
        linear_tm(S, hT_d, KC, a_w_out[j], D, out_tiles, epi, tag="mo5")


def top16(P, src, work, vals, idxs, reads_extra=()):
    sv, vv, iv, wv = src, vals, idxs, work
    P.op("dve", lambda e: e.max(out=vv.ap[:, 0:8], in_=sv.ap), reads=[sv], writes=[vv])
    P.op("dve", lambda e: e.max_index(out=iv.ap[:, 0:8], in_max=vv.ap[:, 0:8], in_values=sv.ap), reads=[sv, vv], writes=[iv])
    P.op("dve", lambda e: e.match_replace(out=wv.ap, in_to_replace=vv.ap[:, 0:8], in_values=sv.ap, imm_value=NEG), reads=[sv, vv], writes=[wv])
    P.op("dve", lambda e: e.max(out=vv.ap[:, 8:16], in_=wv.ap), reads=[wv], writes=[vv])
    P.op("dve", lambda e: e.max_index(out=iv.ap[:, 8:16], in_max=vv.ap[:, 8:16], in_values=wv.ap), reads=[wv, vv], writes=[iv])


def phase_peer(S, li, p_w_q, p_k1, p_k2, p_u, p_v, iota_d, nt, enable=True):
    k, P, hT_d, h_d, ymix, ident, ps = S["k"], S["P"], S["hT_d"], S["h_d"], S["ymix"], S["ident"], S["ps"]
    tiles = list(range(nt))
    qpT_d = k.dscr("p_qT%d" % li, [16, 128, T])
    with ExitStack() as es:
        stg = [Buf(k.sb(es, "pq_s%d" % i, [128, 512])[:]) for i in range(4)]
        cnt = [0]

        def epi_f(c, t0, ntl, pt):
            s = stg[cnt[0] % 4]
            cnt[0] += 1
            n = ntl * 128
            P.op("act" if cnt[0] % 2 else "dve",
                 (lambda e: e.activation(out=s.ap[:, 0:n], in_=pt.ap[:, 0:n], func=AF.Copy)) if cnt[0] % 2 else (lambda e: e.tensor_copy(out=s.ap[:, 0:n], in_=pt.ap[:, 0:n])),
                 reads=[pt], writes=[s])
            P.dma("sp", qpT_d[c, :, t0 * 128:t0 * 128 + n], s.ap[:, 0:n], reads=[s])
        linear_fm(S, hT_d, KC, p_w_q[li], D, tiles, epi_f, tag="pq")
    P.barrier()
    with ExitStack() as es:
        kraw = Buf(k.sb(es, "pp_kraw", [128, 2, 128])[:])
        kT = Buf(k.sb(es, "pp_kT", [128, 2, 128])[:])
        P.dma("sp", kraw.ap[:, 0, :], p_k1[li], writes=[kraw])
        P.dma("sp", kraw.ap[:, 1, :], p_k2[li], writes=[kraw])
        for hf in range(2):
            pt = next_ps(S)
            P.op("pe", lambda e: e.transpose(out=pt.ap[:, 0:128], in_=kraw.ap[:, hf, :], identity=ident.ap), reads=[kraw, ident], writes=[pt])
            P.op("dve", lambda e: e.tensor_copy(out=kT.ap[:, hf, :], in_=pt.ap[:, 0:128]), reads=[pt], writes=[kT])
        io16 = Buf(k.sb(es, "pp_io", [128, 16])[:])
        P.dma("sp", io16.ap, iota_d[0:1, :].to_broadcast([128, 16]), writes=[io16])
        qT = [Buf(k.sb(es, "pp_qT%d" % i, [128, 16, 128])[:]) for i in range(2)]
        hb = [Buf(k.sb(es, "pp_h%d" % i, [128, D])[:]) for i in range(2)]
        sc = Buf(k.sb(es, "pp_sc", [128, 16, 128])[:])
        wk = Buf(k.sb(es, "pp_wk", [128, 256])[:])
        V12 = Buf(k.sb(es, "pp_V12", [128, 16, 16])[:])
        I12 = Buf(k.sb(es, "pp_I12", [128, 16, 16], U32)[:])
        I12f = Buf(k.sb(es, "pp_I12f", [128, 16, 16])[:])
        cand = Buf(k.sb(es, "pp_cand", [128, 8, 256])[:])
        SC = Buf(k.sb(es, "pp_SC", [128, 8, 16])[:])
        CI = Buf(k.sb(es, "pp_CI", [128, 8, 16], U32)[:])
        IA = Buf(k.sb(es, "pp_IA", [128, 8, 16], U32)[:])
        IB = Buf(k.sb(es, "pp_IB", [128, 8, 16], U32)[:])
        IAf = Buf(k.sb(es, "pp_IAf", [128, 8, 16])[:])
        IBf = Buf(k.sb(es, "pp_IBf", [128, 8, 16])[:])
        oh = Buf(k.sb(es, "pp_oh", [128, 8, 16, 16])[:])
        EI = Buf(k.sb(es, "pp_EI", [128, 8, 16])[:])
        EJ = Buf(k.sb(es, "pp_EJ", [128, 8, 16])[:])
        EX = [Buf(k.sb(es, "pp_EX%d" % i, [128, 128], U32)[:]) for i in range(2)]
        gate = Buf(k.sb(es, "pp_gate", [128, 8, 16])[:])
        gs = Buf(k.sb(es, "pp_gs", [128, 8])[:])
        av = Buf(k.sb(es, "pp_a", [128, 128])[:])
        hid = Buf(k.sb(es, "pp_hid", [128, 128])[:])
        junk = Buf(k.sb(es, "pp_junk", [128, D])[:])
        acc = [Buf(k.sb(es, "pp_acc%d" % i, [128, D])[:]) for i in range(2)]
        NG = 6
        gb = [Buf(k.sb(es, "pp_g%d" % i, [128, D])[:]) for i in range(NG)]
        gi = 0
        for t in tiles:
            q_, h_, ex_, acc_ = qT[t % 2], hb[t % 2], EX[t % 2], acc[t % 2]
            P.dma("sp", q_.ap, qpT_d[:, :, t * 128:(t + 1) * 128].rearrange("c d t -> d c t"), writes=[q_])
            P.dma("act", h_.ap, h_d[t * 128:(t + 1) * 128, :], writes=[h_])
            for g in range(4):
                pt = ps[g]
                for q4 in range(4):
                    c = g * 4 + q4
                    P.op("pe", lambda e: e.matmul(pt.ap[:, q4 * 128:(q4 + 1) * 128], lhsT=q_.ap[:, c, :], rhs=kT.ap[:, c % 2, :], start=True, stop=True),
                         reads=[q_, kT], writes=[pt])
                P.op("act", lambda e: e.activation(out=sc.ap[:, g * 4:(g + 1) * 4, :], in_=pt.ap.rearrange("p (a b) -> p a b", a=4), func=AF.Copy), reads=[pt], writes=[sc])
            for c in range(16):
                sv = Buf(sc.ap[:, c, :]); sv.w = sc.w
                wv = Buf(wk.ap[:, 0:128]); wv.w = wk.w; wv.r = wk.r
                vv = Buf(V12.ap[:, c, :]); vv.w = V12.w; vv.r = V12.r
                iv = Buf(I12.ap[:, c, :]); iv.w = I12.w; iv.r = I12.r
                top16(P, sv, wv, vv, iv)
                wk.w, V12.w, I12.w = wv.w, vv.w, iv.w
                wk.r, V12.r, I12.r = wv.r, vv.r, iv.r
                sc.r.update(sv.r)
            V4 = V12.ap.rearrange("p (h f) a -> p h f a", f=2)
            P.op("dve", lambda e: e.tensor_tensor(out=cand.ap.rearrange("p h (a b) -> p h a b", b=16),
                                                  in0=V4[:, :, 0, :].unsqueeze(3).to_broadcast([128, 8, 16, 16]),
                                                  in1=V4[:, :, 1, :].unsqueeze(2).to_broadcast([128, 8, 16, 16]), op=ALU.add), reads=[V12], writes=[cand])
            for hh in range(8):
                sv = Buf(cand.ap[:, hh, :]); sv.w = cand.w
                wv = Buf(wk.ap[:, 0:256]); wv.w = wk.w; wv.r = wk.r
                vv = Buf(SC.ap[:, hh, :]); vv.w = SC.w; vv.r = SC.r
                iv = Buf(CI.ap[:, hh, :]); iv.w = CI.w; iv.r = CI.r
                top16(P, sv, wv, vv, iv)
                wk.w, SC.w, CI.w = wv.w, vv.w, iv.w
                wk.r, SC.r, CI.r = wv.r, vv.r, iv.r
                cand.r.update(sv.r)
            P.op("dve", lambda e: e.tensor_tensor(out=gate.ap, in0=SC.ap, in1=SC.ap[:, :, 0:1].to_broadcast([128, 8, 16]), op=ALU.subtract), reads=[SC], writes=[gate])
            P.op("act", lambda e: e.activation(out=gate.ap, in_=gate.ap, func=AF.Exp), reads=[gate], writes=[gate])
            P.op("dve", lambda e: e.tensor_reduce(out=gs.ap, in_=gate.ap, axis=AX.X, op=ALU.add), reads=[gate], writes=[gs])
            P.op("dve", lambda e: e.reciprocal(out=gs.ap, in_=gs.ap), reads=[gs], writes=[gs])
            P.op("dve", lambda e: e.tensor_tensor(out=gate.ap, in0=gate.ap, in1=gs.ap.unsqueeze(2).to_broadcast([128, 8, 16]), op=ALU.mult), reads=[gate, gs], writes=[gate])
            P.op("dve", lambda e: e.tensor_single_scalar(out=IA.ap, in_=CI.ap, scalar=4, op=ALU.logical_shift_right), reads=[CI], writes=[IA])
            P.op("dve", lambda e: e.tensor_single_scalar(out=IB.ap, in_=CI.ap, scalar=15, op=ALU.bitwise_and), reads=[CI], writes=[IB])
            P.op("dve", lambda e: e.tensor_copy(out=IAf.ap, in_=IA.ap), reads=[IA], writes=[IAf])
            P.op("dve", lambda e: e.tensor_copy(out=IBf.ap, in_=IB.ap), reads=[IB], writes=[IBf])
            P.op("dve", lambda e: e.tensor_copy(out=I12f.ap, in_=I12.ap), reads=[I12], writes=[I12f])
            I4 = I12f.ap.rearrange("p (h f) a -> p h f a", f=2)
            io4 = io16.ap.unsqueeze(1).unsqueeze(1).to_broadcast([128, 8, 16, 16])
            for (src, f, dst) in ((IAf, 0, EI), (IBf, 1, EJ)):
                P.op("dve", lambda e: e.tensor_tensor(out=oh.ap, in0=src.ap.unsqueeze(3).to_broadcast([128, 8, 16, 16]), in1=io4, op=ALU.is_equal), reads=[src, io16], writes=[oh])
                P.op("dve", lambda e: e.tensor_tensor(out=oh.ap, in0=oh.ap, in1=I4[:, :, f, :].unsqueeze(2).to_broadcast([128, 8, 16, 16]), op=ALU.mult), reads=[oh, I12f], writes=[oh])
                P.op("dve", lambda e: e.tensor_reduce(out=dst.ap, in_=oh.ap, axis=AX.X, op=ALU.add), reads=[oh], writes=[dst])
            P.op("dve", lambda e: e.scalar_tensor_tensor(out=EI.ap, in0=EI.ap, scalar=128.0, in1=EJ.ap, op0=ALU.mult, op1=ALU.add), reads=[EI, EJ], writes=[EI])
            if li > 0:
                P.op("dve", lambda e: e.tensor_scalar_add(out=EI.ap, in0=EI.ap, scalar1=float(li * 16384)), reads=[EI], writes=[EI])
            P.op("dve", lambda e: e.tensor_copy(out=ex_.ap, in_=EI.ap.rearrange("p h a -> p (h a)")), reads=[EI], writes=[ex_])
            for s in range(128):
                g_ = gb[gi % NG]
                gi += 1
                P.dma("pool", g_.ap, p_u.rearrange("l e d -> (l e) d"), reads=[ex_], writes=[g_], indirect=bass.IndirectOffsetOnAxis(ap=ex_.ap[:, s:s + 1], axis=0))
                P.op("dve", lambda e: e.scalar_tensor_tensor(out=junk.ap, in0=h_.ap, scalar=1.0, in1=g_.ap, op0=ALU.mult, op1=ALU.mult, accum_out=av.ap[:, s:s + 1]),
                     reads=[h_, g_], writes=[junk, av])
            P.op("act", lambda e: e.activation(out=hid.ap, in_=av.ap, func=AF.Gelu_apprx_tanh), reads=[av], writes=[hid])
            P.op("dve", lambda e: e.tensor_tensor(out=hid.ap, in0=hid.ap, in1=gate.ap.rearrange("p h a -> p (h a)"), op=ALU.mult), reads=[hid, gate], writes=[hid])
            for s in range(128):
                g_ = gb[gi % NG]
                gi += 1
                P.dma("pool", g_.ap, p_v.rearrange("l e d -> (l e) d"), reads=[ex_], writes=[g_], indirect=bass.IndirectOffsetOnAxis(ap=ex_.ap[:, s:s + 1], axis=0))
                if s == 0:
                    P.op("dve", lambda e: e.tensor_scalar(out=acc_.ap, in0=g_.ap, scalar1=hid.ap[:, 0:1], scalar2=None, op0=ALU.mult), reads=[g_, hid], writes=[acc_])
                else:
                    P.op("dve", lambda e: e.scalar_tensor_tensor(out=acc_.ap, in0=g_.ap, scalar=hid.ap[:, s:s + 1], in1=acc_.ap, op0=ALU.mult, op1=ALU.add),
                         reads=[g_, hid, acc_], writes=[acc_])
            P.dma("sp", ymix[t * 128:(t + 1) * 128, :], acc_.ap, reads=[acc_])


_CACHE = {}


def host_consts():
    ident = np.eye(128, dtype=np.float32)
    s = np.arange(128)
    tri = np.stack([(s[:, None] <= s[None, :]), (s[:, None] >= s[None, :])]).astype(np.float32)
    t = np.arange(SEQ)
    rows = (t // 64).astype(np.float32)
    cols = (t % 64).astype(np.float32)
    freqs = (10000.0 ** (-np.arange(32, dtype=np.float32) / np.float32(32))).astype(np.float32)
    ang = np.concatenate([rows[:, None] * freqs[None, :], cols[:, None] * freqs[None, :]], axis=1).astype(np.float32)
    rope = np.stack([np.cos(ang), np.sin(ang)], axis=1).astype(np.float32)
    iota16 = np.arange(16, dtype=np.float32)[None, :]
    return dict(ident=ident, tri=tri, rope=rope, iota16=iota16)


def make_in_maps(inputs):
    consts = host_consts()
    shared = {n: np.ascontiguousarray(inputs[n]) for n in (
        "w_mod", "b_mod", "ln_g", "ln_b", "a_w_in", "a_b_gate", "a_norm_g", "a_w_out", "b_w_in", "b_b_in", "b_ln_g", "b_ln_b",
        "b_w_s", "b_b_s", "b_w_out", "c_w_qkv", "c_q_g", "c_k_g", "c_w_out", "p_w_q", "p_k1", "p_k2", "p_u", "p_v")}
    shared.update(consts)
    maps = []
    for b in range(8):
        m = dict(shared)
        m["xin"] = np.ascontiguousarray(np.concatenate([inputs["x"][b], inputs["ctx"][b]], axis=0))
        m["cc"] = np.ascontiguousarray(np.stack([inputs["c"][b], inputs["c_ctx"]], axis=0))
        maps.append(m)
    return maps


def kernel(**inputs):
    inputs = {k_: np.asarray(v) for k_, v in inputs.items()}
    if "nc" not in _CACHE:
        _CACHE["nc"] = build_program()[0]
    nc = _CACHE["nc"]
    maps = make_in_maps(inputs)
    res = run_bass_kernel_spmd(nc, maps, core_ids=list(range(8)))
    out = np.stack([np.asarray(r["y"]).reshape(SEQ, D) for r in res.results], axis=0)
    return out.astype(np.float32)
```

```python
import numpy as np
from contextlib import ExitStack
import concourse.bass as bass
import concourse.mybir as mybir
from concourse.bass_utils import run_bass_kernel_spmd

F32 = mybir.dt.float32
BF16 = mybir.dt.bfloat16
U32 = mybir.dt.uint32
I32 = mybir.dt.int32
AF = mybir.ActivationFunctionType
ALU = mybir.AluOpType
AX = mybir.AxisListType

D = 2048
KC = 16
SEQ = 4096
CTX = 256
T = SEQ + CTX
NT = T // 128
NTL = SEQ // 128
DEPTH = 4
ALPHA = (2 * DEPTH) ** 0.25
EPS = 1e-6
A_IN = 6160
NEG = -1.0e30


class Buf:
    __slots__ = ("ap", "w", "r")

    def __init__(self, ap):
        self.ap = ap
        self.w = None
        self.r = {}


class Prog:
    LIMIT = 30000

    def __init__(self, nc):
        self.nc = nc
        self.engs = {"pe": nc.tensor, "dve": nc.vector, "act": nc.scalar, "pool": nc.gpsimd, "sp": nc.sync}
        self.sems = {}
        self.cnt = {}
        self.key = {}
        self.gen = {}
        for e in ("pe", "dve", "act", "pool"):
            self.gen[e] = 0
            self._newkey(e)
        self.clock = {e: {} for e in self.engs}
        self.dq = {}
        for q, n in (("sp", 16), ("pool", 16), ("act", 6)):
            keys = []
            for i in range(n):
                k = "d_%s%d" % (q, i)
                self.sems[k] = nc.alloc_semaphore(name=k)
                self.cnt[k] = 0
                keys.append(k)
            self.dq[q] = [keys, 0]
        self.n_inst = 0

    def _newkey(self, e):
        k = "%s_%d" % (e, self.gen[e])
        self.gen[e] += 1
        self.sems[k] = self.nc.alloc_semaphore(name="s_" + k)
        self.cnt[k] = 0
        self.key[e] = k

    def _wait(self, e, tok):
        if tok is None:
            return
        k, v = tok
        if self.clock[e].get(k, 0) >= v:
            return
        if e == "pe" and k.startswith("pe_"):
            return
        self.engs[e].wait_ge(self.sems[k], v)
        self.clock[e][k] = v

    def _deps(self, e, reads, writes):
        for b in reads:
            self._wait(e, b.w)
        for b in writes:
            self._wait(e, b.w)
            for kv in list(b.r.items()):
                self._wait(e, kv)

    def _mark(self, tok, reads, writes):
        k, v = tok
        for b in reads:
            b.r[k] = v
        for b in writes:
            b.w = tok
            b.r = {}

    def op(self, e, fn, reads=(), writes=()):
        self._deps(e, reads, writes)
        inst = fn(self.engs[e])
        k = self.key[e]
        inst.then_inc(self.sems[k], 1)
        self.cnt[k] += 1
        tok = (k, self.cnt[k])
        self._mark(tok, reads, writes)
        if self.cnt[k] >= self.LIMIT:
            self._newkey(e)
        self.n_inst += 1
        return tok

    def dma(self, q, out, in_, reads=(), writes=(), indirect=None, **kw):
        keys, rr = self.dq[q]
        k = keys[rr % len(keys)]
        self.dq[q][1] += 1
        if self.cnt[k] > 0:
            self._wait(q, (k, self.cnt[k]))
        self._deps(q, reads, writes)
        eng = self.engs[q]
        if indirect is not None:
            inst = eng.indirect_dma_start(out=out, out_offset=None, in_=in_, in_offset=indirect, **kw)
        else:
            inst = eng.dma_start(out=out, in_=in_, **kw)
        inst.then_inc(self.sems[k], 16)
        self.cnt[k] += 16
        if self.cnt[k] >= self.LIMIT:
            self._wait(q, (k, self.cnt[k]))
            tok = (k, self.cnt[k])
            self._mark(tok, reads, writes)
            nk = k + "n"
            self.sems[nk] = self.nc.alloc_semaphore(name=nk)
            self.cnt[nk] = 0
            keys[(rr) % len(keys)] = nk
            self.n_inst += 1
            return tok
        tok = (k, self.cnt[k])
        self._mark(tok, reads, writes)
        self.n_inst += 1
        return tok

    def barrier(self):
        for e in self.engs:
            for k, v in self.cnt.items():
                if v > 0:
                    self._wait(e, (k, v))

    def final_wait(self, e="sp"):
        for k, v in self.cnt.items():
            if v > 0:
                self._wait(e, (k, v))


def rsqrt_eps(P, dst, src_ap, reads, scale):
    P.op("dve", lambda e: e.tensor_scalar(out=dst.ap, in0=src_ap, scalar1=float(scale), scalar2=float(EPS), op0=ALU.mult, op1=ALU.add),
         reads=list(reads), writes=[dst])
    P.op("act", lambda e: e.activation(out=dst.ap, in_=dst.ap, func=AF.Sqrt), reads=[dst], writes=[dst])
    P.op("dve", lambda e: e.reciprocal(out=dst.ap, in_=dst.ap), reads=[dst], writes=[dst])


def bcast_rows(ap_row, n):
    return ap_row.to_broadcast([n, ap_row.shape[-1]])


class K:
    def __init__(self, n_layers=DEPTH, dbg=()):
        self.n_layers = n_layers
        self.dbg = set(dbg)
        nc = bass.Bass("TRN2", target_bir_lowering=False)
        self.nc = nc
        self.P = Prog(nc)
        self.es = ExitStack()
        self.inp = {}

    def din(self, name, shape, dt=F32):
        ap = self.nc.dram_tensor(name, list(shape), dt, kind="ExternalInput").ap()
        self.inp[name] = ap
        return ap

    def dscr(self, name, shape, dt=F32, out=False):
        kind = "ExternalOutput" if (out or name in self.dbg) else "Internal"
        return self.nc.dram_tensor(name, list(shape), dt, kind=kind).ap()

    def sb(self, es, name, shape, dt=F32):
        self.uid = getattr(self, "uid", 0) + 1
        t = es.enter_context(self.nc.sbuf_tensor("%s_%d" % (name, self.uid), list(shape), dt))
        return t

    def psum(self, es, name, shape, dt=F32):
        return es.enter_context(self.nc.psum_tensor(name, list(shape), dt))


def build_program(n_layers=DEPTH, dbg=(), peer=True):
    k = K(n_layers, dbg)
    nc, P = k.nc, k.P
    xin = k.din("xin", [T, D])
    cc = k.din("cc", [2, D])
    w_mod = k.din("w_mod", [DEPTH, D, 6 * D])
    b_mod = k.din("b_mod", [DEPTH, 6 * D])
    ln_g = k.din("ln_g", [DEPTH, 2, D])
    ln_b = k.din("ln_b", [DEPTH, 2, D])
    a_w_in = k.din("a_w_in", [2, D, A_IN])
    a_b_gate = k.din("a_b_gate", [2, 16])
    a_norm_g = k.din("a_norm_g", [2, D])
    a_w_out = k.din("a_w_out", [2, D, D])
    b_w_in = k.din("b_w_in", [1, D, 8192])
    b_b_in = k.din("b_b_in", [1, 8192])
    b_ln_g = k.din("b_ln_g", [1, 4096])
    b_ln_b = k.din("b_ln_b", [1, 4096])
    b_w_s = k.din("b_w_s", [1, 8, 128, 128])
    b_b_s = k.din("b_b_s", [1, 8, 128])
    b_w_out = k.din("b_w_out", [1, 4096, D])
    c_w_qkv = k.din("c_w_qkv", [1, D, 3072])
    c_q_g = k.din("c_q_g", [1, 128])
    c_k_g = k.din("c_k_g", [1, 128])
    c_w_out = k.din("c_w_out", [1, D, D])
    p_w_q = k.din("p_w_q", [DEPTH, D, D])
    p_k1 = k.din("p_k1", [DEPTH, 128, 128])
    p_k2 = k.din("p_k2", [DEPTH, 128, 128])
    p_u = k.din("p_u", [DEPTH, 16384, D])
    p_v = k.din("p_v", [DEPTH, 16384, D])
    ident_d = k.din("ident", [128, 128])
    tri_d = k.din("tri", [2, 128, 128])
    rope_d = k.din("rope", [SEQ, 2, 64])
    iota_d = k.din("iota16", [1, 128])

    yout = k.dscr("y", [SEQ, D], out=True)
    xres = k.dscr("xres", [T, D])
    modd = k.dscr("modd", [DEPTH, 2, 6 * D])
    hT_d = k.dscr("hT", [NT, 128, KC, 128], BF16)
    h_d = k.dscr("h_tm", [T, D])
    ymix = k.dscr("ymix", [T, D])
    S = dict(k=k, nc=nc, P=P, xin=xin, xres=xres, modd=modd, hT_d=hT_d, h_d=h_d, ymix=ymix, yout=yout)

    with ExitStack() as top:
        ident = Buf(k.sb(top, "ident_sb", [128, 128])[:])
        identb = Buf(k.sb(top, "identb_sb", [128, 128], BF16)[:])
        P.dma("sp", ident.ap, ident_d[:, :], writes=[ident])
        P.op("dve", lambda e: e.tensor_copy(out=identb.ap, in_=ident.ap), reads=[ident], writes=[identb])
        ps_t = [k.psum(top, "ps%d" % i, [128, 512]) for i in range(8)]
        ps = [Buf(t[:]) for t in ps_t]
        S.update(ident=ident, identb=identb, ps=ps, ps_t=ps_t)
        S["psb"] = [Buf(ps_t[i][:].bitcast(BF16)) for i in (6, 7)]
        S["Gd"] = k.dscr("Gd", [128, 128, T], BF16)

        phase_mod(S, cc, w_mod, b_mod, n_layers)
        P.barrier()
        for li in range(n_layers):
            need_ctx = li < DEPTH - 1
            nt_out = NT if need_ctx else NTL
            src = xin if li == 0 else xres
            mixer, j = li % 3, li // 3
            phase_prep(S, src, li, 0, NT, want_h=False)
            P.barrier()
            if mixer == 0:
                mix_mlstm(S, j, a_w_in, a_b_gate, a_norm_g, a_w_out, tri_d, need_ctx)
            elif mixer == 1:
                mix_gmlp(S, j, b_w_in, b_b_in, b_ln_g, b_ln_b, b_w_s, b_b_s, b_w_out, need_ctx)
            else:
                mix_gqa(S, j, c_w_qkv, c_q_g, c_k_g, c_w_out, rope_d, need_ctx)
            P.barrier()
            phase_ln(S, src, ymix, xres, li, 0, ln_g, ln_b, nt_out)
            P.barrier()
            if "x_mid%d" % li in k.dbg:
                dump(S, xres, "x_mid%d" % li, T)
            phase_prep(S, xres, li, 1, nt_out, want_h=not PEER_DENSE)
            P.barrier()
            phase_peer(S, li, p_w_q, p_k1, p_k2, p_u, p_v, iota_d, nt_out, enable=peer)
            P.barrier()
            last = (li == n_layers - 1)
            phase_ln(S, xres, ymix, (yout if last else xres), li, 1, ln_g, ln_b, nt_out if not last else NTL)
            P.barrier()
        P.final_wait("sp")
        P.final_wait("act")
    return nc, k


def dump(S, src, name, rows):
    k, P = S["k"], S["P"]
    dst = k.dscr(name, [rows, D], out=True)
    for t in range(rows // 128):
        P.dma("sp", dst[t * 128:(t + 1) * 128, :], src[t * 128:(t + 1) * 128, :])
    P.barrier()


def phase_mod(S, cc, w_mod, b_mod, n_layers):
    k, P, ps, modd = S["k"], S["P"], S["ps"], S["modd"]
    with ExitStack() as es:
        craw = Buf(k.sb(es, "craw", [128, 2, KC])[:])
        condT = Buf(k.sb(es, "condT", [128, KC, 2])[:])
        P.dma("sp", craw.ap, cc.rearrange("r (kc p) -> p r kc", p=128), writes=[craw], allow_slow_non_contiguous=True)
        P.op("act", lambda e: e.activation(out=condT.ap.rearrange("p kc r -> p r kc"), in_=craw.ap, func=AF.Silu),
             reads=[craw], writes=[condT])
        wbuf = [Buf(k.sb(es, "wm%d" % i, [128, KC, 512])[:]) for i in range(2)]
        bbuf = [Buf(k.sb(es, "bm%d" % i, [2, 512])[:]) for i in range(2)]
        obuf = [Buf(k.sb(es, "om%d" % i, [2, 512])[:]) for i in range(2)]
        it = 0
        for li in range(n_layers):
            for nci in range(24):
                w, bb, ob, pt = wbuf[it % 2], bbuf[it % 2], obuf[it % 2], ps[it % 2]
                n0 = nci * 512
                P.dma("sp" if it % 2 == 0 else "act", w.ap, w_mod[li, :, n0:n0 + 512].rearrange("(kc p) n -> p kc n", p=128), writes=[w])
                P.dma("sp", bb.ap, b_mod[li:li + 1, n0:n0 + 512].to_broadcast([2, 512]), writes=[bb])
                for kc in range(KC):
                    P.op("pe", lambda e, kc=kc, w=w, pt=pt: e.matmul(pt.ap[0:2, :], lhsT=condT.ap[:, kc, :], rhs=w.ap[:, kc, :],
                                                                 start=(kc == 0), stop=(kc == KC - 1)),
                         reads=[condT, w], writes=[pt])
                P.op("dve", lambda e, pt=pt, bb=bb, ob=ob: e.tensor_tensor(out=ob.ap, in0=pt.ap[0:2, :], in1=bb.ap, op=ALU.add),
                     reads=[pt, bb], writes=[ob])
                P.dma("sp", modd[li, :, n0:n0 + 512], ob.ap, reads=[ob])
                it += 1


def load_mod_bcast(S, es, li, idx, rows, name, plus_one=False):
    k, P, modd = S["k"], S["P"], S["modd"]
    b = Buf(k.sb(es, name, [128, D])[:])
    P.dma("sp", b.ap, modd[li, rows:rows + 1, idx * D:(idx + 1) * D].to_broadcast([128, D]), writes=[b])
    if plus_one:
        P.op("pool", lambda e: e.tensor_scalar_add(out=b.ap, in0=b.ap, scalar1=1.0), reads=[b], writes=[b])
    return b


def phase_prep(S, src, li, sub, nt, want_h):
    k, P, ps, hT_d, h_d, ident = S["k"], S["P"], S["ps"], S["hT_d"], S["h_d"], S["ident"]
    with ExitStack() as es:
        sh = [load_mod_bcast(S, es, li, 3 * sub + 0, r, "sh%d" % r) for r in range(2)]
        sc = [load_mod_bcast(S, es, li, 3 * sub + 1, r, "sc%d" % r, plus_one=True) for r in range(2)]
        xb = [Buf(k.sb(es, "px%d" % i, [128, D])[:]) for i in range(2)]
        hb = [Buf(k.sb(es, "ph%d" % i, [128, D])[:]) for i in range(2)]
        hTb = [Buf(k.sb(es, "phT%d" % i, [128, KC, 128], BF16)[:]) for i in range(2)]
        for t in range(nt):
            r = 0 if t < NTL else 1
            x, h, hT = xb[t % 2], hb[t % 2], hTb[t % 2]
            P.dma("sp", x.ap, src[t * 128:(t + 1) * 128, :], writes=[x])
            P.op("pool", lambda e: e.tensor_tensor(out=h.ap, in0=x.ap, in1=sc[r].ap, op=ALU.mult), reads=[x, sc[r]], writes=[h])
            P.op("dve", lambda e: e.tensor_tensor(out=h.ap, in0=h.ap, in1=sh[r].ap, op=ALU.add), reads=[h, sh[r]], writes=[h])
            if want_h:
                P.dma("act", h_d[t * 128:(t + 1) * 128, :], h.ap, reads=[h])
            for g in range(4):
                pt = ps[(t * 4 + g) % 8]
                for q in range(4):
                    kc = g * 4 + q
                    P.op("pe", lambda e, kc=kc, q=q, pt=pt: e.transpose(out=pt.ap[:, q * 128:(q + 1) * 128], in_=h.ap[:, kc * 128:(kc + 1) * 128],
                                                                   identity=ident.ap), reads=[h, ident], writes=[pt])
                eng = "act" if g % 2 == 0 else "dve"
                if eng == "act":
                    P.op("act", lambda e, g=g, pt=pt: e.activation(out=hT.ap[:, g * 4:(g + 1) * 4, :], in_=pt.ap.rearrange("p (q t) -> p q t", q=4), func=AF.Copy),
                         reads=[pt], writes=[hT])
                else:
                    P.op("dve", lambda e, g=g, pt=pt: e.tensor_copy(out=hT.ap[:, g * 4:(g + 1) * 4, :], in_=pt.ap.rearrange("p (q t) -> p q t", q=4)),
                         reads=[pt], writes=[hT])
            P.dma("sp", hT_d[t], hT.ap, reads=[hT])


def phase_ln(S, xsrc, ysrc, dst, li, sub, ln_g, ln_b, nt):
    k, P = S["k"], S["P"]
    with ExitStack() as es:
        gt = [load_mod_bcast(S, es, li, 3 * sub + 2, r, "lg%d" % r) for r in range(2)]
        lg = Buf(k.sb(es, "lng", [128, D])[:])
        lb = Buf(k.sb(es, "lnb", [128, D])[:])
        P.dma("sp", lg.ap, ln_g[li, sub:sub + 1, :].to_broadcast([128, D]), writes=[lg])
        P.dma("sp", lb.ap, ln_b[li, sub:sub + 1, :].to_broadcast([128, D]), writes=[lb])
        xb = [Buf(k.sb(es, "lx%d" % i, [128, D])[:]) for i in range(2)]
        yb = [Buf(k.sb(es, "ly%d" % i, [128, D])[:]) for i in range(2)]
        st = [Buf(k.sb(es, "lst%d" % i, [128, 4, 6])[:]) for i in range(2)]
        mv = [Buf(k.sb(es, "lmv%d" % i, [128, 2])[:]) for i in range(2)]
        rs = [Buf(k.sb(es, "lrs%d" % i, [128, 1])[:]) for i in range(2)]
        for t in range(nt):
            r = 0 if t < NTL else 1
            x, y, s_, m_, r_ = xb[t % 2], yb[t % 2], st[t % 2], mv[t % 2], rs[t % 2]
            P.dma("sp", x.ap, xsrc[t * 128:(t + 1) * 128, :], writes=[x])
            P.dma("act", y.ap, ysrc[t * 128:(t + 1) * 128, :], writes=[y])
            P.op("pool", lambda e: e.tensor_tensor(out=y.ap, in0=y.ap, in1=gt[r].ap, op=ALU.mult), reads=[y, gt[r]], writes=[y])
            P.op("dve", lambda e: e.scalar_tensor_tensor(out=x.ap, in0=x.ap, scalar=float(ALPHA), in1=y.ap, op0=ALU.mult, op1=ALU.add),
                 reads=[x, y], writes=[x])
            for q in range(4):
                P.op("dve", lambda e, q=q: e.bn_stats(out=s_.ap[:, q, :], in_=x.ap[:, q * 512:(q + 1) * 512]), reads=[x], writes=[s_])
            P.op("dve", lambda e: e.bn_aggr(out=m_.ap, in_=s_.ap.rearrange("p a b -> p (a b)")), reads=[s_], writes=[m_])
            rsqrt_eps(P, r_, m_.ap[:, 1:2], [m_], 1.0)
            P.op("dve", lambda e: e.tensor_scalar(out=x.ap, in0=x.ap, scalar1=m_.ap[:, 0:1], scalar2=r_.ap[:, 0:1], op0=ALU.subtract, op1=ALU.mult),
                 reads=[x, m_, r_], writes=[x])
            P.op("pool", lambda e: e.tensor_tensor(out=x.ap, in0=x.ap, in1=lg.ap, op=ALU.mult), reads=[x, lg], writes=[x])
            P.op("dve", lambda e: e.tensor_tensor(out=x.ap, in0=x.ap, in1=lb.ap, op=ALU.add), reads=[x, lb], writes=[x])
            P.dma("sp", dst[t * 128:(t + 1) * 128, :], x.ap, reads=[x])


def load_hT_group(S, es_bufs, t0, ntile, q="sp"):
    P, hT_d = S["P"], S["hT_d"]
    g = es_bufs
    for i in range(ntile):
        P.dma(q, g.ap[:, :, i * 128:(i + 1) * 128], hT_d[t0 + i], writes=[g])
    return g


def load_w_chunk(S, wbuf, w_ap, k0_chunks, n0, n, q="pool"):
    P = S["P"]
    P.dma(q, wbuf.ap[:, 0:k0_chunks, 0:n], w_ap[:, n0:n0 + n].rearrange("(kc p) n -> p kc n", p=128), writes=[wbuf])


NPS = 8
PEER_DENSE = True


def next_ps(S, lo=0, hi=NPS):
    r = S.setdefault("rot", 0)
    S["rot"] = r + 1
    return S["ps"][lo + r % (hi - lo)]


def linear_tm(S, src_T, kch, w_ap, N, tiles, epi, G=8, tag="l"):
    k, P = S["k"], S["P"]
    with ExitStack() as es:
        hg = [Buf(k.sb(es, tag + "hg%d" % i, [128, kch, 128], BF16)[:]) for i in range(G)]
        wb = [Buf(k.sb(es, tag + "w%d" % i, [128, kch, 512], BF16)[:]) for i in range(2)]
        it = 0
        for g0 in range(0, len(tiles), G):
            grp = tiles[g0:g0 + G]
            for i, t in enumerate(grp):
                P.dma("sp", hg[i].ap, src_T[t], writes=[hg[i]])
            for n0 in range(0, N, 512):
                n = min(512, N - n0)
                w = wb[it % 2]
                it += 1
                P.dma("pool", w.ap[:, :, 0:n], w_ap[:, n0:n0 + n].rearrange("(kc p) n -> p kc n", p=128), writes=[w])
                for i, t in enumerate(grp):
                    pt = next_ps(S)
                    for kc in range(kch):
                        P.op("pe", lambda e: e.matmul(pt.ap[:, 0:n], lhsT=hg[i].ap[:, kc, :], rhs=w.ap[:, kc, 0:n],
                                                      start=(kc == 0), stop=(kc == kch - 1)), reads=[hg[i], w], writes=[pt])
                    epi(t, n0, n, pt)


def linear_fm(S, src_T, kch, w_ap, N, tiles, epi, G=8, tag="f"):
    k, P = S["k"], S["P"]
    with ExitStack() as es:
        hg = [Buf(k.sb(es, tag + "hg%d" % i, [128, kch, 512], BF16)[:]) for i in range(G // 4)]
        wb = [Buf(k.sb(es, tag + "w%d" % i, [128, kch, 512], BF16)[:]) for i in range(2)]
        it = 0
        for g0 in range(0, len(tiles), G):
            grp = tiles[g0:g0 + G]
            blocks = [grp[i:i + 4] for i in range(0, len(grp), 4)]
            for bi, blk in enumerate(blocks):
                for i, t in enumerate(blk):
                    P.dma("sp", hg[bi].ap[:, :, i * 128:(i + 1) * 128], src_T[t], writes=[hg[bi]])
            for n0 in range(0, N, 512):
                n = min(512, N - n0)
                w = wb[it % 2]
                it += 1
                P.dma("pool", w.ap[:, :, 0:n], w_ap[:, n0:n0 + n].rearrange("(kc p) n -> p kc n", p=128), writes=[w])
                for cc in range(n // 128):
                    for bi, blk in enumerate(blocks):
                        ntok = len(blk) * 128
                        pt = next_ps(S)
                        for kc in range(kch):
                            P.op("pe", lambda e: e.matmul(pt.ap[:, 0:ntok], lhsT=w.ap[:, kc, cc * 128:(cc + 1) * 128], rhs=hg[bi].ap[:, kc, 0:ntok],
                                                          start=(kc == 0), stop=(kc == kch - 1)), reads=[hg[bi], w], writes=[pt])
                        epi(n0 // 128 + cc, blk[0], len(blk), pt)


def make_store_epi(S, es, dst, tag="st", nbuf=4, col_off=0):
    k, P = S["k"], S["P"]
    stg = [Buf(k.sb(es, tag + "%d" % i, [128, 512])[:]) for i in range(nbuf)]
    cnt = [0]

    def epi(t, n0, n, pt):
        s = stg[cnt[0] % nbuf]
        eng = "act" if cnt[0] % 2 == 0 else "dve"
        cnt[0] += 1
        if eng == "act":
            P.op("act", lambda e: e.activation(out=s.ap[:, 0:n], in_=pt.ap[:, 0:n], func=AF.Copy), reads=[pt], writes=[s])
        else:
            P.op("dve", lambda e: e.tensor_copy(out=s.ap[:, 0:n], in_=pt.ap[:, 0:n]), reads=[pt], writes=[s])
        P.dma("sp", dst[t * 128:(t + 1) * 128, col_off + n0:col_off + n0 + n], s.ap[:, 0:n], reads=[s])
    return epi


def transpose_to_T(S, src, nch, dstT, t):
    P, ident = S["P"], S["ident"]
    for g in range(nch // 4):
        pt = next_ps(S)
        for q in range(4):
            kc = g * 4 + q
            P.op("pe", lambda e: e.transpose(out=pt.ap[:, q * 128:(q + 1) * 128], in_=src.ap[:, kc * 128:(kc + 1) * 128], identity=ident.ap),
                 reads=[src, ident], writes=[pt])
        if g % 2 == 0:
            P.op("act", lambda e: e.activation(out=dstT.ap[:, g * 4:(g + 1) * 4, :], in_=pt.ap.rearrange("p (q t) -> p q t", q=4), func=AF.Copy),
                 reads=[pt], writes=[dstT])
        else:
            P.op("dve", lambda e: e.tensor_copy(out=dstT.ap[:, g * 4:(g + 1) * 4, :], in_=pt.ap.rearrange("p (q t) -> p q t", q=4)),
                 reads=[pt], writes=[dstT])


def mix_gmlp(S, j, b_w_in, b_b_in, b_ln_g, b_ln_b, b_w_s, b_b_s, b_w_out, need_ctx):
    k, P, hT_d, ymix, ident = S["k"], S["P"], S["hT_d"], S["ymix"], S["ident"]
    tiles = list(range(NT if need_ctx else NTL))
    zd = k.dscr("zd", [T, 8192])
    uvT = k.dscr("uvT", [NT, 128, 32, 128], BF16)
    with ExitStack() as es:
        bias = Buf(k.sb(es, "gb_bias", [128, 8192])[:])
        P.dma("sp", bias.ap, b_b_in[j:j + 1, :].to_broadcast([128, 8192]), writes=[bias])
        stg = [Buf(k.sb(es, "gb_st%d" % i, [128, 512])[:]) for i in range(4)]
        cnt = [0]

        def epi(t, n0, n, pt):
            s = stg[cnt[0] % 4]
            cnt[0] += 1
            P.op("dve", lambda e: e.tensor_tensor(out=s.ap, in0=pt.ap, in1=bias.ap[:, n0:n0 + n], op=ALU.add), reads=[pt, bias], writes=[s])
            P.op("act", lambda e: e.activation(out=s.ap, in_=s.ap, func=AF.Gelu_apprx_tanh), reads=[s], writes=[s])
            P.dma("sp", zd[t * 128:(t + 1) * 128, n0:n0 + n], s.ap, reads=[s])
        linear_tm(S, hT_d, KC, b_w_in[j], 8192, tiles, epi, tag="gb1")
    P.barrier()
    with ExitStack() as es:
        lg = Buf(k.sb(es, "gb_lg", [128, 4096])[:])
        lb = Buf(k.sb(es, "gb_lb", [128, 4096])[:])
        P.dma("sp", lg.ap, b_ln_g[j:j + 1, :].to_broadcast([128, 4096]), writes=[lg])
        P.dma("sp", lb.ap, b_ln_b[j:j + 1, :].to_broadcast([128, 4096]), writes=[lb])
        wsr = Buf(k.sb(es, "gb_wsr", [128, 8, 128])[:])
        wsT = Buf(k.sb(es, "gb_wsT", [128, 8, 128], BF16)[:])
        bsT = Buf(k.sb(es, "gb_bsT", [128, 8])[:])
        P.dma("sp", wsr.ap, b_w_s[j].rearrange("g t s -> t g s"), writes=[wsr])
        P.dma("sp", bsT.ap, b_b_s[j].rearrange("g t -> t g"), writes=[bsT], allow_slow_non_contiguous=True)
        for g in range(8):
            pt = next_ps(S)
            P.op("pe", lambda e: e.transpose(out=pt.ap[:, 0:128], in_=wsr.ap[:, g, :], identity=ident.ap), reads=[wsr, ident], writes=[pt])
            P.op("dve", lambda e: e.tensor_copy(out=wsT.ap[:, g, :], in_=pt.ap[:, 0:128]), reads=[pt], writes=[wsT])
        zb = [Buf(k.sb(es, "gb_z%d" % i, [128, 8192])[:]) for i in range(2)]
        vb = [Buf(k.sb(es, "gb_v%d" % i, [128, 4096], BF16)[:]) for i in range(2)]
        uT = [Buf(k.sb(es, "gb_uT%d" % i, [128, 32, 128], BF16)[:]) for i in range(2)]
        st = [Buf(k.sb(es, "gb_bs%d" % i, [128, 8, 6])[:]) for i in range(2)]
        mv = [Buf(k.sb(es, "gb_mv%d" % i, [128, 2])[:]) for i in range(2)]
        rs = [Buf(k.sb(es, "gb_rs%d" % i, [128, 1])[:]) for i in range(2)]
        for t in tiles:
            z, v, u_T, s_, m_, r_ = zb[t % 2], vb[t % 2], uT[t % 2], st[t % 2], mv[t % 2], rs[t % 2]
            P.dma("sp", z.ap[:, 0:4096], zd[t * 128:(t + 1) * 128, 0:4096], writes=[z])
            P.dma("act", z.ap[:, 4096:8192], zd[t * 128:(t + 1) * 128, 4096:8192], writes=[z])
            for q in range(8):
                P.op("dve", lambda e: e.bn_stats(out=s_.ap[:, q, :], in_=z.ap[:, 4096 + q * 512:4096 + (q + 1) * 512]), reads=[z], writes=[s_])
            P.op("dve", lambda e: e.bn_aggr(out=m_.ap, in_=s_.ap.rearrange("p a b -> p (a b)")), reads=[s_], writes=[m_])
            rsqrt_eps(P, r_, m_.ap[:, 1:2], [m_], 1.0)
            P.op("dve", lambda e: e.tensor_scalar(out=z.ap[:, 4096:8192], in0=z.ap[:, 4096:8192], scalar1=m_.ap[:, 0:1], scalar2=r_.ap[:, 0:1],
                                                  op0=ALU.subtract, op1=ALU.mult), reads=[z, m_, r_], writes=[z])
            P.op("pool", lambda e: e.tensor_tensor(out=z.ap[:, 4096:8192], in0=z.ap[:, 4096:8192], in1=lg.ap, op=ALU.mult), reads=[z, lg], writes=[z])
            P.op("dve", lambda e: e.tensor_tensor(out=v.ap, in0=z.ap[:, 4096:8192], in1=lb.ap, op=ALU.add), reads=[z, lb], writes=[v])
            for g in range(8):
                pt = next_ps(S)
                P.op("pe", lambda e: e.matmul(pt.ap, lhsT=wsT.ap[:, g, :], rhs=v.ap[:, g * 512:(g + 1) * 512], start=True, stop=True),
                     reads=[wsT, v], writes=[pt])
                P.op("dve", lambda e: e.scalar_tensor_tensor(out=z.ap[:, g * 512:(g + 1) * 512], in0=pt.ap, scalar=bsT.ap[:, g:g + 1],
                                                             in1=z.ap[:, g * 512:(g + 1) * 512], op0=ALU.add, op1=ALU.mult),
                     reads=[pt, bsT, z], writes=[z])
            transpose_to_T(S, z, 32, u_T, t)
            P.dma("sp", uvT[t], u_T.ap, reads=[u_T])
    P.barrier()
    with ExitStack() as es:
        epi = make_store_epi(S, es, ymix, tag="gb3s")
        linear_tm(S, uvT, 32, b_w_out[j], D, tiles, epi, G=4, tag="gb3")


def mix_gqa(S, j, c_w_qkv, c_q_g, c_k_g, c_w_out, rope_d, need_ctx):
    k, P, hT_d, ymix, ident = S["k"], S["P"], S["hT_d"], S["ymix"], S["ident"]
    tiles = list(range(NT))
    qkv_d = k.dscr("qkv_d", [T, 3072])
    qkT_d = k.dscr("qkT_d", [20, 128, T], BF16)
    v_d = k.dscr("v_d", [NT, 128, 512], BF16)
    with ExitStack() as es:
        epi = make_store_epi(S, es, qkv_d, tag="gq1s")
        linear_tm(S, hT_d, KC, c_w_qkv[j], 3072, tiles, epi, tag="gq1")
    P.barrier()
    with ExitStack() as es:
        gq = Buf(k.sb(es, "gq_g", [128, 20, 128])[:])
        P.dma("sp", gq.ap[:, 0:16, :], c_q_g[j:j + 1, :].unsqueeze(1).to_broadcast([128, 16, 128]), writes=[gq])
        P.dma("sp", gq.ap[:, 16:20, :], c_k_g[j:j + 1, :].unsqueeze(1).to_broadcast([128, 4, 128]), writes=[gq])
        xb = [Buf(k.sb(es, "gq_x%d" % i, [128, 3072])[:]) for i in range(2)]
        sqb = Buf(k.sb(es, "gq_sq", [128, 2560])[:])
        ssum = Buf(k.sb(es, "gq_ss", [128, 20])[:])
        rp = [Buf(k.sb(es, "gq_rp%d" % i, [128, 2, 64])[:]) for i in range(2)]
        tA = Buf(k.sb(es, "gq_tA", [128, 20, 2, 32])[:])
        tB = Buf(k.sb(es, "gq_tB", [128, 20, 2, 32])[:])
        tC = Buf(k.sb(es, "gq_tC", [128, 20, 2, 32])[:])
        qT = [Buf(k.sb(es, "gq_qT%d" % i, [128, 20, 128], BF16)[:]) for i in range(2)]
        vbf = [Buf(k.sb(es, "gq_vb%d" % i, [128, 512], BF16)[:]) for i in range(2)]
        for t in tiles:
            x, r_, q_T, v_ = xb[t % 2], rp[t % 2], qT[t % 2], vbf[t % 2]
            P.dma("sp", x.ap, qkv_d[t * 128:(t + 1) * 128, :], writes=[x])
            P.op("act", lambda e: e.activation(out=sqb.ap, in_=x.ap[:, 0:2560], func=AF.Square), reads=[x], writes=[sqb])
            P.op("dve", lambda e: e.tensor_reduce(out=ssum.ap, in_=sqb.ap.rearrange("p (h d) -> p h d", d=128), axis=AX.X, op=ALU.add),
                 reads=[sqb], writes=[ssum])
            rsqrt_eps(P, ssum, ssum.ap, [ssum], 1.0 / 128.0)
            x3 = x.ap[:, 0:2560].rearrange("p (h d) -> p h d", d=128)
            P.op("dve", lambda e: e.tensor_tensor(out=x3, in0=x3, in1=ssum.ap.unsqueeze(2).to_broadcast([128, 20, 128]), op=ALU.mult),
                 reads=[x, ssum], writes=[x])
            P.op("pool", lambda e: e.tensor_tensor(out=x3, in0=x3, in1=gq.ap, op=ALU.mult), reads=[x, gq], writes=[x])
            if t < NTL:
                P.dma("act", r_.ap, rope_d[t * 128:(t + 1) * 128, :, :], writes=[r_])
                x5 = x.ap[:, 0:2560].rearrange("p (h a b c) -> p h a b c", a=2, b=2, c=32)
                x1 = x5[:, :, :, 0, :]
                x2 = x5[:, :, :, 1, :]
                cosb = r_.ap[:, 0, :].rearrange("p (a c) -> p a c", a=2).unsqueeze(1).to_broadcast([128, 20, 2, 32])
                sinb = r_.ap[:, 1, :].rearrange("p (a c) -> p a c", a=2).unsqueeze(1).to_broadcast([128, 20, 2, 32])
                P.op("dve", lambda e: e.tensor_tensor(out=tA.ap, in0=x1, in1=sinb, op=ALU.mult), reads=[x, r_], writes=[tA])
                P.op("pool", lambda e: e.tensor_tensor(out=tB.ap, in0=x2, in1=sinb, op=ALU.mult), reads=[x, r_], writes=[tB])
                P.op("dve", lambda e: e.tensor_tensor(out=tC.ap, in0=x2, in1=cosb, op=ALU.mult), reads=[x, r_], writes=[tC])
                P.op("dve", lambda e: e.tensor_tensor(out=x1, in0=x1, in1=cosb, op=ALU.mult), reads=[x, r_], writes=[x])
                P.op("dve", lambda e: e.tensor_tensor(out=x1, in0=x1, in1=tB.ap, op=ALU.subtract), reads=[x, tB], writes=[x])
                P.op("dve", lambda e: e.tensor_tensor(out=x2, in0=tA.ap, in1=tC.ap, op=ALU.add), reads=[tA, tC], writes=[x])
            transpose_to_T(S, x, 20, q_T, t)
            P.dma("sp", qkT_d[:, :, t * 128:(t + 1) * 128].rearrange("h d t -> d h t"), q_T.ap, reads=[q_T])
            P.op("pool", lambda e: e.tensor_copy(out=v_.ap, in_=x.ap[:, 2560:3072]), reads=[x], writes=[v_])
            P.dma("sp", v_d[t], v_.ap, reads=[v_])
    P.barrier()
    scale = 128.0 ** -0.5
    ps = S["ps"]
    with ExitStack() as es:
        kT = Buf(k.sb(es, "ga_kT", [128, T], BF16)[:])
        vv = Buf(k.sb(es, "ga_v", [128, NT, 128], BF16)[:])
        qTb = [Buf(k.sb(es, "ga_qT%d" % i, [128, T], BF16)[:]) for i in range(2)]
        ones = Buf(k.sb(es, "ga_ones", [128, 128], BF16)[:])
        P.op("pool", lambda e: e.memset(ones.ap, 1.0), writes=[ones])
        pT = [Buf(k.sb(es, "ga_pT%d" % i, [128, 512], BF16)[:]) for i in range(3)]
        rec = [Buf(k.sb(es, "ga_rec%d" % i, [128, 512])[:]) for i in range(2)]
        oT = [Buf(k.sb(es, "ga_oT%d" % i, [128, 512], BF16)[:]) for i in range(2)]
        blocks = [(q0, 512, list(range(NT))) for q0 in range(0, SEQ, 512)]
        if need_ctx:
            blocks.append((SEQ, CTX, [32, 33]))
        bc = 0
        pc = 0
        for kh in range(4):
            P.dma("sp", kT.ap, qkT_d[16 + kh], writes=[kT])
            P.dma("act", vv.ap, v_d[:, :, kh * 128:(kh + 1) * 128].rearrange("t p d -> p t d"), writes=[vv])
            for hq in range(4):
                h = kh * 4 + hq
                q_ = qTb[h % 2]
                P.dma("sp", q_.ap, qkT_d[h], writes=[q_])
                for (q0, nq, kts) in blocks:
                    acc_o, acc_s = ps[2 + 2 * (bc % 2)], ps[3 + 2 * (bc % 2)]
                    r_, o_ = rec[bc % 2], oT[bc % 2]
                    bc += 1
                    for ki, kt in enumerate(kts):
                        st = ps[pc % 2]
                        p_ = pT[pc % 3]
                        pc += 1
                        P.op("pe", lambda e: e.matmul(st.ap[:, 0:nq], lhsT=kT.ap[:, kt * 128:(kt + 1) * 128], rhs=q_.ap[:, q0:q0 + nq], start=True, stop=True),
                             reads=[kT, q_], writes=[st])
                        P.op("act", lambda e: e.activation(out=p_.ap[:, 0:nq], in_=st.ap[:, 0:nq], func=AF.Exp, scale=float(scale)), reads=[st], writes=[p_])
                        P.op("pe", lambda e: e.matmul(acc_o.ap[:, 0:nq], lhsT=vv.ap[:, kt, :], rhs=p_.ap[:, 0:nq], start=(ki == 0), stop=(ki == len(kts) - 1)),
                             reads=[vv, p_], writes=[acc_o])
                        P.op("pe", lambda e: e.matmul(acc_s.ap[:, 0:nq], lhsT=ones.ap, rhs=p_.ap[:, 0:nq], start=(ki == 0), stop=(ki == len(kts) - 1)),
                             reads=[ones, p_], writes=[acc_s])
                    P.op("dve", lambda e: e.reciprocal(out=r_.ap[:, 0:nq], in_=acc_s.ap[:, 0:nq]), reads=[acc_s], writes=[r_])
                    P.op("dve", lambda e: e.tensor_tensor(out=o_.ap[:, 0:nq], in0=acc_o.ap[:, 0:nq], in1=r_.ap[:, 0:nq], op=ALU.mult), reads=[acc_o, r_], writes=[o_])
                    t0 = q0 // 128
                    P.dma("sp", hT_d[t0:t0 + nq // 128, :, h, :].rearrange("t p q -> p t q"), o_.ap[:, 0:nq].rearrange("p (t q) -> p t q", q=128), reads=[o_])
    P.barrier()
    with ExitStack() as es:
        epi = make_store_epi(S, es, ymix, tag="gq4s")
        linear_tm(S, hT_d, KC, c_w_out[j], D, list(range(NT if need_ctx else NTL)), epi, tag="gq4")


def mix_mlstm(S, j, a_w_in, a_b_gate, a_norm_g, a_w_out, tri_d, need_ctx):
    k, P, hT_d, ymix, ident, ps = S["k"], S["P"], S["hT_d"], S["ymix"], S["ident"], S["ps"]
    tiles = list(range(NT))
    qkT_d = k.dscr("a_qkT%d" % j, [NT, 128, 16, 128], BF16)
    kt_d = k.dscr("a_kt%d" % j, [T, 1024], BF16)
    v_d = k.dscr("a_v%d" % j, [T, 2048], BF16)
    o_d = k.dscr("a_o%d" % j, [T, 2048])
    g_d = k.dscr("a_g%d" % j, [T, 16])
    hs_d = [k.dscr("a_hs%d_%d" % (j, d_), [T, 2048]) for d_ in range(2)]
    with ExitStack() as es:
        stg = [Buf(k.sb(es, "ma_fs%d" % i, [128, 512], BF16)[:]) for i in range(4)]
        cnt = [0]

        def epi_f(c, t0, ntl, pt):
            s = stg[cnt[0] % 4]
            cnt[0] += 1
            n = ntl * 128
            sc = 1.0 if c < 8 else 1.0 / 16.0
            P.op("act", lambda e: e.activation(out=s.ap[:, 0:n], in_=pt.ap[:, 0:n], func=AF.Copy, scale=float(sc)), reads=[pt], writes=[s])
            P.dma("sp", qkT_d[t0:t0 + ntl, :, c, :].rearrange("t p q -> p t q"), s.ap[:, 0:n].rearrange("p (t q) -> p t q", q=128), reads=[s])
        linear_fm(S, hT_d, KC, a_w_in[j][:, 0:2048], 2048, tiles, epi_f, tag="ma1")
    P.barrier()
    with ExitStack() as es:
        stb = [Buf(k.sb(es, "ma_sb%d" % i, [128, 512], BF16)[:]) for i in range(3)]
        stf = [Buf(k.sb(es, "ma_sf%d" % i, [128, 512])[:]) for i in range(3)]
        cnt = [0]

        def epi_t(t, n0, n, pt):
            i = cnt[0]
            cnt[0] += 1
            rows = slice(t * 128, (t + 1) * 128)
            if n0 < 1024:
                s = stb[i % 3]
                P.op("act", lambda e: e.activation(out=s.ap, in_=pt.ap, func=AF.Copy, scale=1.0 / 16.0), reads=[pt], writes=[s])
                P.dma("sp", kt_d[rows, n0:n0 + 512], s.ap, reads=[s])
            elif n0 < 3072:
                s = stb[i % 3]
                P.op("dve", lambda e: e.tensor_copy(out=s.ap, in_=pt.ap), reads=[pt], writes=[s])
                P.dma("sp", v_d[rows, n0 - 1024:n0 - 1024 + 512], s.ap, reads=[s])
            elif n0 < 5120:
                s = stf[i % 3]
                P.op("act", lambda e: e.activation(out=s.ap, in_=pt.ap, func=AF.Sigmoid), reads=[pt], writes=[s])
                P.dma("sp", o_d[rows, n0 - 3072:n0 - 3072 + 512], s.ap, reads=[s])
            else:
                s = stf[i % 3]
                P.op("dve", lambda e: e.tensor_copy(out=s.ap[:, 0:16], in_=pt.ap[:, 0:16]), reads=[pt], writes=[s])
                P.dma("sp", g_d[rows, :], s.ap[:, 0:16], reads=[s])
        linear_tm(S, hT_d, KC, a_w_in[j][:, 1024:A_IN], A_IN - 1024, tiles, epi_t, tag="ma2")
    P.barrier()
    with ExitStack() as es:
        tri = Buf(k.sb(es, "ma_tri", [128, 2, 128])[:])
        P.dma("sp", tri.ap, tri_d.rearrange("a s t -> s a t"), writes=[tri])
        onesf = Buf(k.sb(es, "ma_1f", [128, 128])[:])
        onesb = Buf(k.sb(es, "ma_1b", [128, 2], BF16)[:])
        P.op("pool", lambda e: e.memset(onesf.ap, 1.0), writes=[onesf])
        P.op("pool", lambda e: e.memset(onesb.ap, 1.0), writes=[onesb])
        G = Buf(k.sb(es, "ma_G", [128, NT, 16])[:])
        bg = Buf(k.sb(es, "ma_bg", [128, 16])[:])
        P.dma("sp", G.ap, g_d.rearrange("(c s) g -> s c g", s=128), writes=[G])
        P.dma("sp", bg.ap, a_b_gate[j:j + 1, :].to_broadcast([128, 16]), writes=[bg])
        P.op("dve", lambda e: e.tensor_tensor(out=G.ap, in0=G.ap, in1=bg.ap.unsqueeze(1).to_broadcast([128, NT, 16]), op=ALU.add), reads=[G, bg], writes=[G])
        P.op("act", lambda e: e.activation(out=G.ap, in_=G.ap, func=AF.Tanh, scale=1.0 / 15.0), reads=[G], writes=[G])
        P.op("dve", lambda e: e.tensor_scalar_mul(out=G.ap, in0=G.ap, scalar1=15.0), reads=[G], writes=[G])
        LI = Buf(k.sb(es, "ma_LI", [128, NT, 8])[:])
        LFN = Buf(k.sb(es, "ma_LFN", [128, NT, 8])[:])
        G4 = G.ap.rearrange("p c (a h) -> p c a h", a=4)
        LI4 = LI.ap.rearrange("p c (a h) -> p c a h", a=2)
        LF4 = LFN.ap.rearrange("p c (a h) -> p c a h", a=2)
        for d_ in range(2):
            P.op("dve", lambda e: e.tensor_copy(out=LI4[:, :, d_, :], in_=G4[:, :, 2 * d_, :]), reads=[G], writes=[LI])
            P.op("act", lambda e: e.activation(out=LF4[:, :, d_, :], in_=G4[:, :, 2 * d_ + 1, :], func=AF.Exp, scale=-1.0), reads=[G], writes=[LFN])
        P.op("act", lambda e: e.activation(out=LFN.ap, in_=LFN.ap, func=AF.Ln, bias=1.0), reads=[LFN], writes=[LFN])
        NB = Buf(k.sb(es, "ma_NB", [128, NT, 8])[:])
        GT = Buf(k.sb(es, "ma_GT", [128, NT, 8])[:])
        AA = Buf(k.sb(es, "ma_A", [128, NT, 8])[:])
        NB4 = NB.ap.rearrange("p c (a h) -> p c a h", a=2)
        pt = ps[0]
        for d_ in range(2):
            P.op("pe", lambda e: e.matmul(pt.ap[:, 0:NT * 4], lhsT=tri.ap[:, d_, :], rhs=LF4[:, :, d_, :], start=True, stop=True), reads=[tri, LFN], writes=[pt])
            P.op("dve", lambda e: e.tensor_copy(out=NB4[:, :, d_, :], in_=pt.ap[:, 0:NT * 4].rearrange("p (c h) -> p c h", h=4)), reads=[pt], writes=[NB])
        pt = ps[1]
        P.op("pe", lambda e: e.matmul(pt.ap[:, 0:NT * 8], lhsT=onesf.ap, rhs=LFN.ap, start=True, stop=True), reads=[onesf, LFN], writes=[pt])
        P.op("dve", lambda e: e.tensor_copy(out=GT.ap, in_=pt.ap[:, 0:NT * 8].rearrange("p (c h) -> p c h", h=8)), reads=[pt], writes=[GT])
        P.op("dve", lambda e: e.tensor_tensor(out=AA.ap, in0=LI.ap, in1=NB.ap, op=ALU.add), reads=[LI, NB], writes=[AA])
        AMX = Buf(k.sb(es, "ma_AMX", [128, NT * 8])[:])
        A2 = AA.ap.rearrange("p c h -> p (c h)")
        col = Buf(k.sb(es, "ma_col", [128, 1])[:])
        colb = Buf(k.sb(es, "ma_colb", [128, 128])[:])
        for (c0, n) in ((0, 128), (128, 128), (256, NT * 8 - 256)):
            pt = ps[2]
            P.op("pe", lambda e: e.transpose(out=pt.ap[0:n, 0:128], in_=A2[:, c0:c0 + n], identity=ident.ap), reads=[AA, ident], writes=[pt])
            P.op("dve", lambda e: e.reduce_max(out=col.ap[0:n, :], in_=pt.ap[0:n, 0:128], axis=AX.X), reads=[pt], writes=[col])
            P.op("dve", lambda e: e.tensor_copy(out=colb.ap[0:n, :], in_=col.ap[0:n, 0:1].to_broadcast([n, 128])), reads=[col], writes=[colb])
            pt2 = ps[3]
            P.op("pe", lambda e: e.matmul(pt2.ap[:, 0:n], lhsT=colb.ap[0:n, :], rhs=ident.ap[0:n, 0:n], start=True, stop=True), reads=[colb, ident], writes=[pt2])
            P.op("dve", lambda e: e.tensor_copy(out=AMX.ap[:, c0:c0 + n], in_=pt2.ap[:, 0:n]), reads=[pt2], writes=[AMX])
        AMX3 = AMX.ap.rearrange("p (c h) -> p c h", h=8)
        RR = Buf(k.sb(es, "ma_R", [128, NT, 8])[:])
        DM = Buf(k.sb(es, "ma_DM", [128, NT, 8])[:])
        mm = Buf(k.sb(es, "ma_m", [128, 8])[:])
        P.op("dve", lambda e: e.memset(mm.ap, 0.0), writes=[mm])
        order = [[32, 33] + list(range(32)), [33, 32] + list(range(31, -1, -1))]
        for i in range(NT):
            for d_ in range(2):
                c = order[d_][i]
                hs = slice(d_ * 4, d_ * 4 + 4)
                P.op("dve", lambda e: e.tensor_tensor(out=RR.ap[:, c, hs], in0=mm.ap[:, hs], in1=AMX3[:, c, hs], op=ALU.max), reads=[mm, AMX], writes=[RR])
                P.op("dve", lambda e: e.tensor_tensor(out=DM.ap[:, c, hs], in0=mm.ap[:, hs], in1=RR.ap[:, c, hs], op=ALU.subtract), reads=[mm, RR], writes=[DM])
                P.op("dve", lambda e: e.tensor_tensor(out=mm.ap[:, hs], in0=RR.ap[:, c, hs], in1=GT.ap[:, c, hs], op=ALU.subtract), reads=[RR, GT], writes=[mm])
        EE = Buf(k.sb(es, "ma_E", [128, NT, 8])[:])
        CL = Buf(k.sb(es, "ma_CL", [128, NT, 8])[:])
        P.op("dve", lambda e: e.tensor_tensor(out=EE.ap, in0=AA.ap, in1=RR.ap, op=ALU.subtract), reads=[AA, RR], writes=[EE])
        P.op("act", lambda e: e.activation(out=EE.ap, in_=EE.ap, func=AF.Exp), reads=[EE], writes=[EE])
        P.op("dve", lambda e: e.tensor_tensor(out=CL.ap, in0=NB.ap, in1=RR.ap, op=ALU.subtract), reads=[NB, RR], writes=[CL])
        P.op("act", lambda e: e.activation(out=CL.ap, in_=CL.ap, func=AF.Exp), reads=[CL], writes=[CL])
        P.op("act", lambda e: e.activation(out=DM.ap, in_=DM.ap, func=AF.Exp), reads=[DM], writes=[DM])
        C32 = [Buf(k.sb(es, "ma_C%d" % i, [128, 2, 512])[:]) for i in range(8)]
        N32 = [Buf(k.sb(es, "ma_N%d" % i, [128, 2])[:]) for i in range(8)]
        Cb = [Buf(k.sb(es, "ma_Cb%d" % i, [128, 2, 512], BF16)[:]) for i in range(8)]
        Nb = [Buf(k.sb(es, "ma_Nb%d" % i, [128, 2], BF16)[:]) for i in range(8)]
        for i in range(8):
            P.op("pool", lambda e: e.memset(C32[i].ap, 0.0), writes=[C32[i]])
            P.op("pool", lambda e: e.memset(N32[i].ap, 0.0), writes=[N32[i]])
        qk = [[Buf(k.sb(es, "ma_qk%d_%d" % (d_, i), [128, 16, 128], BF16)[:]) for i in range(2)] for d_ in range(2)]
        ktm = [[Buf(k.sb(es, "ma_kt%d_%d" % (d_, i), [128, 1024], BF16)[:]) for i in range(2)] for d_ in range(2)]
        vtm = [[Buf(k.sb(es, "ma_vt%d_%d" % (d_, i), [128, 2048], BF16)[:]) for i in range(2)] for d_ in range(2)]
        St = [Buf(k.sb(es, "ma_St%d" % i, [128, 128], BF16)[:]) for i in range(4)]
        ktl = [Buf(k.sb(es, "ma_ktl%d" % i, [128, 256], BF16)[:]) for i in range(4)]
        dn = [Buf(k.sb(es, "ma_dn%d" % i, [128, 1])[:]) for i in range(4)]
        ho = [Buf(k.sb(es, "ma_ho%d" % i, [128, 512])[:]) for i in range(4)]
        it = 0
        for i in range(NT):
            for d_ in range(2):
                c = order[d_][i]
                rows = slice(c * 128, (c + 1) * 128)
                qk_, kt_, vt_ = qk[d_][i % 2], ktm[d_][i % 2], vtm[d_][i % 2]
                P.dma("sp", qk_.ap, qkT_d[c], writes=[qk_])
                P.dma("act", kt_.ap, kt_d[rows, :], writes=[kt_])
                P.dma("sp", vt_.ap, v_d[rows, :], writes=[vt_])
                for hh in range(4):
                    ch = d_ * 4 + hh
                    e_ = EE.ap[:, c, ch:ch + 1]
                    cd = DM.ap[:, c, ch:ch + 1]
                    cl = CL.ap[:, c, ch:ch + 1]
                    s_t, k_l, d_n, h_o = St[it % 4], ktl[it % 4], dn[it % 4], ho[it % 4]
                    p_st, p_num = ps[it % 2], ps[2 + it % 2]
                    p_c = [ps[4], ps[5]]
                    p_den, p_nu = ps[6], ps[7]
                    it += 1
                    vh = vt_.ap[:, hh * 512:(hh + 1) * 512]
                    P.op("dve", lambda e: e.tensor_scalar(out=Cb[ch].ap, in0=C32[ch].ap, scalar1=cd, scalar2=None, op0=ALU.mult), reads=[C32[ch], DM], writes=[Cb[ch]])
                    P.op("dve", lambda e: e.tensor_scalar(out=Nb[ch].ap, in0=N32[ch].ap, scalar1=cd, scalar2=None, op0=ALU.mult), reads=[N32[ch], DM], writes=[Nb[ch]])
                    for dc in range(2):
                        P.op("pe", lambda e: e.matmul(p_st.ap[:, 0:128], lhsT=qk_.ap[:, 8 + hh * 2 + dc, :], rhs=qk_.ap[:, hh * 2 + dc, :], start=(dc == 0), stop=(dc == 1)),
                             reads=[qk_], writes=[p_st])
                    P.op("dve", lambda e: e.scalar_tensor_tensor(out=s_t.ap, in0=p_st.ap[:, 0:128], scalar=e_, in1=tri.ap[:, d_, :], op0=ALU.mult, op1=ALU.mult),
                         reads=[p_st, EE, tri], writes=[s_t])
                    for dc in range(2):
                        P.op("pe", lambda e: e.matmul(p_num.ap, lhsT=qk_.ap[:, hh * 2 + dc, :], rhs=Cb[ch].ap[:, dc, :], start=(dc == 0), stop=False),
                             reads=[qk_, Cb[ch]], writes=[p_num])
                    P.op("pe", lambda e: e.matmul(p_num.ap, lhsT=s_t.ap, rhs=vh, start=False, stop=True), reads=[s_t, vt_], writes=[p_num])
                    for dc in range(2):
                        P.op("pe", lambda e: e.matmul(p_den.ap[:, 0:1], lhsT=qk_.ap[:, hh * 2 + dc, :], rhs=Nb[ch].ap[:, dc:dc + 1], start=(dc == 0), stop=False),
                             reads=[qk_, Nb[ch]], writes=[p_den])
                    P.op("pe", lambda e: e.matmul(p_den.ap[:, 0:1], lhsT=s_t.ap, rhs=onesb.ap[:, 0:1], start=False, stop=True), reads=[s_t, onesb], writes=[p_den])
                    P.op("act", lambda e: e.activation(out=d_n.ap, in_=p_den.ap[:, 0:1], func=AF.Abs), reads=[p_den], writes=[d_n])
                    P.op("dve", lambda e: e.tensor_tensor(out=d_n.ap, in0=d_n.ap, in1=cl, op=ALU.max), reads=[d_n, CL], writes=[d_n])
                    P.op("dve", lambda e: e.reciprocal(out=d_n.ap, in_=d_n.ap), reads=[d_n], writes=[d_n])
                    P.op("act", lambda e: e.activation(out=h_o.ap, in_=p_num.ap, func=AF.Copy, scale=d_n.ap[:, 0:1]), reads=[p_num, d_n], writes=[h_o])
                    P.dma("sp", hs_d[d_][rows, hh * 512:(hh + 1) * 512], h_o.ap, reads=[h_o])
                    P.op("pool", lambda e: e.tensor_scalar(out=k_l.ap, in0=kt_.ap[:, hh * 256:(hh + 1) * 256], scalar1=e_, scalar2=None, op0=ALU.mult),
                         reads=[kt_, EE], writes=[k_l])
                    for dc in range(2):
                        P.op("pe", lambda e: e.matmul(p_c[dc].ap, lhsT=k_l.ap[:, dc * 128:(dc + 1) * 128], rhs=vh, start=True, stop=True), reads=[k_l, vt_], writes=[p_c[dc]])
                        P.op("pe", lambda e: e.matmul(p_nu.ap[:, dc:dc + 1], lhsT=k_l.ap[:, dc * 128:(dc + 1) * 128], rhs=onesb.ap[:, 0:1], start=True, stop=True),
                             reads=[k_l, onesb], writes=[p_nu])
                    for dc in range(2):
                        P.op("dve", lambda e: e.scalar_tensor_tensor(out=C32[ch].ap[:, dc, :], in0=C32[ch].ap[:, dc, :], scalar=cd, in1=p_c[dc].ap, op0=ALU.mult, op1=ALU.add),
                             reads=[C32[ch], DM, p_c[dc]], writes=[C32[ch]])
                    P.op("dve", lambda e: e.scalar_tensor_tensor(out=N32[ch].ap, in0=N32[ch].ap, scalar=cd, in1=p_nu.ap[:, 0:2], op0=ALU.mult, op1=ALU.add),
                         reads=[N32[ch], DM, p_nu], writes=[N32[ch]])
    P.barrier()
    out_tiles = list(range(NT if need_ctx else NTL))
    with ExitStack() as es:
        ng = Buf(k.sb(es, "mo_ng", [128, D])[:])
        P.dma("sp", ng.ap, a_norm_g[j:j + 1, :].to_broadcast([128, D]), writes=[ng])
        hb = [[Buf(k.sb(es, "mo_h%d_%d" % (d_, i), [128, D])[:]) for i in range(2)] for d_ in range(2)]
        ob = [Buf(k.sb(es, "mo_o%d" % i, [128, D])[:]) for i in range(2)]
        sq = Buf(k.sb(es, "mo_sq", [128, D])[:])
        ss = Buf(k.sb(es, "mo_ss", [128, 4])[:])
        hT = [Buf(k.sb(es, "mo_hT%d" % i, [128, KC, 128], BF16)[:]) for i in range(2)]
        for t in out_tiles:
            rows = slice(t * 128, (t + 1) * 128)
            h0, h1, o_, h_T = hb[0][t % 2], hb[1][t % 2], ob[t % 2], hT[t % 2]
            P.dma("sp", h0.ap, hs_d[0][rows, :], writes=[h0])
            P.dma("act", h1.ap, hs_d[1][rows, :], writes=[h1])
            P.dma("sp", o_.ap, o_d[rows, :], writes=[o_])
            P.op("pool", lambda e: e.tensor_tensor(out=h0.ap, in0=h0.ap, in1=h1.ap, op=ALU.add), reads=[h0, h1], writes=[h0])
            P.op("act", lambda e: e.activation(out=sq.ap, in_=h0.ap, func=AF.Square), reads=[h0], writes=[sq])
            P.op("dve", lambda e: e.tensor_reduce(out=ss.ap, in_=sq.ap.rearrange("p (h d) -> p h d", d=512), axis=AX.X, op=ALU.add), reads=[sq], writes=[ss])
            rsqrt_eps(P, ss, ss.ap, [ss], 1.0 / 512.0)
            h3 = h0.ap.rearrange("p (h d) -> p h d", d=512)
            P.op("dve", lambda e: e.tensor_tensor(out=h3, in0=h3, in1=ss.ap.unsqueeze(2).to_broadcast([128, 4, 512]), op=ALU.mult), reads=[h0, ss], writes=[h0])
            P.op("pool", lambda e: e.tensor_tensor(out=o_.ap, in0=o_.ap, in1=ng.ap, op=ALU.mult), reads=[o_, ng], writes=[o_])
            P.op("dve", lambda e: e.tensor_tensor(out=h0.ap, in0=h0.ap, in1=o_.ap, op=ALU.mult), reads=[h0, o_], writes=[h0])
            transpose_to_T(S, h0, KC, h_T, t)
            P.dma("sp", hT_d[t], h_T.ap, reads=[h_T])
    P.barrier()
    with ExitStack() as es:
        epi = make_store_epi(S, es, ymix, tag="mo5s")
        linear_tm(S, hT_d, KC, a_w_out[j], D, out_tiles, epi, tag="mo5")


def top16(P, src, work, vals, idxs, reads_extra=()):
    sv, vv, iv, wv = src, vals, idxs, work
    P.op("dve", lambda e: e.max(out=vv.ap[:, 0:8], in_=sv.ap), reads=[sv], writes=[vv])
    P.op("dve", lambda e: e.max_index(out=iv.ap[:, 0:8], in_max=vv.ap[:, 0:8], in_values=sv.ap), reads=[sv, vv], writes=[iv])
    P.op("dve", lambda e: e.match_replace(out=wv.ap, in_to_replace=vv.ap[:, 0:8], in_values=sv.ap, imm_value=NEG), reads=[sv, vv], writes=[wv])
    P.op("dve", lambda e: e.max(out=vv.ap[:, 8:16], in_=wv.ap), reads=[wv], writes=[vv])
    P.op("dve", lambda e: e.max_index(out=iv.ap[:, 8:16], in_max=vv.ap[:, 8:16], in_values=wv.ap), reads=[wv, vv], writes=[iv])


def phase_peer(S, li, p_w_q, p_k1, p_k2, p_u, p_v, iota_d, nt, enable=True):
    k, P, hT_d, h_d, ymix, ident, ps = S["k"], S["P"], S["hT_d"], S["h_d"], S["ymix"], S["ident"], S["ps"]
    tiles = list(range(nt))
    qpT_d = k.dscr("p_qT%d" % li, [16, 128, T])
    with ExitStack() as es:
        stg = [Buf(k.sb(es, "pq_s%d" % i, [128, 512])[:]) for i in range(4)]
        cnt = [0]

        def epi_f(c, t0, ntl, pt):
            s = stg[cnt[0] % 4]
            cnt[0] += 1
            n = ntl * 128
            P.op("act" if cnt[0] % 2 else "dve",
                 (lambda e: e.activation(out=s.ap[:, 0:n], in_=pt.ap[:, 0:n], func=AF.Copy)) if cnt[0] % 2 else (lambda e: e.tensor_copy(out=s.ap[:, 0:n], in_=pt.ap[:, 0:n])),
                 reads=[pt], writes=[s])
            P.dma("sp", qpT_d[c, :, t0 * 128:t0 * 128 + n], s.ap[:, 0:n], reads=[s])
        linear_fm(S, hT_d, KC, p_w_q[li], D, tiles, epi_f, tag="pq")
    P.barrier()
    with ExitStack() as es:
        kraw = Buf(k.sb(es, "pp_kraw", [128, 2, 128])[:])
        kT = Buf(k.sb(es, "pp_kT", [128, 2, 128])[:])
        P.dma("sp", kraw.ap[:, 0, :], p_k1[li], writes=[kraw])
        P.dma("sp", kraw.ap[:, 1, :], p_k2[li], writes=[kraw])
        for hf in range(2):
            pt = next_ps(S)
            P.op("pe", lambda e: e.transpose(out=pt.ap[:, 0:128], in_=kraw.ap[:, hf, :], identity=ident.ap), reads=[kraw, ident], writes=[pt])
            P.op("dve", lambda e: e.tensor_copy(out=kT.ap[:, hf, :], in_=pt.ap[:, 0:128]), reads=[pt], writes=[kT])
        io16 = Buf(k.sb(es, "pp_io", [128, 16])[:])
        P.dma("sp", io16.ap, iota_d[0:1, 0:16].to_broadcast([128, 16]), writes=[io16])
        qT = [Buf(k.sb(es, "pp_qT%d" % i, [128, 16, 128])[:]) for i in range(2)]
        sc = Buf(k.sb(es, "pp_sc", [128, 16, 128])[:])
        wk = Buf(k.sb(es, "pp_wk", [128, 256])[:])
        V12 = Buf(k.sb(es, "pp_V12", [128, 16, 16])[:])
        I12 = Buf(k.sb(es, "pp_I12", [128, 16, 16], U32)[:])
        I12f = Buf(k.sb(es, "pp_I12f", [128, 16, 16])[:])
        cand = Buf(k.sb(es, "pp_cand", [128, 8, 256])[:])
        SC = Buf(k.sb(es, "pp_SC", [128, 8, 16])[:])
        CI = Buf(k.sb(es, "pp_CI", [128, 8, 16], U32)[:])
        IA = Buf(k.sb(es, "pp_IA", [128, 8, 16], U32)[:])
        IB = Buf(k.sb(es, "pp_IB", [128, 8, 16], U32)[:])
        IAf = Buf(k.sb(es, "pp_IAf", [128, 8, 16])[:])
        IBf = Buf(k.sb(es, "pp_IBf", [128, 8, 16])[:])
        oh = Buf(k.sb(es, "pp_oh", [128, 8, 16, 16])[:])
        EI = Buf(k.sb(es, "pp_EI", [128, 8, 16])[:])
        EJ = Buf(k.sb(es, "pp_EJ", [128, 8, 16])[:])
        gate = Buf(k.sb(es, "pp_gate", [128, 8, 16])[:])
        gs = Buf(k.sb(es, "pp_gs", [128, 8])[:])
        if PEER_DENSE:
            Gd = S["Gd"]
            io128 = Buf(k.sb(es, "pp_io128", [128, 128])[:])
            P.dma("sp", io128.ap, iota_d[0:1, :].to_broadcast([128, 128]), writes=[io128])
            TT = Buf(k.sb(es, "pp_TT", [128, 3, 128])[:])
            OH1 = Buf(k.sb(es, "pp_OH1", [128, 128, 128], BF16)[:])
            OH2 = Buf(k.sb(es, "pp_OH2", [128, 128, 128], BF16)[:])
            G_sb = Buf(k.sb(es, "pp_Gsb", [128, 128, 128], BF16)[:])
            G_sb2 = Buf(G_sb.ap)
            hb = acc = EX = [None, None]
        else:
            hb = [Buf(k.sb(es, "pp_h%d" % i, [128, D])[:]) for i in range(2)]
            EX = [Buf(k.sb(es, "pp_EX%d" % i, [128, 128], U32)[:]) for i in range(2)]
            av = Buf(k.sb(es, "pp_a", [128, 128])[:])
            hid = Buf(k.sb(es, "pp_hid", [128, 128])[:])
            junk = Buf(k.sb(es, "pp_junk", [128, D])[:])
            acc = [Buf(k.sb(es, "pp_acc%d" % i, [128, D])[:]) for i in range(2)]
            NG = 6
            gb = [Buf(k.sb(es, "pp_g%d" % i, [128, D])[:]) for i in range(NG)]
        gi = 0
        for t in tiles:
            q_, h_, ex_, acc_ = qT[t % 2], hb[t % 2], EX[t % 2], acc[t % 2]
            P.dma("sp", q_.ap, qpT_d[:, :, t * 128:(t + 1) * 128].rearrange("c d t -> d c t"), writes=[q_])
            if not PEER_DENSE:
                P.dma("act", h_.ap, h_d[t * 128:(t + 1) * 128, :], writes=[h_])
            for g in range(4):
                pt = ps[g]
                for q4 in range(4):
                    c = g * 4 + q4
                    P.op("pe", lambda e: e.matmul(pt.ap[:, q4 * 128:(q4 + 1) * 128], lhsT=q_.ap[:, c, :], rhs=kT.ap[:, c % 2, :], start=True, stop=True),
                         reads=[q_, kT], writes=[pt])
                P.op("act", lambda e: e.activation(out=sc.ap[:, g * 4:(g + 1) * 4, :], in_=pt.ap.rearrange("p (a b) -> p a b", a=4), func=AF.Copy), reads=[pt], writes=[sc])
            for c in range(16):
                sv = Buf(sc.ap[:, c, :]); sv.w = sc.w
                wv = Buf(wk.ap[:, 0:128]); wv.w = wk.w; wv.r = wk.r
                vv = Buf(V12.ap[:, c, :]); vv.w = V12.w; vv.r = V12.r
                iv = Buf(I12.ap[:, c, :]); iv.w = I12.w; iv.r = I12.r
                top16(P, sv, wv, vv, iv)
                wk.w, V12.w, I12.w = wv.w, vv.w, iv.w
                wk.r, V12.r, I12.r = wv.r, vv.r, iv.r
                sc.r.update(sv.r)
            V4 = V12.ap.rearrange("p (h f) a -> p h f a", f=2)
            P.op("dve", lambda e: e.tensor_tensor(out=cand.ap.rearrange("p h (a b) -> p h a b", b=16),
                                                  in0=V4[:, :, 0, :].unsqueeze(3).to_broadcast([128, 8, 16, 16]),
                                                  in1=V4[:, :, 1, :].unsqueeze(2).to_broadcast([128, 8, 16, 16]), op=ALU.add), reads=[V12], writes=[cand])
            for hh in range(8):
                sv = Buf(cand.ap[:, hh, :]); sv.w = cand.w
                wv = Buf(wk.ap[:, 0:256]); wv.w = wk.w; wv.r = wk.r
                vv = Buf(SC.ap[:, hh, :]); vv.w = SC.w; vv.r = SC.r
                iv = Buf(CI.ap[:, hh, :]); iv.w = CI.w; iv.r = CI.r
                top16(P, sv, wv, vv, iv)
                wk.w, SC.w, CI.w = wv.w, vv.w, iv.w
                wk.r, SC.r, CI.r = wv.r, vv.r, iv.r
                cand.r.update(sv.r)
            P.op("dve", lambda e: e.tensor_tensor(out=gate.ap, in0=SC.ap, in1=SC.ap[:, :, 0:1].to_broadcast([128, 8, 16]), op=ALU.subtract), reads=[SC], writes=[gate])
            P.op("act", lambda e: e.activation(out=gate.ap, in_=gate.ap, func=AF.Exp), reads=[gate], writes=[gate])
            P.op("dve", lambda e: e.tensor_reduce(out=gs.ap, in_=gate.ap, axis=AX.X, op=ALU.add), reads=[gate], writes=[gs])
            P.op("dve", lambda e: e.reciprocal(out=gs.ap, in_=gs.ap), reads=[gs], writes=[gs])
            P.op("dve", lambda e: e.tensor_tensor(out=gate.ap, in0=gate.ap, in1=gs.ap.unsqueeze(2).to_broadcast([128, 8, 16]), op=ALU.mult), reads=[gate, gs], writes=[gate])
            P.op("dve", lambda e: e.tensor_single_scalar(out=IA.ap, in_=CI.ap, scalar=4, op=ALU.logical_shift_right), reads=[CI], writes=[IA])
            P.op("dve", lambda e: e.tensor_single_scalar(out=IB.ap, in_=CI.ap, scalar=15, op=ALU.bitwise_and), reads=[CI], writes=[IB])
            P.op("dve", lambda e: e.tensor_copy(out=IAf.ap, in_=IA.ap), reads=[IA], writes=[IAf])
            P.op("dve", lambda e: e.tensor_copy(out=IBf.ap, in_=IB.ap), reads=[IB], writes=[IBf])
            P.op("dve", lambda e: e.tensor_copy(out=I12f.ap, in_=I12.ap), reads=[I12], writes=[I12f])
            I4 = I12f.ap.rearrange("p (h f) a -> p h f a", f=2)
            io4 = io16.ap.unsqueeze(1).unsqueeze(1).to_broadcast([128, 8, 16, 16])
            for (src, f, dst) in ((IAf, 0, EI), (IBf, 1, EJ)):
                P.op("dve", lambda e: e.tensor_tensor(out=oh.ap, in0=src.ap.unsqueeze(3).to_broadcast([128, 8, 16, 16]), in1=io4, op=ALU.is_equal), reads=[src, io16], writes=[oh])
                P.op("dve", lambda e: e.tensor_tensor(out=oh.ap, in0=oh.ap, in1=I4[:, :, f, :].unsqueeze(2).to_broadcast([128, 8, 16, 16]), op=ALU.mult), reads=[oh, I12f], writes=[oh])
                P.op("dve", lambda e: e.tensor_reduce(out=dst.ap, in_=oh.ap, axis=AX.X, op=ALU.add), reads=[oh], writes=[dst])
            if PEER_DENSE:
                ptT = ps[4]
                for q3, src in enumerate((EI, EJ, gate)):
                    P.op("pe", lambda e: e.transpose(out=ptT.ap[:, q3 * 128:(q3 + 1) * 128], in_=src.ap.rearrange("p h a -> p (h a)"), identity=ident.ap),
                         reads=[src, ident], writes=[ptT])
                P.op("act", lambda e: e.activation(out=TT.ap.rearrange("p a t -> p (a t)"), in_=ptT.ap[:, 0:384], func=AF.Copy), reads=[ptT], writes=[TT])
                iob = io128.ap.unsqueeze(1).to_broadcast([128, 128, 128])
                P.op("dve", lambda e: e.tensor_tensor(out=OH1.ap, in0=iob, in1=TT.ap[:, 0, :].unsqueeze(2).to_broadcast([128, 128, 128]), op=ALU.is_equal),
                     reads=[io128, TT], writes=[OH1])
                P.op("pool", lambda e: e.tensor_tensor(out=OH1.ap, in0=OH1.ap, in1=TT.ap[:, 2, :].unsqueeze(2).to_broadcast([128, 128, 128]), op=ALU.mult),
                     reads=[OH1, TT], writes=[OH1])
                P.op("dve", lambda e: e.tensor_tensor(out=OH2.ap, in0=iob, in1=TT.ap[:, 1, :].unsqueeze(2).to_broadcast([128, 128, 128]), op=ALU.is_equal),
                     reads=[io128, TT], writes=[OH2])
                Gv = G_sb.ap.rearrange("p i t -> p t i")
                for t4 in range(32):
                    pt = ps[t4 % 4]
                    for q in range(4):
                        tok = t4 * 4 + q
                        P.op("pe", lambda e: e.matmul(pt.ap[:, q * 128:(q + 1) * 128], lhsT=OH2.ap[:, tok, :], rhs=OH1.ap[:, tok, :], start=True, stop=True),
                             reads=[OH1, OH2], writes=[pt])
                    if t4 % 2 == 0:
                        P.op("act", lambda e: e.activation(out=Gv[:, t4 * 4:(t4 + 1) * 4, :], in_=pt.ap.rearrange("p (q i) -> p q i", q=4), func=AF.Copy), reads=[pt], writes=[G_sb])
                    else:
                        P.op("dve", lambda e: e.tensor_copy(out=Gv[:, t4 * 4:(t4 + 1) * 4, :], in_=pt.ap.rearrange("p (q i) -> p q i", q=4)), reads=[pt], writes=[G_sb2])
                for cg in range(8):
                    P.dma("sp" if cg % 2 == 0 else "act", Gd[cg * 16:(cg + 1) * 16, :, t * 128:(t + 1) * 128].rearrange("c p t -> p c t"), G_sb.ap[:, cg * 16:(cg + 1) * 16, :], reads=[G_sb, G_sb2])
                continue
            P.op("dve", lambda e: e.scalar_tensor_tensor(out=EI.ap, in0=EI.ap, scalar=128.0, in1=EJ.ap, op0=ALU.mult, op1=ALU.add), reads=[EI, EJ], writes=[EI])
            if li > 0:
                P.op("dve", lambda e: e.tensor_scalar_add(out=EI.ap, in0=EI.ap, scalar1=float(li * 16384)), reads=[EI], writes=[EI])
            P.op("dve", lambda e: e.tensor_copy(out=ex_.ap, in_=EI.ap.rearrange("p h a -> p (h a)")), reads=[EI], writes=[ex_])
            for s in range(128):
                g_ = gb[gi % NG]
                gi += 1
                P.dma("pool", g_.ap, p_u.rearrange("l e d -> (l e) d"), reads=[ex_], writes=[g_], indirect=bass.IndirectOffsetOnAxis(ap=ex_.ap[:, s:s + 1], axis=0))
                P.op("dve", lambda e: e.scalar_tensor_tensor(out=junk.ap, in0=h_.ap, scalar=1.0, in1=g_.ap, op0=ALU.mult, op1=ALU.mult, accum_out=av.ap[:, s:s + 1]),
                     reads=[h_, g_], writes=[junk, av])
            P.op("act", lambda e: e.activation(out=hid.ap, in_=av.ap, func=AF.Gelu_apprx_tanh), reads=[av], writes=[hid])
            P.op("dve", lambda e: e.tensor_tensor(out=hid.ap, in0=hid.ap, in1=gate.ap.rearrange("p h a -> p (h a)"), op=ALU.mult), reads=[hid, gate], writes=[hid])
            for s in range(128):
                g_ = gb[gi % NG]
                gi += 1
                P.dma("pool", g_.ap, p_v.rearrange("l e d -> (l e) d"), reads=[ex_], writes=[g_], indirect=bass.IndirectOffsetOnAxis(ap=ex_.ap[:, s:s + 1], axis=0))
                if s == 0:
                    P.op("dve", lambda e: e.tensor_scalar(out=acc_.ap, in0=g_.ap, scalar1=hid.ap[:, 0:1], scalar2=None, op0=ALU.mult), reads=[g_, hid], writes=[acc_])
                else:
                    P.op("dve", lambda e: e.scalar_tensor_tensor(out=acc_.ap, in0=g_.ap, scalar=hid.ap[:, s:s + 1], in1=acc_.ap, op0=ALU.mult, op1=ALU.add),
                         reads=[g_, hid, acc_], writes=[acc_])
            P.dma("sp", ymix[t * 128:(t + 1) * 128, :], acc_.ap, reads=[acc_])
    if PEER_DENSE:
        P.barrier()
        peer_sweep(S, li, p_u, p_v, nt)


def peer_sweep(S, li, p_u, p_v, nt):
    k, P, hT_d, ymix, identb, ps, psb, Gd = S["k"], S["P"], S["hT_d"], S["ymix"], S["identb"], S["ps"], S["psb"], S["Gd"]
    SCH = 4
    passes = [list(range(i, i + 8)) for i in range(0, NTL, 8)]
    if nt == NT:
        passes.append([32, 33])
    with ExitStack() as es:
        hTb = [Buf(k.sb(es, "ps_hT%d" % i, [128, KC, 512], BF16)[:]) for i in range(2)]
        f_sb = [Buf(k.sb(es, "ps_f%d" % i, [128, D])[:]) for i in range(8)]
        Ubf = [Buf(k.sb(es, "ps_U%d" % i, [128, D], BF16)[:]) for i in range(2)]
        UT = [Buf(k.sb(es, "ps_UT%d" % i, [128, KC, 128], BF16)[:]) for i in range(2 * SCH)]
        Vbf = [Buf(k.sb(es, "ps_V%d" % i, [128, D], BF16)[:]) for i in range(2 * SCH)]
        hid = [Buf(k.sb(es, "ps_hid%d" % i, [128, 512], BF16)[:]) for i in range(2 * SCH)]
        Gc = [Buf(k.sb(es, "ps_G%d" % i, [128, 512], BF16)[:]) for i in range(4)]
        ui = 0
        sci = 0
        gci = 0
        api = 0
        fpi = 0
        evi = 0
        for tl in passes:
            groups = [tl[i:i + 4] for i in range(0, len(tl), 4)]
            for gi_, grp in enumerate(groups):
                for i, t in enumerate(grp):
                    P.dma("sp", hTb[gi_].ap[:, :, i * 128:(i + 1) * 128], hT_d[t], writes=[hTb[gi_]])
            for sc in range(128 // SCH):
                base = (sci % 2) * SCH
                sci += 1
                for c8 in range(SCH):
                    c = sc * SCH + c8
                    u_, ut_, v_ = Ubf[ui % 2], UT[base + c8], Vbf[base + c8]
                    ui += 1
                    P.dma("pool", u_.ap, p_u[li, c * 128:(c + 1) * 128, :], writes=[u_])
                    P.dma("pool", v_.ap, p_v[li, c * 128:(c + 1) * 128, :], writes=[v_])
                    for kc in range(KC):
                        pb = psb[kc // 8]
                        P.op("pe", lambda e: e.transpose(out=pb.ap[:, (kc % 8) * 128:(kc % 8 + 1) * 128], in_=u_.ap[:, kc * 128:(kc + 1) * 128], identity=identb.ap),
                             reads=[u_, identb], writes=[pb])
                    P.op("act", lambda e: e.activation(out=ut_.ap[:, 0:8, :], in_=psb[0].ap.rearrange("p (q t) -> p q t", q=8), func=AF.Copy), reads=[psb[0]], writes=[ut_])
                    P.op("dve", lambda e: e.tensor_copy(out=ut_.ap[:, 8:16, :], in_=psb[1].ap.rearrange("p (q t) -> p q t", q=8)), reads=[psb[1]], writes=[ut_])
                for gi_, grp in enumerate(groups):
                    ng = len(grp) * 128
                    tok0 = grp[0] * 128
                    hset = (gci % 2) * SCH
                    gci += 1
                    for c8 in range(SCH):
                        c = sc * SCH + c8
                        g_ = Gc[api % 4]
                        a_ps = ps[api % 2]
                        api += 1
                        h_ = hid[hset + c8]
                        P.dma("sp", g_.ap[:, 0:ng], Gd[c, :, tok0:tok0 + ng], writes=[g_])
                        for kc in range(KC):
                            P.op("pe", lambda e: e.matmul(a_ps.ap[:, 0:ng], lhsT=UT[base + c8].ap[:, kc, :], rhs=hTb[gi_].ap[:, kc, 0:ng], start=(kc == 0), stop=(kc == KC - 1)),
                                 reads=[UT[base + c8], hTb[gi_]], writes=[a_ps])
                        P.op("act", lambda e: e.activation(out=h_.ap[:, 0:ng], in_=a_ps.ap[:, 0:ng], func=AF.Gelu_apprx_tanh), reads=[a_ps], writes=[h_])
                        P.op("pool", lambda e: e.tensor_tensor(out=h_.ap[:, 0:ng], in0=h_.ap[:, 0:ng], in1=g_.ap[:, 0:ng], op=ALU.mult), reads=[h_, g_], writes=[h_])
                    for i, t in enumerate(grp):
                        fb = f_sb[gi_ * 4 + i]
                        for dch in range(4):
                            f_ps = ps[2 + fpi % 4]
                            fpi += 1
                            for c8 in range(SCH):
                                P.op("pe", lambda e: e.matmul(f_ps.ap, lhsT=hid[hset + c8].ap[:, i * 128:(i + 1) * 128], rhs=Vbf[base + c8].ap[:, dch * 512:(dch + 1) * 512],
                                                              start=(c8 == 0), stop=(c8 == SCH - 1)), reads=[hid[hset + c8], Vbf[base + c8]], writes=[f_ps])
                            dst = fb.ap[:, dch * 512:(dch + 1) * 512]
                            if sc == 0:
                                P.op("act", lambda e: e.activation(out=dst, in_=f_ps.ap, func=AF.Copy), reads=[f_ps], writes=[fb])
                            else:
                                P.op("dve", lambda e: e.tensor_tensor(out=dst, in0=dst, in1=f_ps.ap, op=ALU.add), reads=[fb, f_ps], writes=[fb])
            for i, t in enumerate(tl):
                P.dma("sp", ymix[t * 128:(t + 1) * 128, :], f_sb[i].ap, reads=[f_sb[i]])


_CACHE = {}


def host_consts():
    ident = np.eye(128, dtype=np.float32)
    s = np.arange(128)
    tri = np.stack([(s[:, None] <= s[None, :]), (s[:, None] >= s[None, :])]).astype(np.float32)
    t = np.arange(SEQ)
    rows = (t // 64).astype(np.float32)
    cols = (t % 64).astype(np.float32)
    freqs = (10000.0 ** (-np.arange(32, dtype=np.float32) / np.float32(32))).astype(np.float32)
    ang = np.concatenate([rows[:, None] * freqs[None, :], cols[:, None] * freqs[None, :]], axis=1).astype(np.float32)
    rope = np.stack([np.cos(ang), np.sin(ang)], axis=1).astype(np.float32)
    iota16 = np.arange(128, dtype=np.float32)[None, :]
    return dict(ident=ident, tri=tri, rope=rope, iota16=iota16)


def make_in_maps(inputs):
    consts = host_consts()
    shared = {n: np.ascontiguousarray(inputs[n]) for n in (
        "w_mod", "b_mod", "ln_g", "ln_b", "a_w_in", "a_b_gate", "a_norm_g", "a_w_out", "b_w_in", "b_b_in", "b_ln_g", "b_ln_b",
        "b_w_s", "b_b_s", "b_w_out", "c_w_qkv", "c_q_g", "c_k_g", "c_w_out", "p_w_q", "p_k1", "p_k2", "p_u", "p_v")}
    shared.update(consts)
    maps = []
    for b in range(8):
        m = dict(shared)
        m["xin"] = np.ascontiguousarray(np.concatenate([inputs["x"][b], inputs["ctx"][b]], axis=0))
        m["cc"] = np.ascontiguousarray(np.stack([inputs["c"][b], inputs["c_ctx"]], axis=0))
        maps.append(m)
    return maps


def kernel(**inputs):
    inputs = {k_: np.asarray(v) for k_, v in inputs.items()}
    if "nc" not in _CACHE:
        _CACHE["nc"] = build_program()[0]
    nc = _CACHE["nc"]
    maps = make_in_maps(inputs)
    res = run_bass_kernel_spmd(nc, maps, core_ids=list(range(8)))
    out = np.stack([np.asarray(r["y"]).reshape(SEQ, D) for r in res.results], axis=0)
    return out.astype(np.float32)
```

```python
import numpy as np
from contextlib import ExitStack
import concourse.bass as bass
import concourse.mybir as mybir
from concourse.bass_utils import run_bass_kernel_spmd

F32 = mybir.dt.float32
BF16 = mybir.dt.bfloat16
U32 = mybir.dt.uint32
I32 = mybir.dt.int32
AF = mybir.ActivationFunctionType
ALU = mybir.AluOpType
AX = mybir.AxisListType

D = 2048
KC = 16
SEQ = 4096
CTX = 256
T = SEQ + CTX
NT = T // 128
NTL = SEQ // 128
DEPTH = 4
ALPHA = (2 * DEPTH) ** 0.25
EPS = 1e-6
A_IN = 6160
NEG = -1.0e30


NO_SELF_SYNC = ("pe",)


class Buf:
    __slots__ = ("ap", "w", "r")

    def __init__(self, ap):
        self.ap = ap
        self.w = None
        self.r = {}


class Prog:
    LIMIT = 30000

    def __init__(self, nc):
        self.nc = nc
        self.engs = {"pe": nc.tensor, "dve": nc.vector, "act": nc.scalar, "pool": nc.gpsimd, "sp": nc.sync}
        self.sems = {}
        self.cnt = {}
        self.key = {}
        self.gen = {}
        for e in ("pe", "dve", "act", "pool"):
            self.gen[e] = 0
            self._newkey(e)
        self.clock = {e: {} for e in self.engs}
        self.dq = {}
        for q, n in (("sp", 16), ("pool", 16), ("act", 6)):
            keys = []
            for i in range(n):
                k = "d_%s%d" % (q, i)
                self.sems[k] = nc.alloc_semaphore(name=k)
                self.cnt[k] = 0
                keys.append(k)
            self.dq[q] = [keys, 0]
        self.n_inst = 0

    def _newkey(self, e):
        k = "%s_%d" % (e, self.gen[e])
        self.gen[e] += 1
        self.sems[k] = self.nc.alloc_semaphore(name="s_" + k)
        self.cnt[k] = 0
        self.key[e] = k

    def _wait(self, e, tok):
        if tok is None:
            return
        k, v = tok
        if self.clock[e].get(k, 0) >= v:
            return
        if e in NO_SELF_SYNC and k.startswith(e + "_"):
            return
        self.engs[e].wait_ge(self.sems[k], v)
        self.clock[e][k] = v

    def _deps(self, e, reads, writes):
        for b in reads:
            self._wait(e, b.w)
        for b in writes:
            self._wait(e, b.w)
            for kv in list(b.r.items()):
                self._wait(e, kv)

    def _mark(self, tok, reads, writes):
        k, v = tok
        for b in reads:
            b.r[k] = v
        for b in writes:
            b.w = tok
            b.r = {}

    def op(self, e, fn, reads=(), writes=()):
        self._deps(e, reads, writes)
        inst = fn(self.engs[e])
        k = self.key[e]
        inst.then_inc(self.sems[k], 1)
        self.cnt[k] += 1
        tok = (k, self.cnt[k])
        self._mark(tok, reads, writes)
        if self.cnt[k] >= self.LIMIT:
            self._newkey(e)
        self.n_inst += 1
        return tok

    def dma(self, q, out, in_, reads=(), writes=(), indirect=None, **kw):
        keys, rr = self.dq[q]
        k = keys[rr % len(keys)]
        self.dq[q][1] += 1
        if self.cnt[k] > 0:
            self._wait(q, (k, self.cnt[k]))
        self._deps(q, reads, writes)
        eng = self.engs[q]
        if indirect is not None:
            inst = eng.indirect_dma_start(out=out, out_offset=None, in_=in_, in_offset=indirect, **kw)
        else:
            inst = eng.dma_start(out=out, in_=in_, **kw)
        inst.then_inc(self.sems[k], 16)
        self.cnt[k] += 16
        if self.cnt[k] >= self.LIMIT:
            self._wait(q, (k, self.cnt[k]))
            tok = (k, self.cnt[k])
            self._mark(tok, reads, writes)
            nk = k + "n"
            self.sems[nk] = self.nc.alloc_semaphore(name=nk)
            self.cnt[nk] = 0
            keys[(rr) % len(keys)] = nk
            self.n_inst += 1
            return tok
        tok = (k, self.cnt[k])
        self._mark(tok, reads, writes)
        self.n_inst += 1
        return tok

    def barrier(self):
        for e in self.engs:
            for k, v in self.cnt.items():
                if v > 0:
                    self._wait(e, (k, v))

    def final_wait(self, e="sp"):
        for k, v in self.cnt.items():
            if v > 0:
                self._wait(e, (k, v))


def rsqrt_eps(P, dst, src_ap, reads, scale):
    P.op("dve", lambda e: e.tensor_scalar(out=dst.ap, in0=src_ap, scalar1=float(scale), scalar2=float(EPS), op0=ALU.mult, op1=ALU.add),
         reads=list(reads), writes=[dst])
    P.op("act", lambda e: e.activation(out=dst.ap, in_=dst.ap, func=AF.Sqrt), reads=[dst], writes=[dst])
    P.op("dve", lambda e: e.reciprocal(out=dst.ap, in_=dst.ap), reads=[dst], writes=[dst])


def bcast_rows(ap_row, n):
    return ap_row.to_broadcast([n, ap_row.shape[-1]])


class K:
    def __init__(self, n_layers=DEPTH, dbg=()):
        self.n_layers = n_layers
        self.dbg = set(dbg)
        nc = bass.Bass("TRN2", target_bir_lowering=False)
        self.nc = nc
        self.P = Prog(nc)
        self.es = ExitStack()
        self.inp = {}

    def din(self, name, shape, dt=F32):
        ap = self.nc.dram_tensor(name, list(shape), dt, kind="ExternalInput").ap()
        self.inp[name] = ap
        return ap

    def dscr(self, name, shape, dt=F32, out=False):
        kind = "ExternalOutput" if (out or name in self.dbg) else "Internal"
        return self.nc.dram_tensor(name, list(shape), dt, kind=kind).ap()

    def sb(self, es, name, shape, dt=F32):
        self.uid = getattr(self, "uid", 0) + 1
        t = es.enter_context(self.nc.sbuf_tensor("%s_%d" % (name, self.uid), list(shape), dt))
        return t

    def psum(self, es, name, shape, dt=F32):
        return es.enter_context(self.nc.psum_tensor(name, list(shape), dt))


def build_program(n_layers=DEPTH, dbg=(), peer=True):
    k = K(n_layers, dbg)
    nc, P = k.nc, k.P
    xin = k.din("xin", [T, D])
    cc = k.din("cc", [2, D])
    w_mod = k.din("w_mod", [DEPTH, D, 6 * D])
    b_mod = k.din("b_mod", [DEPTH, 6 * D])
    ln_g = k.din("ln_g", [DEPTH, 2, D])
    ln_b = k.din("ln_b", [DEPTH, 2, D])
    a_w_in = k.din("a_w_in", [2, D, A_IN])
    a_b_gate = k.din("a_b_gate", [2, 16])
    a_norm_g = k.din("a_norm_g", [2, D])
    a_w_out = k.din("a_w_out", [2, D, D])
    b_w_in = k.din("b_w_in", [1, D, 8192])
    b_b_in = k.din("b_b_in", [1, 8192])
    b_ln_g = k.din("b_ln_g", [1, 4096])
    b_ln_b = k.din("b_ln_b", [1, 4096])
    b_w_s = k.din("b_w_s", [1, 8, 128, 128])
    b_b_s = k.din("b_b_s", [1, 8, 128])
    b_w_out = k.din("b_w_out", [1, 4096, D])
    c_w_qkv = k.din("c_w_qkv", [1, D, 3072])
    c_q_g = k.din("c_q_g", [1, 128])
    c_k_g = k.din("c_k_g", [1, 128])
    c_w_out = k.din("c_w_out", [1, D, D])
    p_w_q = k.din("p_w_q", [DEPTH, D, D])
    p_k1 = k.din("p_k1", [DEPTH, 128, 128])
    p_k2 = k.din("p_k2", [DEPTH, 128, 128])
    p_u = k.din("p_u", [DEPTH, 16384, D])
    p_v = k.din("p_v", [DEPTH, 16384, D])
    ident_d = k.din("ident", [128, 128])
    tri_d = k.din("tri", [2, 128, 128])
    rope_d = k.din("rope", [SEQ, 2, 64])
    iota_d = k.din("iota16", [1, 128])

    yout = k.dscr("y", [SEQ, D], out=True)
    xres = k.dscr("xres", [T, D])
    modd = k.dscr("modd", [DEPTH, 2, 6 * D])
    hT_d = k.dscr("hT", [NT, 128, KC, 128], BF16)
    h_d = k.dscr("h_tm", [T, D])
    ymix = k.dscr("ymix", [T, D])
    S = dict(k=k, nc=nc, P=P, xin=xin, xres=xres, modd=modd, hT_d=hT_d, h_d=h_d, ymix=ymix, yout=yout)

    with ExitStack() as top:
        ident = Buf(k.sb(top, "ident_sb", [128, 128])[:])
        identb = Buf(k.sb(top, "identb_sb", [128, 128], BF16)[:])
        P.dma("sp", ident.ap, ident_d[:, :], writes=[ident])
        P.op("dve", lambda e: e.tensor_copy(out=identb.ap, in_=ident.ap), reads=[ident], writes=[identb])
        ps_t = [k.psum(top, "ps%d" % i, [128, 1024]) for i in range(4)]
        ps = [Buf(ps_t[i // 2][:, (i % 2) * 512:(i % 2 + 1) * 512]) for i in range(8)]
        S.update(ident=ident, identb=identb, ps=ps, ps_t=ps_t)
        S["ps2"] = [Buf(t[:]) for t in ps_t]
        S["psb"] = [Buf(ps_t[3][:, i * 512:(i + 1) * 512].bitcast(BF16)) for i in range(2)]
        S["Gd"] = k.dscr("Gd", [128, 128, T], BF16)

        phase_mod(S, cc, w_mod, b_mod, n_layers)
        P.barrier()
        for li in range(n_layers):
            need_ctx = li < DEPTH - 1
            nt_out = NT if need_ctx else NTL
            src = xin if li == 0 else xres
            mixer, j = li % 3, li // 3
            phase_prep(S, src, li, 0, NT, want_h=False)
            P.barrier()
            if mixer == 0:
                mix_mlstm(S, j, a_w_in, a_b_gate, a_norm_g, a_w_out, tri_d, need_ctx)
            elif mixer == 1:
                mix_gmlp(S, j, b_w_in, b_b_in, b_ln_g, b_ln_b, b_w_s, b_b_s, b_w_out, need_ctx)
            else:
                mix_gqa(S, j, c_w_qkv, c_q_g, c_k_g, c_w_out, rope_d, need_ctx)
            P.barrier()
            phase_ln(S, src, ymix, xres, li, 0, ln_g, ln_b, nt_out)
            P.barrier()
            if "x_mid%d" % li in k.dbg:
                dump(S, xres, "x_mid%d" % li, T)
            phase_prep(S, xres, li, 1, nt_out, want_h=not PEER_DENSE)
            P.barrier()
            phase_peer(S, li, p_w_q, p_k1, p_k2, p_u, p_v, iota_d, nt_out, enable=peer)
            P.barrier()
            last = (li == n_layers - 1)
            phase_ln(S, xres, ymix, (yout if last else xres), li, 1, ln_g, ln_b, nt_out if not last else NTL)
            P.barrier()
        P.final_wait("sp")
        P.final_wait("act")
    return nc, k


def dump(S, src, name, rows):
    k, P = S["k"], S["P"]
    dst = k.dscr(name, [rows, D], out=True)
    for t in range(rows // 128):
        P.dma("sp", dst[t * 128:(t + 1) * 128, :], src[t * 128:(t + 1) * 128, :])
    P.barrier()


def phase_mod(S, cc, w_mod, b_mod, n_layers):
    k, P, ps, modd = S["k"], S["P"], S["ps"], S["modd"]
    with ExitStack() as es:
        craw = Buf(k.sb(es, "craw", [128, 2, KC])[:])
        condT = Buf(k.sb(es, "condT", [128, KC, 2])[:])
        P.dma("sp", craw.ap, cc.rearrange("r (kc p) -> p r kc", p=128), writes=[craw], allow_slow_non_contiguous=True)
        P.op("act", lambda e: e.activation(out=condT.ap.rearrange("p kc r -> p r kc"), in_=craw.ap, func=AF.Silu),
             reads=[craw], writes=[condT])
        wbuf = [Buf(k.sb(es, "wm%d" % i, [128, KC, 512])[:]) for i in range(4)]
        bbuf = [Buf(k.sb(es, "bm%d" % i, [2, 512])[:]) for i in range(4)]
        obuf = [Buf(k.sb(es, "om%d" % i, [2, 512])[:]) for i in range(4)]
        it = 0
        for li in range(n_layers):
            for nci in range(24):
                w, bb, ob, pt = wbuf[it % 4], bbuf[it % 4], obuf[it % 4], ps[it % 4]
                n0 = nci * 512
                wsrc = w_mod[li, :, n0:n0 + 512].rearrange("(kc p) n -> p kc n", p=128)
                P.dma("sp", w.ap[:, 0:8, :], wsrc[:, 0:8, :], writes=[w])
                P.dma("act", w.ap[:, 8:16, :], wsrc[:, 8:16, :], writes=[w])
                P.dma("sp", bb.ap, b_mod[li:li + 1, n0:n0 + 512].to_broadcast([2, 512]), writes=[bb])
                for kc in range(KC):
                    P.op("pe", lambda e, kc=kc, w=w, pt=pt: e.matmul(pt.ap[0:2, :], lhsT=condT.ap[:, kc, :], rhs=w.ap[:, kc, :],
                                                                 start=(kc == 0), stop=(kc == KC - 1)),
                         reads=[condT, w], writes=[pt])
                P.op("dve", lambda e, pt=pt, bb=bb, ob=ob: e.tensor_tensor(out=ob.ap, in0=pt.ap[0:2, :], in1=bb.ap, op=ALU.add),
                     reads=[pt, bb], writes=[ob])
                P.dma("sp", modd[li, :, n0:n0 + 512], ob.ap, reads=[ob])
                it += 1


def load_mod_bcast(S, es, li, idx, rows, name, plus_one=False):
    k, P, modd = S["k"], S["P"], S["modd"]
    b = Buf(k.sb(es, name, [128, D])[:])
    P.dma("sp", b.ap, modd[li, rows:rows + 1, idx * D:(idx + 1) * D].to_broadcast([128, D]), writes=[b])
    if plus_one:
        P.op("pool", lambda e: e.tensor_scalar_add(out=b.ap, in0=b.ap, scalar1=1.0), reads=[b], writes=[b])
    return b


def phase_prep(S, src, li, sub, nt, want_h):
    k, P, ps, hT_d, h_d, ident = S["k"], S["P"], S["ps"], S["hT_d"], S["h_d"], S["ident"]
    with ExitStack() as es:
        sh = [load_mod_bcast(S, es, li, 3 * sub + 0, r, "sh%d" % r) for r in range(2)]
        sc = [load_mod_bcast(S, es, li, 3 * sub + 1, r, "sc%d" % r, plus_one=True) for r in range(2)]
        xb = [Buf(k.sb(es, "px%d" % i, [128, D])[:]) for i in range(2)]
        hb = [Buf(k.sb(es, "ph%d" % i, [128, D])[:]) for i in range(2)]
        hTb = [Buf(k.sb(es, "phT%d" % i, [128, KC, 128], BF16)[:]) for i in range(2)]
        for t in range(nt):
            r = 0 if t < NTL else 1
            x, h, hT = xb[t % 2], hb[t % 2], hTb[t % 2]
            P.dma("sp", x.ap, src[t * 128:(t + 1) * 128, :], writes=[x])
            P.op("pool", lambda e: e.tensor_tensor(out=h.ap, in0=x.ap, in1=sc[r].ap, op=ALU.mult), reads=[x, sc[r]], writes=[h])
            P.op("dve", lambda e: e.tensor_tensor(out=h.ap, in0=h.ap, in1=sh[r].ap, op=ALU.add), reads=[h, sh[r]], writes=[h])
            if want_h:
                P.dma("act", h_d[t * 128:(t + 1) * 128, :], h.ap, reads=[h])
            for g in range(4):
                pt = ps[(t * 4 + g) % 8]
                for q in range(4):
                    kc = g * 4 + q
                    P.op("pe", lambda e, kc=kc, q=q, pt=pt: e.transpose(out=pt.ap[:, q * 128:(q + 1) * 128], in_=h.ap[:, kc * 128:(kc + 1) * 128],
                                                                   identity=ident.ap), reads=[h, ident], writes=[pt])
                eng = "act" if g % 2 == 0 else "dve"
                if eng == "act":
                    P.op("act", lambda e, g=g, pt=pt: e.activation(out=hT.ap[:, g * 4:(g + 1) * 4, :], in_=pt.ap.rearrange("p (q t) -> p q t", q=4), func=AF.Copy),
                         reads=[pt], writes=[hT])
                else:
                    P.op("dve", lambda e, g=g, pt=pt: e.tensor_copy(out=hT.ap[:, g * 4:(g + 1) * 4, :], in_=pt.ap.rearrange("p (q t) -> p q t", q=4)),
                         reads=[pt], writes=[hT])
            P.dma("sp", hT_d[t], hT.ap, reads=[hT])


def phase_ln(S, xsrc, ysrc, dst, li, sub, ln_g, ln_b, nt):
    k, P = S["k"], S["P"]
    with ExitStack() as es:
        gt = [load_mod_bcast(S, es, li, 3 * sub + 2, r, "lg%d" % r) for r in range(2)]
        lg = Buf(k.sb(es, "lng", [128, D])[:])
        lb = Buf(k.sb(es, "lnb", [128, D])[:])
        P.dma("sp", lg.ap, ln_g[li, sub:sub + 1, :].to_broadcast([128, D]), writes=[lg])
        P.dma("sp", lb.ap, ln_b[li, sub:sub + 1, :].to_broadcast([128, D]), writes=[lb])
        xb = [Buf(k.sb(es, "lx%d" % i, [128, D])[:]) for i in range(2)]
        yb = [Buf(k.sb(es, "ly%d" % i, [128, D])[:]) for i in range(2)]
        st = [Buf(k.sb(es, "lst%d" % i, [128, 4, 6])[:]) for i in range(2)]
        mv = [Buf(k.sb(es, "lmv%d" % i, [128, 2])[:]) for i in range(2)]
        rs = [Buf(k.sb(es, "lrs%d" % i, [128, 1])[:]) for i in range(2)]
        for t in range(nt):
            r = 0 if t < NTL else 1
            x, y, s_, m_, r_ = xb[t % 2], yb[t % 2], st[t % 2], mv[t % 2], rs[t % 2]
            P.dma("sp", x.ap, xsrc[t * 128:(t + 1) * 128, :], writes=[x])
            P.dma("act", y.ap, ysrc[t * 128:(t + 1) * 128, :], writes=[y])
            P.op("pool", lambda e: e.tensor_tensor(out=y.ap, in0=y.ap, in1=gt[r].ap, op=ALU.mult), reads=[y, gt[r]], writes=[y])
            P.op("dve", lambda e: e.scalar_tensor_tensor(out=x.ap, in0=x.ap, scalar=float(ALPHA), in1=y.ap, op0=ALU.mult, op1=ALU.add),
                 reads=[x, y], writes=[x])
            for q in range(4):
                P.op("dve", lambda e, q=q: e.bn_stats(out=s_.ap[:, q, :], in_=x.ap[:, q * 512:(q + 1) * 512]), reads=[x], writes=[s_])
            P.op("dve", lambda e: e.bn_aggr(out=m_.ap, in_=s_.ap.rearrange("p a b -> p (a b)")), reads=[s_], writes=[m_])
            rsqrt_eps(P, r_, m_.ap[:, 1:2], [m_], 1.0)
            P.op("dve", lambda e: e.tensor_scalar(out=x.ap, in0=x.ap, scalar1=m_.ap[:, 0:1], scalar2=r_.ap[:, 0:1], op0=ALU.subtract, op1=ALU.mult),
                 reads=[x, m_, r_], writes=[x])
            P.op("pool", lambda e: e.tensor_tensor(out=x.ap, in0=x.ap, in1=lg.ap, op=ALU.mult), reads=[x, lg], writes=[x])
            P.op("dve", lambda e: e.tensor_tensor(out=x.ap, in0=x.ap, in1=lb.ap, op=ALU.add), reads=[x, lb], writes=[x])
            P.dma("sp", dst[t * 128:(t + 1) * 128, :], x.ap, reads=[x])


def load_hT_group(S, es_bufs, t0, ntile, q="sp"):
    P, hT_d = S["P"], S["hT_d"]
    g = es_bufs
    for i in range(ntile):
        P.dma(q, g.ap[:, :, i * 128:(i + 1) * 128], hT_d[t0 + i], writes=[g])
    return g


def load_w_chunk(S, wbuf, w_ap, k0_chunks, n0, n, q="pool"):
    P = S["P"]
    P.dma(q, wbuf.ap[:, 0:k0_chunks, 0:n], w_ap[:, n0:n0 + n].rearrange("(kc p) n -> p kc n", p=128), writes=[wbuf])


NPS = 8
PEER_DENSE = True


def next_ps(S, lo=0, hi=NPS):
    r = S.setdefault("rot", 0)
    S["rot"] = r + 1
    return S["ps"][lo + r % (hi - lo)]


def linear_tm(S, src_T, kch, w_ap, N, tiles, epi, G=8, tag="l"):
    k, P = S["k"], S["P"]
    with ExitStack() as es:
        hg = [Buf(k.sb(es, tag + "hg%d" % i, [128, kch, 128], BF16)[:]) for i in range(G)]
        wb = [Buf(k.sb(es, tag + "w%d" % i, [128, kch, 512], BF16)[:]) for i in range(2)]
        it = 0
        for g0 in range(0, len(tiles), G):
            grp = tiles[g0:g0 + G]
            for i, t in enumerate(grp):
                P.dma("sp", hg[i].ap, src_T[t], writes=[hg[i]])
            for n0 in range(0, N, 512):
                n = min(512, N - n0)
                w = wb[it % 2]
                it += 1
                P.dma("pool", w.ap[:, :, 0:n], w_ap[:, n0:n0 + n].rearrange("(kc p) n -> p kc n", p=128), writes=[w])
                for i, t in enumerate(grp):
                    pt = next_ps(S)
                    for kc in range(kch):
                        P.op("pe", lambda e: e.matmul(pt.ap[:, 0:n], lhsT=hg[i].ap[:, kc, :], rhs=w.ap[:, kc, 0:n],
                                                      start=(kc == 0), stop=(kc == kch - 1)), reads=[hg[i], w], writes=[pt])
                    epi(t, n0, n, pt)


def linear_fm(S, src_T, kch, w_ap, N, tiles, epi, G=8, tag="f"):
    k, P = S["k"], S["P"]
    with ExitStack() as es:
        hg = [Buf(k.sb(es, tag + "hg%d" % i, [128, kch, 512], BF16)[:]) for i in range(G // 4)]
        wb = [Buf(k.sb(es, tag + "w%d" % i, [128, kch, 512], BF16)[:]) for i in range(2)]
        it = 0
        for g0 in range(0, len(tiles), G):
            grp = tiles[g0:g0 + G]
            blocks = [grp[i:i + 4] for i in range(0, len(grp), 4)]
            for bi, blk in enumerate(blocks):
                for i, t in enumerate(blk):
                    P.dma("sp", hg[bi].ap[:, :, i * 128:(i + 1) * 128], src_T[t], writes=[hg[bi]])
            for n0 in range(0, N, 512):
                n = min(512, N - n0)
                w = wb[it % 2]
                it += 1
                P.dma("pool", w.ap[:, :, 0:n], w_ap[:, n0:n0 + n].rearrange("(kc p) n -> p kc n", p=128), writes=[w])
                for cc in range(n // 128):
                    for bi, blk in enumerate(blocks):
                        ntok = len(blk) * 128
                        pt = next_ps(S)
                        for kc in range(kch):
                            P.op("pe", lambda e: e.matmul(pt.ap[:, 0:ntok], lhsT=w.ap[:, kc, cc * 128:(cc + 1) * 128], rhs=hg[bi].ap[:, kc, 0:ntok],
                                                          start=(kc == 0), stop=(kc == kch - 1)), reads=[hg[bi], w], writes=[pt])
                        epi(n0 // 128 + cc, blk[0], len(blk), pt)


def make_store_epi(S, es, dst, tag="st", nbuf=4, col_off=0):
    k, P = S["k"], S["P"]
    stg = [Buf(k.sb(es, tag + "%d" % i, [128, 512])[:]) for i in range(nbuf)]
    cnt = [0]

    def epi(t, n0, n, pt):
        s = stg[cnt[0] % nbuf]
        eng = "act" if cnt[0] % 2 == 0 else "dve"
        cnt[0] += 1
        if eng == "act":
            P.op("act", lambda e: e.activation(out=s.ap[:, 0:n], in_=pt.ap[:, 0:n], func=AF.Copy), reads=[pt], writes=[s])
        else:
            P.op("dve", lambda e: e.tensor_copy(out=s.ap[:, 0:n], in_=pt.ap[:, 0:n]), reads=[pt], writes=[s])
        P.dma("sp", dst[t * 128:(t + 1) * 128, col_off + n0:col_off + n0 + n], s.ap[:, 0:n], reads=[s])
    return epi


def transpose_to_T(S, src, nch, dstT, t):
    P, ident = S["P"], S["ident"]
    for g in range(nch // 4):
        pt = next_ps(S)
        for q in range(4):
            kc = g * 4 + q
            P.op("pe", lambda e: e.transpose(out=pt.ap[:, q * 128:(q + 1) * 128], in_=src.ap[:, kc * 128:(kc + 1) * 128], identity=ident.ap),
                 reads=[src, ident], writes=[pt])
        if g % 2 == 0:
            P.op("act", lambda e: e.activation(out=dstT.ap[:, g * 4:(g + 1) * 4, :], in_=pt.ap.rearrange("p (q t) -> p q t", q=4), func=AF.Copy),
                 reads=[pt], writes=[dstT])
        else:
            P.op("dve", lambda e: e.tensor_copy(out=dstT.ap[:, g * 4:(g + 1) * 4, :], in_=pt.ap.rearrange("p (q t) -> p q t", q=4)),
                 reads=[pt], writes=[dstT])


def mix_gmlp(S, j, b_w_in, b_b_in, b_ln_g, b_ln_b, b_w_s, b_b_s, b_w_out, need_ctx):
    k, P, hT_d, ymix, ident = S["k"], S["P"], S["hT_d"], S["ymix"], S["ident"]
    tiles = list(range(NT if need_ctx else NTL))
    zd = k.dscr("zd", [T, 8192])
    uvT = k.dscr("uvT", [NT, 128, 32, 128], BF16)
    with ExitStack() as es:
        bias = Buf(k.sb(es, "gb_bias", [128, 8192])[:])
        P.dma("sp", bias.ap, b_b_in[j:j + 1, :].to_broadcast([128, 8192]), writes=[bias])
        stg = [Buf(k.sb(es, "gb_st%d" % i, [128, 512])[:]) for i in range(4)]
        cnt = [0]

        def epi(t, n0, n, pt):
            s = stg[cnt[0] % 4]
            cnt[0] += 1
            P.op("dve", lambda e: e.tensor_tensor(out=s.ap, in0=pt.ap, in1=bias.ap[:, n0:n0 + n], op=ALU.add), reads=[pt, bias], writes=[s])
            P.op("act", lambda e: e.activation(out=s.ap, in_=s.ap, func=AF.Gelu_apprx_tanh), reads=[s], writes=[s])
            P.dma("sp", zd[t * 128:(t + 1) * 128, n0:n0 + n], s.ap, reads=[s])
        linear_tm(S, hT_d, KC, b_w_in[j], 8192, tiles, epi, tag="gb1")
    P.barrier()
    with ExitStack() as es:
        lg = Buf(k.sb(es, "gb_lg", [128, 4096])[:])
        lb = Buf(k.sb(es, "gb_lb", [128, 4096])[:])
        P.dma("sp", lg.ap, b_ln_g[j:j + 1, :].to_broadcast([128, 4096]), writes=[lg])
        P.dma("sp", lb.ap, b_ln_b[j:j + 1, :].to_broadcast([128, 4096]), writes=[lb])
        wsr = Buf(k.sb(es, "gb_wsr", [128, 8, 128])[:])
        wsT = Buf(k.sb(es, "gb_wsT", [128, 8, 128], BF16)[:])
        bsT = Buf(k.sb(es, "gb_bsT", [128, 8])[:])
        P.dma("sp", wsr.ap, b_w_s[j].rearrange("g t s -> t g s"), writes=[wsr])
        P.dma("sp", bsT.ap, b_b_s[j].rearrange("g t -> t g"), writes=[bsT], allow_slow_non_contiguous=True)
        for g in range(8):
            pt = next_ps(S)
            P.op("pe", lambda e: e.transpose(out=pt.ap[:, 0:128], in_=wsr.ap[:, g, :], identity=ident.ap), reads=[wsr, ident], writes=[pt])
            P.op("dve", lambda e: e.tensor_copy(out=wsT.ap[:, g, :], in_=pt.ap[:, 0:128]), reads=[pt], writes=[wsT])
        zb = [Buf(k.sb(es, "gb_z%d" % i, [128, 8192])[:]) for i in range(2)]
        vb = [Buf(k.sb(es, "gb_v%d" % i, [128, 4096], BF16)[:]) for i in range(2)]
        uT = [Buf(k.sb(es, "gb_uT%d" % i, [128, 32, 128], BF16)[:]) for i in range(2)]
        st = [Buf(k.sb(es, "gb_bs%d" % i, [128, 8, 6])[:]) for i in range(2)]
        mv = [Buf(k.sb(es, "gb_mv%d" % i, [128, 2])[:]) for i in range(2)]
        rs = [Buf(k.sb(es, "gb_rs%d" % i, [128, 1])[:]) for i in range(2)]
        for t in tiles:
            z, v, u_T, s_, m_, r_ = zb[t % 2], vb[t % 2], uT[t % 2], st[t % 2], mv[t % 2], rs[t % 2]
            P.dma("sp", z.ap[:, 0:4096], zd[t * 128:(t + 1) * 128, 0:4096], writes=[z])
            P.dma("act", z.ap[:, 4096:8192], zd[t * 128:(t + 1) * 128, 4096:8192], writes=[z])
            for q in range(8):
                P.op("dve", lambda e: e.bn_stats(out=s_.ap[:, q, :], in_=z.ap[:, 4096 + q * 512:4096 + (q + 1) * 512]), reads=[z], writes=[s_])
            P.op("dve", lambda e: e.bn_aggr(out=m_.ap, in_=s_.ap.rearrange("p a b -> p (a b)")), reads=[s_], writes=[m_])
            rsqrt_eps(P, r_, m_.ap[:, 1:2], [m_], 1.0)
            P.op("dve", lambda e: e.tensor_scalar(out=z.ap[:, 4096:8192], in0=z.ap[:, 4096:8192], scalar1=m_.ap[:, 0:1], scalar2=r_.ap[:, 0:1],
                                                  op0=ALU.subtract, op1=ALU.mult), reads=[z, m_, r_], writes=[z])
            P.op("pool", lambda e: e.tensor_tensor(out=z.ap[:, 4096:8192], in0=z.ap[:, 4096:8192], in1=lg.ap, op=ALU.mult), reads=[z, lg], writes=[z])
            P.op("dve", lambda e: e.tensor_tensor(out=v.ap, in0=z.ap[:, 4096:8192], in1=lb.ap, op=ALU.add), reads=[z, lb], writes=[v])
            for g in range(8):
                pt = next_ps(S)
                P.op("pe", lambda e: e.matmul(pt.ap, lhsT=wsT.ap[:, g, :], rhs=v.ap[:, g * 512:(g + 1) * 512], start=True, stop=True),
                     reads=[wsT, v], writes=[pt])
                P.op("dve", lambda e: e.scalar_tensor_tensor(out=z.ap[:, g * 512:(g + 1) * 512], in0=pt.ap, scalar=bsT.ap[:, g:g + 1],
                                                             in1=z.ap[:, g * 512:(g + 1) * 512], op0=ALU.add, op1=ALU.mult),
                     reads=[pt, bsT, z], writes=[z])
            transpose_to_T(S, z, 32, u_T, t)
            P.dma("sp", uvT[t], u_T.ap, reads=[u_T])
    P.barrier()
    with ExitStack() as es:
        epi = make_store_epi(S, es, ymix, tag="gb3s")
        linear_tm(S, uvT, 32, b_w_out[j], D, tiles, epi, G=4, tag="gb3")


def mix_gqa(S, j, c_w_qkv, c_q_g, c_k_g, c_w_out, rope_d, need_ctx):
    k, P, hT_d, ymix, ident = S["k"], S["P"], S["hT_d"], S["ymix"], S["ident"]
    tiles = list(range(NT))
    qkv_d = k.dscr("qkv_d", [T, 3072])
    qkT_d = k.dscr("qkT_d", [20, 128, T], BF16)
    v_d = k.dscr("v_d", [NT, 128, 512], BF16)
    with ExitStack() as es:
        epi = make_store_epi(S, es, qkv_d, tag="gq1s")
        linear_tm(S, hT_d, KC, c_w_qkv[j], 3072, tiles, epi, tag="gq1")
    P.barrier()
    with ExitStack() as es:
        gq = Buf(k.sb(es, "gq_g", [128, 20, 128])[:])
        P.dma("sp", gq.ap[:, 0:16, :], c_q_g[j:j + 1, :].unsqueeze(1).to_broadcast([128, 16, 128]), writes=[gq])
        P.dma("sp", gq.ap[:, 16:20, :], c_k_g[j:j + 1, :].unsqueeze(1).to_broadcast([128, 4, 128]), writes=[gq])
        xb = [Buf(k.sb(es, "gq_x%d" % i, [128, 3072])[:]) for i in range(2)]
        sqb = Buf(k.sb(es, "gq_sq", [128, 2560])[:])
        ssum = Buf(k.sb(es, "gq_ss", [128, 20])[:])
        rp = [Buf(k.sb(es, "gq_rp%d" % i, [128, 2, 64])[:]) for i in range(2)]
        tA = Buf(k.sb(es, "gq_tA", [128, 20, 2, 32])[:])
        tB = Buf(k.sb(es, "gq_tB", [128, 20, 2, 32])[:])
        tC = Buf(k.sb(es, "gq_tC", [128, 20, 2, 32])[:])
        qT = [Buf(k.sb(es, "gq_qT%d" % i, [128, 20, 128], BF16)[:]) for i in range(2)]
        vbf = [Buf(k.sb(es, "gq_vb%d" % i, [128, 512], BF16)[:]) for i in range(2)]
        for t in tiles:
            x, r_, q_T, v_ = xb[t % 2], rp[t % 2], qT[t % 2], vbf[t % 2]
            P.dma("sp", x.ap, qkv_d[t * 128:(t + 1) * 128, :], writes=[x])
            P.op("act", lambda e: e.activation(out=sqb.ap, in_=x.ap[:, 0:2560], func=AF.Square), reads=[x], writes=[sqb])
            P.op("dve", lambda e: e.tensor_reduce(out=ssum.ap, in_=sqb.ap.rearrange("p (h d) -> p h d", d=128), axis=AX.X, op=ALU.add),
                 reads=[sqb], writes=[ssum])
            rsqrt_eps(P, ssum, ssum.ap, [ssum], 1.0 / 128.0)
            x3 = x.ap[:, 0:2560].rearrange("p (h d) -> p h d", d=128)
            P.op("dve", lambda e: e.tensor_tensor(out=x3, in0=x3, in1=ssum.ap.unsqueeze(2).to_broadcast([128, 20, 128]), op=ALU.mult),
                 reads=[x, ssum], writes=[x])
            P.op("pool", lambda e: e.tensor_tensor(out=x3, in0=x3, in1=gq.ap, op=ALU.mult), reads=[x, gq], writes=[x])
            if t < NTL:
                P.dma("act", r_.ap, rope_d[t * 128:(t + 1) * 128, :, :], writes=[r_])
                x5 = x.ap[:, 0:2560].rearrange("p (h a b c) -> p h a b c", a=2, b=2, c=32)
                x1 = x5[:, :, :, 0, :]
                x2 = x5[:, :, :, 1, :]
                cosb = r_.ap[:, 0, :].rearrange("p (a c) -> p a c", a=2).unsqueeze(1).to_broadcast([128, 20, 2, 32])
                sinb = r_.ap[:, 1, :].rearrange("p (a c) -> p a c", a=2).unsqueeze(1).to_broadcast([128, 20, 2, 32])
                P.op("dve", lambda e: e.tensor_tensor(out=tA.ap, in0=x1, in1=sinb, op=ALU.mult), reads=[x, r_], writes=[tA])
                P.op("pool", lambda e: e.tensor_tensor(out=tB.ap, in0=x2, in1=sinb, op=ALU.mult), reads=[x, r_], writes=[tB])
                P.op("dve", lambda e: e.tensor_tensor(out=tC.ap, in0=x2, in1=cosb, op=ALU.mult), reads=[x, r_], writes=[tC])
                P.op("dve", lambda e: e.tensor_tensor(out=x1, in0=x1, in1=cosb, op=ALU.mult), reads=[x, r_], writes=[x])
                P.op("dve", lambda e: e.tensor_tensor(out=x1, in0=x1, in1=tB.ap, op=ALU.subtract), reads=[x, tB], writes=[x])
                P.op("dve", lambda e: e.tensor_tensor(out=x2, in0=tA.ap, in1=tC.ap, op=ALU.add), reads=[tA, tC], writes=[x])
            transpose_to_T(S, x, 20, q_T, t)
            P.dma("sp", qkT_d[:, :, t * 128:(t + 1) * 128].rearrange("h d t -> d h t"), q_T.ap, reads=[q_T])
            P.op("pool", lambda e: e.tensor_copy(out=v_.ap, in_=x.ap[:, 2560:3072]), reads=[x], writes=[v_])
            P.dma("sp", v_d[t], v_.ap, reads=[v_])
    P.barrier()
    scale = 128.0 ** -0.5
    ps = S["ps"]
    with ExitStack() as es:
        kT = Buf(k.sb(es, "ga_kT", [128, T], BF16)[:])
        vv = Buf(k.sb(es, "ga_v", [128, NT, 128], BF16)[:])
        qTb = [Buf(k.sb(es, "ga_qT%d" % i, [128, T], BF16)[:]) for i in range(2)]
        ones = Buf(k.sb(es, "ga_ones", [128, 128], BF16)[:])
        P.op("pool", lambda e: e.memset(ones.ap, 1.0), writes=[ones])
        pT = [Buf(k.sb(es, "ga_pT%d" % i, [128, 1024], BF16)[:]) for i in range(3)]
        ps2 = S["ps2"]
        rec = [Buf(k.sb(es, "ga_rec%d" % i, [128, 512])[:]) for i in range(2)]
        oT = [Buf(k.sb(es, "ga_oT%d" % i, [128, 512], BF16)[:]) for i in range(2)]
        blocks = [(q0, 512, list(range(NT))) for q0 in range(0, SEQ, 512)]
        if need_ctx:
            blocks.append((SEQ, CTX, [32, 33]))
        bc = 0
        pc = 0
        for kh in range(4):
            P.dma("sp", kT.ap, qkT_d[16 + kh], writes=[kT])
            P.dma("act", vv.ap, v_d[:, :, kh * 128:(kh + 1) * 128].rearrange("t p d -> p t d"), writes=[vv])
            for hq in range(4):
                h = kh * 4 + hq
                q_ = qTb[h % 2]
                P.dma("sp", q_.ap, qkT_d[h], writes=[q_])
                for (q0, nq, kts) in blocks:
                    acc_o, acc_s = ps[4 + 2 * (bc % 2)], ps[5 + 2 * (bc % 2)]
                    r_, o_ = rec[bc % 2], oT[bc % 2]
                    bc += 1
                    npair = len(kts) // 2
                    sts = {}
                    pTs = {}

                    def emit_qk(pi):
                        nonlocal pc
                        st = ps2[pc % 2]
                        p_ = pT[pc % 3]
                        pc += 1
                        sts[pi], pTs[pi] = st, p_
                        for hf in range(2):
                            kt = kts[2 * pi + hf]
                            P.op("pe", lambda e: e.matmul(st.ap[:, hf * 512:hf * 512 + nq], lhsT=kT.ap[:, kt * 128:(kt + 1) * 128], rhs=q_.ap[:, q0:q0 + nq], start=True, stop=True),
                                 reads=[kT, q_], writes=[st])
                        P.op("act", lambda e: e.activation(out=p_.ap.rearrange("p (a n) -> p a n", a=2)[:, :, 0:nq], in_=st.ap.rearrange("p (a n) -> p a n", a=2)[:, :, 0:nq],
                                                           func=AF.Exp, scale=float(scale)), reads=[st], writes=[p_])
                    emit_qk(0)
                    for pi in range(npair):
                        if pi + 1 < npair:
                            emit_qk(pi + 1)
                        p_ = pTs[pi]
                        for hf in range(2):
                            kt = kts[2 * pi + hf]
                            first = (pi == 0 and hf == 0)
                            last = (pi == npair - 1 and hf == 1)
                            P.op("pe", lambda e: e.matmul(acc_o.ap[:, 0:nq], lhsT=vv.ap[:, kt, :], rhs=p_.ap[:, hf * 512:hf * 512 + nq], start=first, stop=last),
                                 reads=[vv, p_], writes=[acc_o])
                            P.op("pe", lambda e: e.matmul(acc_s.ap[:, 0:nq], lhsT=ones.ap, rhs=p_.ap[:, hf * 512:hf * 512 + nq], start=first, stop=last),
                                 reads=[ones, p_], writes=[acc_s])
                    P.op("dve", lambda e: e.reciprocal(out=r_.ap[:, 0:nq], in_=acc_s.ap[:, 0:nq]), reads=[acc_s], writes=[r_])
                    P.op("dve", lambda e: e.tensor_tensor(out=o_.ap[:, 0:nq], in0=acc_o.ap[:, 0:nq], in1=r_.ap[:, 0:nq], op=ALU.mult), reads=[acc_o, r_], writes=[o_])
                    t0 = q0 // 128
                    P.dma("sp", hT_d[t0:t0 + nq // 128, :, h, :].rearrange("t p q -> p t q"), o_.ap[:, 0:nq].rearrange("p (t q) -> p t q", q=128), reads=[o_])
    P.barrier()
    with ExitStack() as es:
        epi = make_store_epi(S, es, ymix, tag="gq4s")
        linear_tm(S, hT_d, KC, c_w_out[j], D, list(range(NT if need_ctx else NTL)), epi, tag="gq4")


def mix_mlstm(S, j, a_w_in, a_b_gate, a_norm_g, a_w_out, tri_d, need_ctx):
    k, P, hT_d, ymix, ident, ps = S["k"], S["P"], S["hT_d"], S["ymix"], S["ident"], S["ps"]
    tiles = list(range(NT))
    qkT_d = k.dscr("a_qkT%d" % j, [NT, 128, 16, 128], BF16)
    kt_d = k.dscr("a_kt%d" % j, [T, 1024], BF16)
    v_d = k.dscr("a_v%d" % j, [T, 2048], BF16)
    o_d = k.dscr("a_o%d" % j, [T, 2048])
    g_d = k.dscr("a_g%d" % j, [T, 16])
    hs_d = [k.dscr("a_hs%d_%d" % (j, d_), [T, 2048]) for d_ in range(2)]
    with ExitStack() as es:
        stg = [Buf(k.sb(es, "ma_fs%d" % i, [128, 512], BF16)[:]) for i in range(4)]
        cnt = [0]

        def epi_f(c, t0, ntl, pt):
            s = stg[cnt[0] % 4]
            cnt[0] += 1
            n = ntl * 128
            sc = 1.0 if c < 8 else 1.0 / 16.0
            P.op("act", lambda e: e.activation(out=s.ap[:, 0:n], in_=pt.ap[:, 0:n], func=AF.Copy, scale=float(sc)), reads=[pt], writes=[s])
            P.dma("sp", qkT_d[t0:t0 + ntl, :, c, :].rearrange("t p q -> p t q"), s.ap[:, 0:n].rearrange("p (t q) -> p t q", q=128), reads=[s])
        linear_fm(S, hT_d, KC, a_w_in[j][:, 0:2048], 2048, tiles, epi_f, tag="ma1")
    P.barrier()
    with ExitStack() as es:
        stb = [Buf(k.sb(es, "ma_sb%d" % i, [128, 512], BF16)[:]) for i in range(3)]
        stf = [Buf(k.sb(es, "ma_sf%d" % i, [128, 512])[:]) for i in range(3)]
        cnt = [0]

        def epi_t(t, n0, n, pt):
            i = cnt[0]
            cnt[0] += 1
            rows = slice(t * 128, (t + 1) * 128)
            if n0 < 1024:
                s = stb[i % 3]
                P.op("act", lambda e: e.activation(out=s.ap, in_=pt.ap, func=AF.Copy, scale=1.0 / 16.0), reads=[pt], writes=[s])
                P.dma("sp", kt_d[rows, n0:n0 + 512], s.ap, reads=[s])
            elif n0 < 3072:
                s = stb[i % 3]
                P.op("dve", lambda e: e.tensor_copy(out=s.ap, in_=pt.ap), reads=[pt], writes=[s])
                P.dma("sp", v_d[rows, n0 - 1024:n0 - 1024 + 512], s.ap, reads=[s])
            elif n0 < 5120:
                s = stf[i % 3]
                P.op("act", lambda e: e.activation(out=s.ap, in_=pt.ap, func=AF.Sigmoid), reads=[pt], writes=[s])
                P.dma("sp", o_d[rows, n0 - 3072:n0 - 3072 + 512], s.ap, reads=[s])
            else:
                s = stf[i % 3]
                P.op("dve", lambda e: e.tensor_copy(out=s.ap[:, 0:16], in_=pt.ap[:, 0:16]), reads=[pt], writes=[s])
                P.dma("sp", g_d[rows, :], s.ap[:, 0:16], reads=[s])
        linear_tm(S, hT_d, KC, a_w_in[j][:, 1024:A_IN], A_IN - 1024, tiles, epi_t, tag="ma2")
    P.barrier()
    with ExitStack() as es:
        tri = Buf(k.sb(es, "ma_tri", [128, 2, 128])[:])
        P.dma("sp", tri.ap, tri_d.rearrange("a s t -> s a t"), writes=[tri])
        onesf = Buf(k.sb(es, "ma_1f", [128, 128])[:])
        onesb = Buf(k.sb(es, "ma_1b", [128, 2], BF16)[:])
        P.op("pool", lambda e: e.memset(onesf.ap, 1.0), writes=[onesf])
        P.op("pool", lambda e: e.memset(onesb.ap, 1.0), writes=[onesb])
        G = Buf(k.sb(es, "ma_G", [128, NT, 16])[:])
        bg = Buf(k.sb(es, "ma_bg", [128, 16])[:])
        P.dma("sp", G.ap, g_d.rearrange("(c s) g -> s c g", s=128), writes=[G])
        P.dma("sp", bg.ap, a_b_gate[j:j + 1, :].to_broadcast([128, 16]), writes=[bg])
        P.op("dve", lambda e: e.tensor_tensor(out=G.ap, in0=G.ap, in1=bg.ap.unsqueeze(1).to_broadcast([128, NT, 16]), op=ALU.add), reads=[G, bg], writes=[G])
        P.op("act", lambda e: e.activation(out=G.ap, in_=G.ap, func=AF.Tanh, scale=1.0 / 15.0), reads=[G], writes=[G])
        P.op("dve", lambda e: e.tensor_scalar_mul(out=G.ap, in0=G.ap, scalar1=15.0), reads=[G], writes=[G])
        LI = Buf(k.sb(es, "ma_LI", [128, NT, 8])[:])
        LFN = Buf(k.sb(es, "ma_LFN", [128, NT, 8])[:])
        G4 = G.ap.rearrange("p c (a h) -> p c a h", a=4)
        LI4 = LI.ap.rearrange("p c (a h) -> p c a h", a=2)
        LF4 = LFN.ap.rearrange("p c (a h) -> p c a h", a=2)
        for d_ in range(2):
            P.op("dve", lambda e: e.tensor_copy(out=LI4[:, :, d_, :], in_=G4[:, :, 2 * d_, :]), reads=[G], writes=[LI])
            P.op("act", lambda e: e.activation(out=LF4[:, :, d_, :], in_=G4[:, :, 2 * d_ + 1, :], func=AF.Exp, scale=-1.0), reads=[G], writes=[LFN])
        P.op("act", lambda e: e.activation(out=LFN.ap, in_=LFN.ap, func=AF.Ln, bias=1.0), reads=[LFN], writes=[LFN])
        NB = Buf(k.sb(es, "ma_NB", [128, NT, 8])[:])
        GT = Buf(k.sb(es, "ma_GT", [128, NT, 8])[:])
        AA = Buf(k.sb(es, "ma_A", [128, NT, 8])[:])
        NB4 = NB.ap.rearrange("p c (a h) -> p c a h", a=2)
        pt = ps[0]
        for d_ in range(2):
            P.op("pe", lambda e: e.matmul(pt.ap[:, 0:NT * 4], lhsT=tri.ap[:, d_, :], rhs=LF4[:, :, d_, :], start=True, stop=True), reads=[tri, LFN], writes=[pt])
            P.op("dve", lambda e: e.tensor_copy(out=NB4[:, :, d_, :], in_=pt.ap[:, 0:NT * 4].rearrange("p (c h) -> p c h", h=4)), reads=[pt], writes=[NB])
        pt = ps[1]
        P.op("pe", lambda e: e.matmul(pt.ap[:, 0:NT * 8], lhsT=onesf.ap, rhs=LFN.ap, start=True, stop=True), reads=[onesf, LFN], writes=[pt])
        P.op("dve", lambda e: e.tensor_copy(out=GT.ap, in_=pt.ap[:, 0:NT * 8].rearrange("p (c h) -> p c h", h=8)), reads=[pt], writes=[GT])
        P.op("dve", lambda e: e.tensor_tensor(out=AA.ap, in0=LI.ap, in1=NB.ap, op=ALU.add), reads=[LI, NB], writes=[AA])
        AMX = Buf(k.sb(es, "ma_AMX", [128, NT * 8])[:])
        A2 = AA.ap.rearrange("p c h -> p (c h)")
        col = Buf(k.sb(es, "ma_col", [128, 1])[:])
        colb = Buf(k.sb(es, "ma_colb", [128, 128])[:])
        for (c0, n) in ((0, 128), (128, 128), (256, NT * 8 - 256)):
            pt = ps[2]
            P.op("pe", lambda e: e.transpose(out=pt.ap[0:n, 0:128], in_=A2[:, c0:c0 + n], identity=ident.ap), reads=[AA, ident], writes=[pt])
            P.op("dve", lambda e: e.reduce_max(out=col.ap[0:n, :], in_=pt.ap[0:n, 0:128], axis=AX.X), reads=[pt], writes=[col])
            P.op("dve", lambda e: e.tensor_copy(out=colb.ap[0:n, :], in_=col.ap[0:n, 0:1].to_broadcast([n, 128])), reads=[col], writes=[colb])
            pt2 = ps[3]
            P.op("pe", lambda e: e.matmul(pt2.ap[:, 0:n], lhsT=colb.ap[0:n, :], rhs=ident.ap[0:n, 0:n], start=True, stop=True), reads=[colb, ident], writes=[pt2])
            P.op("dve", lambda e: e.tensor_copy(out=AMX.ap[:, c0:c0 + n], in_=pt2.ap[:, 0:n]), reads=[pt2], writes=[AMX])
        AMX3 = AMX.ap.rearrange("p (c h) -> p c h", h=8)
        RR = Buf(k.sb(es, "ma_R", [128, NT, 8])[:])
        DM = Buf(k.sb(es, "ma_DM", [128, NT, 8])[:])
        mm = Buf(k.sb(es, "ma_m", [128, 8])[:])
        P.op("dve", lambda e: e.memset(mm.ap, 0.0), writes=[mm])
        order = [[32, 33] + list(range(32)), [33, 32] + list(range(31, -1, -1))]
        for i in range(NT):
            for d_ in range(2):
                c = order[d_][i]
                hs = slice(d_ * 4, d_ * 4 + 4)
                P.op("dve", lambda e: e.tensor_tensor(out=RR.ap[:, c, hs], in0=mm.ap[:, hs], in1=AMX3[:, c, hs], op=ALU.max), reads=[mm, AMX], writes=[RR])
                P.op("dve", lambda e: e.tensor_tensor(out=DM.ap[:, c, hs], in0=mm.ap[:, hs], in1=RR.ap[:, c, hs], op=ALU.subtract), reads=[mm, RR], writes=[DM])
                P.op("dve", lambda e: e.tensor_tensor(out=mm.ap[:, hs], in0=RR.ap[:, c, hs], in1=GT.ap[:, c, hs], op=ALU.subtract), reads=[RR, GT], writes=[mm])
        EE = Buf(k.sb(es, "ma_E", [128, NT, 8])[:])
        CL = Buf(k.sb(es, "ma_CL", [128, NT, 8])[:])
        P.op("dve", lambda e: e.tensor_tensor(out=EE.ap, in0=AA.ap, in1=RR.ap, op=ALU.subtract), reads=[AA, RR], writes=[EE])
        P.op("act", lambda e: e.activation(out=EE.ap, in_=EE.ap, func=AF.Exp), reads=[EE], writes=[EE])
        P.op("dve", lambda e: e.tensor_tensor(out=CL.ap, in0=NB.ap, in1=RR.ap, op=ALU.subtract), reads=[NB, RR], writes=[CL])
        P.op("act", lambda e: e.activation(out=CL.ap, in_=CL.ap, func=AF.Exp), reads=[CL], writes=[CL])
        P.op("act", lambda e: e.activation(out=DM.ap, in_=DM.ap, func=AF.Exp), reads=[DM], writes=[DM])
        C32 = [Buf(k.sb(es, "ma_C%d" % i, [128, 2, 512])[:]) for i in range(8)]
        N32 = [Buf(k.sb(es, "ma_N%d" % i, [128, 2])[:]) for i in range(8)]
        Cb = [Buf(k.sb(es, "ma_Cb%d" % i, [128, 2, 512], BF16)[:]) for i in range(8)]
        Nb = [Buf(k.sb(es, "ma_Nb%d" % i, [128, 2], BF16)[:]) for i in range(8)]
        for i in range(8):
            P.op("pool", lambda e: e.memset(C32[i].ap, 0.0), writes=[C32[i]])
            P.op("pool", lambda e: e.memset(N32[i].ap, 0.0), writes=[N32[i]])
        qk = [[Buf(k.sb(es, "ma_qk%d_%d" % (d_, i), [128, 16, 128], BF16)[:]) for i in range(2)] for d_ in range(2)]
        ktm = [[Buf(k.sb(es, "ma_kt%d_%d" % (d_, i), [128, 1024], BF16)[:]) for i in range(2)] for d_ in range(2)]
        vtm = [[Buf(k.sb(es, "ma_vt%d_%d" % (d_, i), [128, 2048], BF16)[:]) for i in range(2)] for d_ in range(2)]
        St = [Buf(k.sb(es, "ma_St%d" % i, [128, 128], BF16)[:]) for i in range(4)]
        ktl = [Buf(k.sb(es, "ma_ktl%d" % i, [128, 256], BF16)[:]) for i in range(4)]
        dn = [Buf(k.sb(es, "ma_dn%d" % i, [128, 1])[:]) for i in range(4)]
        ho = [Buf(k.sb(es, "ma_ho%d" % i, [128, 512])[:]) for i in range(4)]
        it = 0
        for i in range(NT):
            for d_ in range(2):
                c = order[d_][i]
                rows = slice(c * 128, (c + 1) * 128)
                qk_, kt_, vt_ = qk[d_][i % 2], ktm[d_][i % 2], vtm[d_][i % 2]
                P.dma("sp", qk_.ap, qkT_d[c], writes=[qk_])
                P.dma("act", kt_.ap, kt_d[rows, :], writes=[kt_])
                P.dma("sp", vt_.ap, v_d[rows, :], writes=[vt_])
                for hh in range(4):
                    ch = d_ * 4 + hh
                    e_ = EE.ap[:, c, ch:ch + 1]
                    cd = DM.ap[:, c, ch:ch + 1]
                    cl = CL.ap[:, c, ch:ch + 1]
                    s_t, k_l, d_n, h_o = St[it % 4], ktl[it % 4], dn[it % 4], ho[it % 4]
                    p_st, p_num = ps[it % 2], ps[2 + it % 2]
                    p_c = [ps[4], ps[5]]
                    p_den, p_nu = ps[6], ps[7]
                    it += 1
                    vh = vt_.ap[:, hh * 512:(hh + 1) * 512]
                    P.op("dve", lambda e: e.tensor_scalar(out=Cb[ch].ap, in0=C32[ch].ap, scalar1=cd, scalar2=None, op0=ALU.mult), reads=[C32[ch], DM], writes=[Cb[ch]])
                    P.op("dve", lambda e: e.tensor_scalar(out=Nb[ch].ap, in0=N32[ch].ap, scalar1=cd, scalar2=None, op0=ALU.mult), reads=[N32[ch], DM], writes=[Nb[ch]])
                    for dc in range(2):
                        P.op("pe", lambda e: e.matmul(p_st.ap[:, 0:128], lhsT=qk_.ap[:, 8 + hh * 2 + dc, :], rhs=qk_.ap[:, hh * 2 + dc, :], start=(dc == 0), stop=(dc == 1)),
                             reads=[qk_], writes=[p_st])
                    P.op("dve", lambda e: e.scalar_tensor_tensor(out=s_t.ap, in0=p_st.ap[:, 0:128], scalar=e_, in1=tri.ap[:, d_, :], op0=ALU.mult, op1=ALU.mult),
                         reads=[p_st, EE, tri], writes=[s_t])
                    for dc in range(2):
                        P.op("pe", lambda e: e.matmul(p_num.ap, lhsT=qk_.ap[:, hh * 2 + dc, :], rhs=Cb[ch].ap[:, dc, :], start=(dc == 0), stop=False),
                             reads=[qk_, Cb[ch]], writes=[p_num])
                    P.op("pe", lambda e: e.matmul(p_num.ap, lhsT=s_t.ap, rhs=vh, start=False, stop=True), reads=[s_t, vt_], writes=[p_num])
                    for dc in range(2):
                        P.op("pe", lambda e: e.matmul(p_den.ap[:, 0:1], lhsT=qk_.ap[:, hh * 2 + dc, :], rhs=Nb[ch].ap[:, dc:dc + 1], start=(dc == 0), stop=False),
                             reads=[qk_, Nb[ch]], writes=[p_den])
                    P.op("pe", lambda e: e.matmul(p_den.ap[:, 0:1], lhsT=s_t.ap, rhs=onesb.ap[:, 0:1], start=False, stop=True), reads=[s_t, onesb], writes=[p_den])
                    P.op("act", lambda e: e.activation(out=d_n.ap, in_=p_den.ap[:, 0:1], func=AF.Abs), reads=[p_den], writes=[d_n])
                    P.op("dve", lambda e: e.tensor_tensor(out=d_n.ap, in0=d_n.ap, in1=cl, op=ALU.max), reads=[d_n, CL], writes=[d_n])
                    P.op("dve", lambda e: e.reciprocal(out=d_n.ap, in_=d_n.ap), reads=[d_n], writes=[d_n])
                    P.op("act", lambda e: e.activation(out=h_o.ap, in_=p_num.ap, func=AF.Copy, scale=d_n.ap[:, 0:1]), reads=[p_num, d_n], writes=[h_o])
                    P.dma("sp", hs_d[d_][rows, hh * 512:(hh + 1) * 512], h_o.ap, reads=[h_o])
                    P.op("pool", lambda e: e.tensor_scalar(out=k_l.ap, in0=kt_.ap[:, hh * 256:(hh + 1) * 256], scalar1=e_, scalar2=None, op0=ALU.mult),
                         reads=[kt_, EE], writes=[k_l])
                    for dc in range(2):
                        P.op("pe", lambda e: e.matmul(p_c[dc].ap, lhsT=k_l.ap[:, dc * 128:(dc + 1) * 128], rhs=vh, start=True, stop=True), reads=[k_l, vt_], writes=[p_c[dc]])
                        P.op("pe", lambda e: e.matmul(p_nu.ap[:, dc:dc + 1], lhsT=k_l.ap[:, dc * 128:(dc + 1) * 128], rhs=onesb.ap[:, 0:1], start=True, stop=True),
                             reads=[k_l, onesb], writes=[p_nu])
                    for dc in range(2):
                        P.op("dve", lambda e: e.scalar_tensor_tensor(out=C32[ch].ap[:, dc, :], in0=C32[ch].ap[:, dc, :], scalar=cd, in1=p_c[dc].ap, op0=ALU.mult, op1=ALU.add),
                             reads=[C32[ch], DM, p_c[dc]], writes=[C32[ch]])
                    P.op("dve", lambda e: e.scalar_tensor_tensor(out=N32[ch].ap, in0=N32[ch].ap, scalar=cd, in1=p_nu.ap[:, 0:2], op0=ALU.mult, op1=ALU.add),
                         reads=[N32[ch], DM, p_nu], writes=[N32[ch]])
    P.barrier()
    out_tiles = list(range(NT if need_ctx else NTL))
    with ExitStack() as es:
        ng = Buf(k.sb(es, "mo_ng", [128, D])[:])
        P.dma("sp", ng.ap, a_norm_g[j:j + 1, :].to_broadcast([128, D]), writes=[ng])
        hb = [[Buf(k.sb(es, "mo_h%d_%d" % (d_, i), [128, D])[:]) for i in range(2)] for d_ in range(2)]
        ob = [Buf(k.sb(es, "mo_o%d" % i, [128, D])[:]) for i in range(2)]
        sq = Buf(k.sb(es, "mo_sq", [128, D])[:])
        ss = Buf(k.sb(es, "mo_ss", [128, 4])[:])
        hT = [Buf(k.sb(es, "mo_hT%d" % i, [128, KC, 128], BF16)[:]) for i in range(2)]
        for t in out_tiles:
            rows = slice(t * 128, (t + 1) * 128)
            h0, h1, o_, h_T = hb[0][t % 2], hb[1][t % 2], ob[t % 2], hT[t % 2]
            P.dma("sp", h0.ap, hs_d[0][rows, :], writes=[h0])
            P.dma("act", h1.ap, hs_d[1][rows, :], writes=[h1])
            P.dma("sp", o_.ap, o_d[rows, :], writes=[o_])
            P.op("pool", lambda e: e.tensor_tensor(out=h0.ap, in0=h0.ap, in1=h1.ap, op=ALU.add), reads=[h0, h1], writes=[h0])
            P.op("act", lambda e: e.activation(out=sq.ap, in_=h0.ap, func=AF.Square), reads=[h0], writes=[sq])
            P.op("dve", lambda e: e.tensor_reduce(out=ss.ap, in_=sq.ap.rearrange("p (h d) -> p h d", d=512), axis=AX.X, op=ALU.add), reads=[sq], writes=[ss])
            rsqrt_eps(P, ss, ss.ap, [ss], 1.0 / 512.0)
            h3 = h0.ap.rearrange("p (h d) -> p h d", d=512)
            P.op("dve", lambda e: e.tensor_tensor(out=h3, in0=h3, in1=ss.ap.unsqueeze(2).to_broadcast([128, 4, 512]), op=ALU.mult), reads=[h0, ss], writes=[h0])
            P.op("pool", lambda e: e.tensor_tensor(out=o_.ap, in0=o_.ap, in1=ng.ap, op=ALU.mult), reads=[o_, ng], writes=[o_])
            P.op("dve", lambda e: e.tensor_tensor(out=h0.ap, in0=h0.ap, in1=o_.ap, op=ALU.mult), reads=[h0, o_], writes=[h0])
            transpose_to_T(S, h0, KC, h_T, t)
            P.dma("sp", hT_d[t], h_T.ap, reads=[h_T])
    P.barrier()
    with ExitStack() as es:
        epi = make_store_epi(S, es, ymix, tag="mo5s")
        linear_tm(S, hT_d, KC, a_w_out[j], D, out_tiles, epi, tag="mo5")


def top16(P, src, work, vals, idxs, reads_extra=()):
    sv, vv, iv, wv = src, vals, idxs, work
    P.op("dve", lambda e: e.max(out=vv.ap[:, 0:8], in_=sv.ap), reads=[sv], writes=[vv])
    P.op("dve", lambda e: e.max_index(out=iv.ap[:, 0:8], in_max=vv.ap[:, 0:8], in_values=sv.ap), reads=[sv, vv], writes=[iv])
    P.op("dve", lambda e: e.match_replace(out=wv.ap, in_to_replace=vv.ap[:, 0:8], in_values=sv.ap, imm_value=NEG), reads=[sv, vv], writes=[wv])
    P.op("dve", lambda e: e.max(out=vv.ap[:, 8:16], in_=wv.ap), reads=[wv], writes=[vv])
    P.op("dve", lambda e: e.max_index(out=iv.ap[:, 8:16], in_max=vv.ap[:, 8:16], in_values=wv.ap), reads=[wv, vv], writes=[iv])


def phase_peer(S, li, p_w_q, p_k1, p_k2, p_u, p_v, iota_d, nt, enable=True):
    k, P, hT_d, h_d, ymix, ident, ps = S["k"], S["P"], S["hT_d"], S["h_d"], S["ymix"], S["ident"], S["ps"]
    tiles = list(range(nt))
    qpT_d = k.dscr("p_qT%d" % li, [16, 128, T])
    with ExitStack() as es:
        stg = [Buf(k.sb(es, "pq_s%d" % i, [128, 512])[:]) for i in range(4)]
        cnt = [0]

        def epi_f(c, t0, ntl, pt):
            s = stg[cnt[0] % 4]
            cnt[0] += 1
            n = ntl * 128
            P.op("act" if cnt[0] % 2 else "dve",
                 (lambda e: e.activation(out=s.ap[:, 0:n], in_=pt.ap[:, 0:n], func=AF.Copy)) if cnt[0] % 2 else (lambda e: e.tensor_copy(out=s.ap[:, 0:n], in_=pt.ap[:, 0:n])),
                 reads=[pt], writes=[s])
            P.dma("sp", qpT_d[c, :, t0 * 128:t0 * 128 + n], s.ap[:, 0:n], reads=[s])
        linear_fm(S, hT_d, KC, p_w_q[li], D, tiles, epi_f, tag="pq")
    P.barrier()
    with ExitStack() as es:
        kraw = Buf(k.sb(es, "pp_kraw", [128, 2, 128])[:])
        kT = Buf(k.sb(es, "pp_kT", [128, 2, 128])[:])
        P.dma("sp", kraw.ap[:, 0, :], p_k1[li], writes=[kraw])
        P.dma("sp", kraw.ap[:, 1, :], p_k2[li], writes=[kraw])
        for hf in range(2):
            pt = next_ps(S)
            P.op("pe", lambda e: e.transpose(out=pt.ap[:, 0:128], in_=kraw.ap[:, hf, :], identity=ident.ap), reads=[kraw, ident], writes=[pt])
            P.op("dve", lambda e: e.tensor_copy(out=kT.ap[:, hf, :], in_=pt.ap[:, 0:128]), reads=[pt], writes=[kT])
        io16 = Buf(k.sb(es, "pp_io", [128, 16])[:])
        P.dma("sp", io16.ap, iota_d[0:1, 0:16].to_broadcast([128, 16]), writes=[io16])
        qT = [Buf(k.sb(es, "pp_qT%d" % i, [128, 16, 128])[:]) for i in range(2)]
        sc = Buf(k.sb(es, "pp_sc", [128, 16, 128])[:])
        wk = Buf(k.sb(es, "pp_wk", [128, 256])[:])
        V12 = Buf(k.sb(es, "pp_V12", [128, 16, 16])[:])
        I12 = Buf(k.sb(es, "pp_I12", [128, 16, 16], U32)[:])
        I12f = Buf(k.sb(es, "pp_I12f", [128, 16, 16])[:])
        cand = Buf(k.sb(es, "pp_cand", [128, 8, 256])[:])
        SC = Buf(k.sb(es, "pp_SC", [128, 8, 16])[:])
        CI = Buf(k.sb(es, "pp_CI", [128, 8, 16], U32)[:])
        IA = Buf(k.sb(es, "pp_IA", [128, 8, 16], U32)[:])
        IB = Buf(k.sb(es, "pp_IB", [128, 8, 16], U32)[:])
        IAf = Buf(k.sb(es, "pp_IAf", [128, 8, 16])[:])
        IBf = Buf(k.sb(es, "pp_IBf", [128, 8, 16])[:])
        oh = Buf(k.sb(es, "pp_oh", [128, 8, 16, 16])[:])
        EI = Buf(k.sb(es, "pp_EI", [128, 8, 16])[:])
        EJ = Buf(k.sb(es, "pp_EJ", [128, 8, 16])[:])
        gate = Buf(k.sb(es, "pp_gate", [128, 8, 16])[:])
        gs = Buf(k.sb(es, "pp_gs", [128, 8])[:])
        if PEER_DENSE:
            Gd = S["Gd"]
            io128 = Buf(k.sb(es, "pp_io128", [128, 128])[:])
            P.dma("sp", io128.ap, iota_d[0:1, :].to_broadcast([128, 128]), writes=[io128])
            TT = Buf(k.sb(es, "pp_TT", [128, 3, 128])[:])
            TTb = Buf(k.sb(es, "pp_TTb", [128, 2, 128], BF16)[:])
            io128b = Buf(k.sb(es, "pp_io128b", [128, 128], BF16)[:])
            P.op("dve", lambda e: e.tensor_copy(out=io128b.ap, in_=io128.ap), reads=[io128], writes=[io128b])
            OH1 = [Buf(k.sb(es, "pp_OH1%d" % i, [128, 64, 128], BF16)[:]) for i in range(2)]
            OH2 = [Buf(k.sb(es, "pp_OH2%d" % i, [128, 64, 128], BF16)[:]) for i in range(2)]
            G_sb = Buf(k.sb(es, "pp_Gsb", [128, 128, 128], BF16)[:])
            G_sb2 = Buf(G_sb.ap)
            hb = acc = EX = [None, None]
        else:
            hb = [Buf(k.sb(es, "pp_h%d" % i, [128, D])[:]) for i in range(2)]
            EX = [Buf(k.sb(es, "pp_EX%d" % i, [128, 128], U32)[:]) for i in range(2)]
            av = Buf(k.sb(es, "pp_a", [128, 128])[:])
            hid = Buf(k.sb(es, "pp_hid", [128, 128])[:])
            junk = Buf(k.sb(es, "pp_junk", [128, D])[:])
            acc = [Buf(k.sb(es, "pp_acc%d" % i, [128, D])[:]) for i in range(2)]
            NG = 6
            gb = [Buf(k.sb(es, "pp_g%d" % i, [128, D])[:]) for i in range(NG)]
        gi = 0
        sc2 = [sc, Buf(k.sb(es, "pp_scb", [128, 16, 128])[:])]

        def emit_scores(t):
            q_ = qT[t % 2]
            sc_ = sc2[t % 2]
            P.dma("sp", q_.ap, qpT_d[:, :, t * 128:(t + 1) * 128].rearrange("c d t -> d c t"), writes=[q_])
            for g in range(4):
                pt = ps[5 + (t * 4 + g) % 3]
                for q4 in range(4):
                    c = g * 4 + q4
                    P.op("pe", lambda e: e.matmul(pt.ap[:, q4 * 128:(q4 + 1) * 128], lhsT=q_.ap[:, c, :], rhs=kT.ap[:, c % 2, :], start=True, stop=True),
                         reads=[q_, kT], writes=[pt])
                P.op("act", lambda e: e.activation(out=sc_.ap[:, g * 4:(g + 1) * 4, :], in_=pt.ap.rearrange("p (a b) -> p a b", a=4), func=AF.Copy), reads=[pt], writes=[sc_])
        emit_scores(tiles[0])
        for ti, t in enumerate(tiles):
            q_, h_, ex_, acc_ = qT[t % 2], hb[t % 2], EX[t % 2], acc[t % 2]
            if ti + 1 < len(tiles):
                emit_scores(tiles[ti + 1])
            sc = sc2[t % 2]
            if not PEER_DENSE:
                P.dma("act", h_.ap, h_d[t * 128:(t + 1) * 128, :], writes=[h_])
            for c in range(16):
                sv = Buf(sc.ap[:, c, :]); sv.w = sc.w
                wv = Buf(wk.ap[:, 0:128]); wv.w = wk.w; wv.r = wk.r
                vv = Buf(V12.ap[:, c, :]); vv.w = V12.w; vv.r = V12.r
                iv = Buf(I12.ap[:, c, :]); iv.w = I12.w; iv.r = I12.r
                top16(P, sv, wv, vv, iv)
                wk.w, V12.w, I12.w = wv.w, vv.w, iv.w
                wk.r, V12.r, I12.r = wv.r, vv.r, iv.r
                sc.r.update(sv.r)
            V4 = V12.ap.rearrange("p (h f) a -> p h f a", f=2)
            P.op("dve", lambda e: e.tensor_tensor(out=cand.ap.rearrange("p h (a b) -> p h a b", b=16),
                                                  in0=V4[:, :, 0, :].unsqueeze(3).to_broadcast([128, 8, 16, 16]),
                                                  in1=V4[:, :, 1, :].unsqueeze(2).to_broadcast([128, 8, 16, 16]), op=ALU.add), reads=[V12], writes=[cand])
            for hh in range(8):
                sv = Buf(cand.ap[:, hh, :]); sv.w = cand.w
                wv = Buf(wk.ap[:, 0:256]); wv.w = wk.w; wv.r = wk.r
                vv = Buf(SC.ap[:, hh, :]); vv.w = SC.w; vv.r = SC.r
                iv = Buf(CI.ap[:, hh, :]); iv.w = CI.w; iv.r = CI.r
                top16(P, sv, wv, vv, iv)
                wk.w, SC.w, CI.w = wv.w, vv.w, iv.w
                wk.r, SC.r, CI.r = wv.r, vv.r, iv.r
                cand.r.update(sv.r)
            P.op("dve", lambda e: e.tensor_tensor(out=gate.ap, in0=SC.ap, in1=SC.ap[:, :, 0:1].to_broadcast([128, 8, 16]), op=ALU.subtract), reads=[SC], writes=[gate])
            P.op("act", lambda e: e.activation(out=gate.ap, in_=gate.ap, func=AF.Exp), reads=[gate], writes=[gate])
            P.op("dve", lambda e: e.tensor_reduce(out=gs.ap, in_=gate.ap, axis=AX.X, op=ALU.add), reads=[gate], writes=[gs])
            P.op("dve", lambda e: e.reciprocal(out=gs.ap, in_=gs.ap), reads=[gs], writes=[gs])
            P.op("dve", lambda e: e.tensor_tensor(out=gate.ap, in0=gate.ap, in1=gs.ap.unsqueeze(2).to_broadcast([128, 8, 16]), op=ALU.mult), reads=[gate, gs], writes=[gate])
            P.op("dve", lambda e: e.tensor_single_scalar(out=IA.ap, in_=CI.ap, scalar=4, op=ALU.logical_shift_right), reads=[CI], writes=[IA])
            P.op("dve", lambda e: e.tensor_single_scalar(out=IB.ap, in_=CI.ap, scalar=15, op=ALU.bitwise_and), reads=[CI], writes=[IB])
            P.op("dve", lambda e: e.tensor_copy(out=IAf.ap, in_=IA.ap), reads=[IA], writes=[IAf])
            P.op("dve", lambda e: e.tensor_copy(out=IBf.ap, in_=IB.ap), reads=[IB], writes=[IBf])
            P.op("dve", lambda e: e.tensor_copy(out=I12f.ap, in_=I12.ap), reads=[I12], writes=[I12f])
            I4 = I12f.ap.rearrange("p (h f) a -> p h f a", f=2)
            io4 = io16.ap.unsqueeze(1).unsqueeze(1).to_broadcast([128, 8, 16, 16])
            for (src, f, dst) in ((IAf, 0, EI), (IBf, 1, EJ)):
                P.op("dve", lambda e: e.tensor_tensor(out=oh.ap, in0=src.ap.unsqueeze(3).to_broadcast([128, 8, 16, 16]), in1=io4, op=ALU.is_equal), reads=[src, io16], writes=[oh])
                P.op("dve", lambda e: e.tensor_tensor(out=oh.ap, in0=oh.ap, in1=I4[:, :, f, :].unsqueeze(2).to_broadcast([128, 8, 16, 16]), op=ALU.mult), reads=[oh, I12f], writes=[oh])
                P.op("dve", lambda e: e.tensor_reduce(out=dst.ap, in_=oh.ap, axis=AX.X, op=ALU.add), reads=[oh], writes=[dst])
            if PEER_DENSE:
                ptT = ps[4]
                for q3, src in enumerate((EI, EJ, gate)):
                    P.op("pe", lambda e: e.transpose(out=ptT.ap[:, q3 * 128:(q3 + 1) * 128], in_=src.ap.rearrange("p h a -> p (h a)"), identity=ident.ap),
                         reads=[src, ident], writes=[ptT])
                P.op("act", lambda e: e.activation(out=TT.ap.rearrange("p a t -> p (a t)"), in_=ptT.ap[:, 0:384], func=AF.Copy), reads=[ptT], writes=[TT])
                P.op("act", lambda e: e.activation(out=TTb.ap.rearrange("p a t -> p (a t)"), in_=ptT.ap[:, 0:256], func=AF.Copy), reads=[ptT], writes=[TTb])
                iob = io128b.ap.unsqueeze(1).to_broadcast([128, 64, 128])
                for hf in range(2):
                    o1, o2 = OH1[hf], OH2[hf]
                    tsl = slice(hf * 64, (hf + 1) * 64)
                    P.op("dve", lambda e: e.tensor_tensor(out=o1.ap, in0=iob, in1=TTb.ap[:, 0, tsl].unsqueeze(2).to_broadcast([128, 64, 128]), op=ALU.is_equal),
                         reads=[io128b, TTb], writes=[o1])
                    P.op("pool", lambda e: e.tensor_tensor(out=o1.ap, in0=o1.ap, in1=TT.ap[:, 2, tsl].unsqueeze(2).to_broadcast([128, 64, 128]), op=ALU.mult),
                         reads=[o1, TT], writes=[o1])
                    P.op("dve", lambda e: e.tensor_tensor(out=o2.ap, in0=iob, in1=TTb.ap[:, 1, tsl].unsqueeze(2).to_broadcast([128, 64, 128]), op=ALU.is_equal),
                         reads=[io128b, TTb], writes=[o2])
                for t4 in range(32):
                    pt = ps[t4 % 4]
                    hf = t4 // 16
                    for q in range(4):
                        tok = (t4 % 16) * 4 + q
                        P.op("pe", lambda e: e.matmul(pt.ap[:, q * 128:(q + 1) * 128], lhsT=OH2[hf].ap[:, tok, :], rhs=OH1[hf].ap[:, tok, :], start=True, stop=True),
                             reads=[OH1[hf], OH2[hf]], writes=[pt])
                    P.op("act", lambda e: e.activation(out=G_sb.ap[:, :, t4 * 4:(t4 + 1) * 4], in_=pt.ap.rearrange("p (q i) -> p i q", q=4), func=AF.Copy), reads=[pt], writes=[G_sb])
                for cg in range(8):
                    P.dma("sp" if cg % 2 == 0 else "act", Gd[cg * 16:(cg + 1) * 16, :, t * 128:(t + 1) * 128].rearrange("c p t -> p c t"), G_sb.ap[:, cg * 16:(cg + 1) * 16, :], reads=[G_sb])
                continue
            P.op("dve", lambda e: e.scalar_tensor_tensor(out=EI.ap, in0=EI.ap, scalar=128.0, in1=EJ.ap, op0=ALU.mult, op1=ALU.add), reads=[EI, EJ], writes=[EI])
            if li > 0:
                P.op("dve", lambda e: e.tensor_scalar_add(out=EI.ap, in0=EI.ap, scalar1=float(li * 16384)), reads=[EI], writes=[EI])
            P.op("dve", lambda e: e.tensor_copy(out=ex_.ap, in_=EI.ap.rearrange("p h a -> p (h a)")), reads=[EI], writes=[ex_])
            for s in range(128):
                g_ = gb[gi % NG]
                gi += 1
                P.dma("pool", g_.ap, p_u.rearrange("l e d -> (l e) d"), reads=[ex_], writes=[g_], indirect=bass.IndirectOffsetOnAxis(ap=ex_.ap[:, s:s + 1], axis=0))
                P.op("dve", lambda e: e.scalar_tensor_tensor(out=junk.ap, in0=h_.ap, scalar=1.0, in1=g_.ap, op0=ALU.mult, op1=ALU.mult, accum_out=av.ap[:, s:s + 1]),
                     reads=[h_, g_], writes=[junk, av])
            P.op("act", lambda e: e.activation(out=hid.ap, in_=av.ap, func=AF.Gelu_apprx_tanh), reads=[av], writes=[hid])
            P.op("dve", lambda e: e.tensor_tensor(out=hid.ap, in0=hid.ap, in1=gate.ap.rearrange("p h a -> p (h a)"), op=ALU.mult), reads=[hid, gate], writes=[hid])
            for s in range(128):
                g_ = gb[gi % NG]
                gi += 1
                P.dma("pool", g_.ap, p_v.rearrange("l e d -> (l e) d"), reads=[ex_], writes=[g_], indirect=bass.IndirectOffsetOnAxis(ap=ex_.ap[:, s:s + 1], axis=0))
                if s == 0:
                    P.op("dve", lambda e: e.tensor_scalar(out=acc_.ap, in0=g_.ap, scalar1=hid.ap[:, 0:1], scalar2=None, op0=ALU.mult), reads=[g_, hid], writes=[acc_])
                else:
                    P.op("dve", lambda e: e.scalar_tensor_tensor(out=acc_.ap, in0=g_.ap, scalar=hid.ap[:, s:s + 1], in1=acc_.ap, op0=ALU.mult, op1=ALU.add),
                         reads=[g_, hid, acc_], writes=[acc_])
            P.dma("sp", ymix[t * 128:(t + 1) * 128, :], acc_.ap, reads=[acc_])
    if PEER_DENSE:
        P.barrier()
        peer_sweep(S, li, p_u, p_v, nt)


def peer_sweep(S, li, p_u, p_v, nt):
    k, P, hT_d, ymix, identb, ps, psb, Gd = S["k"], S["P"], S["hT_d"], S["ymix"], S["identb"], S["ps"], S["psb"], S["Gd"]
    SCH = 4
    passes = [list(range(i, i + 8)) for i in range(0, NTL, 8)]
    if nt == NT:
        passes[0].append(32)
        passes[1].append(33)
    with ExitStack() as es:
        hTb = [Buf(k.sb(es, "ps_hT%d" % i, [128, KC, 512], BF16)[:]) for i in range(2)]
        hTb.append(Buf(k.sb(es, "ps_hT2", [128, KC, 128], BF16)[:]))
        f_sb = [Buf(k.sb(es, "ps_f%d" % i, [128, D])[:]) for i in range(9)]
        Ubf = [Buf(k.sb(es, "ps_U%d" % i, [128, D], BF16)[:]) for i in range(2)]
        UT = [Buf(k.sb(es, "ps_UT%d" % i, [128, KC, 128], BF16)[:]) for i in range(2 * SCH)]
        Vbf = [Buf(k.sb(es, "ps_V%d" % i, [128, D], BF16)[:]) for i in range(2 * SCH)]
        hid = [Buf(k.sb(es, "ps_hid%d" % i, [128, 512], BF16)[:]) for i in range(2 * SCH)]
        Gc = [Buf(k.sb(es, "ps_G%d" % i, [128, 512], BF16)[:]) for i in range(4)]
        ui = 0
        sci = 0
        gci = 0
        api = 0
        fpi = 0
        evi = 0
        for tl in passes:
            groups = [tl[i:i + 4] for i in range(0, len(tl), 4)]
            for gi_, grp in enumerate(groups):
                for i, t in enumerate(grp):
                    P.dma("sp", hTb[gi_].ap[:, :, i * 128:(i + 1) * 128], hT_d[t], writes=[hTb[gi_]])
            for sc in range(128 // SCH):
                base = (sci % 2) * SCH
                sci += 1
                for c8 in range(SCH):
                    c = sc * SCH + c8
                    u_, ut_, v_ = Ubf[ui % 2], UT[base + c8], Vbf[base + c8]
                    ui += 1
                    P.dma("pool", u_.ap, p_u[li, c * 128:(c + 1) * 128, :], writes=[u_])
                    P.dma("pool", v_.ap, p_v[li, c * 128:(c + 1) * 128, :], writes=[v_])
                    for kc in range(KC):
                        pb = psb[kc // 8]
                        P.op("pe", lambda e: e.transpose(out=pb.ap[:, (kc % 8) * 128:(kc % 8 + 1) * 128], in_=u_.ap[:, kc * 128:(kc + 1) * 128], identity=identb.ap),
                             reads=[u_, identb], writes=[pb])
                    P.op("act", lambda e: e.activation(out=ut_.ap[:, 0:8, :], in_=psb[0].ap.rearrange("p (q t) -> p q t", q=8), func=AF.Copy), reads=[psb[0]], writes=[ut_])
                    P.op("dve", lambda e: e.tensor_copy(out=ut_.ap[:, 8:16, :], in_=psb[1].ap.rearrange("p (q t) -> p q t", q=8)), reads=[psb[1]], writes=[ut_])
                for gi_, grp in enumerate(groups):
                    ng = len(grp) * 128
                    tok0 = grp[0] * 128
                    hset = (gci % 2) * SCH
                    gci += 1
                    for c8 in range(SCH):
                        c = sc * SCH + c8
                        g_ = Gc[api % 4]
                        a_ps = ps[api % 2]
                        api += 1
                        h_ = hid[hset + c8]
                        P.dma("sp", g_.ap[:, 0:ng], Gd[c, :, tok0:tok0 + ng], writes=[g_])
                        for kc in range(KC):
                            P.op("pe", lambda e: e.matmul(a_ps.ap[:, 0:ng], lhsT=UT[base + c8].ap[:, kc, :], rhs=hTb[gi_].ap[:, kc, 0:ng], start=(kc == 0), stop=(kc == KC - 1)),
                                 reads=[UT[base + c8], hTb[gi_]], writes=[a_ps])
                        P.op("act", lambda e: e.activation(out=h_.ap[:, 0:ng], in_=a_ps.ap[:, 0:ng], func=AF.Gelu_apprx_tanh), reads=[a_ps], writes=[h_])
                        P.op("pool", lambda e: e.tensor_tensor(out=h_.ap[:, 0:ng], in0=h_.ap[:, 0:ng], in1=g_.ap[:, 0:ng], op=ALU.mult), reads=[h_, g_], writes=[h_])
                    for i, t in enumerate(grp):
                        fb = f_sb[gi_ * 4 + i]
                        for dch in range(4):
                            f_ps = ps[2 + fpi % 4]
                            fpi += 1
                            for c8 in range(SCH):
                                P.op("pe", lambda e: e.matmul(f_ps.ap, lhsT=hid[hset + c8].ap[:, i * 128:(i + 1) * 128], rhs=Vbf[base + c8].ap[:, dch * 512:(dch + 1) * 512],
                                                              start=(c8 == 0), stop=(c8 == SCH - 1)), reads=[hid[hset + c8], Vbf[base + c8]], writes=[f_ps])
                            dst = fb.ap[:, dch * 512:(dch + 1) * 512]
                            if sc == 0:
                                P.op("act", lambda e: e.activation(out=dst, in_=f_ps.ap, func=AF.Copy), reads=[f_ps], writes=[fb])
                            else:
                                P.op("dve", lambda e: e.tensor_tensor(out=dst, in0=dst, in1=f_ps.ap, op=ALU.add), reads=[fb, f_ps], writes=[fb])
            for i, t in enumerate(tl):
                P.dma("sp", ymix[t * 128:(t + 1) * 128, :], f_sb[i].ap, reads=[f_sb[i]])


_CACHE = {}


def host_consts():
    ident = np.eye(128, dtype=np.float32)
    s = np.arange(128)
    tri = np.stack([(s[:, None] <= s[None, :]), (s[:, None] >= s[None, :])]).astype(np.float32)
    t = np.arange(SEQ)
    rows = (t // 64).astype(np.float32)
    cols = (t % 64).astype(np.float32)
    freqs = (10000.0 ** (-np.arange(32, dtype=np.float32) / np.float32(32))).astype(np.float32)
    ang = np.concatenate([rows[:, None] * freqs[None, :], cols[:, None] * freqs[None, :]], axis=1).astype(np.float32)
    rope = np.stack([np.cos(ang), np.sin(ang)], axis=1).astype(np.float32)
    iota16 = np.arange(128, dtype=np.float32)[None, :]
    return dict(ident=ident, tri=tri, rope=rope, iota16=iota16)


def make_in_maps(inputs):
    consts = host_consts()
    shared = {n: np.ascontiguousarray(inputs[n]) for n in (
        "w_mod", "b_mod", "ln_g", "ln_b", "a_w_in", "a_b_gate", "a_norm_g", "a_w_out", "b_w_in", "b_b_in", "b_ln_g", "b_ln_b",
        "b_w_s", "b_b_s", "b_w_out", "c_w_qkv", "c_q_g", "c_k_g", "c_w_out", "p_w_q", "p_k1", "p_k2", "p_u", "p_v")}
    shared.update(consts)
    maps = []
    for b in range(8):
        m = dict(shared)
        m["xin"] = np.ascontiguousarray(np.concatenate([inputs["x"][b], inputs["ctx"][b]], axis=0))
        m["cc"] = np.ascontiguousarray(np.stack([inputs["c"][b], inputs["c_ctx"]], axis=0))
        maps.append(m)
    return maps


def kernel(**inputs):
    inputs = {k_: np.asarray(v) for k_, v in inputs.items()}
    if "nc" not in _CACHE:
        _CACHE["nc"] = build_program()[0]
    nc = _CACHE["nc"]
    maps = make_in_maps(inputs)
    res = run_bass_kernel_spmd(nc, maps, core_ids=list(range(8)))
    out = np.stack([np.asarray(r["y"]).reshape(SEQ, D) for r in res.results], axis=0)
    return out.astype(np.float32)
```

```python
import numpy as np
from contextlib import ExitStack
import concourse.bass as bass
import concourse.mybir as mybir
from concourse.bass_utils import run_bass_kernel_spmd

F32 = mybir.dt.float32
BF16 = mybir.dt.bfloat16
U32 = mybir.dt.uint32
I32 = mybir.dt.int32
AF = mybir.ActivationFunctionType
ALU = mybir.AluOpType
AX = mybir.AxisListType

D = 2048
KC = 16
SEQ = 4096
CTX = 256
T = SEQ + CTX
NT = T // 128
NTL = SEQ // 128
DEPTH = 4
ALPHA = (2 * DEPTH) ** 0.25
EPS = 1e-6
A_IN = 6160
NEG = -1.0e30


NO_SELF_SYNC = ("pe",)


class Buf:
    __slots__ = ("ap", "w", "r")

    def __init__(self, ap):
        self.ap = ap
        self.w = None
        self.r = {}


class Prog:
    LIMIT = 30000

    def __init__(self, nc):
        self.nc = nc
        self.engs = {"pe": nc.tensor, "dve": nc.vector, "act": nc.scalar, "pool": nc.gpsimd, "sp": nc.sync}
        self.sems = {}
        self.cnt = {}
        self.key = {}
        self.gen = {}
        for e in ("pe", "dve", "act", "pool"):
            self.gen[e] = 0
            self._newkey(e)
        self.clock = {e: {} for e in self.engs}
        self.dq = {}
        for q, n in (("sp", 16), ("pool", 16), ("act", 6)):
            keys = []
            for i in range(n):
                k = "d_%s%d" % (q, i)
                self.sems[k] = nc.alloc_semaphore(name=k)
                self.cnt[k] = 0
                keys.append(k)
            self.dq[q] = [keys, 0]
        self.n_inst = 0

    def _newkey(self, e):
        k = "%s_%d" % (e, self.gen[e])
        self.gen[e] += 1
        self.sems[k] = self.nc.alloc_semaphore(name="s_" + k)
        self.cnt[k] = 0
        self.key[e] = k

    def _wait(self, e, tok):
        if tok is None:
            return
        k, v = tok
        if self.clock[e].get(k, 0) >= v:
            return
        if e in NO_SELF_SYNC and k.startswith(e + "_"):
            return
        self.engs[e].wait_ge(self.sems[k], v)
        self.clock[e][k] = v

    def _deps(self, e, reads, writes):
        for b in reads:
            self._wait(e, b.w)
        for b in writes:
            self._wait(e, b.w)
            for kv in list(b.r.items()):
                self._wait(e, kv)

    def _mark(self, tok, reads, writes):
        k, v = tok
        for b in reads:
            b.r[k] = v
        for b in writes:
            b.w = tok
            b.r = {}

    def op(self, e, fn, reads=(), writes=()):
        self._deps(e, reads, writes)
        inst = fn(self.engs[e])
        k = self.key[e]
        inst.then_inc(self.sems[k], 1)
        self.cnt[k] += 1
        tok = (k, self.cnt[k])
        self._mark(tok, reads, writes)
        if self.cnt[k] >= self.LIMIT:
            self._newkey(e)
        self.n_inst += 1
        return tok

    def dma(self, q, out, in_, reads=(), writes=(), indirect=None, **kw):
        keys, rr = self.dq[q]
        k = keys[rr % len(keys)]
        self.dq[q][1] += 1
        if self.cnt[k] > 0:
            self._wait(q, (k, self.cnt[k]))
        self._deps(q, reads, writes)
        eng = self.engs[q]
        if indirect is not None:
            inst = eng.indirect_dma_start(out=out, out_offset=None, in_=in_, in_offset=indirect, **kw)
        else:
            inst = eng.dma_start(out=out, in_=in_, **kw)
        inst.then_inc(self.sems[k], 16)
        self.cnt[k] += 16
        if self.cnt[k] >= self.LIMIT:
            self._wait(q, (k, self.cnt[k]))
            tok = (k, self.cnt[k])
            self._mark(tok, reads, writes)
            nk = k + "n"
            self.sems[nk] = self.nc.alloc_semaphore(name=nk)
            self.cnt[nk] = 0
            keys[(rr) % len(keys)] = nk
            self.n_inst += 1
            return tok
        tok = (k, self.cnt[k])
        self._mark(tok, reads, writes)
        self.n_inst += 1
        return tok

    def barrier(self):
        for e in self.engs:
            for k, v in self.cnt.items():
                if v > 0:
                    self._wait(e, (k, v))

    def final_wait(self, e="sp"):
        for k, v in self.cnt.items():
            if v > 0:
                self._wait(e, (k, v))


def rsqrt_eps(P, dst, src_ap, reads, scale):
    P.op("dve", lambda e: e.tensor_scalar(out=dst.ap, in0=src_ap, scalar1=float(scale), scalar2=float(EPS), op0=ALU.mult, op1=ALU.add),
         reads=list(reads), writes=[dst])
    P.op("act", lambda e: e.activation(out=dst.ap, in_=dst.ap, func=AF.Sqrt), reads=[dst], writes=[dst])
    P.op("dve", lambda e: e.reciprocal(out=dst.ap, in_=dst.ap), reads=[dst], writes=[dst])


def pipelined(items, load, body):
    items = list(items)
    if not items:
        return
    load(items[0])
    for i, it in enumerate(items):
        if i + 1 < len(items):
            load(items[i + 1])
        body(it)


def bcast_rows(ap_row, n):
    return ap_row.to_broadcast([n, ap_row.shape[-1]])


class K:
    def __init__(self, n_layers=DEPTH, dbg=()):
        self.n_layers = n_layers
        self.dbg = set(dbg)
        nc = bass.Bass("TRN2", target_bir_lowering=False)
        self.nc = nc
        self.P = Prog(nc)
        self.es = ExitStack()
        self.inp = {}

    def din(self, name, shape, dt=F32):
        ap = self.nc.dram_tensor(name, list(shape), dt, kind="ExternalInput").ap()
        self.inp[name] = ap
        return ap

    def dscr(self, name, shape, dt=F32, out=False):
        kind = "ExternalOutput" if (out or name in self.dbg) else "Internal"
        return self.nc.dram_tensor(name, list(shape), dt, kind=kind).ap()

    def sb(self, es, name, shape, dt=F32):
        self.uid = getattr(self, "uid", 0) + 1
        t = es.enter_context(self.nc.sbuf_tensor("%s_%d" % (name, self.uid), list(shape), dt))
        return t

    def psum(self, es, name, shape, dt=F32):
        return es.enter_context(self.nc.psum_tensor(name, list(shape), dt))


def build_program(n_layers=DEPTH, dbg=(), peer=True):
    k = K(n_layers, dbg)
    nc, P = k.nc, k.P
    xin = k.din("xin", [T, D])
    cc = k.din("cc", [2, D])
    w_mod = k.din("w_mod", [DEPTH, D, 6 * D])
    b_mod = k.din("b_mod", [DEPTH, 6 * D])
    ln_g = k.din("ln_g", [DEPTH, 2, D])
    ln_b = k.din("ln_b", [DEPTH, 2, D])
    a_w_in = k.din("a_w_in", [2, D, A_IN])
    a_b_gate = k.din("a_b_gate", [2, 16])
    a_norm_g = k.din("a_norm_g", [2, D])
    a_w_out = k.din("a_w_out", [2, D, D])
    b_w_in = k.din("b_w_in", [1, D, 8192])
    b_b_in = k.din("b_b_in", [1, 8192])
    b_ln_g = k.din("b_ln_g", [1, 4096])
    b_ln_b = k.din("b_ln_b", [1, 4096])
    b_w_s = k.din("b_w_s", [1, 8, 128, 128])
    b_b_s = k.din("b_b_s", [1, 8, 128])
    b_w_out = k.din("b_w_out", [1, 4096, D])
    c_w_qkv = k.din("c_w_qkv", [1, D, 3072])
    c_q_g = k.din("c_q_g", [1, 128])
    c_k_g = k.din("c_k_g", [1, 128])
    c_w_out = k.din("c_w_out", [1, D, D])
    p_w_q = k.din("p_w_q", [DEPTH, D, D])
    p_k1 = k.din("p_k1", [DEPTH, 128, 128])
    p_k2 = k.din("p_k2", [DEPTH, 128, 128])
    p_u = k.din("p_u", [DEPTH, 16384, D])
    p_v = k.din("p_v", [DEPTH, 16384, D])
    ident_d = k.din("ident", [128, 128])
    tri_d = k.din("tri", [2, 128, 128])
    rope_d = k.din("rope", [SEQ, 2, 64])
    iota_d = k.din("iota16", [1, 128])

    yout = k.dscr("y", [SEQ, D], out=True)
    xres = k.dscr("xres", [T, D])
    modd = k.dscr("modd", [DEPTH, 2, 6 * D])
    hT_d = k.dscr("hT", [NT, 128, KC, 128], BF16)
    h_d = k.dscr("h_tm", [T, D])
    ymix = k.dscr("ymix", [T, D])
    S = dict(k=k, nc=nc, P=P, xin=xin, xres=xres, modd=modd, hT_d=hT_d, h_d=h_d, ymix=ymix, yout=yout)

    with ExitStack() as top:
        ident = Buf(k.sb(top, "ident_sb", [128, 128])[:])
        identb = Buf(k.sb(top, "identb_sb", [128, 128], BF16)[:])
        P.dma("sp", ident.ap, ident_d[:, :], writes=[ident])
        P.op("dve", lambda e: e.tensor_copy(out=identb.ap, in_=ident.ap), reads=[ident], writes=[identb])
        ps_t = [k.psum(top, "ps%d" % i, [128, 1024]) for i in range(4)]
        ps = [Buf(ps_t[i // 2][:, (i % 2) * 512:(i % 2 + 1) * 512]) for i in range(8)]
        S.update(ident=ident, identb=identb, ps=ps, ps_t=ps_t)
        S["ps2"] = [Buf(t[:]) for t in ps_t]
        S["psb"] = [Buf(ps_t[3][:, i * 512:(i + 1) * 512].bitcast(BF16)) for i in range(2)]
        S["Gd"] = k.dscr("Gd", [128, 128, T], BF16)

        phase_mod(S, cc, w_mod, b_mod, n_layers)
        P.barrier()
        for li in range(n_layers):
            need_ctx = li < DEPTH - 1
            nt_out = NT if need_ctx else NTL
            src = xin if li == 0 else xres
            mixer, j = li % 3, li // 3
            phase_prep(S, src, li, 0, NT, want_h=False)
            P.barrier()
            if mixer == 0:
                mix_mlstm(S, j, a_w_in, a_b_gate, a_norm_g, a_w_out, tri_d, need_ctx)
            elif mixer == 1:
                mix_gmlp(S, j, b_w_in, b_b_in, b_ln_g, b_ln_b, b_w_s, b_b_s, b_w_out, need_ctx)
            else:
                mix_gqa(S, j, c_w_qkv, c_q_g, c_k_g, c_w_out, rope_d, need_ctx)
            P.barrier()
            phase_ln(S, src, ymix, xres, li, 0, ln_g, ln_b, nt_out)
            P.barrier()
            if "x_mid%d" % li in k.dbg:
                dump(S, xres, "x_mid%d" % li, T)
            phase_prep(S, xres, li, 1, nt_out, want_h=not PEER_DENSE)
            P.barrier()
            phase_peer(S, li, p_w_q, p_k1, p_k2, p_u, p_v, iota_d, nt_out, enable=peer)
            P.barrier()
            last = (li == n_layers - 1)
            phase_ln(S, xres, ymix, (yout if last else xres), li, 1, ln_g, ln_b, nt_out if not last else NTL)
            P.barrier()
        P.final_wait("sp")
        P.final_wait("act")
    return nc, k


def dump(S, src, name, rows):
    k, P = S["k"], S["P"]
    dst = k.dscr(name, [rows, D], out=True)
    for t in range(rows // 128):
        P.dma("sp", dst[t * 128:(t + 1) * 128, :], src[t * 128:(t + 1) * 128, :])
    P.barrier()


def phase_mod(S, cc, w_mod, b_mod, n_layers):
    k, P, ps, modd = S["k"], S["P"], S["ps"], S["modd"]
    with ExitStack() as es:
        craw = Buf(k.sb(es, "craw", [128, 2, KC])[:])
        condT = Buf(k.sb(es, "condT", [128, KC, 2], BF16)[:])
        P.dma("sp", craw.ap, cc.rearrange("r (kc p) -> p r kc", p=128), writes=[craw], allow_slow_non_contiguous=True)
        P.op("act", lambda e: e.activation(out=condT.ap.rearrange("p kc r -> p r kc"), in_=craw.ap, func=AF.Silu),
             reads=[craw], writes=[condT])
        wbuf = [Buf(k.sb(es, "wm%d" % i, [128, KC, 512], BF16)[:]) for i in range(4)]
        bbuf = [Buf(k.sb(es, "bm%d" % i, [2, 512])[:]) for i in range(4)]
        obuf = [Buf(k.sb(es, "om%d" % i, [2, 512])[:]) for i in range(4)]
        it = 0
        for li in range(n_layers):
            for nci in range(24):
                w, bb, ob, pt = wbuf[it % 4], bbuf[it % 4], obuf[it % 4], ps[it % 4]
                n0 = nci * 512
                wsrc = w_mod[li, :, n0:n0 + 512].rearrange("(kc p) n -> p kc n", p=128)
                P.dma("pool", w.ap, wsrc, writes=[w])
                P.dma("sp", bb.ap, b_mod[li:li + 1, n0:n0 + 512].to_broadcast([2, 512]), writes=[bb])
                for kc in range(KC):
                    P.op("pe", lambda e, kc=kc, w=w, pt=pt: e.matmul(pt.ap[0:2, :], lhsT=condT.ap[:, kc, :], rhs=w.ap[:, kc, :],
                                                                 start=(kc == 0), stop=(kc == KC - 1)),
                         reads=[condT, w], writes=[pt])
                P.op("dve", lambda e, pt=pt, bb=bb, ob=ob: e.tensor_tensor(out=ob.ap, in0=pt.ap[0:2, :], in1=bb.ap, op=ALU.add),
                     reads=[pt, bb], writes=[ob])
                P.dma("sp", modd[li, :, n0:n0 + 512], ob.ap, reads=[ob])
                it += 1


def load_mod_bcast(S, es, li, idx, rows, name, plus_one=False):
    k, P, modd = S["k"], S["P"], S["modd"]
    b = Buf(k.sb(es, name, [128, D])[:])
    P.dma("sp", b.ap, modd[li, rows:rows + 1, idx * D:(idx + 1) * D].to_broadcast([128, D]), writes=[b])
    if plus_one:
        P.op("pool", lambda e: e.tensor_scalar_add(out=b.ap, in0=b.ap, scalar1=1.0), reads=[b], writes=[b])
    return b


def phase_prep(S, src, li, sub, nt, want_h):
    k, P, ps, hT_d, h_d, ident = S["k"], S["P"], S["ps"], S["hT_d"], S["h_d"], S["ident"]
    with ExitStack() as es:
        sh = [load_mod_bcast(S, es, li, 3 * sub + 0, r, "sh%d" % r) for r in range(2)]
        sc = [load_mod_bcast(S, es, li, 3 * sub + 1, r, "sc%d" % r, plus_one=True) for r in range(2)]
        xb = [Buf(k.sb(es, "px%d" % i, [128, D])[:]) for i in range(2)]
        hb = [Buf(k.sb(es, "ph%d" % i, [128, D])[:]) for i in range(2)]
        hTb = [Buf(k.sb(es, "phT%d" % i, [128, KC, 128], BF16)[:]) for i in range(2)]
        def load(t):
            P.dma("sp", xb[t % 2].ap, src[t * 128:(t + 1) * 128, :], writes=[xb[t % 2]])

        def body(t):
            r = 0 if t < NTL else 1
            x, h, hT = xb[t % 2], hb[t % 2], hTb[t % 2]
            P.op("pool", lambda e: e.tensor_tensor(out=h.ap, in0=x.ap, in1=sc[r].ap, op=ALU.mult), reads=[x, sc[r]], writes=[h])
            P.op("dve", lambda e: e.tensor_tensor(out=h.ap, in0=h.ap, in1=sh[r].ap, op=ALU.add), reads=[h, sh[r]], writes=[h])
            if want_h:
                P.dma("act", h_d[t * 128:(t + 1) * 128, :], h.ap, reads=[h])
            for g in range(4):
                pt = ps[(t * 4 + g) % 8]
                for q in range(4):
                    kc = g * 4 + q
                    P.op("pe", lambda e, kc=kc, q=q, pt=pt: e.transpose(out=pt.ap[:, q * 128:(q + 1) * 128], in_=h.ap[:, kc * 128:(kc + 1) * 128],
                                                                   identity=ident.ap), reads=[h, ident], writes=[pt])
                eng = "act" if g % 2 == 0 else "dve"
                if eng == "act":
                    P.op("act", lambda e, g=g, pt=pt: e.activation(out=hT.ap[:, g * 4:(g + 1) * 4, :], in_=pt.ap.rearrange("p (q t) -> p q t", q=4), func=AF.Copy),
                         reads=[pt], writes=[hT])
                else:
                    P.op("dve", lambda e, g=g, pt=pt: e.tensor_copy(out=hT.ap[:, g * 4:(g + 1) * 4, :], in_=pt.ap.rearrange("p (q t) -> p q t", q=4)),
                         reads=[pt], writes=[hT])
            P.dma("sp", hT_d[t], hT.ap, reads=[hT])
        pipelined(range(nt), load, body)


def phase_ln(S, xsrc, ysrc, dst, li, sub, ln_g, ln_b, nt):
    k, P = S["k"], S["P"]
    with ExitStack() as es:
        gt = [load_mod_bcast(S, es, li, 3 * sub + 2, r, "lg%d" % r) for r in range(2)]
        lg = Buf(k.sb(es, "lng", [128, D])[:])
        lb = Buf(k.sb(es, "lnb", [128, D])[:])
        P.dma("sp", lg.ap, ln_g[li, sub:sub + 1, :].to_broadcast([128, D]), writes=[lg])
        P.dma("sp", lb.ap, ln_b[li, sub:sub + 1, :].to_broadcast([128, D]), writes=[lb])
        xb = [Buf(k.sb(es, "lx%d" % i, [128, D])[:]) for i in range(2)]
        yb = [Buf(k.sb(es, "ly%d" % i, [128, D])[:]) for i in range(2)]
        st = [Buf(k.sb(es, "lst%d" % i, [128, 4, 6])[:]) for i in range(2)]
        mv = [Buf(k.sb(es, "lmv%d" % i, [128, 2])[:]) for i in range(2)]
        rs = [Buf(k.sb(es, "lrs%d" % i, [128, 1])[:]) for i in range(2)]
        def load(t):
            P.dma("sp", xb[t % 2].ap, xsrc[t * 128:(t + 1) * 128, :], writes=[xb[t % 2]])
            P.dma("sp", yb[t % 2].ap, ysrc[t * 128:(t + 1) * 128, :], writes=[yb[t % 2]])

        def body(t):
            r = 0 if t < NTL else 1
            x, y, s_, m_, r_ = xb[t % 2], yb[t % 2], st[t % 2], mv[t % 2], rs[t % 2]
            P.op("pool", lambda e: e.tensor_tensor(out=y.ap, in0=y.ap, in1=gt[r].ap, op=ALU.mult), reads=[y, gt[r]], writes=[y])
            P.op("dve", lambda e: e.scalar_tensor_tensor(out=x.ap, in0=x.ap, scalar=float(ALPHA), in1=y.ap, op0=ALU.mult, op1=ALU.add),
                 reads=[x, y], writes=[x])
            for q in range(4):
                P.op("dve", lambda e, q=q: e.bn_stats(out=s_.ap[:, q, :], in_=x.ap[:, q * 512:(q + 1) * 512]), reads=[x], writes=[s_])
            P.op("dve", lambda e: e.bn_aggr(out=m_.ap, in_=s_.ap.rearrange("p a b -> p (a b)")), reads=[s_], writes=[m_])
            rsqrt_eps(P, r_, m_.ap[:, 1:2], [m_], 1.0)
            P.op("dve", lambda e: e.tensor_scalar(out=x.ap, in0=x.ap, scalar1=m_.ap[:, 0:1], scalar2=r_.ap[:, 0:1], op0=ALU.subtract, op1=ALU.mult),
                 reads=[x, m_, r_], writes=[x])
            P.op("pool", lambda e: e.tensor_tensor(out=x.ap, in0=x.ap, in1=lg.ap, op=ALU.mult), reads=[x, lg], writes=[x])
            P.op("dve", lambda e: e.tensor_tensor(out=x.ap, in0=x.ap, in1=lb.ap, op=ALU.add), reads=[x, lb], writes=[x])
            P.dma("sp", dst[t * 128:(t + 1) * 128, :], x.ap, reads=[x])
        pipelined(range(nt), load, body)


def load_hT_group(S, es_bufs, t0, ntile, q="sp"):
    P, hT_d = S["P"], S["hT_d"]
    g = es_bufs
    for i in range(ntile):
        P.dma(q, g.ap[:, :, i * 128:(i + 1) * 128], hT_d[t0 + i], writes=[g])
    return g


def load_w_chunk(S, wbuf, w_ap, k0_chunks, n0, n, q="pool"):
    P = S["P"]
    P.dma(q, wbuf.ap[:, 0:k0_chunks, 0:n], w_ap[:, n0:n0 + n].rearrange("(kc p) n -> p kc n", p=128), writes=[wbuf])


NPS = 8
PEER_DENSE = True


def next_ps(S, lo=0, hi=NPS):
    r = S.setdefault("rot", 0)
    S["rot"] = r + 1
    return S["ps"][lo + r % (hi - lo)]


def linear_tm(S, src_T, kch, w_ap, N, tiles, epi, G=12, tag="l"):
    k, P = S["k"], S["P"]
    with ExitStack() as es:
        hg = [Buf(k.sb(es, tag + "hg%d" % i, [128, kch, 128], BF16)[:]) for i in range(G)]
        wb = [Buf(k.sb(es, tag + "w%d" % i, [128, kch, 512], BF16)[:]) for i in range(2)]
        it = 0
        for g0 in range(0, len(tiles), G):
            grp = tiles[g0:g0 + G]
            for i, t in enumerate(grp):
                P.dma("sp", hg[i].ap, src_T[t], writes=[hg[i]])
            for n0 in range(0, N, 512):
                n = min(512, N - n0)
                w = wb[it % 2]
                it += 1
                P.dma("pool", w.ap[:, :, 0:n], w_ap[:, n0:n0 + n].rearrange("(kc p) n -> p kc n", p=128), writes=[w])
                for i, t in enumerate(grp):
                    pt = next_ps(S)
                    for kc in range(kch):
                        P.op("pe", lambda e: e.matmul(pt.ap[:, 0:n], lhsT=hg[i].ap[:, kc, :], rhs=w.ap[:, kc, 0:n],
                                                      start=(kc == 0), stop=(kc == kch - 1)), reads=[hg[i], w], writes=[pt])
                    epi(t, n0, n, pt)


def linear_fm(S, src_T, kch, w_ap, N, tiles, epi, G=8, tag="f"):
    k, P = S["k"], S["P"]
    with ExitStack() as es:
        hg = [Buf(k.sb(es, tag + "hg%d" % i, [128, kch, 512], BF16)[:]) for i in range(G // 4)]
        wb = [Buf(k.sb(es, tag + "w%d" % i, [128, kch, 512], BF16)[:]) for i in range(2)]
        it = 0
        for g0 in range(0, len(tiles), G):
            grp = tiles[g0:g0 + G]
            blocks = [grp[i:i + 4] for i in range(0, len(grp), 4)]
            for bi, blk in enumerate(blocks):
                for i, t in enumerate(blk):
                    P.dma("sp", hg[bi].ap[:, :, i * 128:(i + 1) * 128], src_T[t], writes=[hg[bi]])
            for n0 in range(0, N, 512):
                n = min(512, N - n0)
                w = wb[it % 2]
                it += 1
                P.dma("pool", w.ap[:, :, 0:n], w_ap[:, n0:n0 + n].rearrange("(kc p) n -> p kc n", p=128), writes=[w])
                for cc in range(n // 128):
                    for bi, blk in enumerate(blocks):
                        ntok = len(blk) * 128
                        pt = next_ps(S)
                        for kc in range(kch):
                            P.op("pe", lambda e: e.matmul(pt.ap[:, 0:ntok], lhsT=w.ap[:, kc, cc * 128:(cc + 1) * 128], rhs=hg[bi].ap[:, kc, 0:ntok],
                                                          start=(kc == 0), stop=(kc == kch - 1)), reads=[hg[bi], w], writes=[pt])
                        epi(n0 // 128 + cc, blk[0], len(blk), pt)


def make_store_epi(S, es, dst, tag="st", nbuf=4, col_off=0):
    k, P = S["k"], S["P"]
    stg = [Buf(k.sb(es, tag + "%d" % i, [128, 512])[:]) for i in range(nbuf)]
    cnt = [0]

    def epi(t, n0, n, pt):
        s = stg[cnt[0] % nbuf]
        eng = "act" if cnt[0] % 2 == 0 else "dve"
        cnt[0] += 1
        if eng == "act":
            P.op("act", lambda e: e.activation(out=s.ap[:, 0:n], in_=pt.ap[:, 0:n], func=AF.Copy), reads=[pt], writes=[s])
        else:
            P.op("dve", lambda e: e.tensor_copy(out=s.ap[:, 0:n], in_=pt.ap[:, 0:n]), reads=[pt], writes=[s])
        P.dma("sp", dst[t * 128:(t + 1) * 128, col_off + n0:col_off + n0 + n], s.ap[:, 0:n], reads=[s])
    return epi


def transpose_to_T(S, src, nch, dstT, t):
    P, ident = S["P"], S["ident"]
    for g in range(nch // 4):
        pt = next_ps(S)
        for q in range(4):
            kc = g * 4 + q
            P.op("pe", lambda e: e.transpose(out=pt.ap[:, q * 128:(q + 1) * 128], in_=src.ap[:, kc * 128:(kc + 1) * 128], identity=ident.ap),
                 reads=[src, ident], writes=[pt])
        if g % 2 == 0:
            P.op("act", lambda e: e.activation(out=dstT.ap[:, g * 4:(g + 1) * 4, :], in_=pt.ap.rearrange("p (q t) -> p q t", q=4), func=AF.Copy),
                 reads=[pt], writes=[dstT])
        else:
            P.op("dve", lambda e: e.tensor_copy(out=dstT.ap[:, g * 4:(g + 1) * 4, :], in_=pt.ap.rearrange("p (q t) -> p q t", q=4)),
                 reads=[pt], writes=[dstT])


def mix_gmlp(S, j, b_w_in, b_b_in, b_ln_g, b_ln_b, b_w_s, b_b_s, b_w_out, need_ctx):
    k, P, hT_d, ymix, ident = S["k"], S["P"], S["hT_d"], S["ymix"], S["ident"]
    tiles = list(range(NT if need_ctx else NTL))
    zd = k.dscr("zd", [T, 8192])
    uvT = k.dscr("uvT", [NT, 128, 32, 128], BF16)
    with ExitStack() as es:
        bias = Buf(k.sb(es, "gb_bias", [128, 8192])[:])
        P.dma("sp", bias.ap, b_b_in[j:j + 1, :].to_broadcast([128, 8192]), writes=[bias])
        stg = [Buf(k.sb(es, "gb_st%d" % i, [128, 512])[:]) for i in range(4)]
        cnt = [0]

        def epi(t, n0, n, pt):
            s = stg[cnt[0] % 4]
            cnt[0] += 1
            P.op("dve", lambda e: e.tensor_tensor(out=s.ap, in0=pt.ap, in1=bias.ap[:, n0:n0 + n], op=ALU.add), reads=[pt, bias], writes=[s])
            P.op("act", lambda e: e.activation(out=s.ap, in_=s.ap, func=AF.Gelu_apprx_tanh), reads=[s], writes=[s])
            P.dma("sp", zd[t * 128:(t + 1) * 128, n0:n0 + n], s.ap, reads=[s])
        linear_tm(S, hT_d, KC, b_w_in[j], 8192, tiles, epi, tag="gb1")
    P.barrier()
    with ExitStack() as es:
        lg = Buf(k.sb(es, "gb_lg", [128, 4096])[:])
        lb = Buf(k.sb(es, "gb_lb", [128, 4096])[:])
        P.dma("sp", lg.ap, b_ln_g[j:j + 1, :].to_broadcast([128, 4096]), writes=[lg])
        P.dma("sp", lb.ap, b_ln_b[j:j + 1, :].to_broadcast([128, 4096]), writes=[lb])
        wsr = Buf(k.sb(es, "gb_wsr", [128, 8, 128])[:])
        wsT = Buf(k.sb(es, "gb_wsT", [128, 8, 128], BF16)[:])
        bsT = Buf(k.sb(es, "gb_bsT", [128, 8])[:])
        P.dma("sp", wsr.ap, b_w_s[j].rearrange("g t s -> t g s"), writes=[wsr])
        P.dma("sp", bsT.ap, b_b_s[j].rearrange("g t -> t g"), writes=[bsT], allow_slow_non_contiguous=True)
        for g in range(8):
            pt = next_ps(S)
            P.op("pe", lambda e: e.transpose(out=pt.ap[:, 0:128], in_=wsr.ap[:, g, :], identity=ident.ap), reads=[wsr, ident], writes=[pt])
            P.op("dve", lambda e: e.tensor_copy(out=wsT.ap[:, g, :], in_=pt.ap[:, 0:128]), reads=[pt], writes=[wsT])
        zb = [Buf(k.sb(es, "gb_z%d" % i, [128, 8192])[:]) for i in range(2)]
        vb = [Buf(k.sb(es, "gb_v%d" % i, [128, 4096], BF16)[:]) for i in range(2)]
        uT = [Buf(k.sb(es, "gb_uT%d" % i, [128, 32, 128], BF16)[:]) for i in range(2)]
        st = [Buf(k.sb(es, "gb_bs%d" % i, [128, 8, 6])[:]) for i in range(2)]
        mv = [Buf(k.sb(es, "gb_mv%d" % i, [128, 2])[:]) for i in range(2)]
        rs = [Buf(k.sb(es, "gb_rs%d" % i, [128, 1])[:]) for i in range(2)]
        def load(t):
            z = zb[t % 2]
            P.dma("sp", z.ap[:, 0:4096], zd[t * 128:(t + 1) * 128, 0:4096], writes=[z])
            P.dma("act", z.ap[:, 4096:8192], zd[t * 128:(t + 1) * 128, 4096:8192], writes=[z])

        def body(t):
            z, v, u_T, s_, m_, r_ = zb[t % 2], vb[t % 2], uT[t % 2], st[t % 2], mv[t % 2], rs[t % 2]
            for q in range(8):
                P.op("dve", lambda e: e.bn_stats(out=s_.ap[:, q, :], in_=z.ap[:, 4096 + q * 512:4096 + (q + 1) * 512]), reads=[z], writes=[s_])
            P.op("dve", lambda e: e.bn_aggr(out=m_.ap, in_=s_.ap.rearrange("p a b -> p (a b)")), reads=[s_], writes=[m_])
            rsqrt_eps(P, r_, m_.ap[:, 1:2], [m_], 1.0)
            P.op("dve", lambda e: e.tensor_scalar(out=z.ap[:, 4096:8192], in0=z.ap[:, 4096:8192], scalar1=m_.ap[:, 0:1], scalar2=r_.ap[:, 0:1],
                                                  op0=ALU.subtract, op1=ALU.mult), reads=[z, m_, r_], writes=[z])
            P.op("pool", lambda e: e.tensor_tensor(out=z.ap[:, 4096:8192], in0=z.ap[:, 4096:8192], in1=lg.ap, op=ALU.mult), reads=[z, lg], writes=[z])
            P.op("dve", lambda e: e.tensor_tensor(out=v.ap, in0=z.ap[:, 4096:8192], in1=lb.ap, op=ALU.add), reads=[z, lb], writes=[v])
            for g in range(8):
                pt = next_ps(S)
                P.op("pe", lambda e: e.matmul(pt.ap, lhsT=wsT.ap[:, g, :], rhs=v.ap[:, g * 512:(g + 1) * 512], start=True, stop=True),
                     reads=[wsT, v], writes=[pt])
                P.op("dve", lambda e: e.scalar_tensor_tensor(out=z.ap[:, g * 512:(g + 1) * 512], in0=pt.ap, scalar=bsT.ap[:, g:g + 1],
                                                             in1=z.ap[:, g * 512:(g + 1) * 512], op0=ALU.add, op1=ALU.mult),
                     reads=[pt, bsT, z], writes=[z])
            transpose_to_T(S, z, 32, u_T, t)
            P.dma("sp", uvT[t], u_T.ap, reads=[u_T])
        pipelined(tiles, load, body)
    P.barrier()
    with ExitStack() as es:
        epi = make_store_epi(S, es, ymix, tag="gb3s")
        linear_tm(S, uvT, 32, b_w_out[j], D, tiles, epi, G=9, tag="gb3")


def mix_gqa(S, j, c_w_qkv, c_q_g, c_k_g, c_w_out, rope_d, need_ctx):
    k, P, hT_d, ymix, ident = S["k"], S["P"], S["hT_d"], S["ymix"], S["ident"]
    tiles = list(range(NT))
    qkv_d = k.dscr("qkv_d", [T, 3072])
    qkT_d = k.dscr("qkT_d", [20, 128, T], BF16)
    v_d = k.dscr("v_d", [NT, 128, 512], BF16)
    with ExitStack() as es:
        epi = make_store_epi(S, es, qkv_d, tag="gq1s")
        linear_tm(S, hT_d, KC, c_w_qkv[j], 3072, tiles, epi, tag="gq1")
    P.barrier()
    with ExitStack() as es:
        gq = Buf(k.sb(es, "gq_g", [128, 20, 128])[:])
        P.dma("sp", gq.ap[:, 0:16, :], c_q_g[j:j + 1, :].unsqueeze(1).to_broadcast([128, 16, 128]), writes=[gq])
        P.dma("sp", gq.ap[:, 16:20, :], c_k_g[j:j + 1, :].unsqueeze(1).to_broadcast([128, 4, 128]), writes=[gq])
        xb = [Buf(k.sb(es, "gq_x%d" % i, [128, 3072])[:]) for i in range(2)]
        sqb = Buf(k.sb(es, "gq_sq", [128, 2560])[:])
        ssum = Buf(k.sb(es, "gq_ss", [128, 20])[:])
        rp = [Buf(k.sb(es, "gq_rp%d" % i, [128, 2, 64])[:]) for i in range(2)]
        tA = Buf(k.sb(es, "gq_tA", [128, 20, 2, 32])[:])
        tB = Buf(k.sb(es, "gq_tB", [128, 20, 2, 32])[:])
        tC = Buf(k.sb(es, "gq_tC", [128, 20, 2, 32])[:])
        qT = [Buf(k.sb(es, "gq_qT%d" % i, [128, 20, 128], BF16)[:]) for i in range(2)]
        vbf = [Buf(k.sb(es, "gq_vb%d" % i, [128, 512], BF16)[:]) for i in range(2)]
        def load(t):
            P.dma("sp", xb[t % 2].ap, qkv_d[t * 128:(t + 1) * 128, :], writes=[xb[t % 2]])
            if t < NTL:
                P.dma("act", rp[t % 2].ap, rope_d[t * 128:(t + 1) * 128, :, :], writes=[rp[t % 2]])

        def body(t):
            x, r_, q_T, v_ = xb[t % 2], rp[t % 2], qT[t % 2], vbf[t % 2]
            P.op("act", lambda e: e.activation(out=sqb.ap, in_=x.ap[:, 0:2560], func=AF.Square), reads=[x], writes=[sqb])
            P.op("dve", lambda e: e.tensor_reduce(out=ssum.ap, in_=sqb.ap.rearrange("p (h d) -> p h d", d=128), axis=AX.X, op=ALU.add),
                 reads=[sqb], writes=[ssum])
            rsqrt_eps(P, ssum, ssum.ap, [ssum], 1.0 / 128.0)
            x3 = x.ap[:, 0:2560].rearrange("p (h d) -> p h d", d=128)
            P.op("dve", lambda e: e.tensor_tensor(out=x3, in0=x3, in1=ssum.ap.unsqueeze(2).to_broadcast([128, 20, 128]), op=ALU.mult),
                 reads=[x, ssum], writes=[x])
            P.op("pool", lambda e: e.tensor_tensor(out=x3, in0=x3, in1=gq.ap, op=ALU.mult), reads=[x, gq], writes=[x])
            if t < NTL:
                x5 = x.ap[:, 0:2560].rearrange("p (h a b c) -> p h a b c", a=2, b=2, c=32)
                x1 = x5[:, :, :, 0, :]
                x2 = x5[:, :, :, 1, :]
                cosb = r_.ap[:, 0, :].rearrange("p (a c) -> p a c", a=2).unsqueeze(1).to_broadcast([128, 20, 2, 32])
                sinb = r_.ap[:, 1, :].rearrange("p (a c) -> p a c", a=2).unsqueeze(1).to_broadcast([128, 20, 2, 32])
                P.op("dve", lambda e: e.tensor_tensor(out=tA.ap, in0=x1, in1=sinb, op=ALU.mult), reads=[x, r_], writes=[tA])
                P.op("pool", lambda e: e.tensor_tensor(out=tB.ap, in0=x2, in1=sinb, op=ALU.mult), reads=[x, r_], writes=[tB])
                P.op("dve", lambda e: e.tensor_tensor(out=tC.ap, in0=x2, in1=cosb, op=ALU.mult), reads=[x, r_], writes=[tC])
                P.op("dve", lambda e: e.tensor_tensor(out=x1, in0=x1, in1=cosb, op=ALU.mult), reads=[x, r_], writes=[x])
                P.op("dve", lambda e: e.tensor_tensor(out=x1, in0=x1, in1=tB.ap, op=ALU.subtract), reads=[x, tB], writes=[x])
                P.op("dve", lambda e: e.tensor_tensor(out=x2, in0=tA.ap, in1=tC.ap, op=ALU.add), reads=[tA, tC], writes=[x])
            transpose_to_T(S, x, 20, q_T, t)
            P.dma("sp", qkT_d[:, :, t * 128:(t + 1) * 128].rearrange("h d t -> d h t"), q_T.ap, reads=[q_T])
            P.op("pool", lambda e: e.tensor_copy(out=v_.ap, in_=x.ap[:, 2560:3072]), reads=[x], writes=[v_])
            P.dma("sp", v_d[t], v_.ap, reads=[v_])
        pipelined(tiles, load, body)
    P.barrier()
    scale = 128.0 ** -0.5
    ps = S["ps"]
    with ExitStack() as es:
        kT = Buf(k.sb(es, "ga_kT", [128, T], BF16)[:])
        vv = Buf(k.sb(es, "ga_v", [128, NT, 128], BF16)[:])
        qTb = [Buf(k.sb(es, "ga_qT%d" % i, [128, T], BF16)[:]) for i in range(2)]
        ones = Buf(k.sb(es, "ga_ones", [128, 128], BF16)[:])
        P.op("pool", lambda e: e.memset(ones.ap, 1.0), writes=[ones])
        pT = [Buf(k.sb(es, "ga_pT%d" % i, [128, 1024], BF16)[:]) for i in range(3)]
        ps2 = S["ps2"]
        rec = [Buf(k.sb(es, "ga_rec%d" % i, [128, 512])[:]) for i in range(2)]
        oT = [Buf(k.sb(es, "ga_oT%d" % i, [128, 512], BF16)[:]) for i in range(2)]
        blocks = [(q0, 512, list(range(NT))) for q0 in range(0, SEQ, 512)]
        if need_ctx:
            blocks.append((SEQ, CTX, [32, 33]))
        bc = 0
        pc = 0
        for kh in range(4):
            P.dma("sp", kT.ap, qkT_d[16 + kh], writes=[kT])
            P.dma("act", vv.ap, v_d[:, :, kh * 128:(kh + 1) * 128].rearrange("t p d -> p t d"), writes=[vv])
            for hq in range(4):
                h = kh * 4 + hq
                q_ = qTb[h % 2]
                P.dma("sp", q_.ap, qkT_d[h], writes=[q_])
                for (q0, nq, kts) in blocks:
                    acc_o, acc_s = ps[4 + 2 * (bc % 2)], ps[5 + 2 * (bc % 2)]
                    r_, o_ = rec[bc % 2], oT[bc % 2]
                    bc += 1
                    npair = len(kts) // 2
                    sts = {}
                    pTs = {}

                    def emit_qk(pi):
                        nonlocal pc
                        st = ps2[pc % 2]
                        p_ = pT[pc % 3]
                        pc += 1
                        sts[pi], pTs[pi] = st, p_
                        for hf in range(2):
                            kt = kts[2 * pi + hf]
                            P.op("pe", lambda e: e.matmul(st.ap[:, hf * 512:hf * 512 + nq], lhsT=kT.ap[:, kt * 128:(kt + 1) * 128], rhs=q_.ap[:, q0:q0 + nq], start=True, stop=True),
                                 reads=[kT, q_], writes=[st])
                        P.op("act", lambda e: e.activation(out=p_.ap.rearrange("p (a n) -> p a n", a=2)[:, :, 0:nq], in_=st.ap.rearrange("p (a n) -> p a n", a=2)[:, :, 0:nq],
                                                           func=AF.Exp, scale=float(scale)), reads=[st], writes=[p_])
                    emit_qk(0)
                    for pi in range(npair):
                        if pi + 1 < npair:
                            emit_qk(pi + 1)
                        p_ = pTs[pi]
                        for hf in range(2):
                            kt = kts[2 * pi + hf]
                            first = (pi == 0 and hf == 0)
                            last = (pi == npair - 1 and hf == 1)
                            P.op("pe", lambda e: e.matmul(acc_o.ap[:, 0:nq], lhsT=vv.ap[:, kt, :], rhs=p_.ap[:, hf * 512:hf * 512 + nq], start=first, stop=last),
                                 reads=[vv, p_], writes=[acc_o])
                            P.op("pe", lambda e: e.matmul(acc_s.ap[:, 0:nq], lhsT=ones.ap, rhs=p_.ap[:, hf * 512:hf * 512 + nq], start=first, stop=last),
                                 reads=[ones, p_], writes=[acc_s])
                    P.op("dve", lambda e: e.reciprocal(out=r_.ap[:, 0:nq], in_=acc_s.ap[:, 0:nq]), reads=[acc_s], writes=[r_])
                    P.op("dve", lambda e: e.tensor_tensor(out=o_.ap[:, 0:nq], in0=acc_o.ap[:, 0:nq], in1=r_.ap[:, 0:nq], op=ALU.mult), reads=[acc_o, r_], writes=[o_])
                    t0 = q0 // 128
                    P.dma("sp", hT_d[t0:t0 + nq // 128, :, h, :].rearrange("t p q -> p t q"), o_.ap[:, 0:nq].rearrange("p (t q) -> p t q", q=128), reads=[o_])
    P.barrier()
    with ExitStack() as es:
        epi = make_store_epi(S, es, ymix, tag="gq4s")
        linear_tm(S, hT_d, KC, c_w_out[j], D, list(range(NT if need_ctx else NTL)), epi, tag="gq4")


def mix_mlstm(S, j, a_w_in, a_b_gate, a_norm_g, a_w_out, tri_d, need_ctx):
    k, P, hT_d, ymix, ident, ps = S["k"], S["P"], S["hT_d"], S["ymix"], S["ident"], S["ps"]
    tiles = list(range(NT))
    qkT_d = k.dscr("a_qkT%d" % j, [NT, 128, 16, 128], BF16)
    kt_d = k.dscr("a_kt%d" % j, [T, 1024], BF16)
    v_d = k.dscr("a_v%d" % j, [T, 2048], BF16)
    o_d = k.dscr("a_o%d" % j, [T, 2048])
    g_d = k.dscr("a_g%d" % j, [T, 16])
    hs_d = [k.dscr("a_hs%d_%d" % (j, d_), [T, 2048]) for d_ in range(2)]
    with ExitStack() as es:
        stg = [Buf(k.sb(es, "ma_fs%d" % i, [128, 512], BF16)[:]) for i in range(4)]
        cnt = [0]

        def epi_f(c, t0, ntl, pt):
            s = stg[cnt[0] % 4]
            cnt[0] += 1
            n = ntl * 128
            sc = 1.0 if c < 8 else 1.0 / 16.0
            P.op("act", lambda e: e.activation(out=s.ap[:, 0:n], in_=pt.ap[:, 0:n], func=AF.Copy, scale=float(sc)), reads=[pt], writes=[s])
            P.dma("sp", qkT_d[t0:t0 + ntl, :, c, :].rearrange("t p q -> p t q"), s.ap[:, 0:n].rearrange("p (t q) -> p t q", q=128), reads=[s])
        linear_fm(S, hT_d, KC, a_w_in[j][:, 0:2048], 2048, tiles, epi_f, tag="ma1")
    P.barrier()
    with ExitStack() as es:
        stb = [Buf(k.sb(es, "ma_sb%d" % i, [128, 512], BF16)[:]) for i in range(3)]
        stf = [Buf(k.sb(es, "ma_sf%d" % i, [128, 512])[:]) for i in range(3)]
        cnt = [0]

        def epi_t(t, n0, n, pt):
            i = cnt[0]
            cnt[0] += 1
            rows = slice(t * 128, (t + 1) * 128)
            if n0 < 1024:
                s = stb[i % 3]
                P.op("act", lambda e: e.activation(out=s.ap, in_=pt.ap, func=AF.Copy, scale=1.0 / 16.0), reads=[pt], writes=[s])
                P.dma("sp", kt_d[rows, n0:n0 + 512], s.ap, reads=[s])
            elif n0 < 3072:
                s = stb[i % 3]
                P.op("dve", lambda e: e.tensor_copy(out=s.ap, in_=pt.ap), reads=[pt], writes=[s])
                P.dma("sp", v_d[rows, n0 - 1024:n0 - 1024 + 512], s.ap, reads=[s])
            elif n0 < 5120:
                s = stf[i % 3]
                P.op("act", lambda e: e.activation(out=s.ap, in_=pt.ap, func=AF.Sigmoid), reads=[pt], writes=[s])
                P.dma("sp", o_d[rows, n0 - 3072:n0 - 3072 + 512], s.ap, reads=[s])
            else:
                s = stf[i % 3]
                P.op("dve", lambda e: e.tensor_copy(out=s.ap[:, 0:16], in_=pt.ap[:, 0:16]), reads=[pt], writes=[s])
                P.dma("sp", g_d[rows, :], s.ap[:, 0:16], reads=[s])
        linear_tm(S, hT_d, KC, a_w_in[j][:, 1024:A_IN], A_IN - 1024, tiles, epi_t, tag="ma2")
    P.barrier()
    with ExitStack() as es:
        tri = Buf(k.sb(es, "ma_tri", [128, 2, 128])[:])
        P.dma("sp", tri.ap, tri_d.rearrange("a s t -> s a t"), writes=[tri])
        onesf = Buf(k.sb(es, "ma_1f", [128, 128])[:])
        onesb = Buf(k.sb(es, "ma_1b", [128, 2], BF16)[:])
        P.op("pool", lambda e: e.memset(onesf.ap, 1.0), writes=[onesf])
        P.op("pool", lambda e: e.memset(onesb.ap, 1.0), writes=[onesb])
        G = Buf(k.sb(es, "ma_G", [128, NT, 16])[:])
        bg = Buf(k.sb(es, "ma_bg", [128, 16])[:])
        P.dma("sp", G.ap, g_d.rearrange("(c s) g -> s c g", s=128), writes=[G])
        P.dma("sp", bg.ap, a_b_gate[j:j + 1, :].to_broadcast([128, 16]), writes=[bg])
        P.op("dve", lambda e: e.tensor_tensor(out=G.ap, in0=G.ap, in1=bg.ap.unsqueeze(1).to_broadcast([128, NT, 16]), op=ALU.add), reads=[G, bg], writes=[G])
        P.op("act", lambda e: e.activation(out=G.ap, in_=G.ap, func=AF.Tanh, scale=1.0 / 15.0), reads=[G], writes=[G])
        P.op("dve", lambda e: e.tensor_scalar_mul(out=G.ap, in0=G.ap, scalar1=15.0), reads=[G], writes=[G])
        LI = Buf(k.sb(es, "ma_LI", [128, NT, 8])[:])
        LFN = Buf(k.sb(es, "ma_LFN", [128, NT, 8])[:])
        G4 = G.ap.rearrange("p c (a h) -> p c a h", a=4)
        LI4 = LI.ap.rearrange("p c (a h) -> p c a h", a=2)
        LF4 = LFN.ap.rearrange("p c (a h) -> p c a h", a=2)
        for d_ in range(2):
            P.op("dve", lambda e: e.tensor_copy(out=LI4[:, :, d_, :], in_=G4[:, :, 2 * d_, :]), reads=[G], writes=[LI])
            P.op("act", lambda e: e.activation(out=LF4[:, :, d_, :], in_=G4[:, :, 2 * d_ + 1, :], func=AF.Exp, scale=-1.0), reads=[G], writes=[LFN])
        P.op("act", lambda e: e.activation(out=LFN.ap, in_=LFN.ap, func=AF.Ln, bias=1.0), reads=[LFN], writes=[LFN])
        NB = Buf(k.sb(es, "ma_NB", [128, NT, 8])[:])
        GT = Buf(k.sb(es, "ma_GT", [128, NT, 8])[:])
        AA = Buf(k.sb(es, "ma_A", [128, NT, 8])[:])
        NB4 = NB.ap.rearrange("p c (a h) -> p c a h", a=2)
        pt = ps[0]
        for d_ in range(2):
            P.op("pe", lambda e: e.matmul(pt.ap[:, 0:NT * 4], lhsT=tri.ap[:, d_, :], rhs=LF4[:, :, d_, :], start=True, stop=True), reads=[tri, LFN], writes=[pt])
            P.op("dve", lambda e: e.tensor_copy(out=NB4[:, :, d_, :], in_=pt.ap[:, 0:NT * 4].rearrange("p (c h) -> p c h", h=4)), reads=[pt], writes=[NB])
        pt = ps[1]
        P.op("pe", lambda e: e.matmul(pt.ap[:, 0:NT * 8], lhsT=onesf.ap, rhs=LFN.ap, start=True, stop=True), reads=[onesf, LFN], writes=[pt])
        P.op("dve", lambda e: e.tensor_copy(out=GT.ap, in_=pt.ap[:, 0:NT * 8].rearrange("p (c h) -> p c h", h=8)), reads=[pt], writes=[GT])
        P.op("dve", lambda e: e.tensor_tensor(out=AA.ap, in0=LI.ap, in1=NB.ap, op=ALU.add), reads=[LI, NB], writes=[AA])
        AMX = Buf(k.sb(es, "ma_AMX", [128, NT * 8])[:])
        A2 = AA.ap.rearrange("p c h -> p (c h)")
        col = Buf(k.sb(es, "ma_col", [128, 1])[:])
        colb = Buf(k.sb(es, "ma_colb", [128, 128])[:])
        for (c0, n) in ((0, 128), (128, 128), (256, NT * 8 - 256)):
            pt = ps[2]
            P.op("pe", lambda e: e.transpose(out=pt.ap[0:n, 0:128], in_=A2[:, c0:c0 + n], identity=ident.ap), reads=[AA, ident], writes=[pt])
            P.op("dve", lambda e: e.reduce_max(out=col.ap[0:n, :], in_=pt.ap[0:n, 0:128], axis=AX.X), reads=[pt], writes=[col])
            P.op("dve", lambda e: e.tensor_copy(out=colb.ap[0:n, :], in_=col.ap[0:n, 0:1].to_broadcast([n, 128])), reads=[col], writes=[colb])
            pt2 = ps[3]
            P.op("pe", lambda e: e.matmul(pt2.ap[:, 0:n], lhsT=colb.ap[0:n, :], rhs=ident.ap[0:n, 0:n], start=True, stop=True), reads=[colb, ident], writes=[pt2])
            P.op("dve", lambda e: e.tensor_copy(out=AMX.ap[:, c0:c0 + n], in_=pt2.ap[:, 0:n]), reads=[pt2], writes=[AMX])
        AMX3 = AMX.ap.rearrange("p (c h) -> p c h", h=8)
        RR = Buf(k.sb(es, "ma_R", [128, NT, 8])[:])
        DM = Buf(k.sb(es, "ma_DM", [128, NT, 8])[:])
        mm = Buf(k.sb(es, "ma_m", [128, 8])[:])
        P.op("dve", lambda e: e.memset(mm.ap, 0.0), writes=[mm])
        order = [[32, 33] + list(range(32)), [33, 32] + list(range(31, -1, -1))]
        for i in range(NT):
            for d_ in range(2):
                c = order[d_][i]
                hs = slice(d_ * 4, d_ * 4 + 4)
                P.op("dve", lambda e: e.tensor_tensor(out=RR.ap[:, c, hs], in0=mm.ap[:, hs], in1=AMX3[:, c, hs], op=ALU.max), reads=[mm, AMX], writes=[RR])
                P.op("dve", lambda e: e.tensor_tensor(out=DM.ap[:, c, hs], in0=mm.ap[:, hs], in1=RR.ap[:, c, hs], op=ALU.subtract), reads=[mm, RR], writes=[DM])
                P.op("dve", lambda e: e.tensor_tensor(out=mm.ap[:, hs], in0=RR.ap[:, c, hs], in1=GT.ap[:, c, hs], op=ALU.subtract), reads=[RR, GT], writes=[mm])
        EE = Buf(k.sb(es, "ma_E", [128, NT, 8])[:])
        CL = Buf(k.sb(es, "ma_CL", [128, NT, 8])[:])
        P.op("dve", lambda e: e.tensor_tensor(out=EE.ap, in0=AA.ap, in1=RR.ap, op=ALU.subtract), reads=[AA, RR], writes=[EE])
        P.op("act", lambda e: e.activation(out=EE.ap, in_=EE.ap, func=AF.Exp), reads=[EE], writes=[EE])
        P.op("dve", lambda e: e.tensor_tensor(out=CL.ap, in0=NB.ap, in1=RR.ap, op=ALU.subtract), reads=[NB, RR], writes=[CL])
        P.op("act", lambda e: e.activation(out=CL.ap, in_=CL.ap, func=AF.Exp), reads=[CL], writes=[CL])
        P.op("act", lambda e: e.activation(out=DM.ap, in_=DM.ap, func=AF.Exp), reads=[DM], writes=[DM])
        C32 = [Buf(k.sb(es, "ma_C%d" % i, [128, 2, 512])[:]) for i in range(8)]
        N32 = [Buf(k.sb(es, "ma_N%d" % i, [128, 2])[:]) for i in range(8)]
        Cb = [Buf(k.sb(es, "ma_Cb%d" % i, [128, 2, 512], BF16)[:]) for i in range(8)]
        Nb = [Buf(k.sb(es, "ma_Nb%d" % i, [128, 2], BF16)[:]) for i in range(8)]
        for i in range(8):
            P.op("pool", lambda e: e.memset(C32[i].ap, 0.0), writes=[C32[i]])
            P.op("pool", lambda e: e.memset(N32[i].ap, 0.0), writes=[N32[i]])
        qk = [[Buf(k.sb(es, "ma_qk%d_%d" % (d_, i), [128, 16, 128], BF16)[:]) for i in range(2)] for d_ in range(2)]
        ktm = [[Buf(k.sb(es, "ma_kt%d_%d" % (d_, i), [128, 1024], BF16)[:]) for i in range(2)] for d_ in range(2)]
        vtm = [[Buf(k.sb(es, "ma_vt%d_%d" % (d_, i), [128, 2048], BF16)[:]) for i in range(2)] for d_ in range(2)]
        St = [Buf(k.sb(es, "ma_St%d" % i, [128, 128], BF16)[:]) for i in range(4)]
        ktl = [Buf(k.sb(es, "ma_ktl%d" % i, [128, 256], BF16)[:]) for i in range(4)]
        dn = [Buf(k.sb(es, "ma_dn%d" % i, [128, 1])[:]) for i in range(4)]
        ho = [Buf(k.sb(es, "ma_ho%d" % i, [128, 512])[:]) for i in range(4)]
        it = 0

        def scan_load(i):
            for d_ in range(2):
                c = order[d_][i]
                rows = slice(c * 128, (c + 1) * 128)
                qk_, kt_, vt_ = qk[d_][i % 2], ktm[d_][i % 2], vtm[d_][i % 2]
                P.dma("sp", qk_.ap, qkT_d[c], writes=[qk_])
                P.dma("sp", kt_.ap, kt_d[rows, :], writes=[kt_])
                P.dma("sp", vt_.ap, v_d[rows, :], writes=[vt_])
        scan_load(0)
        for i in range(NT):
            if i + 1 < NT:
                scan_load(i + 1)
            for d_ in range(2):
                c = order[d_][i]
                rows = slice(c * 128, (c + 1) * 128)
                qk_, kt_, vt_ = qk[d_][i % 2], ktm[d_][i % 2], vtm[d_][i % 2]
                for hh in range(4):
                    ch = d_ * 4 + hh
                    e_ = EE.ap[:, c, ch:ch + 1]
                    cd = DM.ap[:, c, ch:ch + 1]
                    cl = CL.ap[:, c, ch:ch + 1]
                    s_t, k_l, d_n, h_o = St[it % 4], ktl[it % 4], dn[it % 4], ho[it % 4]
                    p_st, p_num = ps[it % 2], ps[2 + it % 2]
                    p_c = [ps[4], ps[5]]
                    p_den, p_nu = ps[6], ps[7]
                    it += 1
                    vh = vt_.ap[:, hh * 512:(hh + 1) * 512]
                    P.op("dve", lambda e: e.tensor_scalar(out=Cb[ch].ap, in0=C32[ch].ap, scalar1=cd, scalar2=None, op0=ALU.mult), reads=[C32[ch], DM], writes=[Cb[ch]])
                    P.op("dve", lambda e: e.tensor_scalar(out=Nb[ch].ap, in0=N32[ch].ap, scalar1=cd, scalar2=None, op0=ALU.mult), reads=[N32[ch], DM], writes=[Nb[ch]])
                    for dc in range(2):
                        P.op("pe", lambda e: e.matmul(p_st.ap[:, 0:128], lhsT=qk_.ap[:, 8 + hh * 2 + dc, :], rhs=qk_.ap[:, hh * 2 + dc, :], start=(dc == 0), stop=(dc == 1)),
                             reads=[qk_], writes=[p_st])
                    P.op("dve", lambda e: e.scalar_tensor_tensor(out=s_t.ap, in0=p_st.ap[:, 0:128], scalar=e_, in1=tri.ap[:, d_, :], op0=ALU.mult, op1=ALU.mult),
                         reads=[p_st, EE, tri], writes=[s_t])
                    for dc in range(2):
                        P.op("pe", lambda e: e.matmul(p_num.ap, lhsT=qk_.ap[:, hh * 2 + dc, :], rhs=Cb[ch].ap[:, dc, :], start=(dc == 0), stop=False),
                             reads=[qk_, Cb[ch]], writes=[p_num])
                    P.op("pe", lambda e: e.matmul(p_num.ap, lhsT=s_t.ap, rhs=vh, start=False, stop=True), reads=[s_t, vt_], writes=[p_num])
                    for dc in range(2):
                        P.op("pe", lambda e: e.matmul(p_den.ap[:, 0:1], lhsT=qk_.ap[:, hh * 2 + dc, :], rhs=Nb[ch].ap[:, dc:dc + 1], start=(dc == 0), stop=False),
                             reads=[qk_, Nb[ch]], writes=[p_den])
                    P.op("pe", lambda e: e.matmul(p_den.ap[:, 0:1], lhsT=s_t.ap, rhs=onesb.ap[:, 0:1], start=False, stop=True), reads=[s_t, onesb], writes=[p_den])
                    P.op("act", lambda e: e.activation(out=d_n.ap, in_=p_den.ap[:, 0:1], func=AF.Abs), reads=[p_den], writes=[d_n])
                    P.op("dve", lambda e: e.tensor_tensor(out=d_n.ap, in0=d_n.ap, in1=cl, op=ALU.max), reads=[d_n, CL], writes=[d_n])
                    P.op("dve", lambda e: e.reciprocal(out=d_n.ap, in_=d_n.ap), reads=[d_n], writes=[d_n])
                    P.op("act", lambda e: e.activation(out=h_o.ap, in_=p_num.ap, func=AF.Copy, scale=d_n.ap[:, 0:1]), reads=[p_num, d_n], writes=[h_o])
                    P.dma("act", hs_d[d_][rows, hh * 512:(hh + 1) * 512], h_o.ap, reads=[h_o])
                    P.op("pool", lambda e: e.tensor_scalar(out=k_l.ap, in0=kt_.ap[:, hh * 256:(hh + 1) * 256], scalar1=e_, scalar2=None, op0=ALU.mult),
                         reads=[kt_, EE], writes=[k_l])
                    for dc in range(2):
                        P.op("pe", lambda e: e.matmul(p_c[dc].ap, lhsT=k_l.ap[:, dc * 128:(dc + 1) * 128], rhs=vh, start=True, stop=True), reads=[k_l, vt_], writes=[p_c[dc]])
                        P.op("pe", lambda e: e.matmul(p_nu.ap[:, dc:dc + 1], lhsT=k_l.ap[:, dc * 128:(dc + 1) * 128], rhs=onesb.ap[:, 0:1], start=True, stop=True),
                             reads=[k_l, onesb], writes=[p_nu])
                    for dc in range(2):
                        P.op("dve", lambda e: e.scalar_tensor_tensor(out=C32[ch].ap[:, dc, :], in0=C32[ch].ap[:, dc, :], scalar=cd, in1=p_c[dc].ap, op0=ALU.mult, op1=ALU.add),
                             reads=[C32[ch], DM, p_c[dc]], writes=[C32[ch]])
                    P.op("dve", lambda e: e.scalar_tensor_tensor(out=N32[ch].ap, in0=N32[ch].ap, scalar=cd, in1=p_nu.ap[:, 0:2], op0=ALU.mult, op1=ALU.add),
                         reads=[N32[ch], DM, p_nu], writes=[N32[ch]])
    P.barrier()
    out_tiles = list(range(NT if need_ctx else NTL))
    with ExitStack() as es:
        ng = Buf(k.sb(es, "mo_ng", [128, D])[:])
        P.dma("sp", ng.ap, a_norm_g[j:j + 1, :].to_broadcast([128, D]), writes=[ng])
        hb = [[Buf(k.sb(es, "mo_h%d_%d" % (d_, i), [128, D])[:]) for i in range(2)] for d_ in range(2)]
        ob = [Buf(k.sb(es, "mo_o%d" % i, [128, D])[:]) for i in range(2)]
        sq = Buf(k.sb(es, "mo_sq", [128, D])[:])
        ss = Buf(k.sb(es, "mo_ss", [128, 4])[:])
        hT = [Buf(k.sb(es, "mo_hT%d" % i, [128, KC, 128], BF16)[:]) for i in range(2)]
        def load(t):
            rows = slice(t * 128, (t + 1) * 128)
            h0, h1, o_ = hb[0][t % 2], hb[1][t % 2], ob[t % 2]
            P.dma("sp", h0.ap, hs_d[0][rows, :], writes=[h0])
            P.dma("act", h1.ap, hs_d[1][rows, :], writes=[h1])
            P.dma("sp", o_.ap, o_d[rows, :], writes=[o_])

        def body(t):
            h0, h1, o_, h_T = hb[0][t % 2], hb[1][t % 2], ob[t % 2], hT[t % 2]
            P.op("pool", lambda e: e.tensor_tensor(out=h0.ap, in0=h0.ap, in1=h1.ap, op=ALU.add), reads=[h0, h1], writes=[h0])
            P.op("act", lambda e: e.activation(out=sq.ap, in_=h0.ap, func=AF.Square), reads=[h0], writes=[sq])
            P.op("dve", lambda e: e.tensor_reduce(out=ss.ap, in_=sq.ap.rearrange("p (h d) -> p h d", d=512), axis=AX.X, op=ALU.add), reads=[sq], writes=[ss])
            rsqrt_eps(P, ss, ss.ap, [ss], 1.0 / 512.0)
            h3 = h0.ap.rearrange("p (h d) -> p h d", d=512)
            P.op("dve", lambda e: e.tensor_tensor(out=h3, in0=h3, in1=ss.ap.unsqueeze(2).to_broadcast([128, 4, 512]), op=ALU.mult), reads=[h0, ss], writes=[h0])
            P.op("pool", lambda e: e.tensor_tensor(out=o_.ap, in0=o_.ap, in1=ng.ap, op=ALU.mult), reads=[o_, ng], writes=[o_])
            P.op("dve", lambda e: e.tensor_tensor(out=h0.ap, in0=h0.ap, in1=o_.ap, op=ALU.mult), reads=[h0, o_], writes=[h0])
            transpose_to_T(S, h0, KC, h_T, t)
            P.dma("sp", hT_d[t], h_T.ap, reads=[h_T])
        pipelined(out_tiles, load, body)
    P.barrier()
    with ExitStack() as es:
        epi = make_store_epi(S, es, ymix, tag="mo5s")
        linear_tm(S, hT_d, KC, a_w_out[j], D, out_tiles, epi, tag="mo5")


def top16(P, src, work, vals, idxs, reads_extra=()):
    sv, vv, iv, wv = src, vals, idxs, work
    P.op("dve", lambda e: e.max(out=vv.ap[:, 0:8], in_=sv.ap), reads=[sv], writes=[vv])
    P.op("dve", lambda e: e.max_index(out=iv.ap[:, 0:8], in_max=vv.ap[:, 0:8], in_values=sv.ap), reads=[sv, vv], writes=[iv])
    P.op("dve", lambda e: e.match_replace(out=wv.ap, in_to_replace=vv.ap[:, 0:8], in_values=sv.ap, imm_value=NEG), reads=[sv, vv], writes=[wv])
    P.op("dve", lambda e: e.max(out=vv.ap[:, 8:16], in_=wv.ap), reads=[wv], writes=[vv])
    P.op("dve", lambda e: e.max_index(out=iv.ap[:, 8:16], in_max=vv.ap[:, 8:16], in_values=wv.ap), reads=[wv, vv], writes=[iv])


def phase_peer(S, li, p_w_q, p_k1, p_k2, p_u, p_v, iota_d, nt, enable=True):
    k, P, hT_d, h_d, ymix, ident, ps = S["k"], S["P"], S["hT_d"], S["h_d"], S["ymix"], S["ident"], S["ps"]
    tiles = list(range(nt))
    qpT_d = k.dscr("p_qT%d" % li, [16, 128, T])
    with ExitStack() as es:
        stg = [Buf(k.sb(es, "pq_s%d" % i, [128, 512])[:]) for i in range(4)]
        cnt = [0]

        def epi_f(c, t0, ntl, pt):
            s = stg[cnt[0] % 4]
            cnt[0] += 1
            n = ntl * 128
            P.op("act" if cnt[0] % 2 else "dve",
                 (lambda e: e.activation(out=s.ap[:, 0:n], in_=pt.ap[:, 0:n], func=AF.Copy)) if cnt[0] % 2 else (lambda e: e.tensor_copy(out=s.ap[:, 0:n], in_=pt.ap[:, 0:n])),
                 reads=[pt], writes=[s])
            P.dma("sp", qpT_d[c, :, t0 * 128:t0 * 128 + n], s.ap[:, 0:n], reads=[s])
        linear_fm(S, hT_d, KC, p_w_q[li], D, tiles, epi_f, tag="pq")
    P.barrier()
    with ExitStack() as es:
        kraw = Buf(k.sb(es, "pp_kraw", [128, 2, 128])[:])
        kT = Buf(k.sb(es, "pp_kT", [128, 2, 128])[:])
        P.dma("sp", kraw.ap[:, 0, :], p_k1[li], writes=[kraw])
        P.dma("sp", kraw.ap[:, 1, :], p_k2[li], writes=[kraw])
        for hf in range(2):
            pt = next_ps(S)
            P.op("pe", lambda e: e.transpose(out=pt.ap[:, 0:128], in_=kraw.ap[:, hf, :], identity=ident.ap), reads=[kraw, ident], writes=[pt])
            P.op("dve", lambda e: e.tensor_copy(out=kT.ap[:, hf, :], in_=pt.ap[:, 0:128]), reads=[pt], writes=[kT])
        io16 = Buf(k.sb(es, "pp_io", [128, 16])[:])
        P.dma("sp", io16.ap, iota_d[0:1, 0:16].to_broadcast([128, 16]), writes=[io16])
        qT = [Buf(k.sb(es, "pp_qT%d" % i, [128, 16, 128])[:]) for i in range(2)]
        sc = Buf(k.sb(es, "pp_sc", [128, 16, 128])[:])
        wk = Buf(k.sb(es, "pp_wk", [128, 256])[:])
        V12 = Buf(k.sb(es, "pp_V12", [128, 16, 16])[:])
        I12 = Buf(k.sb(es, "pp_I12", [128, 16, 16], U32)[:])
        I12f = Buf(k.sb(es, "pp_I12f", [128, 16, 16])[:])
        cand = Buf(k.sb(es, "pp_cand", [128, 8, 256])[:])
        SC = Buf(k.sb(es, "pp_SC", [128, 8, 16])[:])
        CI = Buf(k.sb(es, "pp_CI", [128, 8, 16], U32)[:])
        IA = Buf(k.sb(es, "pp_IA", [128, 8, 16], U32)[:])
        IB = Buf(k.sb(es, "pp_IB", [128, 8, 16], U32)[:])
        IAf = Buf(k.sb(es, "pp_IAf", [128, 8, 16])[:])
        IBf = Buf(k.sb(es, "pp_IBf", [128, 8, 16])[:])
        oh = Buf(k.sb(es, "pp_oh", [128, 8, 16, 16])[:])
        EI = Buf(k.sb(es, "pp_EI", [128, 8, 16])[:])
        EJ = Buf(k.sb(es, "pp_EJ", [128, 8, 16])[:])
        gate = Buf(k.sb(es, "pp_gate", [128, 8, 16])[:])
        gs = Buf(k.sb(es, "pp_gs", [128, 8])[:])
        if PEER_DENSE:
            Gd = S["Gd"]
            io128 = Buf(k.sb(es, "pp_io128", [128, 128])[:])
            P.dma("sp", io128.ap, iota_d[0:1, :].to_broadcast([128, 128]), writes=[io128])
            TT = Buf(k.sb(es, "pp_TT", [128, 3, 128])[:])
            TTb = Buf(k.sb(es, "pp_TTb", [128, 2, 128], BF16)[:])
            io128b = Buf(k.sb(es, "pp_io128b", [128, 128], BF16)[:])
            P.op("dve", lambda e: e.tensor_copy(out=io128b.ap, in_=io128.ap), reads=[io128], writes=[io128b])
            OH1 = [Buf(k.sb(es, "pp_OH1%d" % i, [128, 64, 128], BF16)[:]) for i in range(2)]
            OH2 = [Buf(k.sb(es, "pp_OH2%d" % i, [128, 64, 128], BF16)[:]) for i in range(2)]
            G_sb = Buf(k.sb(es, "pp_Gsb", [128, 128, 128], BF16)[:])
            G_sb2 = Buf(G_sb.ap)
            hb = acc = EX = [None, None]
        else:
            hb = [Buf(k.sb(es, "pp_h%d" % i, [128, D])[:]) for i in range(2)]
            EX = [Buf(k.sb(es, "pp_EX%d" % i, [128, 128], U32)[:]) for i in range(2)]
            av = Buf(k.sb(es, "pp_a", [128, 128])[:])
            hid = Buf(k.sb(es, "pp_hid", [128, 128])[:])
            junk = Buf(k.sb(es, "pp_junk", [128, D])[:])
            acc = [Buf(k.sb(es, "pp_acc%d" % i, [128, D])[:]) for i in range(2)]
            NG = 6
            gb = [Buf(k.sb(es, "pp_g%d" % i, [128, D])[:]) for i in range(NG)]
        gi = 0
        sc2 = [sc, Buf(k.sb(es, "pp_scb", [128, 16, 128])[:])]

        def emit_scores(t):
            q_ = qT[t % 2]
            sc_ = sc2[t % 2]
            P.dma("sp", q_.ap, qpT_d[:, :, t * 128:(t + 1) * 128].rearrange("c d t -> d c t"), writes=[q_])
            for g in range(4):
                pt = ps[5 + (t * 4 + g) % 3]
                for q4 in range(4):
                    c = g * 4 + q4
                    P.op("pe", lambda e: e.matmul(pt.ap[:, q4 * 128:(q4 + 1) * 128], lhsT=q_.ap[:, c, :], rhs=kT.ap[:, c % 2, :], start=True, stop=True),
                         reads=[q_, kT], writes=[pt])
                P.op("act", lambda e: e.activation(out=sc_.ap[:, g * 4:(g + 1) * 4, :], in_=pt.ap.rearrange("p (a b) -> p a b", a=4), func=AF.Copy), reads=[pt], writes=[sc_])
        emit_scores(tiles[0])
        for ti, t in enumerate(tiles):
            q_, h_, ex_, acc_ = qT[t % 2], hb[t % 2], EX[t % 2], acc[t % 2]
            if ti + 1 < len(tiles):
                emit_scores(tiles[ti + 1])
            sc = sc2[t % 2]
            if not PEER_DENSE:
                P.dma("act", h_.ap, h_d[t * 128:(t + 1) * 128, :], writes=[h_])
            for c in range(16):
                sv = Buf(sc.ap[:, c, :]); sv.w = sc.w
                wv = Buf(wk.ap[:, 0:128]); wv.w = wk.w; wv.r = wk.r
                vv = Buf(V12.ap[:, c, :]); vv.w = V12.w; vv.r = V12.r
                iv = Buf(I12.ap[:, c, :]); iv.w = I12.w; iv.r = I12.r
                top16(P, sv, wv, vv, iv)
                wk.w, V12.w, I12.w = wv.w, vv.w, iv.w
                wk.r, V12.r, I12.r = wv.r, vv.r, iv.r
                sc.r.update(sv.r)
            V4 = V12.ap.rearrange("p (h f) a -> p h f a", f=2)
            P.op("dve", lambda e: e.tensor_tensor(out=cand.ap.rearrange("p h (a b) -> p h a b", b=16),
                                                  in0=V4[:, :, 0, :].unsqueeze(3).to_broadcast([128, 8, 16, 16]),
                                                  in1=V4[:, :, 1, :].unsqueeze(2).to_broadcast([128, 8, 16, 16]), op=ALU.add), reads=[V12], writes=[cand])
            for hh in range(8):
                sv = Buf(cand.ap[:, hh, :]); sv.w = cand.w
                wv = Buf(wk.ap[:, 0:256]); wv.w = wk.w; wv.r = wk.r
                vv = Buf(SC.ap[:, hh, :]); vv.w = SC.w; vv.r = SC.r
                iv = Buf(CI.ap[:, hh, :]); iv.w = CI.w; iv.r = CI.r
                top16(P, sv, wv, vv, iv)
                wk.w, SC.w, CI.w = wv.w, vv.w, iv.w
                wk.r, SC.r, CI.r = wv.r, vv.r, iv.r
                cand.r.update(sv.r)
            P.op("dve", lambda e: e.tensor_tensor(out=gate.ap, in0=SC.ap, in1=SC.ap[:, :, 0:1].to_broadcast([128, 8, 16]), op=ALU.subtract), reads=[SC], writes=[gate])
            P.op("act", lambda e: e.activation(out=gate.ap, in_=gate.ap, func=AF.Exp), reads=[gate], writes=[gate])
            P.op("dve", lambda e: e.tensor_reduce(out=gs.ap, in_=gate.ap, axis=AX.X, op=ALU.add), reads=[gate], writes=[gs])
            P.op("dve", lambda e: e.reciprocal(out=gs.ap, in_=gs.ap), reads=[gs], writes=[gs])
            P.op("dve", lambda e: e.tensor_tensor(out=gate.ap, in0=gate.ap, in1=gs.ap.unsqueeze(2).to_broadcast([128, 8, 16]), op=ALU.mult), reads=[gate, gs], writes=[gate])
            P.op("dve", lambda e: e.tensor_single_scalar(out=IA.ap, in_=CI.ap, scalar=4, op=ALU.logical_shift_right), reads=[CI], writes=[IA])
            P.op("dve", lambda e: e.tensor_single_scalar(out=IB.ap, in_=CI.ap, scalar=15, op=ALU.bitwise_and), reads=[CI], writes=[IB])
            P.op("dve", lambda e: e.tensor_copy(out=IAf.ap, in_=IA.ap), reads=[IA], writes=[IAf])
            P.op("dve", lambda e: e.tensor_copy(out=IBf.ap, in_=IB.ap), reads=[IB], writes=[IBf])
            P.op("dve", lambda e: e.tensor_copy(out=I12f.ap, in_=I12.ap), reads=[I12], writes=[I12f])
            I4 = I12f.ap.rearrange("p (h f) a -> p h f a", f=2)
            io4 = io16.ap.unsqueeze(1).unsqueeze(1).to_broadcast([128, 8, 16, 16])
            for (src, f, dst) in ((IAf, 0, EI), (IBf, 1, EJ)):
                P.op("dve", lambda e: e.tensor_tensor(out=oh.ap, in0=src.ap.unsqueeze(3).to_broadcast([128, 8, 16, 16]), in1=io4, op=ALU.is_equal), reads=[src, io16], writes=[oh])
                P.op("dve", lambda e: e.tensor_tensor(out=oh.ap, in0=oh.ap, in1=I4[:, :, f, :].unsqueeze(2).to_broadcast([128, 8, 16, 16]), op=ALU.mult), reads=[oh, I12f], writes=[oh])
                P.op("dve", lambda e: e.tensor_reduce(out=dst.ap, in_=oh.ap, axis=AX.X, op=ALU.add), reads=[oh], writes=[dst])
            if PEER_DENSE:
                ptT = ps[4]
                for q3, src in enumerate((EI, EJ, gate)):
                    P.op("pe", lambda e: e.transpose(out=ptT.ap[:, q3 * 128:(q3 + 1) * 128], in_=src.ap.rearrange("p h a -> p (h a)"), identity=ident.ap),
                         reads=[src, ident], writes=[ptT])
                P.op("act", lambda e: e.activation(out=TT.ap.rearrange("p a t -> p (a t)"), in_=ptT.ap[:, 0:384], func=AF.Copy), reads=[ptT], writes=[TT])
                P.op("act", lambda e: e.activation(out=TTb.ap.rearrange("p a t -> p (a t)"), in_=ptT.ap[:, 0:256], func=AF.Copy), reads=[ptT], writes=[TTb])
                iob = io128b.ap.unsqueeze(1).to_broadcast([128, 64, 128])
                for hf in range(2):
                    o1, o2 = OH1[hf], OH2[hf]
                    tsl = slice(hf * 64, (hf + 1) * 64)
                    P.op("dve", lambda e: e.tensor_tensor(out=o1.ap, in0=iob, in1=TTb.ap[:, 0, tsl].unsqueeze(2).to_broadcast([128, 64, 128]), op=ALU.is_equal),
                         reads=[io128b, TTb], writes=[o1])
                    P.op("pool", lambda e: e.tensor_tensor(out=o1.ap, in0=o1.ap, in1=TT.ap[:, 2, tsl].unsqueeze(2).to_broadcast([128, 64, 128]), op=ALU.mult),
                         reads=[o1, TT], writes=[o1])
                    P.op("dve", lambda e: e.tensor_tensor(out=o2.ap, in0=iob, in1=TTb.ap[:, 1, tsl].unsqueeze(2).to_broadcast([128, 64, 128]), op=ALU.is_equal),
                         reads=[io128b, TTb], writes=[o2])
                for t4 in range(32):
                    pt = ps[t4 % 4]
                    hf = t4 // 16
                    for q in range(4):
                        tok = (t4 % 16) * 4 + q
                        P.op("pe", lambda e: e.matmul(pt.ap[:, q * 128:(q + 1) * 128], lhsT=OH2[hf].ap[:, tok, :], rhs=OH1[hf].ap[:, tok, :], start=True, stop=True),
                             reads=[OH1[hf], OH2[hf]], writes=[pt])
                    P.op("act", lambda e: e.activation(out=G_sb.ap[:, :, t4 * 4:(t4 + 1) * 4], in_=pt.ap.rearrange("p (q i) -> p i q", q=4), func=AF.Copy), reads=[pt], writes=[G_sb])
                for cg in range(8):
                    P.dma("sp" if cg % 2 == 0 else "act", Gd[cg * 16:(cg + 1) * 16, :, t * 128:(t + 1) * 128].rearrange("c p t -> p c t"), G_sb.ap[:, cg * 16:(cg + 1) * 16, :], reads=[G_sb])
                continue
            P.op("dve", lambda e: e.scalar_tensor_tensor(out=EI.ap, in0=EI.ap, scalar=128.0, in1=EJ.ap, op0=ALU.mult, op1=ALU.add), reads=[EI, EJ], writes=[EI])
            if li > 0:
                P.op("dve", lambda e: e.tensor_scalar_add(out=EI.ap, in0=EI.ap, scalar1=float(li * 16384)), reads=[EI], writes=[EI])
            P.op("dve", lambda e: e.tensor_copy(out=ex_.ap, in_=EI.ap.rearrange("p h a -> p (h a)")), reads=[EI], writes=[ex_])
            for s in range(128):
                g_ = gb[gi % NG]
                gi += 1
                P.dma("pool", g_.ap, p_u.rearrange("l e d -> (l e) d"), reads=[ex_], writes=[g_], indirect=bass.IndirectOffsetOnAxis(ap=ex_.ap[:, s:s + 1], axis=0))
                P.op("dve", lambda e: e.scalar_tensor_tensor(out=junk.ap, in0=h_.ap, scalar=1.0, in1=g_.ap, op0=ALU.mult, op1=ALU.mult, accum_out=av.ap[:, s:s + 1]),
                     reads=[h_, g_], writes=[junk, av])
            P.op("act", lambda e: e.activation(out=hid.ap, in_=av.ap, func=AF.Gelu_apprx_tanh), reads=[av], writes=[hid])
            P.op("dve", lambda e: e.tensor_tensor(out=hid.ap, in0=hid.ap, in1=gate.ap.rearrange("p h a -> p (h a)"), op=ALU.mult), reads=[hid, gate], writes=[hid])
            for s in range(128):
                g_ = gb[gi % NG]
                gi += 1
                P.dma("pool", g_.ap, p_v.rearrange("l e d -> (l e) d"), reads=[ex_], writes=[g_], indirect=bass.IndirectOffsetOnAxis(ap=ex_.ap[:, s:s + 1], axis=0))
                if s == 0:
                    P.op("dve", lambda e: e.tensor_scalar(out=acc_.ap, in0=g_.ap, scalar1=hid.ap[:, 0:1], scalar2=None, op0=ALU.mult), reads=[g_, hid], writes=[acc_])
                else:
                    P.op("dve", lambda e: e.scalar_tensor_tensor(out=acc_.ap, in0=g_.ap, scalar=hid.ap[:, s:s + 1], in1=acc_.ap, op0=ALU.mult, op1=ALU.add),
                         reads=[g_, hid, acc_], writes=[acc_])
            P.dma("sp", ymix[t * 128:(t + 1) * 128, :], acc_.ap, reads=[acc_])
    if PEER_DENSE:
        P.barrier()
        peer_sweep(S, li, p_u, p_v, nt)


def peer_sweep(S, li, p_u, p_v, nt):
    k, P, hT_d, ymix, identb, ps, psb, Gd = S["k"], S["P"], S["hT_d"], S["ymix"], S["identb"], S["ps"], S["psb"], S["Gd"]
    SCH = 4
    passes = [list(range(i, i + 8)) for i in range(0, NTL, 8)]
    if nt == NT:
        passes[0].append(32)
        passes[1].append(33)
    with ExitStack() as es:
        hTb = [Buf(k.sb(es, "ps_hT%d" % i, [128, KC, 512], BF16)[:]) for i in range(2)]
        hTb.append(Buf(k.sb(es, "ps_hT2", [128, KC, 128], BF16)[:]))
        f_sb = [Buf(k.sb(es, "ps_f%d" % i, [128, D])[:]) for i in range(9)]
        Ubf = [Buf(k.sb(es, "ps_U%d" % i, [128, D], BF16)[:]) for i in range(2)]
        UT = [Buf(k.sb(es, "ps_UT%d" % i, [128, KC, 128], BF16)[:]) for i in range(2 * SCH)]
        Vbf = [Buf(k.sb(es, "ps_V%d" % i, [128, D], BF16)[:]) for i in range(2 * SCH)]
        hid = [Buf(k.sb(es, "ps_hid%d" % i, [128, 512], BF16)[:]) for i in range(2 * SCH)]
        Gc = [Buf(k.sb(es, "ps_G%d" % i, [128, 512], BF16)[:]) for i in range(4)]
        ui = 0
        sci = 0
        gci = 0
        api = 0
        fpi = 0
        evi = 0
        for tl in passes:
            groups = [tl[i:i + 4] for i in range(0, len(tl), 4)]
            for gi_, grp in enumerate(groups):
                for i, t in enumerate(grp):
                    P.dma("sp", hTb[gi_].ap[:, :, i * 128:(i + 1) * 128], hT_d[t], writes=[hTb[gi_]])
            for sc in range(128 // SCH):
                base = (sci % 2) * SCH
                sci += 1
                for c8 in range(SCH):
                    c = sc * SCH + c8
                    u_, ut_, v_ = Ubf[ui % 2], UT[base + c8], Vbf[base + c8]
                    ui += 1
                    P.dma("pool", u_.ap, p_u[li, c * 128:(c + 1) * 128, :], writes=[u_])
                    P.dma("pool", v_.ap, p_v[li, c * 128:(c + 1) * 128, :], writes=[v_])
                    for kc in range(KC):
                        pb = psb[kc // 8]
                        P.op("pe", lambda e: e.transpose(out=pb.ap[:, (kc % 8) * 128:(kc % 8 + 1) * 128], in_=u_.ap[:, kc * 128:(kc + 1) * 128], identity=identb.ap),
                             reads=[u_, identb], writes=[pb])
                    P.op("act", lambda e: e.activation(out=ut_.ap[:, 0:8, :], in_=psb[0].ap.rearrange("p (q t) -> p q t", q=8), func=AF.Copy), reads=[psb[0]], writes=[ut_])
                    P.op("dve", lambda e: e.tensor_copy(out=ut_.ap[:, 8:16, :], in_=psb[1].ap.rearrange("p (q t) -> p q t", q=8)), reads=[psb[1]], writes=[ut_])
                for gi_, grp in enumerate(groups):
                    ng = len(grp) * 128
                    tok0 = grp[0] * 128
                    hset = (gci % 2) * SCH
                    gci += 1
                    for c8 in range(SCH):
                        c = sc * SCH + c8
                        g_ = Gc[api % 4]
                        a_ps = ps[api % 2]
                        api += 1
                        h_ = hid[hset + c8]
                        P.dma("sp", g_.ap[:, 0:ng], Gd[c, :, tok0:tok0 + ng], writes=[g_])
                        for kc in range(KC):
                            P.op("pe", lambda e: e.matmul(a_ps.ap[:, 0:ng], lhsT=UT[base + c8].ap[:, kc, :], rhs=hTb[gi_].ap[:, kc, 0:ng], start=(kc == 0), stop=(kc == KC - 1)),
                                 reads=[UT[base + c8], hTb[gi_]], writes=[a_ps])
                        P.op("act", lambda e: e.activation(out=h_.ap[:, 0:ng], in_=a_ps.ap[:, 0:ng], func=AF.Gelu_apprx_tanh), reads=[a_ps], writes=[h_])
                        P.op("pool", lambda e: e.tensor_tensor(out=h_.ap[:, 0:ng], in0=h_.ap[:, 0:ng], in1=g_.ap[:, 0:ng], op=ALU.mult), reads=[h_, g_], writes=[h_])
                    for i, t in enumerate(grp):
                        fb = f_sb[gi_ * 4 + i]
                        for dch in range(4):
                            f_ps = ps[2 + fpi % 4]
                            fpi += 1
                            for c8 in range(SCH):
                                P.op("pe", lambda e: e.matmul(f_ps.ap, lhsT=hid[hset + c8].ap[:, i * 128:(i + 1) * 128], rhs=Vbf[base + c8].ap[:, dch * 512:(dch + 1) * 512],
                                                              start=(c8 == 0), stop=(c8 == SCH - 1)), reads=[hid[hset + c8], Vbf[base + c8]], writes=[f_ps])
                            dst = fb.ap[:, dch * 512:(dch + 1) * 512]
                            if sc == 0:
                                P.op("act", lambda e: e.activation(out=dst, in_=f_ps.ap, func=AF.Copy), reads=[f_ps], writes=[fb])
                            else:
                                P.op("dve", lambda e: e.tensor_tensor(out=dst, in0=dst, in1=f_ps.ap, op=ALU.add), reads=[fb, f_ps], writes=[fb])
            for i, t in enumerate(tl):
                P.dma("sp", ymix[t * 128:(t + 1) * 128, :], f_sb[i].ap, reads=[f_sb[i]])


_CACHE = {}


def host_consts():
    ident = np.eye(128, dtype=np.float32)
    s = np.arange(128)
    tri = np.stack([(s[:, None] <= s[None, :]), (s[:, None] >= s[None, :])]).astype(np.float32)
    t = np.arange(SEQ)
    rows = (t // 64).astype(np.float32)
    cols = (t % 64).astype(np.float32)
    freqs = (10000.0 ** (-np.arange(32, dtype=np.float32) / np.float32(32))).astype(np.float32)
    ang = np.concatenate([rows[:, None] * freqs[None, :], cols[:, None] * freqs[None, :]], axis=1).astype(np.float32)
    rope = np.stack([np.cos(ang), np.sin(ang)], axis=1).astype(np.float32)
    iota16 = np.arange(128, dtype=np.float32)[None, :]
    return dict(ident=ident, tri=tri, rope=rope, iota16=iota16)


def make_in_maps(inputs):
    consts = host_consts()
    shared = {n: np.ascontiguousarray(inputs[n]) for n in (
        "w_mod", "b_mod", "ln_g", "ln_b", "a_w_in", "a_b_gate", "a_norm_g", "a_w_out", "b_w_in", "b_b_in", "b_ln_g", "b_ln_b",
        "b_w_s", "b_b_s", "b_w_out", "c_w_qkv", "c_q_g", "c_k_g", "c_w_out", "p_w_q", "p_k1", "p_k2", "p_u", "p_v")}
    shared.update(consts)
    maps = []
    for b in range(8):
        m = dict(shared)
        m["xin"] = np.ascontiguousarray(np.concatenate([inputs["x"][b], inputs["ctx"][b]], axis=0))
        m["cc"] = np.ascontiguousarray(np.stack([inputs["c"][b], inputs["c_ctx"]], axis=0))
        maps.append(m)
    return maps


def kernel(**inputs):
    inputs = {k_: np.asarray(v) for k_, v in inputs.items()}
    if "nc" not in _CACHE:
        _CACHE["nc"] = build_program()[0]
    nc = _CACHE["nc"]
    maps = make_in_maps(inputs)
    res = run_bass_kernel_spmd(nc, maps, core_ids=list(range(8)))
    out = np.stack([np.asarray(r["y"]).reshape(SEQ, D) for r in res.results], axis=0)
    return out.astype(np.float32)
```

```python
import numpy as np
from contextlib import ExitStack
import concourse.bass as bass
import concourse.mybir as mybir
from concourse.bass_utils import run_bass_kernel_spmd

F32 = mybir.dt.float32
BF16 = mybir.dt.bfloat16
U32 = mybir.dt.uint32
I32 = mybir.dt.int32
AF = mybir.ActivationFunctionType
ALU = mybir.AluOpType
AX = mybir.AxisListType

D = 2048
KC = 16
SEQ = 4096
CTX = 256
T = SEQ + CTX
NT = T // 128
NTL = SEQ // 128
DEPTH = 4
ALPHA = (2 * DEPTH) ** 0.25
EPS = 1e-6
A_IN = 6160
NEG = -1.0e30


NO_SELF_SYNC = ("pe",)


class Buf:
    __slots__ = ("ap", "w", "r")

    def __init__(self, ap):
        self.ap = ap
        self.w = None
        self.r = {}


class Prog:
    LIMIT = 30000

    def __init__(self, nc):
        self.nc = nc
        self.engs = {"pe": nc.tensor, "dve": nc.vector, "act": nc.scalar, "pool": nc.gpsimd, "sp": nc.sync}
        self.sems = {}
        self.cnt = {}
        self.key = {}
        self.gen = {}
        for e in ("pe", "dve", "act", "pool"):
            self.gen[e] = 0
            self._newkey(e)
        self.clock = {e: {} for e in self.engs}
        self.dq = {}
        for q, n in (("sp", 16), ("pool", 16), ("act", 6)):
            keys = []
            for i in range(n):
                k = "d_%s%d" % (q, i)
                self.sems[k] = nc.alloc_semaphore(name=k)
                self.cnt[k] = 0
                keys.append(k)
            self.dq[q] = [keys, 0]
        self.n_inst = 0

    def _newkey(self, e):
        k = "%s_%d" % (e, self.gen[e])
        self.gen[e] += 1
        self.sems[k] = self.nc.alloc_semaphore(name="s_" + k)
        self.cnt[k] = 0
        self.key[e] = k

    def _wait(self, e, tok):
        if tok is None:
            return
        k, v = tok
        if self.clock[e].get(k, 0) >= v:
            return
        if e in NO_SELF_SYNC and k.startswith(e + "_"):
            return
        self.engs[e].wait_ge(self.sems[k], v)
        self.clock[e][k] = v

    def _deps(self, e, reads, writes):
        for b in reads:
            self._wait(e, b.w)
        for b in writes:
            self._wait(e, b.w)
            for kv in list(b.r.items()):
                self._wait(e, kv)

    def _mark(self, tok, reads, writes):
        k, v = tok
        for b in reads:
            b.r[k] = v
        for b in writes:
            b.w = tok
            b.r = {}

    def op(self, e, fn, reads=(), writes=()):
        self._deps(e, reads, writes)
        inst = fn(self.engs[e])
        k = self.key[e]
        inst.then_inc(self.sems[k], 1)
        self.cnt[k] += 1
        tok = (k, self.cnt[k])
        self._mark(tok, reads, writes)
        if self.cnt[k] >= self.LIMIT:
            self._newkey(e)
        self.n_inst += 1
        return tok

    def dma(self, q, out, in_, reads=(), writes=(), indirect=None, **kw):
        keys, rr = self.dq[q]
        k = keys[rr % len(keys)]
        self.dq[q][1] += 1
        if self.cnt[k] > 0:
            self._wait(q, (k, self.cnt[k]))
        self._deps(q, reads, writes)
        eng = self.engs[q]
        if indirect is not None:
            inst = eng.indirect_dma_start(out=out, out_offset=None, in_=in_, in_offset=indirect, **kw)
        else:
            inst = eng.dma_start(out=out, in_=in_, **kw)
        inst.then_inc(self.sems[k], 16)
        self.cnt[k] += 16
        if self.cnt[k] >= self.LIMIT:
            self._wait(q, (k, self.cnt[k]))
            tok = (k, self.cnt[k])
            self._mark(tok, reads, writes)
            nk = k + "n"
            self.sems[nk] = self.nc.alloc_semaphore(name=nk)
            self.cnt[nk] = 0
            keys[(rr) % len(keys)] = nk
            self.n_inst += 1
            return tok
        tok = (k, self.cnt[k])
        self._mark(tok, reads, writes)
        self.n_inst += 1
        return tok

    def barrier(self):
        for e in self.engs:
            for k, v in self.cnt.items():
                if v > 0:
                    self._wait(e, (k, v))

    def final_wait(self, e="sp"):
        for k, v in self.cnt.items():
            if v > 0:
                self._wait(e, (k, v))


def rsqrt_eps(P, dst, src_ap, reads, scale):
    P.op("dve", lambda e: e.tensor_scalar(out=dst.ap, in0=src_ap, scalar1=float(scale), scalar2=float(EPS), op0=ALU.mult, op1=ALU.add),
         reads=list(reads), writes=[dst])
    P.op("act", lambda e: e.activation(out=dst.ap, in_=dst.ap, func=AF.Sqrt), reads=[dst], writes=[dst])
    P.op("dve", lambda e: e.reciprocal(out=dst.ap, in_=dst.ap), reads=[dst], writes=[dst])


def pipelined(items, load, body):
    items = list(items)
    if not items:
        return
    load(items[0])
    for i, it in enumerate(items):
        if i + 1 < len(items):
            load(items[i + 1])
        body(it)


def bcast_rows(ap_row, n):
    return ap_row.to_broadcast([n, ap_row.shape[-1]])


class K:
    def __init__(self, n_layers=DEPTH, dbg=()):
        self.n_layers = n_layers
        self.dbg = set(dbg)
        nc = bass.Bass("TRN2", target_bir_lowering=False)
        self.nc = nc
        self.P = Prog(nc)
        self.es = ExitStack()
        self.inp = {}

    def din(self, name, shape, dt=F32):
        ap = self.nc.dram_tensor(name, list(shape), dt, kind="ExternalInput").ap()
        self.inp[name] = ap
        return ap

    def dscr(self, name, shape, dt=F32, out=False):
        kind = "ExternalOutput" if (out or name in self.dbg) else "Internal"
        return self.nc.dram_tensor(name, list(shape), dt, kind=kind).ap()

    def sb(self, es, name, shape, dt=F32):
        self.uid = getattr(self, "uid", 0) + 1
        t = es.enter_context(self.nc.sbuf_tensor("%s_%d" % (name, self.uid), list(shape), dt))
        return t

    def psum(self, es, name, shape, dt=F32):
        return es.enter_context(self.nc.psum_tensor(name, list(shape), dt))


def build_program(n_layers=DEPTH, dbg=(), peer=True):
    k = K(n_layers, dbg)
    nc, P = k.nc, k.P
    xin = k.din("xin", [T, D])
    cc = k.din("cc", [2, D])
    w_mod = k.din("w_mod", [DEPTH, D, 6 * D])
    b_mod = k.din("b_mod", [DEPTH, 6 * D])
    ln_g = k.din("ln_g", [DEPTH, 2, D])
    ln_b = k.din("ln_b", [DEPTH, 2, D])
    a_w_in = k.din("a_w_in", [2, D, A_IN])
    a_b_gate = k.din("a_b_gate", [2, 16])
    a_norm_g = k.din("a_norm_g", [2, D])
    a_w_out = k.din("a_w_out", [2, D, D])
    b_w_in = k.din("b_w_in", [1, D, 8192])
    b_b_in = k.din("b_b_in", [1, 8192])
    b_ln_g = k.din("b_ln_g", [1, 4096])
    b_ln_b = k.din("b_ln_b", [1, 4096])
    b_w_s = k.din("b_w_s", [1, 8, 128, 128])
    b_b_s = k.din("b_b_s", [1, 8, 128])
    b_w_out = k.din("b_w_out", [1, 4096, D])
    c_w_qkv = k.din("c_w_qkv", [1, D, 3072])
    c_q_g = k.din("c_q_g", [1, 128])
    c_k_g = k.din("c_k_g", [1, 128])
    c_w_out = k.din("c_w_out", [1, D, D])
    p_w_q = k.din("p_w_q", [DEPTH, D, D])
    p_k1 = k.din("p_k1", [DEPTH, 128, 128])
    p_k2 = k.din("p_k2", [DEPTH, 128, 128])
    p_u = k.din("p_u", [DEPTH, 16384, D])
    p_v = k.din("p_v", [DEPTH, 16384, D])
    ident_d = k.din("ident", [128, 128])
    tri_d = k.din("tri", [2, 128, 128])
    rope_d = k.din("rope", [SEQ, 2, 64])
    iota_d = k.din("iota16", [1, 128])

    yout = k.dscr("y", [SEQ, D], out=True)
    xres = k.dscr("xres", [T, D])
    modd = k.dscr("modd", [DEPTH, 2, 6 * D])
    hT_d = k.dscr("hT", [NT, 128, KC, 128], BF16)
    h_d = k.dscr("h_tm", [T, D])
    ymix = k.dscr("ymix", [T, D])
    S = dict(k=k, nc=nc, P=P, xin=xin, xres=xres, modd=modd, hT_d=hT_d, h_d=h_d, ymix=ymix, yout=yout)

    with ExitStack() as top:
        ident = Buf(k.sb(top, "ident_sb", [128, 128])[:])
        identb = Buf(k.sb(top, "identb_sb", [128, 128], BF16)[:])
        P.dma("sp", ident.ap, ident_d[:, :], writes=[ident])
        P.op("dve", lambda e: e.tensor_copy(out=identb.ap, in_=ident.ap), reads=[ident], writes=[identb])
        ps_t = [k.psum(top, "ps%d" % i, [128, 1024]) for i in range(4)]
        ps = [Buf(ps_t[i // 2][:, (i % 2) * 512:(i % 2 + 1) * 512]) for i in range(8)]
        S.update(ident=ident, identb=identb, ps=ps, ps_t=ps_t)
        S["ps2"] = [Buf(t[:]) for t in ps_t]
        S["psb"] = [Buf(ps_t[3][:, i * 512:(i + 1) * 512].bitcast(BF16)) for i in range(2)]
        S["Gd"] = k.dscr("Gd", [128, 128, T], BF16)

        phase_mod(S, cc, w_mod, b_mod, n_layers)
        P.barrier()
        for li in range(n_layers):
            need_ctx = li < DEPTH - 1
            nt_out = NT if need_ctx else NTL
            src = xin if li == 0 else xres
            mixer, j = li % 3, li // 3
            if li == 0:
                phase_prep(S, src, li, 0, NT, want_h=False)
                P.barrier()
            if mixer == 0:
                mix_mlstm(S, j, a_w_in, a_b_gate, a_norm_g, a_w_out, tri_d, need_ctx)
            elif mixer == 1:
                mix_gmlp(S, j, b_w_in, b_b_in, b_ln_g, b_ln_b, b_w_s, b_b_s, b_w_out, need_ctx)
            else:
                mix_gqa(S, j, c_w_qkv, c_q_g, c_k_g, c_w_out, rope_d, need_ctx)
            P.barrier()
            phase_ln(S, src, ymix, xres, li, 0, ln_g, ln_b, nt_out, prep=(li, 1, nt_out))
            P.barrier()
            if "x_mid%d" % li in k.dbg:
                dump(S, xres, "x_mid%d" % li, T)
            phase_peer(S, li, p_w_q, p_k1, p_k2, p_u, p_v, iota_d, nt_out, enable=peer)
            P.barrier()
            last = (li == n_layers - 1)
            phase_ln(S, xres, ymix, (yout if last else xres), li, 1, ln_g, ln_b, nt_out if not last else NTL,
                     prep=(None if last else (li + 1, 0, NT)))
            P.barrier()
        P.final_wait("sp")
        P.final_wait("act")
    return nc, k


def dump(S, src, name, rows):
    k, P = S["k"], S["P"]
    dst = k.dscr(name, [rows, D], out=True)
    for t in range(rows // 128):
        P.dma("sp", dst[t * 128:(t + 1) * 128, :], src[t * 128:(t + 1) * 128, :])
    P.barrier()


def phase_mod(S, cc, w_mod, b_mod, n_layers):
    k, P, ps, modd = S["k"], S["P"], S["ps"], S["modd"]
    with ExitStack() as es:
        craw = Buf(k.sb(es, "craw", [128, 2, KC])[:])
        condT = Buf(k.sb(es, "condT", [128, KC, 2], BF16)[:])
        P.dma("sp", craw.ap, cc.rearrange("r (kc p) -> p r kc", p=128), writes=[craw], allow_slow_non_contiguous=True)
        P.op("act", lambda e: e.activation(out=condT.ap.rearrange("p kc r -> p r kc"), in_=craw.ap, func=AF.Silu),
             reads=[craw], writes=[condT])
        wbuf = [Buf(k.sb(es, "wm%d" % i, [128, KC, 512], BF16)[:]) for i in range(4)]
        bbuf = [Buf(k.sb(es, "bm%d" % i, [2, 512])[:]) for i in range(4)]
        obuf = [Buf(k.sb(es, "om%d" % i, [2, 512])[:]) for i in range(4)]
        it = 0
        for li in range(n_layers):
            for nci in range(24):
                w, bb, ob, pt = wbuf[it % 4], bbuf[it % 4], obuf[it % 4], ps[it % 4]
                n0 = nci * 512
                wsrc = w_mod[li, :, n0:n0 + 512].rearrange("(kc p) n -> p kc n", p=128)
                P.dma("pool", w.ap, wsrc, writes=[w])
                P.dma("sp", bb.ap, b_mod[li:li + 1, n0:n0 + 512].to_broadcast([2, 512]), writes=[bb])
                for kc in range(KC):
                    P.op("pe", lambda e, kc=kc, w=w, pt=pt: e.matmul(pt.ap[0:2, :], lhsT=condT.ap[:, kc, :], rhs=w.ap[:, kc, :],
                                                                 start=(kc == 0), stop=(kc == KC - 1)),
                         reads=[condT, w], writes=[pt])
                P.op("dve", lambda e, pt=pt, bb=bb, ob=ob: e.tensor_tensor(out=ob.ap, in0=pt.ap[0:2, :], in1=bb.ap, op=ALU.add),
                     reads=[pt, bb], writes=[ob])
                P.dma("sp", modd[li, :, n0:n0 + 512], ob.ap, reads=[ob])
                it += 1


def load_mod_bcast(S, es, li, idx, rows, name, plus_one=False):
    k, P, modd = S["k"], S["P"], S["modd"]
    b = Buf(k.sb(es, name, [128, D])[:])
    P.dma("sp", b.ap, modd[li, rows:rows + 1, idx * D:(idx + 1) * D].to_broadcast([128, D]), writes=[b])
    if plus_one:
        P.op("pool", lambda e: e.tensor_scalar_add(out=b.ap, in0=b.ap, scalar1=1.0), reads=[b], writes=[b])
    return b


def phase_prep(S, src, li, sub, nt, want_h):
    k, P, ps, hT_d, h_d, ident = S["k"], S["P"], S["ps"], S["hT_d"], S["h_d"], S["ident"]
    with ExitStack() as es:
        sh = [load_mod_bcast(S, es, li, 3 * sub + 0, r, "sh%d" % r) for r in range(2)]
        sc = [load_mod_bcast(S, es, li, 3 * sub + 1, r, "sc%d" % r, plus_one=True) for r in range(2)]
        xb = [Buf(k.sb(es, "px%d" % i, [128, D])[:]) for i in range(2)]
        hb = [Buf(k.sb(es, "ph%d" % i, [128, D])[:]) for i in range(2)]
        hTb = [Buf(k.sb(es, "phT%d" % i, [128, KC, 128], BF16)[:]) for i in range(2)]
        def load(t):
            P.dma("sp", xb[t % 2].ap, src[t * 128:(t + 1) * 128, :], writes=[xb[t % 2]])

        def body(t):
            r = 0 if t < NTL else 1
            x, h, hT = xb[t % 2], hb[t % 2], hTb[t % 2]
            P.op("pool", lambda e: e.tensor_tensor(out=h.ap, in0=x.ap, in1=sc[r].ap, op=ALU.mult), reads=[x, sc[r]], writes=[h])
            P.op("dve", lambda e: e.tensor_tensor(out=h.ap, in0=h.ap, in1=sh[r].ap, op=ALU.add), reads=[h, sh[r]], writes=[h])
            if want_h:
                P.dma("act", h_d[t * 128:(t + 1) * 128, :], h.ap, reads=[h])
            for g in range(4):
                pt = ps[(t * 4 + g) % 8]
                for q in range(4):
                    kc = g * 4 + q
                    P.op("pe", lambda e, kc=kc, q=q, pt=pt: e.transpose(out=pt.ap[:, q * 128:(q + 1) * 128], in_=h.ap[:, kc * 128:(kc + 1) * 128],
                                                                   identity=ident.ap), reads=[h, ident], writes=[pt])
                eng = "act" if g % 2 == 0 else "dve"
                if eng == "act":
                    P.op("act", lambda e, g=g, pt=pt: e.activation(out=hT.ap[:, g * 4:(g + 1) * 4, :], in_=pt.ap.rearrange("p (q t) -> p q t", q=4), func=AF.Copy),
                         reads=[pt], writes=[hT])
                else:
                    P.op("dve", lambda e, g=g, pt=pt: e.tensor_copy(out=hT.ap[:, g * 4:(g + 1) * 4, :], in_=pt.ap.rearrange("p (q t) -> p q t", q=4)),
                         reads=[pt], writes=[hT])
            P.dma("sp", hT_d[t], hT.ap, reads=[hT])
        pipelined(range(nt), load, body)


def phase_ln(S, xsrc, ysrc, dst, li, sub, ln_g, ln_b, nt, prep=None):
    k, P = S["k"], S["P"]
    hT_d, ident, ps = S["hT_d"], S["ident"], S["ps"]
    with ExitStack() as es:
        if prep is not None:
            li2, sub2, nt2 = prep
            sh = [load_mod_bcast(S, es, li2, 3 * sub2 + 0, r, "fsh%d" % r) for r in range(2)]
            sc = [load_mod_bcast(S, es, li2, 3 * sub2 + 1, r, "fsc%d" % r, plus_one=True) for r in range(2)]
            hb = [Buf(k.sb(es, "fph%d" % i, [128, D])[:]) for i in range(2)]
            hTb = [Buf(k.sb(es, "fphT%d" % i, [128, KC, 128], BF16)[:]) for i in range(2)]
        gt = [load_mod_bcast(S, es, li, 3 * sub + 2, r, "lg%d" % r) for r in range(2)]
        lg = Buf(k.sb(es, "lng", [128, D])[:])
        lb = Buf(k.sb(es, "lnb", [128, D])[:])
        P.dma("sp", lg.ap, ln_g[li, sub:sub + 1, :].to_broadcast([128, D]), writes=[lg])
        P.dma("sp", lb.ap, ln_b[li, sub:sub + 1, :].to_broadcast([128, D]), writes=[lb])
        xb = [Buf(k.sb(es, "lx%d" % i, [128, D])[:]) for i in range(2)]
        yb = [Buf(k.sb(es, "ly%d" % i, [128, D])[:]) for i in range(2)]
        st = [Buf(k.sb(es, "lst%d" % i, [128, 4, 6])[:]) for i in range(2)]
        mv = [Buf(k.sb(es, "lmv%d" % i, [128, 2])[:]) for i in range(2)]
        rs = [Buf(k.sb(es, "lrs%d" % i, [128, 1])[:]) for i in range(2)]
        def load(t):
            P.dma("sp", xb[t % 2].ap, xsrc[t * 128:(t + 1) * 128, :], writes=[xb[t % 2]])
            P.dma("sp", yb[t % 2].ap, ysrc[t * 128:(t + 1) * 128, :], writes=[yb[t % 2]])

        def body(t):
            r = 0 if t < NTL else 1
            x, y, s_, m_, r_ = xb[t % 2], yb[t % 2], st[t % 2], mv[t % 2], rs[t % 2]
            P.op("pool", lambda e: e.tensor_tensor(out=y.ap, in0=y.ap, in1=gt[r].ap, op=ALU.mult), reads=[y, gt[r]], writes=[y])
            P.op("dve", lambda e: e.scalar_tensor_tensor(out=x.ap, in0=x.ap, scalar=float(ALPHA), in1=y.ap, op0=ALU.mult, op1=ALU.add),
                 reads=[x, y], writes=[x])
            for q in range(4):
                P.op("dve", lambda e, q=q: e.bn_stats(out=s_.ap[:, q, :], in_=x.ap[:, q * 512:(q + 1) * 512]), reads=[x], writes=[s_])
            P.op("dve", lambda e: e.bn_aggr(out=m_.ap, in_=s_.ap.rearrange("p a b -> p (a b)")), reads=[s_], writes=[m_])
            rsqrt_eps(P, r_, m_.ap[:, 1:2], [m_], 1.0)
            P.op("dve", lambda e: e.scalar_tensor_tensor(out=m_.ap[:, 1:2], in0=m_.ap[:, 0:1], scalar=-1.0, in1=r_.ap[:, 0:1], op0=ALU.mult, op1=ALU.mult),
                 reads=[m_, r_], writes=[m_])
            P.op("act", lambda e: e.activation(out=x.ap, in_=x.ap, func=AF.Identity, scale=r_.ap[:, 0:1], bias=m_.ap[:, 1:2]),
                 reads=[x, m_, r_], writes=[x])
            P.op("pool", lambda e: e.tensor_tensor(out=x.ap, in0=x.ap, in1=lg.ap, op=ALU.mult), reads=[x, lg], writes=[x])
            P.op("dve", lambda e: e.tensor_tensor(out=x.ap, in0=x.ap, in1=lb.ap, op=ALU.add), reads=[x, lb], writes=[x])
            P.dma("sp", dst[t * 128:(t + 1) * 128, :], x.ap, reads=[x])
            if prep is not None and t < nt2:
                h, hT = hb[t % 2], hTb[t % 2]
                P.op("pool", lambda e: e.tensor_tensor(out=h.ap, in0=x.ap, in1=sc[r].ap, op=ALU.mult), reads=[x, sc[r]], writes=[h])
                P.op("dve", lambda e: e.tensor_tensor(out=h.ap, in0=h.ap, in1=sh[r].ap, op=ALU.add), reads=[h, sh[r]], writes=[h])
                for g in range(4):
                    pt = ps[(t * 4 + g) % 8]
                    for q in range(4):
                        kc = g * 4 + q
                        P.op("pe", lambda e: e.transpose(out=pt.ap[:, q * 128:(q + 1) * 128], in_=h.ap[:, kc * 128:(kc + 1) * 128], identity=ident.ap),
                             reads=[h, ident], writes=[pt])
                    P.op("act", lambda e: e.activation(out=hT.ap[:, g * 4:(g + 1) * 4, :], in_=pt.ap.rearrange("p (q t) -> p q t", q=4), func=AF.Copy),
                         reads=[pt], writes=[hT])
                P.dma("sp", hT_d[t], hT.ap, reads=[hT])
        pipelined(range(nt), load, body)


def load_hT_group(S, es_bufs, t0, ntile, q="sp"):
    P, hT_d = S["P"], S["hT_d"]
    g = es_bufs
    for i in range(ntile):
        P.dma(q, g.ap[:, :, i * 128:(i + 1) * 128], hT_d[t0 + i], writes=[g])
    return g


def load_w_chunk(S, wbuf, w_ap, k0_chunks, n0, n, q="pool"):
    P = S["P"]
    P.dma(q, wbuf.ap[:, 0:k0_chunks, 0:n], w_ap[:, n0:n0 + n].rearrange("(kc p) n -> p kc n", p=128), writes=[wbuf])


NPS = 8
PEER_DENSE = True


def next_ps(S, lo=0, hi=NPS):
    r = S.setdefault("rot", 0)
    S["rot"] = r + 1
    return S["ps"][lo + r % (hi - lo)]


def linear_tm(S, src_T, kch, w_ap, N, tiles, epi, G=12, tag="l"):
    k, P = S["k"], S["P"]
    with ExitStack() as es:
        hg = [Buf(k.sb(es, tag + "hg%d" % i, [128, kch, 128], BF16)[:]) for i in range(G)]
        wb = [Buf(k.sb(es, tag + "w%d" % i, [128, kch, 512], BF16)[:]) for i in range(2)]
        it = 0
        for g0 in range(0, len(tiles), G):
            grp = tiles[g0:g0 + G]
            for i, t in enumerate(grp):
                P.dma("sp", hg[i].ap, src_T[t], writes=[hg[i]])
            for n0 in range(0, N, 512):
                n = min(512, N - n0)
                w = wb[it % 2]
                it += 1
                P.dma("pool", w.ap[:, :, 0:n], w_ap[:, n0:n0 + n].rearrange("(kc p) n -> p kc n", p=128), writes=[w])
                for i, t in enumerate(grp):
                    pt = next_ps(S)
                    for kc in range(kch):
                        P.op("pe", lambda e: e.matmul(pt.ap[:, 0:n], lhsT=hg[i].ap[:, kc, :], rhs=w.ap[:, kc, 0:n],
                                                      start=(kc == 0), stop=(kc == kch - 1)), reads=[hg[i], w], writes=[pt])
                    epi(t, n0, n, pt)


def linear_fm(S, src_T, kch, w_ap, N, tiles, epi, G=8, tag="f"):
    k, P = S["k"], S["P"]
    with ExitStack() as es:
        hg = [Buf(k.sb(es, tag + "hg%d" % i, [128, kch, 512], BF16)[:]) for i in range(G // 4)]
        wb = [Buf(k.sb(es, tag + "w%d" % i, [128, kch, 512], BF16)[:]) for i in range(2)]
        it = 0
        for g0 in range(0, len(tiles), G):
            grp = tiles[g0:g0 + G]
            blocks = [grp[i:i + 4] for i in range(0, len(grp), 4)]
            for bi, blk in enumerate(blocks):
                for i, t in enumerate(blk):
                    P.dma("sp", hg[bi].ap[:, :, i * 128:(i + 1) * 128], src_T[t], writes=[hg[bi]])
            for n0 in range(0, N, 512):
                n = min(512, N - n0)
                w = wb[it % 2]
                it += 1
                P.dma("pool", w.ap[:, :, 0:n], w_ap[:, n0:n0 + n].rearrange("(kc p) n -> p kc n", p=128), writes=[w])
                for cc in range(n // 128):
                    for bi, blk in enumerate(blocks):
                        ntok = len(blk) * 128
                        pt = next_ps(S)
                        for kc in range(kch):
                            P.op("pe", lambda e: e.matmul(pt.ap[:, 0:ntok], lhsT=w.ap[:, kc, cc * 128:(cc + 1) * 128], rhs=hg[bi].ap[:, kc, 0:ntok],
                                                          start=(kc == 0), stop=(kc == kch - 1)), reads=[hg[bi], w], writes=[pt])
                        epi(n0 // 128 + cc, blk[0], len(blk), pt)


def make_store_epi(S, es, dst, tag="st", nbuf=4, col_off=0):
    k, P = S["k"], S["P"]
    stg = [Buf(k.sb(es, tag + "%d" % i, [128, 512])[:]) for i in range(nbuf)]
    cnt = [0]

    def epi(t, n0, n, pt):
        s = stg[cnt[0] % nbuf]
        eng = "act" if cnt[0] % 2 == 0 else "dve"
        cnt[0] += 1
        if eng == "act":
            P.op("act", lambda e: e.activation(out=s.ap[:, 0:n], in_=pt.ap[:, 0:n], func=AF.Copy), reads=[pt], writes=[s])
        else:
            P.op("dve", lambda e: e.tensor_copy(out=s.ap[:, 0:n], in_=pt.ap[:, 0:n]), reads=[pt], writes=[s])
        P.dma("sp", dst[t * 128:(t + 1) * 128, col_off + n0:col_off + n0 + n], s.ap[:, 0:n], reads=[s])
    return epi


def transpose_to_T(S, src, nch, dstT, t):
    P, ident = S["P"], S["ident"]
    for g in range(nch // 4):
        pt = next_ps(S)
        for q in range(4):
            kc = g * 4 + q
            P.op("pe", lambda e: e.transpose(out=pt.ap[:, q * 128:(q + 1) * 128], in_=src.ap[:, kc * 128:(kc + 1) * 128], identity=ident.ap),
                 reads=[src, ident], writes=[pt])
        if g % 2 == 0:
            P.op("act", lambda e: e.activation(out=dstT.ap[:, g * 4:(g + 1) * 4, :], in_=pt.ap.rearrange("p (q t) -> p q t", q=4), func=AF.Copy),
                 reads=[pt], writes=[dstT])
        else:
            P.op("dve", lambda e: e.tensor_copy(out=dstT.ap[:, g * 4:(g + 1) * 4, :], in_=pt.ap.rearrange("p (q t) -> p q t", q=4)),
                 reads=[pt], writes=[dstT])


def mix_gmlp(S, j, b_w_in, b_b_in, b_ln_g, b_ln_b, b_w_s, b_b_s, b_w_out, need_ctx):
    k, P, hT_d, ymix, ident = S["k"], S["P"], S["hT_d"], S["ymix"], S["ident"]
    tiles = list(range(NT if need_ctx else NTL))
    zd = k.dscr("zd", [T, 8192])
    uvT = k.dscr("uvT", [NT, 128, 32, 128], BF16)
    with ExitStack() as es:
        bias = Buf(k.sb(es, "gb_bias", [128, 8192])[:])
        P.dma("sp", bias.ap, b_b_in[j:j + 1, :].to_broadcast([128, 8192]), writes=[bias])
        stg = [Buf(k.sb(es, "gb_st%d" % i, [128, 512])[:]) for i in range(4)]
        cnt = [0]

        def epi(t, n0, n, pt):
            s = stg[cnt[0] % 4]
            cnt[0] += 1
            P.op("dve", lambda e: e.tensor_tensor(out=s.ap, in0=pt.ap, in1=bias.ap[:, n0:n0 + n], op=ALU.add), reads=[pt, bias], writes=[s])
            P.op("act", lambda e: e.activation(out=s.ap, in_=s.ap, func=AF.Gelu_apprx_tanh), reads=[s], writes=[s])
            P.dma("sp", zd[t * 128:(t + 1) * 128, n0:n0 + n], s.ap, reads=[s])
        linear_tm(S, hT_d, KC, b_w_in[j], 8192, tiles, epi, tag="gb1")
    P.barrier()
    with ExitStack() as es:
        lg = Buf(k.sb(es, "gb_lg", [128, 4096])[:])
        lb = Buf(k.sb(es, "gb_lb", [128, 4096])[:])
        P.dma("sp", lg.ap, b_ln_g[j:j + 1, :].to_broadcast([128, 4096]), writes=[lg])
        P.dma("sp", lb.ap, b_ln_b[j:j + 1, :].to_broadcast([128, 4096]), writes=[lb])
        wsr = Buf(k.sb(es, "gb_wsr", [128, 8, 128])[:])
        wsT = Buf(k.sb(es, "gb_wsT", [128, 8, 128], BF16)[:])
        bsT = Buf(k.sb(es, "gb_bsT", [128, 8])[:])
        P.dma("sp", wsr.ap, b_w_s[j].rearrange("g t s -> t g s"), writes=[wsr])
        P.dma("sp", bsT.ap, b_b_s[j].rearrange("g t -> t g"), writes=[bsT], allow_slow_non_contiguous=True)
        for g in range(8):
            pt = next_ps(S)
            P.op("pe", lambda e: e.transpose(out=pt.ap[:, 0:128], in_=wsr.ap[:, g, :], identity=ident.ap), reads=[wsr, ident], writes=[pt])
            P.op("dve", lambda e: e.tensor_copy(out=wsT.ap[:, g, :], in_=pt.ap[:, 0:128]), reads=[pt], writes=[wsT])
        zb = [Buf(k.sb(es, "gb_z%d" % i, [128, 8192])[:]) for i in range(2)]
        vb = [Buf(k.sb(es, "gb_v%d" % i, [128, 4096], BF16)[:]) for i in range(2)]
        uT = [Buf(k.sb(es, "gb_uT%d" % i, [128, 32, 128], BF16)[:]) for i in range(2)]
        st = [Buf(k.sb(es, "gb_bs%d" % i, [128, 8, 6])[:]) for i in range(2)]
        mv = [Buf(k.sb(es, "gb_mv%d" % i, [128, 2])[:]) for i in range(2)]
        rs = [Buf(k.sb(es, "gb_rs%d" % i, [128, 1])[:]) for i in range(2)]
        def load(t):
            z = zb[t % 2]
            P.dma("sp", z.ap[:, 0:4096], zd[t * 128:(t + 1) * 128, 0:4096], writes=[z])
            P.dma("act", z.ap[:, 4096:8192], zd[t * 128:(t + 1) * 128, 4096:8192], writes=[z])

        def body(t):
            z, v, u_T, s_, m_, r_ = zb[t % 2], vb[t % 2], uT[t % 2], st[t % 2], mv[t % 2], rs[t % 2]
            for q in range(8):
                P.op("dve", lambda e: e.bn_stats(out=s_.ap[:, q, :], in_=z.ap[:, 4096 + q * 512:4096 + (q + 1) * 512]), reads=[z], writes=[s_])
            P.op("dve", lambda e: e.bn_aggr(out=m_.ap, in_=s_.ap.rearrange("p a b -> p (a b)")), reads=[s_], writes=[m_])
            rsqrt_eps(P, r_, m_.ap[:, 1:2], [m_], 1.0)
            P.op("dve", lambda e: e.tensor_scalar(out=z.ap[:, 4096:8192], in0=z.ap[:, 4096:8192], scalar1=m_.ap[:, 0:1], scalar2=r_.ap[:, 0:1],
                                                  op0=ALU.subtract, op1=ALU.mult), reads=[z, m_, r_], writes=[z])
            P.op("pool", lambda e: e.tensor_tensor(out=z.ap[:, 4096:8192], in0=z.ap[:, 4096:8192], in1=lg.ap, op=ALU.mult), reads=[z, lg], writes=[z])
            P.op("dve", lambda e: e.tensor_tensor(out=v.ap, in0=z.ap[:, 4096:8192], in1=lb.ap, op=ALU.add), reads=[z, lb], writes=[v])
            for g in range(8):
                pt = next_ps(S)
                P.op("pe", lambda e: e.matmul(pt.ap, lhsT=wsT.ap[:, g, :], rhs=v.ap[:, g * 512:(g + 1) * 512], start=True, stop=True),
                     reads=[wsT, v], writes=[pt])
                P.op("dve", lambda e: e.scalar_tensor_tensor(out=z.ap[:, g * 512:(g + 1) * 512], in0=pt.ap, scalar=bsT.ap[:, g:g + 1],
                                                             in1=z.ap[:, g * 512:(g + 1) * 512], op0=ALU.add, op1=ALU.mult),
                     reads=[pt, bsT, z], writes=[z])
            transpose_to_T(S, z, 32, u_T, t)
            P.dma("sp", uvT[t], u_T.ap, reads=[u_T])
        pipelined(tiles, load, body)
    P.barrier()
    with ExitStack() as es:
        epi = make_store_epi(S, es, ymix, tag="gb3s")
        linear_tm(S, uvT, 32, b_w_out[j], D, tiles, epi, G=9, tag="gb3")


def mix_gqa(S, j, c_w_qkv, c_q_g, c_k_g, c_w_out, rope_d, need_ctx):
    k, P, hT_d, ymix, ident = S["k"], S["P"], S["hT_d"], S["ymix"], S["ident"]
    tiles = list(range(NT))
    qkv_d = k.dscr("qkv_d", [T, 3072])
    qkT_d = k.dscr("qkT_d", [20, 128, T], BF16)
    v_d = k.dscr("v_d", [NT, 128, 512], BF16)
    with ExitStack() as es:
        epi = make_store_epi(S, es, qkv_d, tag="gq1s")
        linear_tm(S, hT_d, KC, c_w_qkv[j], 3072, tiles, epi, tag="gq1")
    P.barrier()
    with ExitStack() as es:
        gq = Buf(k.sb(es, "gq_g", [128, 20, 128])[:])
        P.dma("sp", gq.ap[:, 0:16, :], c_q_g[j:j + 1, :].unsqueeze(1).to_broadcast([128, 16, 128]), writes=[gq])
        P.dma("sp", gq.ap[:, 16:20, :], c_k_g[j:j + 1, :].unsqueeze(1).to_broadcast([128, 4, 128]), writes=[gq])
        xb = [Buf(k.sb(es, "gq_x%d" % i, [128, 3072])[:]) for i in range(2)]
        sqb = Buf(k.sb(es, "gq_sq", [128, 2560])[:])
        ssum = Buf(k.sb(es, "gq_ss", [128, 20])[:])
        rp = [Buf(k.sb(es, "gq_rp%d" % i, [128, 2, 64])[:]) for i in range(2)]
        tA = Buf(k.sb(es, "gq_tA", [128, 20, 2, 32])[:])
        tB = Buf(k.sb(es, "gq_tB", [128, 20, 2, 32])[:])
        tC = Buf(k.sb(es, "gq_tC", [128, 20, 2, 32])[:])
        qT = [Buf(k.sb(es, "gq_qT%d" % i, [128, 20, 128], BF16)[:]) for i in range(2)]
        vbf = [Buf(k.sb(es, "gq_vb%d" % i, [128, 512], BF16)[:]) for i in range(2)]
        def load(t):
            P.dma("sp", xb[t % 2].ap, qkv_d[t * 128:(t + 1) * 128, :], writes=[xb[t % 2]])
            if t < NTL:
                P.dma("act", rp[t % 2].ap, rope_d[t * 128:(t + 1) * 128, :, :], writes=[rp[t % 2]])

        def body(t):
            x, r_, q_T, v_ = xb[t % 2], rp[t % 2], qT[t % 2], vbf[t % 2]
            P.op("act", lambda e: e.activation(out=sqb.ap, in_=x.ap[:, 0:2560], func=AF.Square), reads=[x], writes=[sqb])
            P.op("dve", lambda e: e.tensor_reduce(out=ssum.ap, in_=sqb.ap.rearrange("p (h d) -> p h d", d=128), axis=AX.X, op=ALU.add),
                 reads=[sqb], writes=[ssum])
            rsqrt_eps(P, ssum, ssum.ap, [ssum], 1.0 / 128.0)
            x3 = x.ap[:, 0:2560].rearrange("p (h d) -> p h d", d=128)
            P.op("dve", lambda e: e.tensor_tensor(out=x3, in0=x3, in1=ssum.ap.unsqueeze(2).to_broadcast([128, 20, 128]), op=ALU.mult),
                 reads=[x, ssum], writes=[x])
            P.op("pool", lambda e: e.tensor_tensor(out=x3, in0=x3, in1=gq.ap, op=ALU.mult), reads=[x, gq], writes=[x])
            if t < NTL:
                x5 = x.ap[:, 0:2560].rearrange("p (h a b c) -> p h a b c", a=2, b=2, c=32)
                x1 = x5[:, :, :, 0, :]
                x2 = x5[:, :, :, 1, :]
                cosb = r_.ap[:, 0, :].rearrange("p (a c) -> p a c", a=2).unsqueeze(1).to_broadcast([128, 20, 2, 32])
                sinb = r_.ap[:, 1, :].rearrange("p (a c) -> p a c", a=2).unsqueeze(1).to_broadcast([128, 20, 2, 32])
                P.op("dve", lambda e: e.tensor_tensor(out=tA.ap, in0=x1, in1=sinb, op=ALU.mult), reads=[x, r_], writes=[tA])
                P.op("pool", lambda e: e.tensor_tensor(out=tB.ap, in0=x2, in1=sinb, op=ALU.mult), reads=[x, r_], writes=[tB])
                P.op("dve", lambda e: e.tensor_tensor(out=tC.ap, in0=x2, in1=cosb, op=ALU.mult), reads=[x, r_], writes=[tC])
                P.op("dve", lambda e: e.tensor_tensor(out=x1, in0=x1, in1=cosb, op=ALU.mult), reads=[x, r_], writes=[x])
                P.op("dve", lambda e: e.tensor_tensor(out=x1, in0=x1, in1=tB.ap, op=ALU.subtract), reads=[x, tB], writes=[x])
                P.op("dve", lambda e: e.tensor_tensor(out=x2, in0=tA.ap, in1=tC.ap, op=ALU.add), reads=[tA, tC], writes=[x])
            transpose_to_T(S, x, 20, q_T, t)
            P.dma("sp", qkT_d[:, :, t * 128:(t + 1) * 128].rearrange("h d t -> d h t"), q_T.ap, reads=[q_T])
            P.op("pool", lambda e: e.tensor_copy(out=v_.ap, in_=x.ap[:, 2560:3072]), reads=[x], writes=[v_])
            P.dma("sp", v_d[t], v_.ap, reads=[v_])
        pipelined(tiles, load, body)
    P.barrier()
    scale = 128.0 ** -0.5
    ps = S["ps"]
    with ExitStack() as es:
        kT = Buf(k.sb(es, "ga_kT", [128, T], BF16)[:])
        vv = Buf(k.sb(es, "ga_v", [128, NT, 128], BF16)[:])
        qTb = [Buf(k.sb(es, "ga_qT%d" % i, [128, T], BF16)[:]) for i in range(2)]
        ones = Buf(k.sb(es, "ga_ones", [128, 128], BF16)[:])
        P.op("pool", lambda e: e.memset(ones.ap, 1.0), writes=[ones])
        pT = [Buf(k.sb(es, "ga_pT%d" % i, [128, 1024], BF16)[:]) for i in range(3)]
        ps2 = S["ps2"]
        rec = [Buf(k.sb(es, "ga_rec%d" % i, [128, 512])[:]) for i in range(2)]
        oT = [Buf(k.sb(es, "ga_oT%d" % i, [128, 512], BF16)[:]) for i in range(2)]
        blocks = [(q0, 512, list(range(NT))) for q0 in range(0, SEQ, 512)]
        if need_ctx:
            blocks.append((SEQ, CTX, [32, 33]))
        bc = 0
        pc = 0
        for kh in range(4):
            P.dma("sp", kT.ap, qkT_d[16 + kh], writes=[kT])
            P.dma("act", vv.ap, v_d[:, :, kh * 128:(kh + 1) * 128].rearrange("t p d -> p t d"), writes=[vv])
            for hq in range(4):
                h = kh * 4 + hq
                q_ = qTb[h % 2]
                P.dma("sp", q_.ap, qkT_d[h], writes=[q_])
                for (q0, nq, kts) in blocks:
                    acc_o, acc_s = ps[4 + 2 * (bc % 2)], ps[5 + 2 * (bc % 2)]
                    r_, o_ = rec[bc % 2], oT[bc % 2]
                    bc += 1
                    npair = len(kts) // 2
                    sts = {}
                    pTs = {}

                    def emit_qk(pi):
                        nonlocal pc
                        st = ps2[pc % 2]
                        p_ = pT[pc % 3]
                        pc += 1
                        sts[pi], pTs[pi] = st, p_
                        for hf in range(2):
                            kt = kts[2 * pi + hf]
                            P.op("pe", lambda e: e.matmul(st.ap[:, hf * 512:hf * 512 + nq], lhsT=kT.ap[:, kt * 128:(kt + 1) * 128], rhs=q_.ap[:, q0:q0 + nq], start=True, stop=True),
                                 reads=[kT, q_], writes=[st])
                        P.op("act", lambda e: e.activation(out=p_.ap.rearrange("p (a n) -> p a n", a=2)[:, :, 0:nq], in_=st.ap.rearrange("p (a n) -> p a n", a=2)[:, :, 0:nq],
                                                           func=AF.Exp, scale=float(scale)), reads=[st], writes=[p_])
                    emit_qk(0)
                    for pi in range(npair):
                        if pi + 1 < npair:
                            emit_qk(pi + 1)
                        p_ = pTs[pi]
                        for hf in range(2):
                            kt = kts[2 * pi + hf]
                            first = (pi == 0 and hf == 0)
                            last = (pi == npair - 1 and hf == 1)
                            P.op("pe", lambda e: e.matmul(acc_o.ap[:, 0:nq], lhsT=vv.ap[:, kt, :], rhs=p_.ap[:, hf * 512:hf * 512 + nq], start=first, stop=last),
                                 reads=[vv, p_], writes=[acc_o])
                            P.op("pe", lambda e: e.matmul(acc_s.ap[:, 0:nq], lhsT=ones.ap, rhs=p_.ap[:, hf * 512:hf * 512 + nq], start=first, stop=last),
                                 reads=[ones, p_], writes=[acc_s])
                    P.op("dve", lambda e: e.reciprocal(out=r_.ap[:, 0:nq], in_=acc_s.ap[:, 0:nq]), reads=[acc_s], writes=[r_])
                    P.op("dve", lambda e: e.tensor_tensor(out=o_.ap[:, 0:nq], in0=acc_o.ap[:, 0:nq], in1=r_.ap[:, 0:nq], op=ALU.mult), reads=[acc_o, r_], writes=[o_])
                    t0 = q0 // 128
                    P.dma("sp", hT_d[t0:t0 + nq // 128, :, h, :].rearrange("t p q -> p t q"), o_.ap[:, 0:nq].rearrange("p (t q) -> p t q", q=128), reads=[o_])
    P.barrier()
    with ExitStack() as es:
        epi = make_store_epi(S, es, ymix, tag="gq4s")
        linear_tm(S, hT_d, KC, c_w_out[j], D, list(range(NT if need_ctx else NTL)), epi, tag="gq4")


def mix_mlstm(S, j, a_w_in, a_b_gate, a_norm_g, a_w_out, tri_d, need_ctx):
    k, P, hT_d, ymix, ident, ps = S["k"], S["P"], S["hT_d"], S["ymix"], S["ident"], S["ps"]
    tiles = list(range(NT))
    qkT_d = k.dscr("a_qkT%d" % j, [NT, 128, 16, 128], BF16)
    kt_d = k.dscr("a_kt%d" % j, [T, 1024], BF16)
    v_d = k.dscr("a_v%d" % j, [T, 2048], BF16)
    o_d = k.dscr("a_o%d" % j, [T, 2048])
    g_d = k.dscr("a_g%d" % j, [T, 16])
    hs_d = [k.dscr("a_hs%d_%d" % (j, d_), [T, 2048]) for d_ in range(2)]
    with ExitStack() as es:
        stg = [Buf(k.sb(es, "ma_fs%d" % i, [128, 512], BF16)[:]) for i in range(4)]
        cnt = [0]

        def epi_f(c, t0, ntl, pt):
            s = stg[cnt[0] % 4]
            cnt[0] += 1
            n = ntl * 128
            sc = 1.0 if c < 8 else 1.0 / 16.0
            P.op("act", lambda e: e.activation(out=s.ap[:, 0:n], in_=pt.ap[:, 0:n], func=AF.Copy, scale=float(sc)), reads=[pt], writes=[s])
            P.dma("sp", qkT_d[t0:t0 + ntl, :, c, :].rearrange("t p q -> p t q"), s.ap[:, 0:n].rearrange("p (t q) -> p t q", q=128), reads=[s])
        linear_fm(S, hT_d, KC, a_w_in[j][:, 0:2048], 2048, tiles, epi_f, tag="ma1")
    P.barrier()
    with ExitStack() as es:
        stb = [Buf(k.sb(es, "ma_sb%d" % i, [128, 512], BF16)[:]) for i in range(3)]
        stf = [Buf(k.sb(es, "ma_sf%d" % i, [128, 512])[:]) for i in range(3)]
        cnt = [0]

        def epi_t(t, n0, n, pt):
            i = cnt[0]
            cnt[0] += 1
            rows = slice(t * 128, (t + 1) * 128)
            if n0 < 1024:
                s = stb[i % 3]
                P.op("act", lambda e: e.activation(out=s.ap, in_=pt.ap, func=AF.Copy, scale=1.0 / 16.0), reads=[pt], writes=[s])
                P.dma("sp", kt_d[rows, n0:n0 + 512], s.ap, reads=[s])
            elif n0 < 3072:
                s = stb[i % 3]
                P.op("dve", lambda e: e.tensor_copy(out=s.ap, in_=pt.ap), reads=[pt], writes=[s])
                P.dma("sp", v_d[rows, n0 - 1024:n0 - 1024 + 512], s.ap, reads=[s])
            elif n0 < 5120:
                s = stf[i % 3]
                P.op("act", lambda e: e.activation(out=s.ap, in_=pt.ap, func=AF.Sigmoid), reads=[pt], writes=[s])
                P.dma("sp", o_d[rows, n0 - 3072:n0 - 3072 + 512], s.ap, reads=[s])
            else:
                s = stf[i % 3]
                P.op("dve", lambda e: e.tensor_copy(out=s.ap[:, 0:16], in_=pt.ap[:, 0:16]), reads=[pt], writes=[s])
                P.dma("sp", g_d[rows, :], s.ap[:, 0:16], reads=[s])
        linear_tm(S, hT_d, KC, a_w_in[j][:, 1024:A_IN], A_IN - 1024, tiles, epi_t, tag="ma2")
    P.barrier()
    with ExitStack() as es:
        tri = Buf(k.sb(es, "ma_tri", [128, 2, 128])[:])
        P.dma("sp", tri.ap, tri_d.rearrange("a s t -> s a t"), writes=[tri])
        onesf = Buf(k.sb(es, "ma_1f", [128, 128])[:])
        onesb = Buf(k.sb(es, "ma_1b", [128, 2], BF16)[:])
        P.op("pool", lambda e: e.memset(onesf.ap, 1.0), writes=[onesf])
        P.op("pool", lambda e: e.memset(onesb.ap, 1.0), writes=[onesb])
        G = Buf(k.sb(es, "ma_G", [128, NT, 16])[:])
        bg = Buf(k.sb(es, "ma_bg", [128, 16])[:])
        P.dma("sp", G.ap, g_d.rearrange("(c s) g -> s c g", s=128), writes=[G])
        P.dma("sp", bg.ap, a_b_gate[j:j + 1, :].to_broadcast([128, 16]), writes=[bg])
        P.op("dve", lambda e: e.tensor_tensor(out=G.ap, in0=G.ap, in1=bg.ap.unsqueeze(1).to_broadcast([128, NT, 16]), op=ALU.add), reads=[G, bg], writes=[G])
        P.op("act", lambda e: e.activation(out=G.ap, in_=G.ap, func=AF.Tanh, scale=1.0 / 15.0), reads=[G], writes=[G])
        P.op("dve", lambda e: e.tensor_scalar_mul(out=G.ap, in0=G.ap, scalar1=15.0), reads=[G], writes=[G])
        LI = Buf(k.sb(es, "ma_LI", [128, NT, 8])[:])
        LFN = Buf(k.sb(es, "ma_LFN", [128, NT, 8])[:])
        G4 = G.ap.rearrange("p c (a h) -> p c a h", a=4)
        LI4 = LI.ap.rearrange("p c (a h) -> p c a h", a=2)
        LF4 = LFN.ap.rearrange("p c (a h) -> p c a h", a=2)
        for d_ in range(2):
            P.op("dve", lambda e: e.tensor_copy(out=LI4[:, :, d_, :], in_=G4[:, :, 2 * d_, :]), reads=[G], writes=[LI])
            P.op("act", lambda e: e.activation(out=LF4[:, :, d_, :], in_=G4[:, :, 2 * d_ + 1, :], func=AF.Exp, scale=-1.0), reads=[G], writes=[LFN])
        P.op("act", lambda e: e.activation(out=LFN.ap, in_=LFN.ap, func=AF.Ln, bias=1.0), reads=[LFN], writes=[LFN])
        NB = Buf(k.sb(es, "ma_NB", [128, NT, 8])[:])
        GT = Buf(k.sb(es, "ma_GT", [128, NT, 8])[:])
        AA = Buf(k.sb(es, "ma_A", [128, NT, 8])[:])
        NB4 = NB.ap.rearrange("p c (a h) -> p c a h", a=2)
        pt = ps[0]
        for d_ in range(2):
            P.op("pe", lambda e: e.matmul(pt.ap[:, 0:NT * 4], lhsT=tri.ap[:, d_, :], rhs=LF4[:, :, d_, :], start=True, stop=True), reads=[tri, LFN], writes=[pt])
            P.op("dve", lambda e: e.tensor_copy(out=NB4[:, :, d_, :], in_=pt.ap[:, 0:NT * 4].rearrange("p (c h) -> p c h", h=4)), reads=[pt], writes=[NB])
        pt = ps[1]
        P.op("pe", lambda e: e.matmul(pt.ap[:, 0:NT * 8], lhsT=onesf.ap, rhs=LFN.ap, start=True, stop=True), reads=[onesf, LFN], writes=[pt])
        P.op("dve", lambda e: e.tensor_copy(out=GT.ap, in_=pt.ap[:, 0:NT * 8].rearrange("p (c h) -> p c h", h=8)), reads=[pt], writes=[GT])
        P.op("dve", lambda e: e.tensor_tensor(out=AA.ap, in0=LI.ap, in1=NB.ap, op=ALU.add), reads=[LI, NB], writes=[AA])
        AMX = Buf(k.sb(es, "ma_AMX", [128, NT * 8])[:])
        A2 = AA.ap.rearrange("p c h -> p (c h)")
        col = Buf(k.sb(es, "ma_col", [128, 1])[:])
        colb = Buf(k.sb(es, "ma_colb", [128, 128])[:])
        for (c0, n) in ((0, 128), (128, 128), (256, NT * 8 - 256)):
            pt = ps[2]
            P.op("pe", lambda e: e.transpose(out=pt.ap[0:n, 0:128], in_=A2[:, c0:c0 + n], identity=ident.ap), reads=[AA, ident], writes=[pt])
            P.op("dve", lambda e: e.reduce_max(out=col.ap[0:n, :], in_=pt.ap[0:n, 0:128], axis=AX.X), reads=[pt], writes=[col])
            P.op("dve", lambda e: e.tensor_copy(out=colb.ap[0:n, :], in_=col.ap[0:n, 0:1].to_broadcast([n, 128])), reads=[col], writes=[colb])
            pt2 = ps[3]
            P.op("pe", lambda e: e.matmul(pt2.ap[:, 0:n], lhsT=colb.ap[0:n, :], rhs=ident.ap[0:n, 0:n], start=True, stop=True), reads=[colb, ident], writes=[pt2])
            P.op("dve", lambda e: e.tensor_copy(out=AMX.ap[:, c0:c0 + n], in_=pt2.ap[:, 0:n]), reads=[pt2], writes=[AMX])
        AMX3 = AMX.ap.rearrange("p (c h) -> p c h", h=8)
        RR = Buf(k.sb(es, "ma_R", [128, NT, 8])[:])
        DM = Buf(k.sb(es, "ma_DM", [128, NT, 8])[:])
        mm = Buf(k.sb(es, "ma_m", [128, 8])[:])
        P.op("dve", lambda e: e.memset(mm.ap, 0.0), writes=[mm])
        order = [[32, 33] + list(range(32)), [33, 32] + list(range(31, -1, -1))]
        for i in range(NT):
            for d_ in range(2):
                c = order[d_][i]
                hs = slice(d_ * 4, d_ * 4 + 4)
                P.op("dve", lambda e: e.tensor_tensor(out=RR.ap[:, c, hs], in0=mm.ap[:, hs], in1=AMX3[:, c, hs], op=ALU.max), reads=[mm, AMX], writes=[RR])
                P.op("dve", lambda e: e.tensor_tensor(out=DM.ap[:, c, hs], in0=mm.ap[:, hs], in1=RR.ap[:, c, hs], op=ALU.subtract), reads=[mm, RR], writes=[DM])
                P.op("dve", lambda e: e.tensor_tensor(out=mm.ap[:, hs], in0=RR.ap[:, c, hs], in1=GT.ap[:, c, hs], op=ALU.subtract), reads=[RR, GT], writes=[mm])
        EE = Buf(k.sb(es, "ma_E", [128, NT, 8])[:])
        CL = Buf(k.sb(es, "ma_CL", [128, NT, 8])[:])
        P.op("dve", lambda e: e.tensor_tensor(out=EE.ap, in0=AA.ap, in1=RR.ap, op=ALU.subtract), reads=[AA, RR], writes=[EE])
        P.op("act", lambda e: e.activation(out=EE.ap, in_=EE.ap, func=AF.Exp), reads=[EE], writes=[EE])
        P.op("dve", lambda e: e.tensor_tensor(out=CL.ap, in0=NB.ap, in1=RR.ap, op=ALU.subtract), reads=[NB, RR], writes=[CL])
        P.op("act", lambda e: e.activation(out=CL.ap, in_=CL.ap, func=AF.Exp), reads=[CL], writes=[CL])
        P.op("act", lambda e: e.activation(out=DM.ap, in_=DM.ap, func=AF.Exp), reads=[DM], writes=[DM])
        C32 = [Buf(k.sb(es, "ma_C%d" % i, [128, 2, 512])[:]) for i in range(8)]
        N32 = [Buf(k.sb(es, "ma_N%d" % i, [128, 2])[:]) for i in range(8)]
        Cb = [Buf(k.sb(es, "ma_Cb%d" % i, [128, 2, 512], BF16)[:]) for i in range(8)]
        Nb = [Buf(k.sb(es, "ma_Nb%d" % i, [128, 2], BF16)[:]) for i in range(8)]
        for i in range(8):
            P.op("pool", lambda e: e.memset(C32[i].ap, 0.0), writes=[C32[i]])
            P.op("pool", lambda e: e.memset(N32[i].ap, 0.0), writes=[N32[i]])
        qk = [[Buf(k.sb(es, "ma_qk%d_%d" % (d_, i), [128, 16, 128], BF16)[:]) for i in range(2)] for d_ in range(2)]
        ktm = [[Buf(k.sb(es, "ma_kt%d_%d" % (d_, i), [128, 1024], BF16)[:]) for i in range(2)] for d_ in range(2)]
        vtm = [[Buf(k.sb(es, "ma_vt%d_%d" % (d_, i), [128, 2048], BF16)[:]) for i in range(2)] for d_ in range(2)]
        St = [Buf(k.sb(es, "ma_St%d" % i, [128, 128], BF16)[:]) for i in range(4)]
        ktl = [Buf(k.sb(es, "ma_ktl%d" % i, [128, 256], BF16)[:]) for i in range(4)]
        dn = [Buf(k.sb(es, "ma_dn%d" % i, [128, 1])[:]) for i in range(4)]
        ho = [Buf(k.sb(es, "ma_ho%d" % i, [128, 512])[:]) for i in range(4)]
        it = 0

        def scan_load(i):
            for d_ in range(2):
                c = order[d_][i]
                rows = slice(c * 128, (c + 1) * 128)
                qk_, kt_, vt_ = qk[d_][i % 2], ktm[d_][i % 2], vtm[d_][i % 2]
                P.dma("sp", qk_.ap, qkT_d[c], writes=[qk_])
                P.dma("sp", kt_.ap, kt_d[rows, :], writes=[kt_])
                P.dma("sp", vt_.ap, v_d[rows, :], writes=[vt_])
        scan_load(0)
        for i in range(NT):
            if i + 1 < NT:
                scan_load(i + 1)
            for d_ in range(2):
                c = order[d_][i]
                rows = slice(c * 128, (c + 1) * 128)
                qk_, kt_, vt_ = qk[d_][i % 2], ktm[d_][i % 2], vtm[d_][i % 2]
                for hh in range(4):
                    ch = d_ * 4 + hh
                    e_ = EE.ap[:, c, ch:ch + 1]
                    cd = DM.ap[:, c, ch:ch + 1]
                    cl = CL.ap[:, c, ch:ch + 1]
                    s_t, k_l, d_n, h_o = St[it % 4], ktl[it % 4], dn[it % 4], ho[it % 4]
                    p_st, p_num = ps[it % 2], ps[2 + it % 2]
                    p_c = [ps[4], ps[5]]
                    p_den, p_nu = ps[6], ps[7]
                    it += 1
                    vh = vt_.ap[:, hh * 512:(hh + 1) * 512]
                    P.op("dve", lambda e: e.tensor_scalar(out=Cb[ch].ap, in0=C32[ch].ap, scalar1=cd, scalar2=None, op0=ALU.mult), reads=[C32[ch], DM], writes=[Cb[ch]])
                    P.op("dve", lambda e: e.tensor_scalar(out=Nb[ch].ap, in0=N32[ch].ap, scalar1=cd, scalar2=None, op0=ALU.mult), reads=[N32[ch], DM], writes=[Nb[ch]])
                    for dc in range(2):
                        P.op("pe", lambda e: e.matmul(p_st.ap[:, 0:128], lhsT=qk_.ap[:, 8 + hh * 2 + dc, :], rhs=qk_.ap[:, hh * 2 + dc, :], start=(dc == 0), stop=(dc == 1)),
                             reads=[qk_], writes=[p_st])
                    P.op("dve", lambda e: e.scalar_tensor_tensor(out=s_t.ap, in0=p_st.ap[:, 0:128], scalar=e_, in1=tri.ap[:, d_, :], op0=ALU.mult, op1=ALU.mult),
                         reads=[p_st, EE, tri], writes=[s_t])
                    for dc in range(2):
                        P.op("pe", lambda e: e.matmul(p_num.ap, lhsT=qk_.ap[:, hh * 2 + dc, :], rhs=Cb[ch].ap[:, dc, :], start=(dc == 0), stop=False),
                             reads=[qk_, Cb[ch]], writes=[p_num])
                    P.op("pe", lambda e: e.matmul(p_num.ap, lhsT=s_t.ap, rhs=vh, start=False, stop=True), reads=[s_t, vt_], writes=[p_num])
                    for dc in range(2):
                        P.op("pe", lambda e: e.matmul(p_den.ap[:, 0:1], lhsT=qk_.ap[:, hh * 2 + dc, :], rhs=Nb[ch].ap[:, dc:dc + 1], start=(dc == 0), stop=False),
                             reads=[qk_, Nb[ch]], writes=[p_den])
                    P.op("pe", lambda e: e.matmul(p_den.ap[:, 0:1], lhsT=s_t.ap, rhs=onesb.ap[:, 0:1], start=False, stop=True), reads=[s_t, onesb], writes=[p_den])
                    P.op("act", lambda e: e.activation(out=d_n.ap, in_=p_den.ap[:, 0:1], func=AF.Abs), reads=[p_den], writes=[d_n])
                    P.op("dve", lambda e: e.tensor_tensor(out=d_n.ap, in0=d_n.ap, in1=cl, op=ALU.max), reads=[d_n, CL], writes=[d_n])
                    P.op("dve", lambda e: e.reciprocal(out=d_n.ap, in_=d_n.ap), reads=[d_n], writes=[d_n])
                    P.op("act", lambda e: e.activation(out=h_o.ap, in_=p_num.ap, func=AF.Copy, scale=d_n.ap[:, 0:1]), reads=[p_num, d_n], writes=[h_o])
                    P.dma("act", hs_d[d_][rows, hh * 512:(hh + 1) * 512], h_o.ap, reads=[h_o])
                    P.op("pool", lambda e: e.tensor_scalar(out=k_l.ap, in0=kt_.ap[:, hh * 256:(hh + 1) * 256], scalar1=e_, scalar2=None, op0=ALU.mult),
                         reads=[kt_, EE], writes=[k_l])
                    for dc in range(2):
                        P.op("pe", lambda e: e.matmul(p_c[dc].ap, lhsT=k_l.ap[:, dc * 128:(dc + 1) * 128], rhs=vh, start=True, stop=True), reads=[k_l, vt_], writes=[p_c[dc]])
                        P.op("pe", lambda e: e.matmul(p_nu.ap[:, dc:dc + 1], lhsT=k_l.ap[:, dc * 128:(dc + 1) * 128], rhs=onesb.ap[:, 0:1], start=True, stop=True),
                             reads=[k_l, onesb], writes=[p_nu])
                    for dc in range(2):
                        P.op("dve", lambda e: e.scalar_tensor_tensor(out=C32[ch].ap[:, dc, :], in0=C32[ch].ap[:, dc, :], scalar=cd, in1=p_c[dc].ap, op0=ALU.mult, op1=ALU.add),
                             reads=[C32[ch], DM, p_c[dc]], writes=[C32[ch]])
                    P.op("dve", lambda e: e.scalar_tensor_tensor(out=N32[ch].ap, in0=N32[ch].ap, scalar=cd, in1=p_nu.ap[:, 0:2], op0=ALU.mult, op1=ALU.add),
                         reads=[N32[ch], DM, p_nu], writes=[N32[ch]])
    P.barrier()
    out_tiles = list(range(NT if need_ctx else NTL))
    with ExitStack() as es:
        ng = Buf(k.sb(es, "mo_ng", [128, D])[:])
        P.dma("sp", ng.ap, a_norm_g[j:j + 1, :].to_broadcast([128, D]), writes=[ng])
        hb = [[Buf(k.sb(es, "mo_h%d_%d" % (d_, i), [128, D])[:]) for i in range(2)] for d_ in range(2)]
        ob = [Buf(k.sb(es, "mo_o%d" % i, [128, D])[:]) for i in range(2)]
        sq = Buf(k.sb(es, "mo_sq", [128, D])[:])
        ss = Buf(k.sb(es, "mo_ss", [128, 4])[:])
        hT = [Buf(k.sb(es, "mo_hT%d" % i, [128, KC, 128], BF16)[:]) for i in range(2)]
        def load(t):
            rows = slice(t * 128, (t + 1) * 128)
            h0, h1, o_ = hb[0][t % 2], hb[1][t % 2], ob[t % 2]
            P.dma("sp", h0.ap, hs_d[0][rows, :], writes=[h0])
            P.dma("act", h1.ap, hs_d[1][rows, :], writes=[h1])
            P.dma("sp", o_.ap, o_d[rows, :], writes=[o_])

        def body(t):
            h0, h1, o_, h_T = hb[0][t % 2], hb[1][t % 2], ob[t % 2], hT[t % 2]
            P.op("pool", lambda e: e.tensor_tensor(out=h0.ap, in0=h0.ap, in1=h1.ap, op=ALU.add), reads=[h0, h1], writes=[h0])
            P.op("act", lambda e: e.activation(out=sq.ap, in_=h0.ap, func=AF.Square), reads=[h0], writes=[sq])
            P.op("dve", lambda e: e.tensor_reduce(out=ss.ap, in_=sq.ap.rearrange("p (h d) -> p h d", d=512), axis=AX.X, op=ALU.add), reads=[sq], writes=[ss])
            rsqrt_eps(P, ss, ss.ap, [ss], 1.0 / 512.0)
            h3 = h0.ap.rearrange("p (h d) -> p h d", d=512)
            P.op("dve", lambda e: e.tensor_tensor(out=h3, in0=h3, in1=ss.ap.unsqueeze(2).to_broadcast([128, 4, 512]), op=ALU.mult), reads=[h0, ss], writes=[h0])
            P.op("pool", lambda e: e.tensor_tensor(out=o_.ap, in0=o_.ap, in1=ng.ap, op=ALU.mult), reads=[o_, ng], writes=[o_])
            P.op("dve", lambda e: e.tensor_tensor(out=h0.ap, in0=h0.ap, in1=o_.ap, op=ALU.mult), reads=[h0, o_], writes=[h0])
            transpose_to_T(S, h0, KC, h_T, t)
            P.dma("sp", hT_d[t], h_T.ap, reads=[h_T])
        pipelined(out_tiles, load, body)
    P.barrier()
    with ExitStack() as es:
        epi = make_store_epi(S, es, ymix, tag="mo5s")
        linear_tm(S, hT_d, KC, a_w_out[j], D, out_tiles, epi, tag="mo5")


def top16(P, src, work, vals, idxs, reads_extra=()):
    sv, vv, iv, wv = src, vals, idxs, work
    P.op("dve", lambda e: e.max(out=vv.ap[:, 0:8], in_=sv.ap), reads=[sv], writes=[vv])
    P.op("dve", lambda e: e.max_index(out=iv.ap[:, 0:8], in_max=vv.ap[:, 0:8], in_values=sv.ap), reads=[sv, vv], writes=[iv])
    P.op("dve", lambda e: e.match_replace(out=wv.ap, in_to_replace=vv.ap[:, 0:8], in_values=sv.ap, imm_value=NEG), reads=[sv, vv], writes=[wv])
    P.op("dve", lambda e: e.max(out=vv.ap[:, 8:16], in_=wv.ap), reads=[wv], writes=[vv])
    P.op("dve", lambda e: e.max_index(out=iv.ap[:, 8:16], in_max=vv.ap[:, 8:16], in_values=wv.ap), reads=[wv, vv], writes=[iv])


def phase_peer(S, li, p_w_q, p_k1, p_k2, p_u, p_v, iota_d, nt, enable=True):
    k, P, hT_d, h_d, ymix, ident, ps = S["k"], S["P"], S["hT_d"], S["h_d"], S["ymix"], S["ident"], S["ps"]
    tiles = list(range(nt))
    qpT_d = k.dscr("p_qT%d" % li, [16, 128, T])
    with ExitStack() as es:
        stg = [Buf(k.sb(es, "pq_s%d" % i, [128, 512])[:]) for i in range(4)]
        cnt = [0]

        def epi_f(c, t0, ntl, pt):
            s = stg[cnt[0] % 4]
            cnt[0] += 1
            n = ntl * 128
            P.op("act" if cnt[0] % 2 else "dve",
                 (lambda e: e.activation(out=s.ap[:, 0:n], in_=pt.ap[:, 0:n], func=AF.Copy)) if cnt[0] % 2 else (lambda e: e.tensor_copy(out=s.ap[:, 0:n], in_=pt.ap[:, 0:n])),
                 reads=[pt], writes=[s])
            P.dma("sp", qpT_d[c, :, t0 * 128:t0 * 128 + n], s.ap[:, 0:n], reads=[s])
        linear_fm(S, hT_d, KC, p_w_q[li], D, tiles, epi_f, tag="pq")
    P.barrier()
    with ExitStack() as es:
        kraw = Buf(k.sb(es, "pp_kraw", [128, 2, 128])[:])
        kT = Buf(k.sb(es, "pp_kT", [128, 2, 128])[:])
        P.dma("sp", kraw.ap[:, 0, :], p_k1[li], writes=[kraw])
        P.dma("sp", kraw.ap[:, 1, :], p_k2[li], writes=[kraw])
        for hf in range(2):
            pt = next_ps(S)
            P.op("pe", lambda e: e.transpose(out=pt.ap[:, 0:128], in_=kraw.ap[:, hf, :], identity=ident.ap), reads=[kraw, ident], writes=[pt])
            P.op("dve", lambda e: e.tensor_copy(out=kT.ap[:, hf, :], in_=pt.ap[:, 0:128]), reads=[pt], writes=[kT])
        io16 = Buf(k.sb(es, "pp_io", [128, 16])[:])
        P.dma("sp", io16.ap, iota_d[0:1, 0:16].to_broadcast([128, 16]), writes=[io16])
        qT = [Buf(k.sb(es, "pp_qT%d" % i, [128, 16, 128])[:]) for i in range(2)]
        sc = Buf(k.sb(es, "pp_sc", [128, 16, 128])[:])
        wk = Buf(k.sb(es, "pp_wk", [128, 256])[:])
        V12 = Buf(k.sb(es, "pp_V12", [128, 16, 16])[:])
        I12 = Buf(k.sb(es, "pp_I12", [128, 16, 16], U32)[:])
        I12f = Buf(k.sb(es, "pp_I12f", [128, 16, 16])[:])
        cand = Buf(k.sb(es, "pp_cand", [128, 8, 256])[:])
        SC = Buf(k.sb(es, "pp_SC", [128, 8, 16])[:])
        CI = Buf(k.sb(es, "pp_CI", [128, 8, 16], U32)[:])
        IA = Buf(k.sb(es, "pp_IA", [128, 8, 16], U32)[:])
        IB = Buf(k.sb(es, "pp_IB", [128, 8, 16], U32)[:])
        IAf = Buf(k.sb(es, "pp_IAf", [128, 8, 16])[:])
        IBf = Buf(k.sb(es, "pp_IBf", [128, 8, 16])[:])
        oh = Buf(k.sb(es, "pp_oh", [128, 8, 16, 16])[:])
        EI = Buf(k.sb(es, "pp_EI", [128, 8, 16])[:])
        EJ = Buf(k.sb(es, "pp_EJ", [128, 8, 16])[:])
        gate = Buf(k.sb(es, "pp_gate", [128, 8, 16])[:])
        gs = Buf(k.sb(es, "pp_gs", [128, 8])[:])
        if PEER_DENSE:
            Gd = S["Gd"]
            io128 = Buf(k.sb(es, "pp_io128", [128, 128])[:])
            P.dma("sp", io128.ap, iota_d[0:1, :].to_broadcast([128, 128]), writes=[io128])
            TT = Buf(k.sb(es, "pp_TT", [128, 3, 128])[:])
            TTb = Buf(k.sb(es, "pp_TTb", [128, 2, 128], BF16)[:])
            io128b = Buf(k.sb(es, "pp_io128b", [128, 128], BF16)[:])
            P.op("dve", lambda e: e.tensor_copy(out=io128b.ap, in_=io128.ap), reads=[io128], writes=[io128b])
            OH1 = [Buf(k.sb(es, "pp_OH1%d" % i, [128, 64, 128], BF16)[:]) for i in range(2)]
            OH2 = [Buf(k.sb(es, "pp_OH2%d" % i, [128, 64, 128], BF16)[:]) for i in range(2)]
            G_sb = Buf(k.sb(es, "pp_Gsb", [128, 128, 128], BF16)[:])
            G_sb2 = Buf(G_sb.ap)
            hb = acc = EX = [None, None]
        else:
            hb = [Buf(k.sb(es, "pp_h%d" % i, [128, D])[:]) for i in range(2)]
            EX = [Buf(k.sb(es, "pp_EX%d" % i, [128, 128], U32)[:]) for i in range(2)]
            av = Buf(k.sb(es, "pp_a", [128, 128])[:])
            hid = Buf(k.sb(es, "pp_hid", [128, 128])[:])
            junk = Buf(k.sb(es, "pp_junk", [128, D])[:])
            acc = [Buf(k.sb(es, "pp_acc%d" % i, [128, D])[:]) for i in range(2)]
            NG = 6
            gb = [Buf(k.sb(es, "pp_g%d" % i, [128, D])[:]) for i in range(NG)]
        gi = 0
        sc2 = [sc, Buf(k.sb(es, "pp_scb", [128, 16, 128])[:])]

        def emit_scores(t):
            q_ = qT[t % 2]
            sc_ = sc2[t % 2]
            P.dma("sp", q_.ap, qpT_d[:, :, t * 128:(t + 1) * 128].rearrange("c d t -> d c t"), writes=[q_])
            for g in range(4):
                pt = ps[5 + (t * 4 + g) % 3]
                for q4 in range(4):
                    c = g * 4 + q4
                    P.op("pe", lambda e: e.matmul(pt.ap[:, q4 * 128:(q4 + 1) * 128], lhsT=q_.ap[:, c, :], rhs=kT.ap[:, c % 2, :], start=True, stop=True),
                         reads=[q_, kT], writes=[pt])
                P.op("act", lambda e: e.activation(out=sc_.ap[:, g * 4:(g + 1) * 4, :], in_=pt.ap.rearrange("p (a b) -> p a b", a=4), func=AF.Copy), reads=[pt], writes=[sc_])
        emit_scores(tiles[0])
        for ti, t in enumerate(tiles):
            q_, h_, ex_, acc_ = qT[t % 2], hb[t % 2], EX[t % 2], acc[t % 2]
            if ti + 1 < len(tiles):
                emit_scores(tiles[ti + 1])
            sc = sc2[t % 2]
            if not PEER_DENSE:
                P.dma("act", h_.ap, h_d[t * 128:(t + 1) * 128, :], writes=[h_])
            for c in range(16):
                sv = Buf(sc.ap[:, c, :]); sv.w = sc.w
                wv = Buf(wk.ap[:, 0:128]); wv.w = wk.w; wv.r = wk.r
                vv = Buf(V12.ap[:, c, :]); vv.w = V12.w; vv.r = V12.r
                iv = Buf(I12.ap[:, c, :]); iv.w = I12.w; iv.r = I12.r
                top16(P, sv, wv, vv, iv)
                wk.w, V12.w, I12.w = wv.w, vv.w, iv.w
                wk.r, V12.r, I12.r = wv.r, vv.r, iv.r
                sc.r.update(sv.r)
            V4 = V12.ap.rearrange("p (h f) a -> p h f a", f=2)
            P.op("dve", lambda e: e.tensor_tensor(out=cand.ap.rearrange("p h (a b) -> p h a b", b=16),
                                                  in0=V4[:, :, 0, :].unsqueeze(3).to_broadcast([128, 8, 16, 16]),
                                                  in1=V4[:, :, 1, :].unsqueeze(2).to_broadcast([128, 8, 16, 16]), op=ALU.add), reads=[V12], writes=[cand])
            for hh in range(8):
                sv = Buf(cand.ap[:, hh, :]); sv.w = cand.w
                wv = Buf(wk.ap[:, 0:256]); wv.w = wk.w; wv.r = wk.r
                vv = Buf(SC.ap[:, hh, :]); vv.w = SC.w; vv.r = SC.r
                iv = Buf(CI.ap[:, hh, :]); iv.w = CI.w; iv.r = CI.r
                top16(P, sv, wv, vv, iv)
                wk.w, SC.w, CI.w = wv.w, vv.w, iv.w
                wk.r, SC.r, CI.r = wv.r, vv.r, iv.r
                cand.r.update(sv.r)
            P.op("dve", lambda e: e.tensor_tensor(out=gate.ap, in0=SC.ap, in1=SC.ap[:, :, 0:1].to_broadcast([128, 8, 16]), op=ALU.subtract), reads=[SC], writes=[gate])
            P.op("act", lambda e: e.activation(out=gate.ap, in_=gate.ap, func=AF.Exp), reads=[gate], writes=[gate])
            P.op("dve", lambda e: e.tensor_reduce(out=gs.ap, in_=gate.ap, axis=AX.X, op=ALU.add), reads=[gate], writes=[gs])
            P.op("dve", lambda e: e.reciprocal(out=gs.ap, in_=gs.ap), reads=[gs], writes=[gs])
            P.op("dve", lambda e: e.tensor_tensor(out=gate.ap, in0=gate.ap, in1=gs.ap.unsqueeze(2).to_broadcast([128, 8, 16]), op=ALU.mult), reads=[gate, gs], writes=[gate])
            P.op("dve", lambda e: e.tensor_single_scalar(out=IA.ap, in_=CI.ap, scalar=4, op=ALU.logical_shift_right), reads=[CI], writes=[IA])
            P.op("dve", lambda e: e.tensor_single_scalar(out=IB.ap, in_=CI.ap, scalar=15, op=ALU.bitwise_and), reads=[CI], writes=[IB])
            P.op("dve", lambda e: e.tensor_copy(out=IAf.ap, in_=IA.ap), reads=[IA], writes=[IAf])
            P.op("dve", lambda e: e.tensor_copy(out=IBf.ap, in_=IB.ap), reads=[IB], writes=[IBf])
            P.op("dve", lambda e: e.tensor_copy(out=I12f.ap, in_=I12.ap), reads=[I12], writes=[I12f])
            I4 = I12f.ap.rearrange("p (h f) a -> p h f a", f=2)
            io4 = io16.ap.unsqueeze(1).unsqueeze(1).to_broadcast([128, 8, 16, 16])
            for (src, f, dst) in ((IAf, 0, EI), (IBf, 1, EJ)):
                P.op("dve", lambda e: e.tensor_tensor(out=oh.ap, in0=src.ap.unsqueeze(3).to_broadcast([128, 8, 16, 16]), in1=io4, op=ALU.is_equal), reads=[src, io16], writes=[oh])
                P.op("dve", lambda e: e.tensor_tensor(out=oh.ap, in0=oh.ap, in1=I4[:, :, f, :].unsqueeze(2).to_broadcast([128, 8, 16, 16]), op=ALU.mult), reads=[oh, I12f], writes=[oh])
                P.op("dve", lambda e: e.tensor_reduce(out=dst.ap, in_=oh.ap, axis=AX.X, op=ALU.add), reads=[oh], writes=[dst])
            if PEER_DENSE:
                ptT = ps[4]
                for q3, src in enumerate((EI, EJ, gate)):
                    P.op("pe", lambda e: e.transpose(out=ptT.ap[:, q3 * 128:(q3 + 1) * 128], in_=src.ap.rearrange("p h a -> p (h a)"), identity=ident.ap),
                         reads=[src, ident], writes=[ptT])
                P.op("act", lambda e: e.activation(out=TT.ap.rearrange("p a t -> p (a t)"), in_=ptT.ap[:, 0:384], func=AF.Copy), reads=[ptT], writes=[TT])
                P.op("act", lambda e: e.activation(out=TTb.ap.rearrange("p a t -> p (a t)"), in_=ptT.ap[:, 0:256], func=AF.Copy), reads=[ptT], writes=[TTb])
                iob = io128b.ap.unsqueeze(1).to_broadcast([128, 64, 128])
                for hf in range(2):
                    o1, o2 = OH1[hf], OH2[hf]
                    tsl = slice(hf * 64, (hf + 1) * 64)
                    P.op("dve", lambda e: e.tensor_tensor(out=o1.ap, in0=iob, in1=TTb.ap[:, 0, tsl].unsqueeze(2).to_broadcast([128, 64, 128]), op=ALU.is_equal),
                         reads=[io128b, TTb], writes=[o1])
                    P.op("pool", lambda e: e.tensor_tensor(out=o1.ap, in0=o1.ap, in1=TT.ap[:, 2, tsl].unsqueeze(2).to_broadcast([128, 64, 128]), op=ALU.mult),
                         reads=[o1, TT], writes=[o1])
                    P.op("dve", lambda e: e.tensor_tensor(out=o2.ap, in0=iob, in1=TTb.ap[:, 1, tsl].unsqueeze(2).to_broadcast([128, 64, 128]), op=ALU.is_equal),
                         reads=[io128b, TTb], writes=[o2])
                for t4 in range(32):
                    pt = ps[t4 % 4]
                    hf = t4 // 16
                    for q in range(4):
                        tok = (t4 % 16) * 4 + q
                        P.op("pe", lambda e: e.matmul(pt.ap[:, q * 128:(q + 1) * 128], lhsT=OH2[hf].ap[:, tok, :], rhs=OH1[hf].ap[:, tok, :], start=True, stop=True),
                             reads=[OH1[hf], OH2[hf]], writes=[pt])
                    P.op("act", lambda e: e.activation(out=G_sb.ap[:, :, t4 * 4:(t4 + 1) * 4], in_=pt.ap.rearrange("p (q i) -> p i q", q=4), func=AF.Copy), reads=[pt], writes=[G_sb])
                for cg in range(8):
                    P.dma("sp" if cg % 2 == 0 else "act", Gd[cg * 16:(cg + 1) * 16, :, t * 128:(t + 1) * 128].rearrange("c p t -> p c t"), G_sb.ap[:, cg * 16:(cg + 1) * 16, :], reads=[G_sb])
                continue
            P.op("dve", lambda e: e.scalar_tensor_tensor(out=EI.ap, in0=EI.ap, scalar=128.0, in1=EJ.ap, op0=ALU.mult, op1=ALU.add), reads=[EI, EJ], writes=[EI])
            if li > 0:
                P.op("dve", lambda e: e.tensor_scalar_add(out=EI.ap, in0=EI.ap, scalar1=float(li * 16384)), reads=[EI], writes=[EI])
            P.op("dve", lambda e: e.tensor_copy(out=ex_.ap, in_=EI.ap.rearrange("p h a -> p (h a)")), reads=[EI], writes=[ex_])
            for s in range(128):
                g_ = gb[gi % NG]
                gi += 1
                P.dma("pool", g_.ap, p_u.rearrange("l e d -> (l e) d"), reads=[ex_], writes=[g_], indirect=bass.IndirectOffsetOnAxis(ap=ex_.ap[:, s:s + 1], axis=0))
                P.op("dve", lambda e: e.scalar_tensor_tensor(out=junk.ap, in0=h_.ap, scalar=1.0, in1=g_.ap, op0=ALU.mult, op1=ALU.mult, accum_out=av.ap[:, s:s + 1]),
                     reads=[h_, g_], writes=[junk, av])
            P.op("act", lambda e: e.activation(out=hid.ap, in_=av.ap, func=AF.Gelu_apprx_tanh), reads=[av], writes=[hid])
            P.op("dve", lambda e: e.tensor_tensor(out=hid.ap, in0=hid.ap, in1=gate.ap.rearrange("p h a -> p (h a)"), op=ALU.mult), reads=[hid, gate], writes=[hid])
            for s in range(128):
                g_ = gb[gi % NG]
                gi += 1
                P.dma("pool", g_.ap, p_v.rearrange("l e d -> (l e) d"), reads=[ex_], writes=[g_], indirect=bass.IndirectOffsetOnAxis(ap=ex_.ap[:, s:s + 1], axis=0))
                if s == 0:
                    P.op("dve", lambda e: e.tensor_scalar(out=acc_.ap, in0=g_.ap, scalar1=hid.ap[:, 0:1], scalar2=None, op0=ALU.mult), reads=[g_, hid], writes=[acc_])
                else:
                    P.op("dve", lambda e: e.scalar_tensor_tensor(out=acc_.ap, in0=g_.ap, scalar=hid.ap[:, s:s + 1], in1=acc_.ap, op0=ALU.mult, op1=ALU.add),
                         reads=[g_, hid, acc_], writes=[acc_])
            P.dma("sp", ymix[t * 128:(t + 1) * 128, :], acc_.ap, reads=[acc_])
    if PEER_DENSE:
        P.barrier()
        peer_sweep(S, li, p_u, p_v, nt)


def peer_sweep(S, li, p_u, p_v, nt):
    k, P, hT_d, ymix, identb, ps, psb, Gd = S["k"], S["P"], S["hT_d"], S["ymix"], S["identb"], S["ps"], S["psb"], S["Gd"]
    SCH = 4
    passes = [list(range(i, i + 8)) for i in range(0, NTL, 8)]
    if nt == NT:
        passes[0].append(32)
        passes[1].append(33)
    with ExitStack() as es:
        hTb = [Buf(k.sb(es, "ps_hT%d" % i, [128, KC, 512], BF16)[:]) for i in range(2)]
        hTb.append(Buf(k.sb(es, "ps_hT2", [128, KC, 128], BF16)[:]))
        f_sb = [Buf(k.sb(es, "ps_f%d" % i, [128, D])[:]) for i in range(9)]
        Ubf = [Buf(k.sb(es, "ps_U%d" % i, [128, D], BF16)[:]) for i in range(2)]
        UT = [Buf(k.sb(es, "ps_UT%d" % i, [128, KC, 128], BF16)[:]) for i in range(2 * SCH)]
        Vbf = [Buf(k.sb(es, "ps_V%d" % i, [128, D], BF16)[:]) for i in range(2 * SCH)]
        hid = [Buf(k.sb(es, "ps_hid%d" % i, [128, 512], BF16)[:]) for i in range(2 * SCH)]
        Gc = [Buf(k.sb(es, "ps_G%d" % i, [128, 512], BF16)[:]) for i in range(4)]
        ui = 0
        sci = 0
        gci = 0
        api = 0
        fpi = 0
        evi = 0
        for tl in passes:
            groups = [tl[i:i + 4] for i in range(0, len(tl), 4)]
            for gi_, grp in enumerate(groups):
                for i, t in enumerate(grp):
                    P.dma("sp", hTb[gi_].ap[:, :, i * 128:(i + 1) * 128], hT_d[t], writes=[hTb[gi_]])
            for sc in range(128 // SCH):
                base = (sci % 2) * SCH
                sci += 1
                for c8 in range(SCH):
                    c = sc * SCH + c8
                    u_, ut_, v_ = Ubf[ui % 2], UT[base + c8], Vbf[base + c8]
                    ui += 1
                    P.dma("pool", u_.ap, p_u[li, c * 128:(c + 1) * 128, :], writes=[u_])
                    P.dma("pool", v_.ap, p_v[li, c * 128:(c + 1) * 128, :], writes=[v_])
                    for kc in range(KC):
                        pb = psb[kc // 8]
                        P.op("pe", lambda e: e.transpose(out=pb.ap[:, (kc % 8) * 128:(kc % 8 + 1) * 128], in_=u_.ap[:, kc * 128:(kc + 1) * 128], identity=identb.ap),
                             reads=[u_, identb], writes=[pb])
                    P.op("act", lambda e: e.activation(out=ut_.ap[:, 0:8, :], in_=psb[0].ap.rearrange("p (q t) -> p q t", q=8), func=AF.Copy), reads=[psb[0]], writes=[ut_])
                    P.op("dve", lambda e: e.tensor_copy(out=ut_.ap[:, 8:16, :], in_=psb[1].ap.rearrange("p (q t) -> p q t", q=8)), reads=[psb[1]], writes=[ut_])
                for gi_, grp in enumerate(groups):
                    ng = len(grp) * 128
                    tok0 = grp[0] * 128
                    hset = (gci % 2) * SCH
                    gci += 1
                    for c8 in range(SCH):
                        c = sc * SCH + c8
                        g_ = Gc[api % 4]
                        a_ps = ps[api % 2]
                        api += 1
                        h_ = hid[hset + c8]
                        P.dma("sp", g_.ap[:, 0:ng], Gd[c, :, tok0:tok0 + ng], writes=[g_])
                        for kc in range(KC):
                            P.op("pe", lambda e: e.matmul(a_ps.ap[:, 0:ng], lhsT=UT[base + c8].ap[:, kc, :], rhs=hTb[gi_].ap[:, kc, 0:ng], start=(kc == 0), stop=(kc == KC - 1)),
                                 reads=[UT[base + c8], hTb[gi_]], writes=[a_ps])
                        P.op("act", lambda e: e.activation(out=h_.ap[:, 0:ng], in_=a_ps.ap[:, 0:ng], func=AF.Gelu_apprx_tanh), reads=[a_ps], writes=[h_])
                        P.op("pool", lambda e: e.tensor_tensor(out=h_.ap[:, 0:ng], in0=h_.ap[:, 0:ng], in1=g_.ap[:, 0:ng], op=ALU.mult), reads=[h_, g_], writes=[h_])
                    for i, t in enumerate(grp):
                        fb = f_sb[gi_ * 4 + i]
                        for dch in range(4):
                            f_ps = ps[2 + fpi % 4]
                            fpi += 1
                            for c8 in range(SCH):
                                P.op("pe", lambda e: e.matmul(f_ps.ap, lhsT=hid[hset + c8].ap[:, i * 128:(i + 1) * 128], rhs=Vbf[base + c8].ap[:, dch * 512:(dch + 1) * 512],
                                                              start=(c8 == 0), stop=(c8 == SCH - 1)), reads=[hid[hset + c8], Vbf[base + c8]], writes=[f_ps])
                            dst = fb.ap[:, dch * 512:(dch + 1) * 512]
                            if sc == 0:
                                P.op("act", lambda e: e.activation(out=dst, in_=f_ps.ap, func=AF.Copy), reads=[f_ps], writes=[fb])
                            else:
                                P.op("dve", lambda e: e.tensor_tensor(out=dst, in0=dst, in1=f_ps.ap, op=ALU.add), reads=[fb, f_ps], writes=[fb])
            for i, t in enumerate(tl):
                P.dma("sp", ymix[t * 128:(t + 1) * 128, :], f_sb[i].ap, reads=[f_sb[i]])


_CACHE = {}


def host_consts():
    ident = np.eye(128, dtype=np.float32)
    s = np.arange(128)
    tri = np.stack([(s[:, None] <= s[None, :]), (s[:, None] >= s[None, :])]).astype(np.float32)
    t = np.arange(SEQ)
    rows = (t // 64).astype(np.float32)
    cols = (t % 64).astype(np.float32)
    freqs = (10000.0 ** (-np.arange(32, dtype=np.float32) / np.float32(32))).astype(np.float32)
    ang = np.concatenate([rows[:, None] * freqs[None, :], cols[:, None] * freqs[None, :]], axis=1).astype(np.float32)
    rope = np.stack([np.cos(ang), np.sin(ang)], axis=1).astype(np.float32)
    iota16 = np.arange(128, dtype=np.float32)[None, :]
    return dict(ident=ident, tri=tri, rope=rope, iota16=iota16)


def make_in_maps(inputs):
    consts = host_consts()
    shared = {n: np.ascontiguousarray(inputs[n]) for n in (
        "w_mod", "b_mod", "ln_g", "ln_b", "a_w_in", "a_b_gate", "a_norm_g", "a_w_out", "b_w_in", "b_b_in", "b_ln_g", "b_ln_b",
        "b_w_s", "b_b_s", "b_w_out", "c_w_qkv", "c_q_g", "c_k_g", "c_w_out", "p_w_q", "p_k1", "p_k2", "p_u", "p_v")}
    shared.update(consts)
    maps = []
    for b in range(8):
        m = dict(shared)
        m["xin"] = np.ascontiguousarray(np.concatenate([inputs["x"][b], inputs["ctx"][b]], axis=0))
        m["cc"] = np.ascontiguousarray(np.stack([inputs["c"][b], inputs["c_ctx"]], axis=0))
        maps.append(m)
    return maps


def kernel(**inputs):
    inputs = {k_: np.asarray(v) for k_, v in inputs.items()}
    if "nc" not in _CACHE:
        _CACHE["nc"] = build_program()[0]
    nc = _CACHE["nc"]
    maps = make_in_maps(inputs)
    res = run_bass_kernel_spmd(nc, maps, core_ids=list(range(8)))
    out = np.stack([np.asarray(r["y"]).reshape(SEQ, D) for r in res.results], axis=0)
    return out.astype(np.float32)
```
